# Optimizing a Trainium2 kernel written in Bass

```python
import math
import jax, jax.numpy as jnp
from jax import lax
import numpy as np

D_MODEL = 1024
BATCH = 4
SEQ = 4096
DEPTH = 2

EXPAND = 2
D_INNER = EXPAND * D_MODEL
IN_COLS = 3 * D_INNER
POOL_WIDTH = D_INNER // 2
POOL_GROUPS = 4
POOL_GROUP_DIM = POOL_WIDTH // POOL_GROUPS
POOL_WINDOWS = (2, 4, 8, 16)
ATTN_WIDTH = D_INNER - POOL_WIDTH
N_DIFF_HEADS = 8
DIFF_VDIM = ATTN_WIDTH // N_DIFF_HEADS
DIFF_QKDIM = DIFF_VDIM // 2
ROPE_THETA = 500000.0
ROPE_DIM = DIFF_QKDIM // 4
Q_BLOCK = 128
SGU_CHUNK = 128
SGU_GROUPS = 8
SGU_GROUP_DIM = D_INNER // SGU_GROUPS
EPS = 1e-6
N_AB = (DEPTH + 1) // 2
N_C = DEPTH // 2

kernel_name = "hybrid_pool_diffattn_sgu_trunk"


def rms_norm(x, g, eps=EPS):
    xf = x.astype(jnp.float32)
    y = xf * lax.rsqrt(jnp.mean(xf * xf, axis=-1, keepdims=True) + eps)
    return (y * g.astype(jnp.float32)).astype(x.dtype)


def multiscale_pool(a, pool_w, pool_scale):
    B, S, _ = a.shape
    ag = a.reshape(B, S, POOL_GROUPS, POOL_GROUP_DIM).astype(jnp.float32)
    csum = lax.cumsum(ag, axis=1)
    pos = jnp.arange(S)
    means = []
    for g, w in enumerate(POOL_WINDOWS):
        c = csum[:, :, g]
        lagged = jnp.pad(c, ((0, 0), (w, 0), (0, 0)))[:, :S]
        cnt = jnp.minimum(pos + 1, w).astype(jnp.float32)[None, :, None]
        means.append((c - lagged) / cnt)
    pooled = (jnp.stack(means, axis=2) - ag).astype(a.dtype)
    y = jnp.einsum('bsgc,gcd->bsgd', pooled, pool_w)
    y = y * pool_scale.reshape(POOL_GROUPS, POOL_GROUP_DIM)
    return y.reshape(B, S, POOL_WIDTH)


def apply_partial_rope(t, positions):
    half = ROPE_DIM // 2
    inv_freq = jnp.power(ROPE_THETA, -jnp.arange(half, dtype=jnp.float32) * 2.0 / ROPE_DIM)
    ang = positions.astype(jnp.float32)[:, :, None] * inv_freq
    cos = jnp.cos(ang)[:, :, None, None, :]
    sin = jnp.sin(ang)[:, :, None, None, :]
    tf = t.astype(jnp.float32)
    x1 = tf[..., :half]
    x2 = tf[..., half:ROPE_DIM]
    rot = jnp.concatenate([x1 * cos - x2 * sin, x2 * cos + x1 * sin, tf[..., ROPE_DIM:]], axis=-1)
    return rot.astype(t.dtype)


def diff_attention(q, k, v, lam, positions):
    B, S = q.shape[:2]
    q = apply_partial_rope(q, positions)
    k = apply_partial_rope(k, positions)
    nb = S // Q_BLOCK
    qb = q.reshape(B, nb, Q_BLOCK, N_DIFF_HEADS, 2, DIFF_QKDIM).swapaxes(0, 1)
    kpos = jnp.arange(S)
    scale = DIFF_QKDIM ** -0.5

    def one_block(args):
        q_blk, blk = args
        s = jnp.einsum('bqhcd,bkhcd->bhcqk', q_blk, k).astype(jnp.float32) * scale
        qpos = blk * Q_BLOCK + jnp.arange(Q_BLOCK)
        s = jnp.where(kpos[None, :] <= qpos[:, None], s, -jnp.inf)
        p = jax.nn.softmax(s, axis=-1)
        w = (p[:, :, 0] - lam * p[:, :, 1]).astype(v.dtype)
        return jnp.einsum('bhqk,bkhd->bqhd', w, v)

    o = lax.map(one_block, (qb, jnp.arange(nb)))
    return o.swapaxes(0, 1).reshape(B, S, N_DIFF_HEADS, DIFF_VDIM)


def chunked_sgu(u, v, ln_g, ln_b, w_s, b_s):
    B, S, _ = u.shape
    vf = v.astype(jnp.float32)
    mu = jnp.mean(vf, axis=-1, keepdims=True)
    var = jnp.mean(jnp.square(vf - mu), axis=-1, keepdims=True)
    vn = ((vf - mu) * lax.rsqrt(var + EPS) * ln_g + ln_b).astype(v.dtype)
    nc = S // SGU_CHUNK
    vc = vn.reshape(B, nc, SGU_CHUNK, SGU_GROUPS, SGU_GROUP_DIM)
    mask = jnp.tril(jnp.ones((SGU_CHUNK, SGU_CHUNK), dtype=bool))
    ws = jnp.where(mask[None], w_s, jnp.zeros_like(w_s))
    mixed = jnp.einsum('gts,bcsgd->bctgd', ws, vc) + b_s.T[None, None, :, :, None]
    return u * mixed.reshape(B, S, D_INNER)


def setup_inputs(seed: int = 0) -> dict:
    key = jax.random.key(seed)
    ks = jax.random.split(key, 20)
    f32 = jnp.float32
    nrm = lambda k, shape, s: (jax.random.normal(k, shape, f32) * s).astype(f32)
    x = jax.random.normal(ks[0], (BATCH, SEQ, D_MODEL), f32)
    positions = jnp.broadcast_to(jnp.arange(SEQ, dtype=jnp.int32), (BATCH, SEQ))
    pre_norm = 1.0 + nrm(ks[1], (DEPTH, D_MODEL), 0.02)
    post_norm = 1.0 + nrm(ks[2], (DEPTH, D_MODEL), 0.02)
    w_in = nrm(ks[3], (DEPTH, D_MODEL, IN_COLS), D_MODEL ** -0.5)
    w_out = nrm(ks[4], (DEPTH, D_INNER, D_MODEL), D_INNER ** -0.5)
    pool_w = nrm(ks[5], (N_AB, POOL_GROUPS, POOL_GROUP_DIM, POOL_GROUP_DIM), POOL_GROUP_DIM ** -0.5)
    pool_scale = 1.0 + nrm(ks[6], (N_AB, POOL_WIDTH), 0.02)
    lam_q1 = nrm(ks[7], (N_AB, DIFF_QKDIM), 0.1)
    lam_k1 = nrm(ks[8], (N_AB, DIFF_QKDIM), 0.1)
    lam_q2 = nrm(ks[9], (N_AB, DIFF_QKDIM), 0.1)
    lam_k2 = nrm(ks[10], (N_AB, DIFF_QKDIM), 0.1)
    diff_subln = 1.0 + nrm(ks[11], (N_AB, DIFF_VDIM), 0.02)
    sgu_ln_g = 1.0 + nrm(ks[12], (N_C, D_INNER), 0.02)
    sgu_ln_b = nrm(ks[13], (N_C, D_INNER), 0.02)
    sgu_w = nrm(ks[14], (N_C, SGU_GROUPS, SGU_CHUNK, SGU_CHUNK), SGU_CHUNK ** -0.5)
    sgu_b = 1.0 + nrm(ks[15], (N_C, SGU_GROUPS, SGU_CHUNK), 0.02)
    return {"x": x, "positions": positions, "pre_norm": pre_norm, "post_norm": post_norm,
            "w_in": w_in, "w_out": w_out, "pool_w": pool_w, "pool_scale": pool_scale,
            "lam_q1": lam_q1, "lam_k1": lam_k1, "lam_q2": lam_q2, "lam_k2": lam_k2,
            "diff_subln": diff_subln, "sgu_ln_g": sgu_ln_g, "sgu_ln_b": sgu_ln_b,
            "sgu_w": sgu_w, "sgu_b": sgu_b}


def reference(x, positions, pre_norm, post_norm, w_in, w_out, pool_w, pool_scale,
              lam_q1, lam_k1, lam_q2, lam_k2, diff_subln, sgu_ln_g, sgu_ln_b, sgu_w, sgu_b):
    B, S, _ = x.shape
    for layer in range(DEPTH):
        h = rms_norm(x, pre_norm[layer])
        proj = jnp.einsum('bsd,de->bse', h, w_in[layer])
        if layer % 2 == 0:
            i = layer // 2
            a = proj[..., :POOL_WIDTH]
            q = proj[..., POOL_WIDTH:2 * POOL_WIDTH].reshape(B, S, N_DIFF_HEADS, 2, DIFF_QKDIM)
            k = proj[..., 2 * POOL_WIDTH:3 * POOL_WIDTH].reshape(B, S, N_DIFF_HEADS, 2, DIFF_QKDIM)
            v = proj[..., 3 * POOL_WIDTH:D_INNER + ATTN_WIDTH * 2].reshape(B, S, N_DIFF_HEADS, DIFF_VDIM)
            gate = proj[..., 2 * D_INNER:]
            pool_out = multiscale_pool(a, pool_w[i], pool_scale[i])
            lam_init = 0.8 - 0.6 * math.exp(-0.3 * layer)
            lam = (jnp.exp(jnp.sum(lam_q1[i].astype(jnp.float32) * lam_k1[i].astype(jnp.float32)))
                   - jnp.exp(jnp.sum(lam_q2[i].astype(jnp.float32) * lam_k2[i].astype(jnp.float32)))
                   + lam_init)
            o = diff_attention(q, k, v, lam, positions)
            o = rms_norm(o, diff_subln[i], eps=1e-5) * (1.0 - lam_init)
            y = jnp.concatenate([pool_out, o.reshape(B, S, ATTN_WIDTH)], axis=-1)
        else:
            i = layer // 2
            u = jax.nn.gelu(proj[..., :D_INNER])
            v = jax.nn.gelu(proj[..., D_INNER:2 * D_INNER])
            gate = proj[..., 2 * D_INNER:]
            y = chunked_sgu(u, v, sgu_ln_g[i], sgu_ln_b[i], sgu_w[i], sgu_b[i])
        out = jnp.einsum('bse,ed->bsd', jax.nn.silu(gate) * y, w_out[layer])
        x = x + rms_norm(out, post_norm[layer])
    return x
```

```python
import math
import numpy as np
import concourse.bass as bass
import concourse.mybir as mybir
from concourse.bass_utils import run_bass_kernel_spmd
from contextlib import ExitStack

F32 = mybir.dt.float32
BF16 = mybir.dt.bfloat16
I32 = mybir.dt.int32
AF = mybir.ActivationFunctionType
ALU = mybir.AluOpType
AX = mybir.AxisListType

D = 1024
NS = 16
TOK = 2048
NALL = 4352
NEG = -30000.0
EPS = 1e-6
LAM_INIT = 0.8 - 0.6 * math.exp(-0.3 * 0)
MAGIC = 12582912.0
SEM_LIMIT = 30000


class Dep:
    __slots__ = ("w", "r")

    def __init__(self):
        self.w = None
        self.r = {}


class Tok:
    __slots__ = ("key", "eng", "val")

    def __init__(self, key, eng, val):
        self.key = key
        self.eng = eng
        self.val = val


class Sched:
    ENG = ("pe", "act", "dve", "pool", "sp")

    def __init__(self, nc, es):
        self.nc = nc
        self.es = es
        self.e = dict(pe=nc.tensor, act=nc.scalar, dve=nc.vector, pool=nc.gpsimd, sp=nc.sync)
        self.gen = {k: 0 for k in self.ENG}
        self.semh = {}
        self.cnt = {}
        for k in self.ENG:
            self._newsem(k)
        self.lazy = {k: [] for k in self.ENG}
        self.seen = {k: {} for k in self.ENG}
        self.nwaits = 0
        self.nins = {k: 0 for k in self.ENG}

    def _newsem(self, eng):
        key = "%s%d" % (eng, self.gen[eng])
        self.semh[key] = self.es.enter_context(self.nc.semaphore("s_" + key))
        self.cnt[key] = 0
        return key

    def _curkey(self, eng):
        return "%s%d" % (eng, self.gen[eng])

    def _wait(self, eng, toks):
        need = {}
        for t in toks:
            if t is None:
                continue
            if t.eng == eng and eng == "pe":
                continue
            if t.val is None:
                raise RuntimeError("wait on instruction without inc: %s" % (t.key,))
            if need.get(t.key, 0) < t.val:
                need[t.key] = t.val
        for key, val in need.items():
            if self.seen[eng].get(key, 0) >= val:
                continue
            self.e[eng].wait_ge(self.semh[key], val)
            self.seen[eng][key] = val
            self.nwaits += 1

    def _deps(self, eng, reads, writes):
        toks = []
        for d in reads:
            toks.append(d.w)
        for d in writes:
            if d.w is not None and d.w.eng != eng:
                toks.append(d.w)
            for t in d.r.values():
                if t.eng == eng:
                    continue
                toks.append(t)
        self._wait(eng, toks)

    def _record(self, tok, reads, writes):
        for d in reads:
            d.r[tok.key] = tok
        for d in writes:
            d.w = tok
            d.r = {}

    def op(self, eng, fn, reads=(), writes=(), inc=True):
        self._deps(eng, reads, writes)
        ins = fn(self.e[eng])
        self.nins[eng] += 1
        key = self._curkey(eng)
        tok = Tok(key, eng, None)
        if inc:
            ins.then_inc(self.semh[key], 1)
            self.cnt[key] += 1
            tok.val = self.cnt[key]
            for t in self.lazy[eng]:
                t.val = tok.val
            self.lazy[eng] = []
            if self.cnt[key] >= SEM_LIMIT:
                self.gen[eng] += 1
                self._newsem(eng)
        else:
            self.lazy[eng].append(tok)
        self._record(tok, reads, writes)
        return tok

    def dma(self, q, out, in_, reads=(), writes=(), key=None):
        self._deps(q, reads, writes)
        if key is None:
            self.nauto = getattr(self, "nauto", 0) + 1
            key = "auto%d" % self.nauto
        key = "d_" + key
        if key not in self.semh:
            self.semh[key] = self.es.enter_context(self.nc.semaphore(key))
            self.cnt[key] = 0
        self.e[q].dma_start(out=out, in_=in_).then_inc(self.semh[key], 16)
        self.nins[q] += 1
        self.cnt[key] += 16
        assert self.cnt[key] < 2 * SEM_LIMIT
        tok = Tok(key, "dma", self.cnt[key])
        self._record(tok, reads, writes)
        return tok

    def barrier(self):
        for e in self.ENG:
            assert not self.lazy[e], "barrier with un-incremented %s instructions" % e
        keys = [(k, v) for k, v in self.cnt.items() if v > 0]
        for eng in self.ENG:
            own = self._curkey(eng)
            for key, val in keys:
                if key == own or self.seen[eng].get(key, 0) >= val:
                    continue
                self.e[eng].wait_ge(self.semh[key], val)
                self.seen[eng][key] = val
                self.nwaits += 1

    def wait_all(self, eng, deps):
        toks = []
        for d in deps:
            toks.append(d.w)
            toks.extend(d.r.values())
        self._wait(eng, toks)


def bcast_last(ap, n):
    dims = [list(x) for x in ap.ap]
    assert dims[-1][1] == 1
    dims[-1] = [0, n]
    return bass.AP(ap.tensor, ap.offset, dims)


def bcast_mid(ap, n):
    dims = [list(x) for x in ap.ap]
    assert len(dims) == 2
    return bass.AP(ap.tensor, ap.offset, [dims[0], [0, n], dims[1]])


class Builder:
    def __init__(self, mode, debug=False):
        self.mode = mode
        self.debug = debug
        self.dbg_names = []
        self.nc = bass.Bass("TRN2", target_bir_lowering=False)
        self.es = ExitStack()

    def dram_in(self, name, shape, dt=F32):
        return self.nc.dram_tensor(name, list(shape), dt, kind="ExternalInput").ap()

    def dump(self, name, ap, deps):
        if not getattr(self, "debug", False):
            return
        t = self.nc.dram_tensor("dbg_" + name, list(ap.shape), ap.dtype, kind="ExternalOutput").ap()
        self.S.dma("sp", t[:], ap, reads=deps)
        self.dbg_names.append("dbg_" + name)

    def sb(self, es, name, shape, dt):
        return es.enter_context(self.nc.sbuf_tensor("sb_" + name, list(shape), dt))

    def build(self):
        nc = self.nc
        mode = self.mode
        with self.es as es:
            S = self.S = Sched(nc, es)
            dr = self.dr = {}
            if mode in ("fused", "L0"):
                dr["xin"] = self.dram_in("xin", [NALL, D])
                dr["pos"] = self.dram_in("pos", [1, 4096], I32)
                dr["ropec"] = self.dram_in("ropec", [128, 2])
                dr["mask"] = self.dram_in("mask", [128, 256])
                dr["invc"] = self.dram_in("invc", [1, 512])
                dr["whd"] = self.dram_in("whd", [8, 128, 8, 768])
                dr["wag"] = self.dram_in("wag", [8, 128, 8, 256])
                dr["wo0"] = self.dram_in("wo0", [16, 128, 1024])
                dr["poolw"] = self.dram_in("poolw", [128, 4, 2, 256])
                dr["pscale"] = self.dram_in("pscale", [128, 8])
                dr["lamv"] = self.dram_in("lamv", [1, 256])
                dr["subln"] = self.dram_in("subln", [128, 1])
            if mode in ("fused", "L1"):
                dr["w1g"] = self.dram_in("w1g", [8, 128, 8, 256])
                dr["w1u"] = self.dram_in("w1u", [8, 128, 8, 256])
                dr["w1v"] = self.dram_in("w1v", [8, 128, 8, 256])
                dr["wo1"] = self.dram_in("wo1", [16, 128, 1024])
                dr["lng"] = self.dram_in("lng", [1, 2048])
                dr["lnb"] = self.dram_in("lnb", [1, 2048])
                dr["wsT"] = self.dram_in("wsT", [128, 8, 128])
                dr["tril"] = self.dram_in("tril", [128, 128])
                dr["sgub"] = self.dram_in("sgub", [1, 1024])
            if mode == "L1":
                dr["x1in"] = self.dram_in("x1in", [TOK, D])
            dr["pren"] = self.dram_in("pren", [128, 16])
            dr["postn"] = self.dram_in("postn", [1, 2048])
            dr["out"] = nc.dram_tensor("out", [TOK, D], F32, kind="ExternalOutput").ap()

            self.pb = [es.enter_context(nc.psum_tensor("pb%d" % i, [128, 512], F32)) for i in range(8)]
            self.dpb = [Dep() for _ in range(8)]

            self.ident = self.sb(es, "ident", [128, 128], BF16)
            self.d_ident = Dep()
            io = self.sb(es, "iota_f", [128, 128], F32)
            ip = self.sb(es, "iota_p", [128, 1], F32)
            d_io, d_ip = Dep(), Dep()
            S.op("pool", lambda e: e.iota(io[:], [[1, 128]], base=0, channel_multiplier=0,
                                          allow_small_or_imprecise_dtypes=True), writes=[d_io])
            S.op("pool", lambda e: e.iota(ip[:], [[1, 1]], base=0, channel_multiplier=1,
                                          allow_small_or_imprecise_dtypes=True), writes=[d_ip])
            S.op("dve", lambda e: e.tensor_scalar(self.ident[:], io[:], ip[:, 0:1], None, ALU.is_equal),
                 reads=[d_io, d_ip], writes=[self.d_ident])
            self.pren = self.sb(es, "pren", [128, 16], F32)
            self.d_pren = Dep()
            S.dma("sp", self.pren[:], dr["pren"][:], writes=[self.d_pren])
            self.small = self.sb(es, "small", [128, 64], F32)
            self.small_i = 0
            self.ostore = []

            if mode in ("fused", "L0"):
                self.ygA = self.sb(es, "ygA", [128, 8, TOK], BF16)
                self.d_ygA = [[Dep() for _ in range(4)] for _ in range(8)]
                self.aTh = self.sb(es, "aTh", [128, 8, 256], F32)
                self.d_aTh = Dep()
                with ExitStack() as es1:
                    self.phase_attention(es1)
            if mode in ("fused", "L0"):
                self.dump("ygA", self.ygA[:], [d for l in self.d_ygA for d in l])
                self.dump("aTh", self.aTh[:], [self.d_aTh])
            S.barrier()
            with ExitStack() as es2:
                self.phase_final(es2)
            S.wait_all("sp", self.ostore)
        return nc

    def rms_transpose(self, x_ap, d_x, hn, d_hn, hT_out, d_hT, gain_ap, d_gain, ptr_i, scratch):
        S = self.S
        ss, d_ss = scratch
        junk = self.junk
        S.op("act", lambda e: e.activation(junk[:], x_ap, AF.Square, accum_out=ss[:, 0:1]),
             reads=[d_x], writes=[self.d_junk, d_ss])
        S.op("dve", lambda e: e.tensor_scalar(ss[:, 1:2], ss[:, 0:1], 1.0 / D, EPS, ALU.mult, ALU.add),
             reads=[d_ss], writes=[d_ss])
        S.op("act", lambda e: e.activation(ss[:, 2:3], ss[:, 1:2], AF.Sqrt), reads=[d_ss], writes=[d_ss])
        S.op("dve", lambda e: e.reciprocal(ss[:, 3:4], ss[:, 2:3]), reads=[d_ss], writes=[d_ss])
        S.op("act", lambda e: e.activation(hn[:], x_ap, AF.Copy, scale=ss[:, 3:4]),
             reads=[d_x, d_ss], writes=[d_hn])
        ptr = self.pb[ptr_i][:].bitcast(BF16)
        for kt in range(8):
            S.op("pe", lambda e, kt=kt: e.transpose(ptr[:, kt * 128:(kt + 1) * 128],
                                                     hn[:, kt * 128:(kt + 1) * 128], self.ident[:]),
                 reads=[d_hn, self.d_ident], writes=[self.dpb[ptr_i]], inc=(kt == 7))
        S.op("dve", lambda e: e.tensor_tensor(hT_out, ptr.rearrange("p (k t) -> p k t", k=8),
                                              bcast_last(gain_ap, 128), ALU.mult),
             reads=[self.dpb[ptr_i], d_gain], writes=[d_hT])

    def phase_attention(self, es):
        S, nc, dr = self.S, self.nc, self.dr
        sb = lambda n, s, d: self.sb(es, n, s, d)
        hT = sb("hT", [128, 8, NALL], BF16)
        d_hT = [Dep() for _ in range(34)]
        Ct = sb("ropeC", [128, 4096], BF16)
        St = sb("ropeS", [128, 4096], BF16)
        d_C = [Dep() for _ in range(4)]
        d_St = [Dep() for _ in range(4)]
        ropec = sb("ropec", [128, 2], F32)
        d_ropec = Dep()
        S.dma("sp", ropec[:], dr["ropec"][:], writes=[d_ropec])
        maskb = sb("maskb", [128, 256], BF16)
        d_mask = Dep()
        S.dma("pool", maskb[:], dr["mask"][:], writes=[d_mask])
        lamv = sb("lamv", [128, 256], F32)
        d_lamv = Dep()
        S.dma("sp", lamv[:], dr["lamv"].partition_broadcast(128), writes=[d_lamv])
        subln = sb("subln", [128, 1], F32)
        d_subln = Dep()
        S.dma("sp", subln[:], dr["subln"][:], writes=[d_subln])

        lsm = sb("lam_small", [128, 8], F32)
        d_lsm = Dep()
        ltmp = sb("lam_tmp", [128, 128], F32)
        d_ltmp = Dep()
        lv4 = lamv[:].rearrange("p (a d) -> p a d", a=4)
        S.op("dve", lambda e: e.tensor_tensor(ltmp[:, 0:64], lv4[:, 0, :], lv4[:, 1, :], ALU.mult),
             reads=[d_lamv], writes=[d_ltmp])
        S.op("dve", lambda e: e.tensor_tensor(ltmp[:, 64:128], lv4[:, 2, :], lv4[:, 3, :], ALU.mult),
             reads=[d_lamv], writes=[d_ltmp])
        S.op("dve", lambda e: e.reduce_sum(lsm[:, 0:2], ltmp[:].rearrange("p (a d) -> p a d", a=2), AX.X),
             reads=[d_ltmp], writes=[d_lsm])
        S.op("act", lambda e: e.activation(lsm[:, 2:4], lsm[:, 0:2], AF.Exp), reads=[d_lsm], writes=[d_lsm])
        S.op("dve", lambda e: e.scalar_tensor_tensor(lsm[:, 4:5], lsm[:, 3:4], -LAM_INIT, lsm[:, 2:3],
                                                      ALU.add, ALU.subtract), reads=[d_lsm], writes=[d_lsm])
        lamvec = sb("lamvec", [128, 4, 2], F32)
        d_lamvec = Dep()
        S.op("dve", lambda e: e.memset(lamvec[:], 1.0), writes=[d_lamvec])
        S.op("dve", lambda e: e.tensor_copy(lamvec[:, :, 1:2], bcast_mid(lsm[:, 4:5], 4)),
             reads=[d_lsm], writes=[d_lamvec])

        wst = [sb("wst%d" % i, [128, 8, 768], BF16) for i in range(2)]
        d_wst = [Dep() for _ in range(2)]
        est = ExitStack()
        with est:
            sbt = lambda n, s_, d: self.sb(est, n, s_, d)
            self.junk = sbt("junk", [128, 1024], F32)
            self.d_junk = Dep()
            xb = [sbt("xb%d" % i, [128, D], F32) for i in range(3)]
            d_xb = [Dep() for _ in range(3)]
            hnb = [sbt("hnb%d" % i, [128, D], BF16) for i in range(2)]
            d_hnb = [Dep() for _ in range(2)]
            ssb = [sbt("ssb%d" % i, [128, 4], F32) for i in range(2)]
            d_ssb = [Dep() for _ in range(2)]
            pre0 = self.pren[:, 0:8]
            for blk in range(34):
                i3, i2 = blk % 3, blk % 2
                S.dma("sp", xb[i3][:], dr["xin"][blk * 128:(blk + 1) * 128, :], writes=[d_xb[i3]], key="x%d" % i3)
                self.rms_transpose(xb[i3][:], d_xb[i3], hnb[i2], d_hnb[i2],
                                   hT[:, :, blk * 128:(blk + 1) * 128], d_hT[blk],
                                   pre0.rearrange("p (k o) -> p k o", o=1), self.d_pren, 6 + i2, (ssb[i2], d_ssb[i2]))

            posi = sbt("posi", [128, 1024], I32)
            d_posi = Dep()
            rt = [sbt("rt%d" % i, [128, 1024], F32) for i in range(3)]
            d_rt = [Dep() for _ in range(3)]
            for c in range(4):
                cs = slice(c * 1024, (c + 1) * 1024)
                S.dma("sp", posi[:], dr["pos"][:, cs].partition_broadcast(128), writes=[d_posi], key="pos")
                S.op("dve", lambda e: e.tensor_copy(rt[0][:], posi[:]), reads=[d_posi], writes=[d_rt[0]])
                S.op("dve", lambda e: e.tensor_scalar(rt[1][:], rt[0][:], ropec[:, 0:1], float(np.float32(1.0 / (2 * np.pi))),
                                                      ALU.mult, ALU.mult), reads=[d_rt[0], d_ropec], writes=[d_rt[1]])
                S.op("dve", lambda e: e.tensor_scalar(rt[2][:], rt[1][:], MAGIC, MAGIC, ALU.add, ALU.subtract),
                     reads=[d_rt[1]], writes=[d_rt[2]])
                S.op("dve", lambda e: e.tensor_sub(rt[1][:], rt[1][:], rt[2][:]), reads=[d_rt[1], d_rt[2]], writes=[d_rt[1]])
                S.op("act", lambda e, cs=cs: e.activation(St[:, cs], rt[1][:], AF.Sin, scale=ropec[:, 1:2]),
                     reads=[d_rt[1], d_ropec], writes=[d_St[c]])
                S.op("dve", lambda e: e.tensor_scalar(rt[2][:], rt[1][:], -1.0, None, ALU.mult),
                     reads=[d_rt[1]], writes=[d_rt[2]])
                S.op("dve", lambda e: e.tensor_tensor(rt[2][:], rt[2][:], rt[1][:], ALU.max),
                     reads=[d_rt[1], d_rt[2]], writes=[d_rt[2]])
                S.op("dve", lambda e: e.tensor_scalar(rt[2][:], rt[2][:], float(-2 * np.pi), float(np.pi / 2), ALU.mult, ALU.add),
                     reads=[d_rt[2]], writes=[d_rt[2]])
                S.op("act", lambda e, cs=cs: e.activation(Ct[:, cs], rt[2][:], AF.Sin),
                     reads=[d_rt[2]], writes=[d_C[c]])

        S.barrier()
        self.dump("hT0", hT[:, :, 0:256], d_hT[0:2])
        self.dump("hTh", hT[:, :, 4096:4352], d_hT[32:34])
        self.dump("Ct", Ct[:], d_C)
        self.dump("St", St[:], d_St)
        self.dump("lsm", lsm[:], [d_lsm])
        for ct in range(8):
            s = ct % 2
            S.dma("pool", wst[s][:, :, 0:256], dr["wag"][ct], writes=[d_wst[s]], key="wh%d" % s)
            pi = 4 + s
            for kt in range(8):
                S.op("pe", lambda e, kt=kt, s=s, pi=pi: e.matmul(self.pb[pi][:, 0:256], lhsT=wst[s][:, kt, 0:128],
                                                                 rhs=hT[:, kt, 4096:4352], start=(kt == 0), stop=(kt == 7)),
                     reads=[d_wst[s], d_hT[32], d_hT[33]], writes=[self.dpb[pi]], inc=(kt == 7))
            S.op("act", lambda e, ct=ct, pi=pi: e.activation(self.aTh[:, ct, :], self.pb[pi][:, 0:256], AF.Copy),
                 reads=[self.dpb[pi]], writes=[self.d_aTh])

        KT = sb("KT", [128, 4096], BF16)
        d_KT = [Dep() for _ in range(8)]
        QT = sb("QT", [128, TOK], BF16)
        d_QT = [Dep() for _ in range(4)]
        V = sb("V", [128, 32, 129], BF16)
        d_V = [Dep() for _ in range(8)]
        S.op("pool", lambda e: e.memset(V[:, :, 128:129], 1.0), writes=d_V)
        sgT = sb("sgT", [128, TOK], BF16)
        d_sgT = [Dep() for _ in range(4)]
        ET = [sb("ET%d" % i, [128, 512], BF16) for i in range(4)]
        d_ET = [Dep() for _ in range(4)]
        rtmp = [sb("rtmp%d" % i, [128, 512], F32) for i in range(2)]
        d_rtmp = [Dep() for _ in range(2)]
        stage = [sb("stage%d" % i, [128, 4, 2, 129], F32) for i in range(2)]
        d_stage = [Dep() for _ in range(2)]
        eo = [sb("eo%d" % i, [128, 4, 128], F32) for i in range(2)]
        d_eo = [Dep() for _ in range(2)]
        eon = sb("eon", [128, 4, 128], BF16)
        d_eon = Dep()
        esm = sb("esm", [128, 32], F32)
        d_esm = Dep()

        et_i = [0]
        st_i = [0]
        rt_i = [0]
        S.dma("pool", wst[0][:], dr["whd"][0], writes=[d_wst[0]], key="wh0")
        for h in range(8):
            s = h % 2
            if h + 1 < 8:
                S.dma("pool", wst[1 - s][:], dr["whd"][h + 1], writes=[d_wst[1 - s]], key="wh%d" % (1 - s))
            w = wst[s]
            dw = d_wst[s]

            def proj_fm(col0, tok0, pi, first_blk):
                for kt in range(8):
                    S.op("pe", lambda e, kt=kt: e.matmul(self.pb[pi][:, :], lhsT=w[:, kt, col0:col0 + 128],
                                                         rhs=hT[:, kt, tok0:tok0 + 512], start=(kt == 0), stop=(kt == 7)),
                         reads=[dw] + d_hT[first_blk:first_blk + 4], writes=[self.dpb[pi]], inc=(kt == 7))

            def rope_chunk(col0, tok0, dst, d_dst, cidx):
                proj_fm(col0, tok0, 6, tok0 // 128)
                proj_fm(col0 + 128, tok0, 7, tok0 // 128)
                a, b = 0, 1
                S.op("dve", lambda e: e.tensor_tensor(rtmp[a][:], self.pb[6][:, :], Ct[:, tok0:tok0 + 512], ALU.mult),
                     reads=[self.dpb[6], d_C[cidx]], writes=[d_rtmp[a]])
                S.op("dve", lambda e: e.tensor_tensor(rtmp[b][:], self.pb[7][:, :], St[:, tok0:tok0 + 512], ALU.mult),
                     reads=[self.dpb[7], d_St[cidx]], writes=[d_rtmp[b]])
                S.op("pool", lambda e: e.tensor_tensor(dst, rtmp[a][:], rtmp[b][:], ALU.add),
                     reads=[d_rtmp[a], d_rtmp[b]], writes=[d_dst])

            for c in range(8):
                rope_chunk(256, c * 512, KT[:, c * 512:(c + 1) * 512], d_KT[c], c // 2)
            for c in range(4):
                rope_chunk(0, c * 512, QT[:, c * 512:(c + 1) * 512], d_QT[c], c // 2)
            for c in range(4):
                pi = 6 + c % 2
                proj_fm(640, c * 512, pi, c * 4)
                S.op("act", lambda e, c=c, pi=pi: e.activation(sgT[:, c * 512:(c + 1) * 512], self.pb[pi][:, :], AF.Silu),
                     reads=[self.dpb[pi]], writes=[d_sgT[c]])
            for g4 in range(8):
                pi = 6 + g4 % 2
                for i in range(4):
                    kb = g4 * 4 + i
                    for kt in range(8):
                        S.op("pe", lambda e, kt=kt, kb=kb, i=i: e.matmul(
                            self.pb[pi][:, i * 128:(i + 1) * 128], lhsT=hT[:, kt, kb * 128:(kb + 1) * 128],
                            rhs=w[:, kt, 512:640], start=(kt == 0), stop=(kt == 7)),
                             reads=[dw, d_hT[kb]], writes=[self.dpb[pi]], inc=(kt == 7 and i == 3))
                S.op("act", lambda e, g4=g4, pi=pi: e.activation(
                    V[:, g4 * 4:(g4 + 1) * 4, 0:128], self.pb[pi][:, :].rearrange("p (a d) -> p a d", a=4), AF.Copy),
                     reads=[self.dpb[pi]], writes=[d_V[g4]])

            if h == 0:
                self.dump("KT", KT[:], d_KT)
                self.dump("QT", QT[:], d_QT)
                self.dump("V", V[:], d_V)
                self.dump("sgT", sgT[:], d_sgT)
            tiles = []
            for G in range(4):
                blocks = [(i, 0, None) for i in range(4 * G)] + [(16 + i, 0, None) for i in range(4 * G)]
                for a4 in range(4):
                    blocks.append((4 * G + a4, a4, 0))
                    blocks.append((16 + 4 * G + a4, a4, 1))
                for c in range(2):
                    for bi, (kb, a4, m) in enumerate(blocks):
                        tiles.append(dict(G=G, c=c, kb=kb, a=a4, m=m, first=(bi == 0), endgrp=(bi == len(blocks) - 1 and c == 1)))

            def emit_qk(n):
                t = tiles[n]
                G, c, kb, a4, m = t["G"], t["c"], t["kb"], t["a"], t["m"]
                ps = slice(c * 64, (c + 1) * 64)
                sbk = n % 2
                q0 = (4 * G + a4) * 128
                q1 = (4 * G + 4) * 128
                rd = [d_KT[kb // 4], d_QT[G]]
                if m is None:
                    S.op("pe", lambda e: e.matmul(self.pb[sbk][:, :], lhsT=KT[ps, kb * 128:(kb + 1) * 128],
                                                  rhs=QT[ps, q0:q1], start=True, stop=True),
                         reads=rd, writes=[self.dpb[sbk]], inc=True)
                else:
                    c0 = a4 * 128
                    S.op("pe", lambda e: e.matmul(self.pb[sbk][:, c0:c0 + 128], lhsT=KT[ps, kb * 128:(kb + 1) * 128],
                                                  rhs=QT[ps, q0:q0 + 128], start=True, stop=False),
                         reads=rd, writes=[self.dpb[sbk]], inc=False)
                    S.op("pe", lambda e: e.matmul(self.pb[sbk][:, c0:c0 + 128], lhsT=self.ident[:, :],
                                                  rhs=maskb[:, m * 128:(m + 1) * 128], start=False, stop=True),
                         reads=[self.d_ident, d_mask], writes=[self.dpb[sbk]], inc=(a4 == 3))
                    if a4 < 3:
                        S.op("pe", lambda e: e.matmul(self.pb[sbk][:, c0 + 128:512], lhsT=KT[ps, kb * 128:(kb + 1) * 128],
                                                      rhs=QT[ps, q0 + 128:q1], start=True, stop=True),
                             reads=rd, writes=[self.dpb[sbk]], inc=True)

            def emit_exp_pv(n):
                t = tiles[n]
                G, c, kb, a4, m = t["G"], t["c"], t["kb"], t["a"], t["m"]
                sbk = n % 2
                ei = n % 4
                c0 = a4 * 128
                S.op("act", lambda e: e.activation(ET[ei][:, c0:512], self.pb[sbk][:, c0:512], AF.Exp, scale=0.125),
                     reads=[self.dpb[sbk]], writes=[d_ET[ei]])
                for sl in range(a4, 4):
                    ob = 2 + sl
                    O = self.pb[ob][:, 0:258].rearrange("p (c d) -> p c d", c=2)
                    last = (m == 1 and a4 == sl)
                    S.op("pe", lambda e, sl=sl, O=O, last=last: e.matmul(
                        O[:, c, :], lhsT=ET[ei][:, sl * 128:(sl + 1) * 128], rhs=V[:, kb, :],
                        start=t["first"], stop=last),
                         reads=[d_ET[ei], d_V[kb // 4]], writes=[self.dpb[ob]], inc=(last and c == 1))

            pending = None
            emit_qk(0)
            for n in range(len(tiles)):
                if n + 1 < len(tiles):
                    emit_qk(n + 1)
                emit_exp_pv(n)
                t = tiles[n]
                if not t["endgrp"]:
                    continue
                q4 = t["G"]
                stg = stage[q4 % 2]
                for sl in range(4):
                    O = self.pb[2 + sl][:, 0:258].rearrange("p (c d) -> p c d", c=2)
                    S.op("dve", lambda e, O=O, sl=sl: e.tensor_copy(stg[:, sl, :, :], O),
                         reads=[self.dpb[2 + sl]], writes=[d_stage[q4 % 2]])
                sl = 3
                if pending is not None:
                    self.attn_epilogue_b(*pending)
                    pending = None
                if sl == 3 and h == 0 and q4 == 0:
                    self.dump("stage", stg[:], [d_stage[0]])
                if sl == 3:
                    ds = d_stage[q4 % 2]
                    rz0 = esm[:, 0:8].rearrange("p (a c) -> p a c", c=2)
                    rz = esm[:, 8:16].rearrange("p (a c) -> p a c", c=2)
                    S.op("dve", lambda e, stg=stg: e.reciprocal(rz0, stg[:, :, :, 128]), reads=[ds], writes=[d_esm])
                    S.op("dve", lambda e: e.tensor_tensor(rz, rz0, lamvec[:], ALU.mult),
                         reads=[d_esm, d_lamvec], writes=[d_esm])
                    S.op("dve", lambda e, stg=stg: e.tensor_tensor(eo[0][:], stg[:, :, 0, 0:128], bcast_last(rz[:, :, 0:1], 128), ALU.mult),
                         reads=[ds, d_esm], writes=[d_eo[0]])
                    S.op("dve", lambda e, stg=stg: e.tensor_tensor(eo[1][:], stg[:, :, 1, 0:128], bcast_last(rz[:, :, 1:2], 128), ALU.mult),
                         reads=[ds, d_esm], writes=[d_eo[1]])
                    S.op("pool", lambda e: e.tensor_tensor(eo[0][:], eo[0][:], eo[1][:], ALU.add),
                         reads=[d_eo[0], d_eo[1]], writes=[d_eo[0]])
                    S.op("pool", lambda e: e.tensor_tensor(eo[1][:], eo[0][:], eo[0][:], ALU.mult),
                         reads=[d_eo[0]], writes=[d_eo[1]])
                    S.op("dve", lambda e: e.reduce_sum(esm[:, 16:20], eo[1][:], AX.X), reads=[d_eo[1]], writes=[d_esm])
                    S.op("dve", lambda e: e.tensor_scalar(esm[:, 20:24], esm[:, 16:20], 1.0 / 128, 1e-5, ALU.mult, ALU.add),
                         reads=[d_esm], writes=[d_esm])
                    S.op("act", lambda e: e.activation(esm[:, 24:28], esm[:, 20:24], AF.Ln), reads=[d_esm], writes=[d_esm])
                    S.op("act", lambda e: e.activation(esm[:, 28:32], esm[:, 24:28], AF.Exp, scale=-0.5), reads=[d_esm], writes=[d_esm])
                    S.op("dve", lambda e: e.scalar_tensor_tensor(eon[:], eo[0][:], 1.0 - LAM_INIT,
                                                                  bcast_last(esm[:, 28:32].rearrange("p (a o) -> p a o", o=1), 128),
                                                                  ALU.mult, ALU.mult),
                         reads=[d_eo[0], d_esm], writes=[d_eon])
                    pending = (h, q4, eon, d_eon, subln, d_subln, sgT, d_sgT)
                    if h == 0 and q4 == 0:
                        self.dump("eon", eon[:], [d_eon])
                        self.dump("esm", esm[:], [d_esm])
            if pending is not None:
                self.attn_epilogue_b(*pending)
                pending = None

    def attn_epilogue_b(self, h, q4, eon, d_eon, subln, d_subln, sgT, d_sgT):
        S = self.S
        ptr = self.pb[7][:].bitcast(BF16)
        for i in range(4):
            S.op("pe", lambda e, i=i: e.transpose(ptr[:, i * 128:(i + 1) * 128], eon[:, i, :], self.ident[:]),
                 reads=[d_eon, self.d_ident], writes=[self.dpb[7]], inc=(i == 3))
        S.op("dve", lambda e: e.scalar_tensor_tensor(self.ygA[:, h, q4 * 512:(q4 + 1) * 512], ptr[:, 0:512], subln[:, 0:1],
                                                      sgT[:, q4 * 512:(q4 + 1) * 512], ALU.mult, ALU.mult),
             reads=[self.dpb[7], d_subln, d_sgT[q4]], writes=[self.d_ygA[h][q4]])

    def phase_final(self, es):
        S, nc, dr, mode = self.S, self.nc, self.dr, self.mode
        sb = lambda n, s, d: self.sb(es, n, s, d)
        L0 = mode in ("fused", "L0")
        L1 = mode in ("fused", "L1")
        self.junk = sb("junkf", [128, 1024], F32)
        self.d_junk = Dep()
        postn = sb("postn", [128, 2, 1024], F32)
        d_postn = Dep()
        S.dma("sp", postn[:].rearrange("p a d -> p (a d)"), dr["postn"].partition_broadcast(128), writes=[d_postn])
        xs = sb("xs", [128, 4, D], F32)
        d_xs = [Dep() for _ in range(4)]
        hn = [sb("hnf%d" % i, [128, D], BF16) for i in range(4)]
        d_hn = [Dep() for _ in range(4)]
        ss4 = [sb("ssf%d" % i, [128, 8], F32) for i in range(4)]
        d_ss4 = [Dep() for _ in range(4)]
        hTs = sb("hTs", [128, 8, 512], BF16)
        d_hTs = [Dep() for _ in range(4)]
        ysb = sb("ysb", [128, 16, 512], BF16)
        d_ysb = [Dep() for _ in range(16)]
        tmpf = [sb("tmpf%d" % i, [128, D], F32) for i in range(2)]
        d_tmpf = [Dep() for _ in range(2)]
        wr = [sb("wr%d" % i, [128, 8, 256], BF16) for i in range(6)]
        d_wr = [Dep() for _ in range(6)]
        wo = [sb("wor%d" % i, [128, 1024], BF16) for i in range(3)]
        d_wo = [Dep() for _ in range(3)]
        if L0:
            invc = sb("invc", [128, 4, 128], F32)
            d_invc = Dep()
            S.dma("sp", invc[:].rearrange("p a d -> p (a d)"), dr["invc"].partition_broadcast(128), writes=[d_invc])
            poolw = sb("poolw", [128, 4, 2, 256], BF16)
            d_poolw = Dep()
            S.dma("pool", poolw[:], dr["poolw"][:], writes=[d_poolw])
            pscale = sb("pscale", [128, 8], F32)
            d_pscale = Dep()
            S.dma("sp", pscale[:], dr["pscale"][:], writes=[d_pscale])
            Abuf = [sb("Abuf%d" % i, [128, 4, 144], F32) for i in range(2)]
            d_A = [Dep() for _ in range(2)]
            Sb = [sb("Sbuf%d" % i, [128, 4, 144], F32) for i in range(2)]
            d_Sb = [Dep() for _ in range(2)]
            pooled = [sb("pooled%d" % i, [128, 2, 512], BF16) for i in range(2)]
            d_pooled = [[Dep() for _ in range(2)] for _ in range(2)]
            ptmp = sb("ptmp", [128, 128], F32)
            d_ptmp = Dep()
        if L1:
            lng = sb("lng", [128, 2048], F32)
            lnb = sb("lnb", [128, 2048], F32)
            d_ln = Dep()
            S.dma("sp", lng[:], dr["lng"].partition_broadcast(128), writes=[d_ln], key="ln")
            S.dma("sp", lnb[:], dr["lnb"].partition_broadcast(128), writes=[d_ln], key="ln")
            wsf = sb("wsf", [128, 8, 128], F32)
            trl = sb("trl", [128, 128], F32)
            d_wsf = Dep()
            S.dma("sp", wsf[:], dr["wsT"][:], writes=[d_wsf], key="wsf")
            S.dma("sp", trl[:], dr["tril"][:], writes=[d_wsf], key="wsf")
            wsT = sb("wsTb", [128, 8, 128], BF16)
            d_wsT = Dep()
            S.op("dve", lambda e: e.tensor_tensor(wsT[:], wsf[:], bcast_mid(trl[:], 8), ALU.mult), reads=[d_wsf], writes=[d_wsT])
            bsb = sb("bsb", [128, 8, 128], F32)
            d_bsb = Dep()
            S.dma("sp", bsb[:].rearrange("p a d -> p (a d)"), dr["sgub"].partition_broadcast(128), writes=[d_bsb])
            vb = sb("vb", [128, 4, 2048], BF16)
            d_vb = [Dep() for _ in range(4)]
            utmp = [sb("utmp%d" % i, [128, 512], BF16) for i in range(2)]
            d_utmp = [Dep() for _ in range(2)]
            lsm4 = [sb("lnsm%d" % i, [128, 12], F32) for i in range(1)]
            d_lsm4 = [Dep() for _ in range(1)]
            lsq = sb("lsq", [128, 40], F32)
            d_lsq = Dep()
            mtmp = [sb("mtmp%d" % i, [128, 512], F32) for i in range(2)]
            d_mtmp = [Dep() for _ in range(2)]

        per_sb = []
        if L0:
            for ct in range(8):
                per_sb.append(("wr", dr["wag"][ct]))
            for kt in range(16):
                per_sb.append(("wo", dr["wo0"][kt]))
        if L1:
            for cc in range(8):
                per_sb.append(("wr", dr["w1v"][cc]))
            for i in range(8):
                per_sb.append(("wr", dr["w1g"][i]))
            for i in range(8):
                per_sb.append(("wr", dr["w1u"][i]))
            for kt in range(16):
                per_sb.append(("wo", dr["wo1"][kt]))
        nper = len(per_sb)
        n_wr = sum(1 for k, _ in per_sb if k == "wr")
        n_wo = nper - n_wr
        scr_wr = nc.dram_tensor("scr_wr", [n_wr, 128, 8, 256], BF16).ap()
        scr_wo = nc.dram_tensor("scr_wo", [n_wo, 128, 1024], BF16).ap()
        scr_of = []
        c_wr = c_wo = 0
        for k, _ in per_sb:
            if k == "wr":
                scr_of.append(scr_wr[c_wr]); c_wr += 1
            else:
                scr_of.append(scr_wo[c_wo]); c_wo += 1
        d_scr = [Dep() for _ in range(nper)]
        items = per_sb * 4
        ring = {"wr": (wr, d_wr), "wo": (wo, d_wo)}
        cnt = {"wr": 0, "wo": 0}
        slot_of = []
        for kind, _ in items:
            slot_of.append(cnt[kind] % len(ring[kind][0]))
            cnt[kind] += 1
        issued = [0]
        inflight = {"wr": 0, "wo": 0}
        consumed = [0]

        def pump():
            while issued[0] < len(items):
                i = issued[0]
                kind, src = items[i]
                bufs, deps = ring[kind]
                if inflight[kind] >= len(bufs):
                    break
                sl = slot_of[i]
                j = i % nper
                if i < nper:
                    S.dma("pool", bufs[sl][:], src, writes=[deps[sl]], key="%s%d" % (kind, sl))
                    S.dma("sp", scr_of[j], bufs[sl][:], reads=[deps[sl]], writes=[d_scr[j]], key="sw%d" % (j % 32))
                else:
                    S.dma("sp", bufs[sl][:], scr_of[j], reads=[d_scr[j]], writes=[deps[sl]], key="%s%d" % (kind, sl))
                inflight[kind] += 1
                issued[0] += 1

        def take(kind):
            i = consumed[0]
            assert items[i][0] == kind, (items[i][0], kind)
            assert i < issued[0]
            sl = slot_of[i]
            bufs, deps = ring[kind]
            return bufs[sl], deps[sl]

        def release(kind):
            consumed[0] += 1
            inflight[kind] -= 1
            pump()

        pump()
        pre0 = self.pren[:, 0:8].rearrange("p (k o) -> p k o", o=1)
        pre1 = self.pren[:, 8:16].rearrange("p (k o) -> p k o", o=1)

        def post_norm_all(layer):
            for tb in range(4):
                for hf in range(2):
                    S.op("act", lambda e, hf=hf, tb=tb: e.activation(self.junk[:, hf * 512:(hf + 1) * 512], self.pb[2 * tb + hf][:, :], AF.Square,
                                                                     accum_out=ssq[:, 2 * tb + hf:2 * tb + hf + 1]),
                         reads=[self.dpb[2 * tb + hf]], writes=[self.d_junk, d_ssq])
            S.op("dve", lambda e: e.reduce_sum(ssq[:, 8:12], ssq[:, 0:8].rearrange("p (a c) -> p a c", c=2), AX.X), reads=[d_ssq], writes=[d_ssq])
            S.op("dve", lambda e: e.tensor_scalar(ssq[:, 8:12], ssq[:, 8:12], 1.0 / D, EPS, ALU.mult, ALU.add), reads=[d_ssq], writes=[d_ssq])
            S.op("act", lambda e: e.activation(ssq[:, 12:16], ssq[:, 8:12], AF.Sqrt), reads=[d_ssq], writes=[d_ssq])
            S.op("dve", lambda e: e.reciprocal(ssq[:, 8:12], ssq[:, 12:16]), reads=[d_ssq], writes=[d_ssq])
            for tb in range(4):
                tf, d_tf = tmpf[tb % 2], d_tmpf[tb % 2]
                for hf in range(2):
                    S.op("dve", lambda e, hf=hf, tb=tb, tf=tf: e.scalar_tensor_tensor(
                        tf[:, hf * 512:(hf + 1) * 512], self.pb[2 * tb + hf][:, :], ssq[:, 8 + tb:9 + tb],
                        postn[:, layer, hf * 512:(hf + 1) * 512], ALU.mult, ALU.mult),
                         reads=[self.dpb[2 * tb + hf], d_ssq, d_postn], writes=[d_tf])
                S.op("pool", lambda e, tb=tb, tf=tf: e.tensor_tensor(xs[:, tb, :], xs[:, tb, :], tf[:], ALU.add),
                     reads=[d_tf, d_xs[tb]], writes=[d_xs[tb]])

        def out_proj(layer, lhs_of):
            for kt in range(16):
                wbuf, dwb = take("wo")
                for tb in range(4):
                    lhsT, dl = lhs_of(kt, tb)
                    for hf in range(2):
                        S.op("pe", lambda e, tb=tb, hf=hf, lhsT=lhsT: e.matmul(self.pb[2 * tb + hf][:, :], lhsT=lhsT,
                                                                                rhs=wbuf[:, hf * 512:(hf + 1) * 512],
                                                                                start=(kt == 0), stop=(kt == 15)),
                             reads=[dwb, dl], writes=[self.dpb[2 * tb + hf]], inc=(kt == 15 or (tb == 3 and hf == 1)))
                release("wo")
            post_norm_all(layer)

        ssq = sb("ssq", [128, 16], F32)
        d_ssq = Dep()

        def rms4(gain):
            for tb in range(4):
                S.op("act", lambda e, tb=tb: e.activation(self.junk[:], xs[:, tb, :], AF.Square, accum_out=ssq[:, tb:tb + 1]),
                     reads=[d_xs[tb]], writes=[self.d_junk, d_ssq])
            S.op("dve", lambda e: e.tensor_scalar(ssq[:, 4:8], ssq[:, 0:4], 1.0 / D, EPS, ALU.mult, ALU.add), reads=[d_ssq], writes=[d_ssq])
            S.op("act", lambda e: e.activation(ssq[:, 8:12], ssq[:, 4:8], AF.Sqrt), reads=[d_ssq], writes=[d_ssq])
            S.op("dve", lambda e: e.reciprocal(ssq[:, 12:16], ssq[:, 8:12]), reads=[d_ssq], writes=[d_ssq])
            for tb in range(4):
                S.op("act", lambda e, tb=tb: e.activation(hn[tb][:], xs[:, tb, :], AF.Copy, scale=ssq[:, 12 + tb:13 + tb]),
                     reads=[d_xs[tb], d_ssq], writes=[d_hn[tb]])
            for tb in range(4):
                ptr = self.pb[4 + tb][:].bitcast(BF16)
                for kt in range(8):
                    S.op("pe", lambda e, kt=kt, tb=tb, ptr=ptr: e.transpose(ptr[:, kt * 128:(kt + 1) * 128],
                                                                             hn[tb][:, kt * 128:(kt + 1) * 128], self.ident[:]),
                         reads=[d_hn[tb], self.d_ident], writes=[self.dpb[4 + tb]], inc=(kt == 7))
            for tb in range(4):
                ptr = self.pb[4 + tb][:].bitcast(BF16)
                S.op("dve", lambda e, tb=tb, ptr=ptr: e.tensor_tensor(hTs[:, :, tb * 128:(tb + 1) * 128], ptr.rearrange("p (k t) -> p k t", k=8),
                                                                      bcast_last(gain, 128), ALU.mult),
                     reads=[self.dpb[4 + tb], self.d_pren], writes=[d_hTs[tb]])

        for sbi in range(4):
            tok0 = sbi * 512
            if L0:
                for tb in range(4):
                    S.dma("sp", xs[:, tb, :], dr["xin"][tok0 + tb * 128: tok0 + (tb + 1) * 128, :], writes=[d_xs[tb]], key="xs%d" % tb)
                rms4(pre0)

                def pool_mm(g):
                    pg = g % 2
                    pbank = 2 + 3 * pg
                    for dt in range(2):
                        ct = 2 * g + dt
                        for ci in range(2):
                            S.op("pe", lambda e, ci=ci, dt=dt: e.matmul(self.pb[pbank + dt][:, :],
                                                                         lhsT=poolw[:, g, ci, dt * 128:(dt + 1) * 128],
                                                                         rhs=pooled[pg][:, ci, :], start=(ci == 0), stop=(ci == 1)),
                                 reads=[d_poolw, d_pooled[pg][ci]], writes=[self.dpb[pbank + dt]], inc=(ci == 1))
                        S.op("dve", lambda e, ct=ct, dt=dt: e.scalar_tensor_tensor(ysb[:, ct, :], self.pb[pbank + dt][:, :], pscale[:, ct:ct + 1],
                                                                                   ysb[:, 8 + ct, :], ALU.mult, ALU.mult),
                             reads=[self.dpb[pbank + dt], d_pscale, d_ysb[8 + ct]], writes=[d_ysb[ct]])

                for g in range(4):
                    pg = g % 2
                    for ci in range(2):
                        ct = 2 * g + ci
                        ab, d_ab = Abuf[ci], d_A[ci]
                        wbuf, dwb = take("wr")
                        for kt in range(8):
                            S.op("pe", lambda e, kt=kt: e.matmul(self.pb[ci][:, :], lhsT=wbuf[:, kt, 0:128], rhs=hTs[:, kt, :],
                                                                 start=(kt == 0), stop=(kt == 7)),
                                 reads=[dwb] + d_hTs, writes=[self.dpb[ci]], inc=(kt == 7))
                        for kt in range(8):
                            S.op("pe", lambda e, kt=kt: e.matmul(self.pb[[4, 7][ci]][:, :],
                                                                 lhsT=wbuf[:, kt, 128:256], rhs=hTs[:, kt, :],
                                                                 start=(kt == 0), stop=(kt == 7)),
                                 reads=[dwb] + d_hTs, writes=[self.dpb[[4, 7][ci]]], inc=(kt == 7))
                        release("wr")
                        S.op("act", lambda e: e.activation(ab[:, :, 16:144], self.pb[ci][:, :].rearrange("p (a t) -> p a t", a=4), AF.Copy),
                             reads=[self.dpb[ci]], writes=[d_ab])
                        S.op("pool", lambda e, ct=ct: e.tensor_copy(
                            ab[:, :, 0:16], self.aTh[:, ct, sbi * 64:(sbi + 1) * 64].rearrange("p (a t) -> p a t", a=4)),
                             reads=[self.d_aTh], writes=[d_ab])
                        S.op("act", lambda e, ct=ct: e.activation(ysb[:, 8 + ct, :], self.pb[[4, 7][ci]][:, :], AF.Silu),
                             reads=[self.dpb[[4, 7][ci]]], writes=[d_ysb[8 + ct]])
                        src, dsrc = ab, d_ab
                        for step in range(g + 1):
                            sh = 1 << step
                            dst, ddst = Sb[step % 2], d_Sb[step % 2]
                            S.op("dve", lambda e, src=src, dst=dst, sh=sh: e.tensor_tensor(
                                dst[:, :, sh:144], src[:, :, sh:144], src[:, :, 0:144 - sh], ALU.add),
                                 reads=[dsrc], writes=[ddst])
                            src, dsrc = dst, ddst
                        w = 2 << g
                        pv = pooled[pg][:, ci, :].rearrange("p (a t) -> p a t", a=4)
                        dpl = d_pooled[pg][ci]
                        if sbi == 0:
                            S.op("dve", lambda e, src=src, g=g: e.tensor_tensor(ptmp[:], src[:, 0, 16:144], invc[:, g, :], ALU.mult),
                                 reads=[dsrc, d_invc], writes=[d_ptmp])
                            S.op("dve", lambda e, pv=pv: e.tensor_tensor(pv[:, 0, :], ptmp[:], ab[:, 0, 16:144], ALU.subtract),
                                 reads=[d_ptmp, d_ab], writes=[dpl])
                            S.op("dve", lambda e, src=src, w=w, pv=pv: e.scalar_tensor_tensor(
                                pv[:, 1:4, :], src[:, 1:4, 16:144], 1.0 / w, ab[:, 1:4, 16:144], ALU.mult, ALU.subtract),
                                 reads=[dsrc, d_ab], writes=[dpl])
                        else:
                            S.op("dve", lambda e, src=src, w=w, pv=pv: e.scalar_tensor_tensor(
                                pv, src[:, :, 16:144], 1.0 / w, ab[:, :, 16:144], ALU.mult, ALU.subtract),
                                 reads=[dsrc, d_ab], writes=[dpl])
                    if g >= 1:
                        pool_mm(g - 1)
                pool_mm(3)

                def lhs0(kt, tb):
                    if kt < 8:
                        return ysb[:, kt, tb * 128:(tb + 1) * 128], d_ysb[kt]
                    return self.ygA[:, kt - 8, tok0 + tb * 128: tok0 + (tb + 1) * 128], self.d_ygA[kt - 8][sbi]
                out_proj(0, lhs0)
            else:
                for tb in range(4):
                    S.dma("sp", xs[:, tb, :], dr["x1in"][tok0 + tb * 128: tok0 + (tb + 1) * 128, :], writes=[d_xs[tb]], key="xs%d" % tb)

            if L1:
                rms4(pre1)
                for cc in range(8):
                    wbuf, dwb = take("wr")
                    for half in range(2):
                        pi = 2 + half
                        for t2 in range(2):
                            tb = half * 2 + t2
                            for kt in range(8):
                                S.op("pe", lambda e, kt=kt, tb=tb, t2=t2, pi=pi: e.matmul(
                                    self.pb[pi][:, t2 * 256:(t2 + 1) * 256], lhsT=hTs[:, kt, tb * 128:(tb + 1) * 128],
                                    rhs=wbuf[:, kt, :], start=(kt == 0), stop=(kt == 7)),
                                     reads=[dwb, d_hTs[tb]], writes=[self.dpb[pi]], inc=(kt == 7 and t2 == 1))
                        S.op("act", lambda e, half=half, cc=cc, pi=pi: e.activation(
                            vb[:, half * 2:half * 2 + 2, cc * 256:(cc + 1) * 256],
                            self.pb[pi][:, :].rearrange("p (a d) -> p a d", a=2), AF.Gelu_apprx_tanh),
                             reads=[self.dpb[pi]], writes=[d_vb[half * 2], d_vb[half * 2 + 1]])
                    release("wr")
                lsm, d_lsmf = lsm4[0], d_lsm4[0]

                def ln_stage_a():
                    for tb in range(4):
                        S.op("dve", lambda e, tb=tb: e.reduce_sum(lsq[:, tb:tb + 1], vb[:, tb, :], AX.X), reads=[d_vb[tb]], writes=[d_lsq])
                        S.op("act", lambda e, tb=tb: e.activation(self.junk[:, :], vb[:, tb, 0:1024], AF.Square, accum_out=lsq[:, 4 + tb:5 + tb]),
                             reads=[d_vb[tb]], writes=[self.d_junk, d_lsq])
                        S.op("act", lambda e, tb=tb: e.activation(self.junk[:, :], vb[:, tb, 1024:2048], AF.Square, accum_out=lsq[:, 8 + tb:9 + tb]),
                             reads=[d_vb[tb]], writes=[self.d_junk, d_lsq])

                def ln_stage_b():
                    S.op("dve", lambda e: e.tensor_scalar(lsq[:, 12:16], lsq[:, 0:4], 1.0 / 2048, None, ALU.mult), reads=[d_lsq], writes=[d_lsq])
                    S.op("dve", lambda e: e.tensor_tensor(lsq[:, 16:20], lsq[:, 4:8], lsq[:, 8:12], ALU.add), reads=[d_lsq], writes=[d_lsq])
                    S.op("dve", lambda e: e.tensor_tensor(lsq[:, 20:24], lsq[:, 12:16], lsq[:, 12:16], ALU.mult), reads=[d_lsq], writes=[d_lsq])
                    S.op("dve", lambda e: e.scalar_tensor_tensor(lsq[:, 24:28], lsq[:, 16:20], 1.0 / 2048, lsq[:, 20:24], ALU.mult, ALU.subtract),
                         reads=[d_lsq], writes=[d_lsq])
                    S.op("dve", lambda e: e.tensor_scalar(lsq[:, 24:28], lsq[:, 24:28], EPS, None, ALU.add), reads=[d_lsq], writes=[d_lsq])
                    S.op("act", lambda e: e.activation(lsq[:, 28:32], lsq[:, 24:28], AF.Sqrt), reads=[d_lsq], writes=[d_lsq])
                    S.op("dve", lambda e: e.reciprocal(lsq[:, 32:36], lsq[:, 28:32]), reads=[d_lsq], writes=[d_lsq])
                    S.op("dve", lambda e: e.scalar_tensor_tensor(lsq[:, 36:40], lsq[:, 12:16], -1.0, lsq[:, 32:36], ALU.mult, ALU.mult),
                         reads=[d_lsq], writes=[d_lsq])

                def ln_stage_c(tb):
                    for hf in range(2):
                        cs = slice(hf * 1024, (hf + 1) * 1024)
                        tf, d_tf = tmpf[hf], d_tmpf[hf]
                        S.op("act", lambda e, cs=cs, tf=tf: e.activation(tf[:], vb[:, tb, cs], AF.Identity, bias=lsq[:, 36 + tb:37 + tb], scale=lsq[:, 32 + tb:33 + tb]),
                             reads=[d_vb[tb], d_lsq], writes=[d_tf])
                        S.op("dve", lambda e, cs=cs, tf=tf: e.tensor_tensor(tf[:], tf[:], lng[:, cs], ALU.mult), reads=[d_tf, d_ln], writes=[d_tf])
                        S.op("pool", lambda e, cs=cs, tf=tf: e.tensor_tensor(vb[:, tb, cs], tf[:], lnb[:, cs], ALU.add),
                             reads=[d_tf, d_ln], writes=[d_vb[tb]])

                def sgu_ct(ct):
                    g = ct // 2
                    pi = 4 + ct % 2
                    for tb in range(4):
                        S.op("pe", lambda e, tb=tb, pi=pi: e.matmul(
                            self.pb[pi][:, tb * 128:(tb + 1) * 128], lhsT=vb[:, tb, ct * 128:(ct + 1) * 128], rhs=wsT[:, g, :],
                            start=True, stop=True),
                             reads=[d_vb[tb], d_wsT], writes=[self.dpb[pi]], inc=(tb == 3))
                    mi = ct % 2
                    S.op("dve", lambda e, pi=pi, mi=mi: e.tensor_tensor(
                        mtmp[mi][:].rearrange("p (a t) -> p a t", a=4), self.pb[pi][:, :].rearrange("p (a t) -> p a t", a=4),
                        bcast_mid(bsb[:, g, :], 4), ALU.add),
                         reads=[self.dpb[pi], d_bsb], writes=[d_mtmp[mi]])
                    S.op("pool", lambda e, mi=mi: e.tensor_tensor(ysb[:, ct, :], ysb[:, ct, :], mtmp[mi][:], ALU.mult),
                         reads=[d_mtmp[mi], d_ysb[ct]], writes=[d_ysb[ct]])

                ln_stage_a()
                for which in range(2):
                    for i in range(8):
                        wbuf, dwb = take("wr")
                        for c2 in range(2):
                            ct = 2 * i + c2
                            pi = c2
                            for kt in range(8):
                                S.op("pe", lambda e, kt=kt, c2=c2, pi=pi: e.matmul(
                                    self.pb[pi][:, :], lhsT=wbuf[:, kt, c2 * 128:(c2 + 1) * 128], rhs=hTs[:, kt, :],
                                    start=(kt == 0), stop=(kt == 7)),
                                     reads=[dwb] + d_hTs, writes=[self.dpb[pi]], inc=(kt == 7))
                            if which == 0:
                                S.op("act", lambda e, ct=ct, pi=pi: e.activation(ysb[:, ct, :], self.pb[pi][:, :], AF.Silu),
                                     reads=[self.dpb[pi]], writes=[d_ysb[ct]])
                            else:
                                ui = ct % 2
                                S.op("act", lambda e, ui=ui, pi=pi: e.activation(utmp[ui][:], self.pb[pi][:, :], AF.Gelu_apprx_tanh),
                                     reads=[self.dpb[pi]], writes=[d_utmp[ui]])
                                S.op("pool", lambda e, ct=ct, ui=ui: e.tensor_tensor(ysb[:, ct, :], ysb[:, ct, :], utmp[ui][:], ALU.mult),
                                     reads=[d_utmp[ui], d_ysb[ct]], writes=[d_ysb[ct]])
                        release("wr")
                        if which == 0:
                            if i == 1:
                                ln_stage_b()
                            if 2 <= i <= 5:
                                ln_stage_c(i - 2)
                        else:
                            sgu_ct(2 * i)
                            sgu_ct(2 * i + 1)

                def lhs1(kt, tb):
                    return ysb[:, kt, tb * 128:(tb + 1) * 128], d_ysb[kt]
                out_proj(1, lhs1)

            for tb in range(4):
                S.dma("sp", dr["out"][tok0 + tb * 128: tok0 + (tb + 1) * 128, :], xs[:, tb, :], reads=[d_xs[tb]], key="o%d" % tb)
            self.ostore = d_xs


def _tile_cols(w):
    return np.ascontiguousarray(w.reshape(8, 128, -1).transpose(1, 0, 2))


def _partner_perm():
    perm = np.arange(128)
    for p in range(128):
        d = p % 64
        if d < 8:
            perm[p] = p + 8
        elif d < 16:
            perm[p] = p - 8
    return perm


_NC_CACHE = {}


def _get_nc(mode):
    if mode not in _NC_CACHE:
        _NC_CACHE[mode] = Builder(mode).build()
    return _NC_CACHE[mode]


def _shared_inputs(inp):
    f = np.float32
    w0 = np.asarray(inp["w_in"][0], f)
    w1 = np.asarray(inp["w_in"][1], f)
    perm = _partner_perm()
    sh = {}
    whd = np.empty((8, 128, 8, 768), f)
    for h in range(8):
        q = w0[:, 1024 + 128 * h: 1024 + 128 * (h + 1)]
        k = w0[:, 2048 + 128 * h: 2048 + 128 * (h + 1)]
        v = w0[:, 3072 + 128 * h: 3072 + 128 * (h + 1)]
        g = w0[:, 4096 + 1024 + 128 * h: 4096 + 1024 + 128 * (h + 1)]
        whd[h] = _tile_cols(np.concatenate([q, q[:, perm], k, k[:, perm], v, g], axis=1))
    sh["whd"] = whd
    wag = np.empty((8, 128, 8, 256), f)
    for ct in range(8):
        a = w0[:, 128 * ct:128 * (ct + 1)]
        g = w0[:, 4096 + 128 * ct: 4096 + 128 * (ct + 1)]
        wag[ct] = _tile_cols(np.concatenate([a, g], axis=1))
    sh["wag"] = wag
    sh["wo0"] = np.ascontiguousarray(np.asarray(inp["w_out"][0], f).reshape(16, 128, 1024))
    sh["wo1"] = np.ascontiguousarray(np.asarray(inp["w_out"][1], f).reshape(16, 128, 1024))
    w1g = np.empty((8, 128, 8, 256), f)
    w1u = np.empty((8, 128, 8, 256), f)
    for i in range(8):
        w1u[i] = _tile_cols(w1[:, 256 * i:256 * (i + 1)])
        w1g[i] = _tile_cols(w1[:, 4096 + 256 * i: 4096 + 256 * (i + 1)])
    sh["w1g"] = w1g
    sh["w1u"] = w1u
    w1v = np.empty((8, 128, 8, 256), f)
    for cc in range(8):
        w1v[cc] = _tile_cols(w1[:, 2048 + 256 * cc: 2048 + 256 * (cc + 1)])
    sh["w1v"] = w1v
    pw = np.asarray(inp["pool_w"][0], f)
    sh["poolw"] = np.ascontiguousarray(pw.reshape(4, 2, 128, 256).transpose(2, 0, 1, 3))
    sh["pscale"] = np.ascontiguousarray(np.asarray(inp["pool_scale"][0], f).reshape(8, 128).T)
    sh["lamv"] = np.concatenate([np.asarray(inp[k][0], f) for k in ("lam_q1", "lam_k1", "lam_q2", "lam_k2")])[None, :]
    sh["subln"] = np.ascontiguousarray(np.asarray(inp["diff_subln"][0], f).reshape(128, 1))
    sh["lng"] = np.asarray(inp["sgu_ln_g"], f).reshape(1, 2048)
    sh["lnb"] = np.asarray(inp["sgu_ln_b"], f).reshape(1, 2048)
    sh["wsT"] = np.ascontiguousarray(np.asarray(inp["sgu_w"][0], f).transpose(2, 0, 1))
    sh["tril"] = np.ascontiguousarray(np.tril(np.ones((128, 128), f)).T)
    sh["sgub"] = np.asarray(inp["sgu_b"][0], f).reshape(1, 1024)
    pre = np.asarray(inp["pre_norm"], f)
    sh["pren"] = np.ascontiguousarray(pre.reshape(2, 8, 128).transpose(2, 0, 1).reshape(128, 16))
    sh["postn"] = np.asarray(inp["post_norm"], f).reshape(1, 2048)
    inv_freq = np.power(np.float32(500000.0), -np.arange(8, dtype=f) * np.float32(2.0) / np.float32(16)).astype(f)
    ropec = np.zeros((128, 2), f)
    for p in range(128):
        d = p % 64
        if d < 16:
            ropec[p, 0] = inv_freq[d % 8]
            ropec[p, 1] = (-2 * np.pi) if d < 8 else (2 * np.pi)
    sh["ropec"] = ropec
    return sh


def _core_inputs(inp, b, r):
    f = np.float32
    x = np.asarray(inp["x"][b], f)
    blocks = x.reshape(32, 128, D)
    own = [2 * j + r for j in range(16)]
    oth = [2 * j + (1 - r) for j in range(16)]
    halo = np.zeros((16, 16, D), f)
    for j in range(16):
        s0 = own[j] * 128
        if s0 > 0:
            halo[j] = x[s0 - 16:s0]
    xin = np.concatenate([blocks[own].reshape(-1, D), blocks[oth].reshape(-1, D), halo.reshape(-1, D)], axis=0)
    pos = np.asarray(inp["positions"][b]).astype(np.int32).reshape(32, 128)
    pos = np.concatenate([pos[own].reshape(-1), pos[oth].reshape(-1)])[None, :]
    mask = np.zeros((128, 256), f)
    kk = np.arange(128)[:, None]
    qq = np.arange(128)[None, :]
    mask[:, 0:128] = np.where(kk <= qq, 0.0, NEG)
    mask[:, 128:256] = NEG if r == 0 else 0.0
    invc = np.zeros((4, 128), f)
    for g, w in enumerate((2, 4, 8, 16)):
        if r == 0:
            invc[g] = 1.0 / np.minimum(np.arange(128) + 1, w)
        else:
            invc[g] = 1.0 / w
    return {"xin": np.ascontiguousarray(xin), "pos": np.ascontiguousarray(pos), "mask": mask, "invc": invc.reshape(1, 512)}


L0_KEYS = ("whd", "wag", "wo0", "poolw", "pscale", "lamv", "subln", "ropec", "pren", "postn")
L1_KEYS = ("w1g", "w1u", "w1v", "wo1", "lng", "lnb", "wsT", "tril", "sgub", "pren", "postn")

MODE = "fused"


def kernel(**inp):
    sh = _shared_inputs(inp)
    cores = [(b, r) for b in range(4) for r in range(2)]
    per = [_core_inputs(inp, b, r) for (b, r) in cores]
    if MODE == "fused":
        nc = _get_nc("fused")
        maps = []
        for c in per:
            m = dict(c)
            for k in set(L0_KEYS) | set(L1_KEYS):
                m[k] = sh[k]
            maps.append(m)
        res = run_bass_kernel_spmd(nc, maps, core_ids=list(range(8)))
        outs = [r["out"] for r in res.results]
    else:
        nc0 = _get_nc("L0")
        maps = []
        for c in per:
            m = dict(c)
            for k in L0_KEYS:
                m[k] = sh[k]
            maps.append(m)
        res = run_bass_kernel_spmd(nc0, maps, core_ids=list(range(8)))
        x1 = [np.asarray(r["out"]) for r in res.results]
        nc1 = _get_nc("L1")
        maps = []
        for i in range(8):
            m = {"x1in": x1[i]}
            for k in L1_KEYS:
                m[k] = sh[k]
            maps.append(m)
        res = run_bass_kernel_spmd(nc1, maps, core_ids=list(range(8)))
        outs = [r["out"] for r in res.results]
    out = np.empty((4, 4096, D), np.float32)
    for (b, r), o in zip(cores, outs):
        ob = out[b].reshape(32, 128, D)
        ob[[2 * j + r for j in range(16)]] = np.asarray(o, np.float32).reshape(16, 128, D)
    return out
```

```python
import math
import numpy as np
import concourse.bass as bass
import concourse.mybir as mybir
from concourse.bass_utils import run_bass_kernel_spmd
from contextlib import ExitStack

F32 = mybir.dt.float32
BF16 = mybir.dt.bfloat16
I32 = mybir.dt.int32
AF = mybir.ActivationFunctionType
ALU = mybir.AluOpType
AX = mybir.AxisListType

D = 1024
NS = 16
TOK = 2048
NALL = 4352
NEG = -30000.0
EPS = 1e-6
LAM_INIT = 0.8 - 0.6 * math.exp(-0.3 * 0)
MAGIC = 12582912.0
SEM_LIMIT = 30000


class Dep:
    __slots__ = ("w", "r")

    def __init__(self):
        self.w = None
        self.r = {}


class Tok:
    __slots__ = ("key", "eng", "val")

    def __init__(self, key, eng, val):
        self.key = key
        self.eng = eng
        self.val = val


class Sched:
    ENG = ("pe", "act", "dve", "pool", "sp")

    def __init__(self, nc, es):
        self.nc = nc
        self.es = es
        self.e = dict(pe=nc.tensor, act=nc.scalar, dve=nc.vector, pool=nc.gpsimd, sp=nc.sync)
        self.gen = {k: 0 for k in self.ENG}
        self.semh = {}
        self.cnt = {}
        for k in self.ENG:
            self._newsem(k)
        self.lazy = {k: [] for k in self.ENG}
        self.seen = {k: {} for k in self.ENG}
        self.nwaits = 0
        self.nins = {k: 0 for k in self.ENG}

    def _newsem(self, eng):
        key = "%s%d" % (eng, self.gen[eng])
        self.semh[key] = self.es.enter_context(self.nc.semaphore("s_" + key))
        self.cnt[key] = 0
        return key

    def _curkey(self, eng):
        return "%s%d" % (eng, self.gen[eng])

    def _wait(self, eng, toks):
        need = {}
        for t in toks:
            if t is None:
                continue
            if t.eng == eng and eng == "pe":
                continue
            if t.val is None:
                raise RuntimeError("wait on instruction without inc: %s" % (t.key,))
            if need.get(t.key, 0) < t.val:
                need[t.key] = t.val
        for key, val in need.items():
            if self.seen[eng].get(key, 0) >= val:
                continue
            self.e[eng].wait_ge(self.semh[key], val)
            self.seen[eng][key] = val
            self.nwaits += 1

    def _deps(self, eng, reads, writes):
        toks = []
        for d in reads:
            toks.append(d.w)
        for d in writes:
            if d.w is not None and d.w.eng != eng:
                toks.append(d.w)
            for t in d.r.values():
                if t.eng == eng:
                    continue
                toks.append(t)
        self._wait(eng, toks)

    def _record(self, tok, reads, writes):
        for d in reads:
            d.r[tok.key] = tok
        for d in writes:
            d.w = tok
            d.r = {}

    def op(self, eng, fn, reads=(), writes=(), inc=True):
        self._deps(eng, reads, writes)
        ins = fn(self.e[eng])
        self.nins[eng] += 1
        key = self._curkey(eng)
        tok = Tok(key, eng, None)
        if inc:
            ins.then_inc(self.semh[key], 1)
            self.cnt[key] += 1
            tok.val = self.cnt[key]
            for t in self.lazy[eng]:
                t.val = tok.val
            self.lazy[eng] = []
            if self.cnt[key] >= SEM_LIMIT:
                self.gen[eng] += 1
                self._newsem(eng)
        else:
            self.lazy[eng].append(tok)
        self._record(tok, reads, writes)
        return tok

    def dma(self, q, out, in_, reads=(), writes=(), key=None):
        self._deps(q, reads, writes)
        if key is None:
            self.nauto = getattr(self, "nauto", 0) + 1
            key = "auto%d" % self.nauto
        key = "d_" + key
        if key not in self.semh:
            self.semh[key] = self.es.enter_context(self.nc.semaphore(key))
            self.cnt[key] = 0
        self.e[q].dma_start(out=out, in_=in_).then_inc(self.semh[key], 16)
        self.nins[q] += 1
        self.cnt[key] += 16
        assert self.cnt[key] < 2 * SEM_LIMIT
        tok = Tok(key, "dma", self.cnt[key])
        self._record(tok, reads, writes)
        return tok

    def barrier(self):
        for e in self.ENG:
            assert not self.lazy[e], "barrier with un-incremented %s instructions" % e
        keys = [(k, v) for k, v in self.cnt.items() if v > 0]
        for eng in self.ENG:
            own = self._curkey(eng)
            for key, val in keys:
                if key == own or self.seen[eng].get(key, 0) >= val:
                    continue
                self.e[eng].wait_ge(self.semh[key], val)
                self.seen[eng][key] = val
                self.nwaits += 1

    def wait_all(self, eng, deps):
        toks = []
        for d in deps:
            toks.append(d.w)
            toks.extend(d.r.values())
        self._wait(eng, toks)


def bcast_last(ap, n):
    dims = [list(x) for x in ap.ap]
    assert dims[-1][1] == 1
    dims[-1] = [0, n]
    return bass.AP(ap.tensor, ap.offset, dims)


def bcast_mid(ap, n):
    dims = [list(x) for x in ap.ap]
    assert len(dims) == 2
    return bass.AP(ap.tensor, ap.offset, [dims[0], [0, n], dims[1]])


class Builder:
    def __init__(self, mode, debug=False):
        self.mode = mode
        self.debug = debug
        self.dbg_names = []
        self.nc = bass.Bass("TRN2", target_bir_lowering=False)
        self.es = ExitStack()

    def dram_in(self, name, shape, dt=F32):
        return self.nc.dram_tensor(name, list(shape), dt, kind="ExternalInput").ap()

    def dump(self, name, ap, deps):
        if not getattr(self, "debug", False):
            return
        t = self.nc.dram_tensor("dbg_" + name, list(ap.shape), ap.dtype, kind="ExternalOutput").ap()
        self.S.dma("sp", t[:], ap, reads=deps)
        self.dbg_names.append("dbg_" + name)

    def sb(self, es, name, shape, dt):
        return es.enter_context(self.nc.sbuf_tensor("sb_" + name, list(shape), dt))

    def build(self):
        nc = self.nc
        mode = self.mode
        with self.es as es:
            S = self.S = Sched(nc, es)
            dr = self.dr = {}
            if mode in ("fused", "L0"):
                dr["xin"] = self.dram_in("xin", [NALL, D])
                dr["pos"] = self.dram_in("pos", [1, 4096], I32)
                dr["ropec"] = self.dram_in("ropec", [128, 2])
                dr["mask"] = self.dram_in("mask", [128, 256])
                dr["invc"] = self.dram_in("invc", [1, 512])
                dr["whd"] = self.dram_in("whd", [8, 128, 8, 768])
                dr["wag"] = self.dram_in("wag", [8, 128, 8, 256])
                dr["wo0"] = self.dram_in("wo0", [16, 128, 1024])
                dr["poolw"] = self.dram_in("poolw", [128, 4, 2, 256])
                dr["pscale"] = self.dram_in("pscale", [128, 8])
                dr["lamv"] = self.dram_in("lamv", [1, 256])
                dr["subln"] = self.dram_in("subln", [128, 1])
            if mode in ("fused", "L1"):
                dr["w1g"] = self.dram_in("w1g", [8, 128, 8, 256])
                dr["w1u"] = self.dram_in("w1u", [8, 128, 8, 256])
                dr["w1v"] = self.dram_in("w1v", [8, 128, 8, 256])
                dr["wo1"] = self.dram_in("wo1", [16, 128, 1024])
                dr["lng"] = self.dram_in("lng", [1, 2048])
                dr["lnb"] = self.dram_in("lnb", [1, 2048])
                dr["wsT"] = self.dram_in("wsT", [128, 8, 128])
                dr["tril"] = self.dram_in("tril", [128, 128])
                dr["sgub"] = self.dram_in("sgub", [1, 1024])
            if mode == "L1":
                dr["x1in"] = self.dram_in("x1in", [TOK, D])
            dr["pren"] = self.dram_in("pren", [128, 16])
            dr["postn"] = self.dram_in("postn", [1, 2048])
            dr["out"] = nc.dram_tensor("out", [TOK, D], F32, kind="ExternalOutput").ap()

            self.pb = [es.enter_context(nc.psum_tensor("pb%d" % i, [128, 512], F32)) for i in range(8)]
            self.dpb = [Dep() for _ in range(8)]

            self.ident = self.sb(es, "ident", [128, 128], BF16)
            self.d_ident = Dep()
            io = self.sb(es, "iota_f", [128, 128], F32)
            ip = self.sb(es, "iota_p", [128, 1], F32)
            d_io, d_ip = Dep(), Dep()
            S.op("pool", lambda e: e.iota(io[:], [[1, 128]], base=0, channel_multiplier=0,
                                          allow_small_or_imprecise_dtypes=True), writes=[d_io])
            S.op("pool", lambda e: e.iota(ip[:], [[1, 1]], base=0, channel_multiplier=1,
                                          allow_small_or_imprecise_dtypes=True), writes=[d_ip])
            S.op("dve", lambda e: e.tensor_scalar(self.ident[:], io[:], ip[:, 0:1], None, ALU.is_equal),
                 reads=[d_io, d_ip], writes=[self.d_ident])
            self.pren = self.sb(es, "pren", [128, 16], F32)
            self.d_pren = Dep()
            S.dma("sp", self.pren[:], dr["pren"][:], writes=[self.d_pren])
            self.small = self.sb(es, "small", [128, 64], F32)
            self.small_i = 0
            self.ostore = []

            if mode in ("fused", "L0"):
                self.ygA = self.sb(es, "ygA", [128, 8, TOK], BF16)
                self.d_ygA = [[Dep() for _ in range(4)] for _ in range(8)]
                self.aTh = self.sb(es, "aTh", [128, 8, 256], F32)
                self.d_aTh = Dep()
                with ExitStack() as es1:
                    self.phase_attention(es1)
            if mode in ("fused", "L0"):
                self.dump("ygA", self.ygA[:], [d for l in self.d_ygA for d in l])
                self.dump("aTh", self.aTh[:], [self.d_aTh])
            S.barrier()
            with ExitStack() as es2:
                self.phase_final(es2)
            S.wait_all("sp", self.ostore)
        return nc

    def rms_transpose(self, x_ap, d_x, hn, d_hn, hT_out, d_hT, gain_ap, d_gain, ptr_i, scratch):
        S = self.S
        ss, d_ss = scratch
        junk = self.junk
        S.op("act", lambda e: e.activation(junk[:], x_ap, AF.Square, accum_out=ss[:, 0:1]),
             reads=[d_x], writes=[self.d_junk, d_ss])
        S.op("dve", lambda e: e.tensor_scalar(ss[:, 1:2], ss[:, 0:1], 1.0 / D, EPS, ALU.mult, ALU.add),
             reads=[d_ss], writes=[d_ss])
        S.op("act", lambda e: e.activation(ss[:, 2:3], ss[:, 1:2], AF.Sqrt), reads=[d_ss], writes=[d_ss])
        S.op("dve", lambda e: e.reciprocal(ss[:, 3:4], ss[:, 2:3]), reads=[d_ss], writes=[d_ss])
        S.op("act", lambda e: e.activation(hn[:], x_ap, AF.Copy, scale=ss[:, 3:4]),
             reads=[d_x, d_ss], writes=[d_hn])
        ptr = self.pb[ptr_i][:].bitcast(BF16)
        for kt in range(8):
            S.op("pe", lambda e, kt=kt: e.transpose(ptr[:, kt * 128:(kt + 1) * 128],
                                                     hn[:, kt * 128:(kt + 1) * 128], self.ident[:]),
                 reads=[d_hn, self.d_ident], writes=[self.dpb[ptr_i]], inc=(kt == 7))
        S.op("dve", lambda e: e.tensor_tensor(hT_out, ptr.rearrange("p (k t) -> p k t", k=8),
                                              bcast_last(gain_ap, 128), ALU.mult),
             reads=[self.dpb[ptr_i], d_gain], writes=[d_hT])

    def phase_attention(self, es):
        S, nc, dr = self.S, self.nc, self.dr
        sb = lambda n, s, d: self.sb(es, n, s, d)
        hT = sb("hT", [128, 8, NALL], BF16)
        d_hT = [Dep() for _ in range(34)]
        Ct = sb("ropeC", [128, 4096], BF16)
        St = sb("ropeS", [128, 4096], BF16)
        d_C = [Dep() for _ in range(4)]
        d_St = [Dep() for _ in range(4)]
        ropec = sb("ropec", [128, 2], F32)
        d_ropec = Dep()
        S.dma("sp", ropec[:], dr["ropec"][:], writes=[d_ropec])
        maskb = sb("maskb", [128, 256], BF16)
        d_mask = Dep()
        S.dma("pool", maskb[:], dr["mask"][:], writes=[d_mask])
        lamv = sb("lamv", [128, 256], F32)
        d_lamv = Dep()
        S.dma("sp", lamv[:], dr["lamv"].partition_broadcast(128), writes=[d_lamv])
        subln = sb("subln", [128, 1], F32)
        d_subln = Dep()
        S.dma("sp", subln[:], dr["subln"][:], writes=[d_subln])

        lsm = sb("lam_small", [128, 8], F32)
        d_lsm = Dep()
        ltmp = sb("lam_tmp", [128, 128], F32)
        d_ltmp = Dep()
        lv4 = lamv[:].rearrange("p (a d) -> p a d", a=4)
        S.op("dve", lambda e: e.tensor_tensor(ltmp[:, 0:64], lv4[:, 0, :], lv4[:, 1, :], ALU.mult),
             reads=[d_lamv], writes=[d_ltmp])
        S.op("dve", lambda e: e.tensor_tensor(ltmp[:, 64:128], lv4[:, 2, :], lv4[:, 3, :], ALU.mult),
             reads=[d_lamv], writes=[d_ltmp])
        S.op("dve", lambda e: e.reduce_sum(lsm[:, 0:2], ltmp[:].rearrange("p (a d) -> p a d", a=2), AX.X),
             reads=[d_ltmp], writes=[d_lsm])
        S.op("act", lambda e: e.activation(lsm[:, 2:4], lsm[:, 0:2], AF.Exp), reads=[d_lsm], writes=[d_lsm])
        S.op("dve", lambda e: e.scalar_tensor_tensor(lsm[:, 4:5], lsm[:, 3:4], -LAM_INIT, lsm[:, 2:3],
                                                      ALU.add, ALU.subtract), reads=[d_lsm], writes=[d_lsm])
        lamvec = sb("lamvec", [128, 4, 2], F32)
        d_lamvec = Dep()
        S.op("dve", lambda e: e.memset(lamvec[:], 1.0), writes=[d_lamvec])
        S.op("dve", lambda e: e.tensor_copy(lamvec[:, :, 1:2], bcast_mid(lsm[:, 4:5], 4)),
             reads=[d_lsm], writes=[d_lamvec])

        wst = [sb("wst%d" % i, [128, 8, 768], BF16) for i in range(2)]
        d_wst = [Dep() for _ in range(2)]
        est = ExitStack()
        with est:
            sbt = lambda n, s_, d: self.sb(est, n, s_, d)
            self.junk = sbt("junk", [128, 1024], F32)
            self.d_junk = Dep()
            xb = [sbt("xb%d" % i, [128, D], F32) for i in range(3)]
            d_xb = [Dep() for _ in range(3)]
            hnb = [sbt("hnb%d" % i, [128, D], BF16) for i in range(2)]
            d_hnb = [Dep() for _ in range(2)]
            ssb = [sbt("ssb%d" % i, [128, 4], F32) for i in range(2)]
            d_ssb = [Dep() for _ in range(2)]
            pre0 = self.pren[:, 0:8]
            for blk in range(34):
                i3, i2 = blk % 3, blk % 2
                S.dma("sp", xb[i3][:], dr["xin"][blk * 128:(blk + 1) * 128, :], writes=[d_xb[i3]], key="x%d" % i3)
                self.rms_transpose(xb[i3][:], d_xb[i3], hnb[i2], d_hnb[i2],
                                   hT[:, :, blk * 128:(blk + 1) * 128], d_hT[blk],
                                   pre0.rearrange("p (k o) -> p k o", o=1), self.d_pren, 6 + i2, (ssb[i2], d_ssb[i2]))

            posi = sbt("posi", [128, 1024], I32)
            d_posi = Dep()
            rt = [sbt("rt%d" % i, [128, 1024], F32) for i in range(3)]
            d_rt = [Dep() for _ in range(3)]
            for c in range(4):
                cs = slice(c * 1024, (c + 1) * 1024)
                S.dma("sp", posi[:], dr["pos"][:, cs].partition_broadcast(128), writes=[d_posi], key="pos")
                S.op("dve", lambda e: e.tensor_copy(rt[0][:], posi[:]), reads=[d_posi], writes=[d_rt[0]])
                S.op("dve", lambda e: e.tensor_scalar(rt[1][:], rt[0][:], ropec[:, 0:1], float(np.float32(1.0 / (2 * np.pi))),
                                                      ALU.mult, ALU.mult), reads=[d_rt[0], d_ropec], writes=[d_rt[1]])
                S.op("dve", lambda e: e.tensor_scalar(rt[2][:], rt[1][:], MAGIC, MAGIC, ALU.add, ALU.subtract),
                     reads=[d_rt[1]], writes=[d_rt[2]])
                S.op("dve", lambda e: e.tensor_sub(rt[1][:], rt[1][:], rt[2][:]), reads=[d_rt[1], d_rt[2]], writes=[d_rt[1]])
                S.op("act", lambda e, cs=cs: e.activation(St[:, cs], rt[1][:], AF.Sin, scale=ropec[:, 1:2]),
                     reads=[d_rt[1], d_ropec], writes=[d_St[c]])
                S.op("dve", lambda e: e.tensor_scalar(rt[2][:], rt[1][:], -1.0, None, ALU.mult),
                     reads=[d_rt[1]], writes=[d_rt[2]])
                S.op("dve", lambda e: e.tensor_tensor(rt[2][:], rt[2][:], rt[1][:], ALU.max),
                     reads=[d_rt[1], d_rt[2]], writes=[d_rt[2]])
                S.op("dve", lambda e: e.tensor_scalar(rt[2][:], rt[2][:], float(-2 * np.pi), float(np.pi / 2), ALU.mult, ALU.add),
                     reads=[d_rt[2]], writes=[d_rt[2]])
                S.op("act", lambda e, cs=cs: e.activation(Ct[:, cs], rt[2][:], AF.Sin),
                     reads=[d_rt[2]], writes=[d_C[c]])

        S.barrier()
        self.dump("hT0", hT[:, :, 0:256], d_hT[0:2])
        self.dump("hTh", hT[:, :, 4096:4352], d_hT[32:34])
        self.dump("Ct", Ct[:], d_C)
        self.dump("St", St[:], d_St)
        self.dump("lsm", lsm[:], [d_lsm])
        for ct in range(8):
            s = ct % 2
            S.dma("pool", wst[s][:, :, 0:256], dr["wag"][ct], writes=[d_wst[s]], key="wh%d" % s)
            pi = 4 + s
            for kt in range(8):
                S.op("pe", lambda e, kt=kt, s=s, pi=pi: e.matmul(self.pb[pi][:, 0:256], lhsT=wst[s][:, kt, 0:128],
                                                                 rhs=hT[:, kt, 4096:4352], start=(kt == 0), stop=(kt == 7)),
                     reads=[d_wst[s], d_hT[32], d_hT[33]], writes=[self.dpb[pi]], inc=(kt == 7))
            S.op("act", lambda e, ct=ct, pi=pi: e.activation(self.aTh[:, ct, :], self.pb[pi][:, 0:256], AF.Copy),
                 reads=[self.dpb[pi]], writes=[self.d_aTh])

        KT = sb("KT", [128, 4096], BF16)
        d_KT = [Dep() for _ in range(8)]
        QTc = [sb("QT%d" % i, [128, TOK], BF16) for i in range(2)]
        d_QT = [Dep() for _ in range(4)]
        S.op("pool", lambda e: e.memset(QTc[0][64:128, :], 0.0), writes=d_QT)
        S.op("pool", lambda e: e.memset(QTc[1][0:64, :], 0.0), writes=d_QT)
        V = sb("V", [128, 32, 129], BF16)
        d_V = [Dep() for _ in range(8)]
        S.op("pool", lambda e: e.memset(V[:, :, 128:129], 1.0), writes=d_V)
        sgT = sb("sgT", [128, TOK], BF16)
        d_sgT = [Dep() for _ in range(4)]
        ET = [sb("ET%d" % i, [128, 512], BF16) for i in range(4)]
        d_ET = [Dep() for _ in range(4)]
        rtmp = [sb("rtmp%d" % i, [128, 512], F32) for i in range(2)]
        d_rtmp = [Dep() for _ in range(2)]
        stage = [sb("stage%d" % i, [128, 4, 2, 129], F32) for i in range(2)]
        d_stage = [Dep() for _ in range(2)]
        eo = [sb("eo%d" % i, [128, 4, 128], F32) for i in range(2)]
        d_eo = [Dep() for _ in range(2)]
        eon = sb("eon", [128, 4, 128], BF16)
        d_eon = Dep()
        esm = sb("esm", [128, 32], F32)
        d_esm = Dep()

        et_i = [0]
        st_i = [0]
        rt_i = [0]
        S.dma("pool", wst[0][:], dr["whd"][0], writes=[d_wst[0]], key="wh0")
        for h in range(8):
            s = h % 2
            if h + 1 < 8:
                S.dma("pool", wst[1 - s][:], dr["whd"][h + 1], writes=[d_wst[1 - s]], key="wh%d" % (1 - s))
            w = wst[s]
            dw = d_wst[s]

            def proj_fm(col0, tok0, pi, first_blk):
                for kt in range(8):
                    S.op("pe", lambda e, kt=kt: e.matmul(self.pb[pi][:, :], lhsT=w[:, kt, col0:col0 + 128],
                                                         rhs=hT[:, kt, tok0:tok0 + 512], start=(kt == 0), stop=(kt == 7)),
                         reads=[dw] + d_hT[first_blk:first_blk + 4], writes=[self.dpb[pi]], inc=(kt == 7))

            def rope_chunk(col0, tok0, dst, d_dst, cidx):
                proj_fm(col0, tok0, 6, tok0 // 128)
                proj_fm(col0 + 128, tok0, 7, tok0 // 128)
                a, b = 0, 1
                S.op("dve", lambda e: e.tensor_tensor(rtmp[a][:], self.pb[6][:, :], Ct[:, tok0:tok0 + 512], ALU.mult),
                     reads=[self.dpb[6], d_C[cidx]], writes=[d_rtmp[a]])
                S.op("dve", lambda e: e.tensor_tensor(rtmp[b][:], self.pb[7][:, :], St[:, tok0:tok0 + 512], ALU.mult),
                     reads=[self.dpb[7], d_St[cidx]], writes=[d_rtmp[b]])
                if dst is None:
                    cs_ = slice(tok0, tok0 + 512)
                    S.op("pool", lambda e: e.tensor_tensor(QTc[0][0:64, cs_], rtmp[a][0:64, :], rtmp[b][0:64, :], ALU.add),
                         reads=[d_rtmp[a], d_rtmp[b]], writes=[d_dst])
                    S.op("dve", lambda e: e.tensor_tensor(QTc[1][64:128, cs_], rtmp[a][64:128, :], rtmp[b][64:128, :], ALU.add),
                         reads=[d_rtmp[a], d_rtmp[b]], writes=[d_dst])
                else:
                    S.op("pool", lambda e: e.tensor_tensor(dst, rtmp[a][:], rtmp[b][:], ALU.add),
                         reads=[d_rtmp[a], d_rtmp[b]], writes=[d_dst])

            for c in range(8):
                rope_chunk(256, c * 512, KT[:, c * 512:(c + 1) * 512], d_KT[c], c // 2)
            for c in range(4):
                rope_chunk(0, c * 512, None, d_QT[c], c // 2)
            for c in range(4):
                pi = 6 + c % 2
                proj_fm(640, c * 512, pi, c * 4)
                S.op("act", lambda e, c=c, pi=pi: e.activation(sgT[:, c * 512:(c + 1) * 512], self.pb[pi][:, :], AF.Silu),
                     reads=[self.dpb[pi]], writes=[d_sgT[c]])
            for g4 in range(8):
                pi = 6 + g4 % 2
                for i in range(4):
                    kb = g4 * 4 + i
                    for kt in range(8):
                        S.op("pe", lambda e, kt=kt, kb=kb, i=i: e.matmul(
                            self.pb[pi][:, i * 128:(i + 1) * 128], lhsT=hT[:, kt, kb * 128:(kb + 1) * 128],
                            rhs=w[:, kt, 512:640], start=(kt == 0), stop=(kt == 7)),
                             reads=[dw, d_hT[kb]], writes=[self.dpb[pi]], inc=(kt == 7 and i == 3))
                S.op("act", lambda e, g4=g4, pi=pi: e.activation(
                    V[:, g4 * 4:(g4 + 1) * 4, 0:128], self.pb[pi][:, :].rearrange("p (a d) -> p a d", a=4), AF.Copy),
                     reads=[self.dpb[pi]], writes=[d_V[g4]])

            if h == 0:
                self.dump("KT", KT[:], d_KT)
                self.dump("QT", QTc[0][:], d_QT)
                self.dump("V", V[:], d_V)
                self.dump("sgT", sgT[:], d_sgT)
            tiles = []
            for G in range(4):
                blocks = [(i, 0, None) for i in range(4 * G)] + [(16 + i, 0, None) for i in range(4 * G)]
                for a4 in range(4):
                    blocks.append((4 * G + a4, a4, 0))
                    blocks.append((16 + 4 * G + a4, a4, 1))
                for c in range(2):
                    for bi, (kb, a4, m) in enumerate(blocks):
                        tiles.append(dict(G=G, c=c, kb=kb, a=a4, m=m, first=(bi == 0), endgrp=(bi == len(blocks) - 1 and c == 1)))

            def emit_qk(n):
                t = tiles[n]
                G, c, kb, a4, m = t["G"], t["c"], t["kb"], t["a"], t["m"]
                ps = slice(c * 64, (c + 1) * 64)
                sbk = n % 2
                q0 = (4 * G + a4) * 128
                q1 = (4 * G + 4) * 128
                rd = [d_KT[kb // 4], d_QT[G]]
                if m is None:
                    S.op("pe", lambda e: e.matmul(self.pb[sbk][:, :], lhsT=KT[:, kb * 128:(kb + 1) * 128],
                                                  rhs=QTc[c][:, q0:q1], start=True, stop=True),
                         reads=rd, writes=[self.dpb[sbk]], inc=True)
                else:
                    c0 = a4 * 128
                    S.op("pe", lambda e: e.matmul(self.pb[sbk][:, c0:c0 + 128], lhsT=KT[:, kb * 128:(kb + 1) * 128],
                                                  rhs=QTc[c][:, q0:q0 + 128], start=True, stop=False),
                         reads=rd, writes=[self.dpb[sbk]], inc=False)
                    S.op("pe", lambda e: e.matmul(self.pb[sbk][:, c0:c0 + 128], lhsT=self.ident[:, :],
                                                  rhs=maskb[:, m * 128:(m + 1) * 128], start=False, stop=True),
                         reads=[self.d_ident, d_mask], writes=[self.dpb[sbk]], inc=(a4 == 3))
                    if a4 < 3:
                        S.op("pe", lambda e: e.matmul(self.pb[sbk][:, c0 + 128:512], lhsT=KT[:, kb * 128:(kb + 1) * 128],
                                                      rhs=QTc[c][:, q0 + 128:q1], start=True, stop=True),
                             reads=rd, writes=[self.dpb[sbk]], inc=True)

            def emit_exp_pv(n):
                t = tiles[n]
                G, c, kb, a4, m = t["G"], t["c"], t["kb"], t["a"], t["m"]
                sbk = n % 2
                ei = n % 4
                c0 = a4 * 128
                S.op("act", lambda e: e.activation(ET[ei][:, c0:512], self.pb[sbk][:, c0:512], AF.Exp, scale=0.125),
                     reads=[self.dpb[sbk]], writes=[d_ET[ei]])
                for sl in range(a4, 4):
                    ob = 2 + sl
                    O = self.pb[ob][:, 0:258].rearrange("p (c d) -> p c d", c=2)
                    last = (m == 1 and a4 == sl)
                    S.op("pe", lambda e, sl=sl, O=O, last=last: e.matmul(
                        O[:, c, :], lhsT=ET[ei][:, sl * 128:(sl + 1) * 128], rhs=V[:, kb, :],
                        start=t["first"], stop=last),
                         reads=[d_ET[ei], d_V[kb // 4]], writes=[self.dpb[ob]], inc=(last and c == 1))

            pending = None
            emit_qk(0)
            for n in range(len(tiles)):
                if n + 1 < len(tiles):
                    emit_qk(n + 1)
                emit_exp_pv(n)
                t = tiles[n]
                if not t["endgrp"]:
                    continue
                q4 = t["G"]
                stg = stage[q4 % 2]
                for sl in range(4):
                    O = self.pb[2 + sl][:, 0:258].rearrange("p (c d) -> p c d", c=2)
                    S.op("dve", lambda e, O=O, sl=sl: e.tensor_copy(stg[:, sl, :, :], O),
                         reads=[self.dpb[2 + sl]], writes=[d_stage[q4 % 2]])
                sl = 3
                if pending is not None:
                    self.attn_epilogue_b(*pending)
                    pending = None
                if sl == 3 and h == 0 and q4 == 0:
                    self.dump("stage", stg[:], [d_stage[0]])
                if sl == 3:
                    ds = d_stage[q4 % 2]
                    rz0 = esm[:, 0:8].rearrange("p (a c) -> p a c", c=2)
                    rz = esm[:, 8:16].rearrange("p (a c) -> p a c", c=2)
                    S.op("dve", lambda e, stg=stg: e.reciprocal(rz0, stg[:, :, :, 128]), reads=[ds], writes=[d_esm])
                    S.op("dve", lambda e: e.tensor_tensor(rz, rz0, lamvec[:], ALU.mult),
                         reads=[d_esm, d_lamvec], writes=[d_esm])
                    S.op("dve", lambda e, stg=stg: e.tensor_tensor(eo[0][:], stg[:, :, 0, 0:128], bcast_last(rz[:, :, 0:1], 128), ALU.mult),
                         reads=[ds, d_esm], writes=[d_eo[0]])
                    S.op("dve", lambda e, stg=stg: e.tensor_tensor(eo[1][:], stg[:, :, 1, 0:128], bcast_last(rz[:, :, 1:2], 128), ALU.mult),
                         reads=[ds, d_esm], writes=[d_eo[1]])
                    S.op("pool", lambda e: e.tensor_tensor(eo[0][:], eo[0][:], eo[1][:], ALU.add),
                         reads=[d_eo[0], d_eo[1]], writes=[d_eo[0]])
                    S.op("pool", lambda e: e.tensor_tensor(eo[1][:], eo[0][:], eo[0][:], ALU.mult),
                         reads=[d_eo[0]], writes=[d_eo[1]])
                    S.op("dve", lambda e: e.reduce_sum(esm[:, 16:20], eo[1][:], AX.X), reads=[d_eo[1]], writes=[d_esm])
                    S.op("dve", lambda e: e.tensor_scalar(esm[:, 20:24], esm[:, 16:20], 1.0 / 128, 1e-5, ALU.mult, ALU.add),
                         reads=[d_esm], writes=[d_esm])
                    S.op("act", lambda e: e.activation(esm[:, 24:28], esm[:, 20:24], AF.Ln), reads=[d_esm], writes=[d_esm])
                    S.op("act", lambda e: e.activation(esm[:, 28:32], esm[:, 24:28], AF.Exp, scale=-0.5), reads=[d_esm], writes=[d_esm])
                    S.op("dve", lambda e: e.scalar_tensor_tensor(eon[:], eo[0][:], 1.0 - LAM_INIT,
                                                                  bcast_last(esm[:, 28:32].rearrange("p (a o) -> p a o", o=1), 128),
                                                                  ALU.mult, ALU.mult),
                         reads=[d_eo[0], d_esm], writes=[d_eon])
                    pending = (h, q4, eon, d_eon, subln, d_subln, sgT, d_sgT)
                    if h == 0 and q4 == 0:
                        self.dump("eon", eon[:], [d_eon])
                        self.dump("esm", esm[:], [d_esm])
            if pending is not None:
                self.attn_epilogue_b(*pending)
                pending = None

    def attn_epilogue_b(self, h, q4, eon, d_eon, subln, d_subln, sgT, d_sgT):
        S = self.S
        ptr = self.pb[7][:].bitcast(BF16)
        for i in range(4):
            S.op("pe", lambda e, i=i: e.transpose(ptr[:, i * 128:(i + 1) * 128], eon[:, i, :], self.ident[:]),
                 reads=[d_eon, self.d_ident], writes=[self.dpb[7]], inc=(i == 3))
        S.op("dve", lambda e: e.scalar_tensor_tensor(self.ygA[:, h, q4 * 512:(q4 + 1) * 512], ptr[:, 0:512], subln[:, 0:1],
                                                      sgT[:, q4 * 512:(q4 + 1) * 512], ALU.mult, ALU.mult),
             reads=[self.dpb[7], d_subln, d_sgT[q4]], writes=[self.d_ygA[h][q4]])

    def phase_final(self, es):
        S, nc, dr, mode = self.S, self.nc, self.dr, self.mode
        sb = lambda n, s, d: self.sb(es, n, s, d)
        L0 = mode in ("fused", "L0")
        L1 = mode in ("fused", "L1")
        self.junk = sb("junkf", [128, 1024], F32)
        self.d_junk = Dep()
        postn = sb("postn", [128, 2, 1024], F32)
        d_postn = Dep()
        S.dma("sp", postn[:].rearrange("p a d -> p (a d)"), dr["postn"].partition_broadcast(128), writes=[d_postn])
        xs = sb("xs", [128, 4, D], F32)
        d_xs = [Dep() for _ in range(4)]
        hn = [sb("hnf%d" % i, [128, D], BF16) for i in range(4)]
        d_hn = [Dep() for _ in range(4)]
        ss4 = [sb("ssf%d" % i, [128, 8], F32) for i in range(4)]
        d_ss4 = [Dep() for _ in range(4)]
        hTs = sb("hTs", [128, 8, 512], BF16)
        d_hTs = [Dep() for _ in range(4)]
        ysb = sb("ysb", [128, 16, 512], BF16)
        d_ysb = [Dep() for _ in range(16)]
        tmpf = [sb("tmpf%d" % i, [128, D], F32) for i in range(2)]
        d_tmpf = [Dep() for _ in range(2)]
        wr = [sb("wr%d" % i, [128, 8, 256], BF16) for i in range(6)]
        d_wr = [Dep() for _ in range(6)]
        wo = [sb("wor%d" % i, [128, 1024], BF16) for i in range(3)]
        d_wo = [Dep() for _ in range(3)]
        if L0:
            invc = sb("invc", [128, 4, 128], F32)
            d_invc = Dep()
            S.dma("sp", invc[:].rearrange("p a d -> p (a d)"), dr["invc"].partition_broadcast(128), writes=[d_invc])
            poolw = sb("poolw", [128, 4, 2, 256], BF16)
            d_poolw = Dep()
            S.dma("pool", poolw[:], dr["poolw"][:], writes=[d_poolw])
            pscale = sb("pscale", [128, 8], F32)
            d_pscale = Dep()
            S.dma("sp", pscale[:], dr["pscale"][:], writes=[d_pscale])
            Abuf = [sb("Abuf%d" % i, [128, 4, 144], F32) for i in range(2)]
            d_A = [Dep() for _ in range(2)]
            Sb = [sb("Sbuf%d" % i, [128, 4, 144], F32) for i in range(2)]
            d_Sb = [Dep() for _ in range(2)]
            pooled = [sb("pooled%d" % i, [128, 2, 512], BF16) for i in range(2)]
            d_pooled = [[Dep() for _ in range(2)] for _ in range(2)]
            ptmp = sb("ptmp", [128, 128], F32)
            d_ptmp = Dep()
        if L1:
            lng = sb("lng", [128, 2048], F32)
            lnb = sb("lnb", [128, 2048], F32)
            d_ln = Dep()
            S.dma("sp", lng[:], dr["lng"].partition_broadcast(128), writes=[d_ln], key="ln")
            S.dma("sp", lnb[:], dr["lnb"].partition_broadcast(128), writes=[d_ln], key="ln")
            wsf = sb("wsf", [128, 8, 128], F32)
            trl = sb("trl", [128, 128], F32)
            d_wsf = Dep()
            S.dma("sp", wsf[:], dr["wsT"][:], writes=[d_wsf], key="wsf")
            S.dma("sp", trl[:], dr["tril"][:], writes=[d_wsf], key="wsf")
            wsT = sb("wsTb", [128, 8, 128], BF16)
            d_wsT = Dep()
            S.op("dve", lambda e: e.tensor_tensor(wsT[:], wsf[:], bcast_mid(trl[:], 8), ALU.mult), reads=[d_wsf], writes=[d_wsT])
            bsb = sb("bsb", [128, 8, 128], F32)
            d_bsb = Dep()
            S.dma("sp", bsb[:].rearrange("p a d -> p (a d)"), dr["sgub"].partition_broadcast(128), writes=[d_bsb])
            vb = sb("vb", [128, 4, 2048], BF16)
            d_vb = [Dep() for _ in range(4)]
            utmp = [sb("utmp%d" % i, [128, 512], BF16) for i in range(2)]
            d_utmp = [Dep() for _ in range(2)]
            lsm4 = [sb("lnsm%d" % i, [128, 12], F32) for i in range(1)]
            d_lsm4 = [Dep() for _ in range(1)]
            lsq = sb("lsq", [128, 40], F32)
            d_lsq = Dep()
            mtmp = [sb("mtmp%d" % i, [128, 512], F32) for i in range(2)]
            d_mtmp = [Dep() for _ in range(2)]

        per_sb = []
        if L0:
            for ct in range(8):
                per_sb.append(("wr", dr["wag"][ct]))
            for kt in range(16):
                per_sb.append(("wo", dr["wo0"][kt]))
        if L1:
            for cc in range(8):
                per_sb.append(("wr", dr["w1v"][cc]))
            for i in range(8):
                per_sb.append(("wr", dr["w1g"][i]))
            for i in range(8):
                per_sb.append(("wr", dr["w1u"][i]))
            for kt in range(16):
                per_sb.append(("wo", dr["wo1"][kt]))
        nper = len(per_sb)
        n_wr = sum(1 for k, _ in per_sb if k == "wr")
        n_wo = nper - n_wr
        scr_wr = nc.dram_tensor("scr_wr", [n_wr, 128, 8, 256], BF16).ap()
        scr_wo = nc.dram_tensor("scr_wo", [n_wo, 128, 1024], BF16).ap()
        scr_of = []
        c_wr = c_wo = 0
        for k, _ in per_sb:
            if k == "wr":
                scr_of.append(scr_wr[c_wr]); c_wr += 1
            else:
                scr_of.append(scr_wo[c_wo]); c_wo += 1
        d_scr = [Dep() for _ in range(nper)]
        items = per_sb * 4
        ring = {"wr": (wr, d_wr), "wo": (wo, d_wo)}
        cnt = {"wr": 0, "wo": 0}
        slot_of = []
        for kind, _ in items:
            slot_of.append(cnt[kind] % len(ring[kind][0]))
            cnt[kind] += 1
        issued = [0]
        inflight = {"wr": 0, "wo": 0}
        consumed = [0]

        def pump():
            while issued[0] < len(items):
                i = issued[0]
                kind, src = items[i]
                bufs, deps = ring[kind]
                if inflight[kind] >= len(bufs):
                    break
                sl = slot_of[i]
                j = i % nper
                if i < nper:
                    S.dma("pool", bufs[sl][:], src, writes=[deps[sl]], key="%s%d" % (kind, sl))
                    S.dma("sp", scr_of[j], bufs[sl][:], reads=[deps[sl]], writes=[d_scr[j]], key="sw%d" % (j % 32))
                else:
                    S.dma("sp", bufs[sl][:], scr_of[j], reads=[d_scr[j]], writes=[deps[sl]], key="%s%d" % (kind, sl))
                inflight[kind] += 1
                issued[0] += 1

        def take(kind):
            i = consumed[0]
            assert items[i][0] == kind, (items[i][0], kind)
            assert i < issued[0]
            sl = slot_of[i]
            bufs, deps = ring[kind]
            return bufs[sl], deps[sl]

        def release(kind):
            consumed[0] += 1
            inflight[kind] -= 1
            pump()

        pump()
        pre0 = self.pren[:, 0:8].rearrange("p (k o) -> p k o", o=1)
        pre1 = self.pren[:, 8:16].rearrange("p (k o) -> p k o", o=1)

        def post_norm_all(layer):
            for tb in range(4):
                for hf in range(2):
                    S.op("act", lambda e, hf=hf, tb=tb: e.activation(self.junk[:, hf * 512:(hf + 1) * 512], self.pb[2 * tb + hf][:, :], AF.Square,
                                                                     accum_out=ssq[:, 2 * tb + hf:2 * tb + hf + 1]),
                         reads=[self.dpb[2 * tb + hf]], writes=[self.d_junk, d_ssq])
            S.op("dve", lambda e: e.reduce_sum(ssq[:, 8:12], ssq[:, 0:8].rearrange("p (a c) -> p a c", c=2), AX.X), reads=[d_ssq], writes=[d_ssq])
            S.op("dve", lambda e: e.tensor_scalar(ssq[:, 8:12], ssq[:, 8:12], 1.0 / D, EPS, ALU.mult, ALU.add), reads=[d_ssq], writes=[d_ssq])
            S.op("act", lambda e: e.activation(ssq[:, 12:16], ssq[:, 8:12], AF.Sqrt), reads=[d_ssq], writes=[d_ssq])
            S.op("dve", lambda e: e.reciprocal(ssq[:, 8:12], ssq[:, 12:16]), reads=[d_ssq], writes=[d_ssq])
            for tb in range(4):
                tf, d_tf = tmpf[tb % 2], d_tmpf[tb % 2]
                for hf in range(2):
                    S.op("dve", lambda e, hf=hf, tb=tb, tf=tf: e.scalar_tensor_tensor(
                        tf[:, hf * 512:(hf + 1) * 512], self.pb[2 * tb + hf][:, :], ssq[:, 8 + tb:9 + tb],
                        postn[:, layer, hf * 512:(hf + 1) * 512], ALU.mult, ALU.mult),
                         reads=[self.dpb[2 * tb + hf], d_ssq, d_postn], writes=[d_tf])
                S.op("pool", lambda e, tb=tb, tf=tf: e.tensor_tensor(xs[:, tb, :], xs[:, tb, :], tf[:], ALU.add),
                     reads=[d_tf, d_xs[tb]], writes=[d_xs[tb]])

        def out_proj(layer, lhs_of):
            for kt in range(16):
                wbuf, dwb = take("wo")
                for tb in range(4):
                    lhsT, dl = lhs_of(kt, tb)
                    for hf in range(2):
                        S.op("pe", lambda e, tb=tb, hf=hf, lhsT=lhsT: e.matmul(self.pb[2 * tb + hf][:, :], lhsT=lhsT,
                                                                                rhs=wbuf[:, hf * 512:(hf + 1) * 512],
                                                                                start=(kt == 0), stop=(kt == 15)),
                             reads=[dwb, dl], writes=[self.dpb[2 * tb + hf]], inc=(kt == 15 or (tb == 3 and hf == 1)))
                release("wo")
            post_norm_all(layer)

        ssq = sb("ssq", [128, 16], F32)
        d_ssq = Dep()

        def rms4(gain):
            for tb in range(4):
                S.op("act", lambda e, tb=tb: e.activation(self.junk[:], xs[:, tb, :], AF.Square, accum_out=ssq[:, tb:tb + 1]),
                     reads=[d_xs[tb]], writes=[self.d_junk, d_ssq])
            S.op("dve", lambda e: e.tensor_scalar(ssq[:, 4:8], ssq[:, 0:4], 1.0 / D, EPS, ALU.mult, ALU.add), reads=[d_ssq], writes=[d_ssq])
            S.op("act", lambda e: e.activation(ssq[:, 8:12], ssq[:, 4:8], AF.Sqrt), reads=[d_ssq], writes=[d_ssq])
            S.op("dve", lambda e: e.reciprocal(ssq[:, 12:16], ssq[:, 8:12]), reads=[d_ssq], writes=[d_ssq])
            for tb in range(4):
                S.op("act", lambda e, tb=tb: e.activation(hn[tb][:], xs[:, tb, :], AF.Copy, scale=ssq[:, 12 + tb:13 + tb]),
                     reads=[d_xs[tb], d_ssq], writes=[d_hn[tb]])
            for tb in range(4):
                ptr = self.pb[4 + tb][:].bitcast(BF16)
                for kt in range(8):
                    S.op("pe", lambda e, kt=kt, tb=tb, ptr=ptr: e.transpose(ptr[:, kt * 128:(kt + 1) * 128],
                                                                             hn[tb][:, kt * 128:(kt + 1) * 128], self.ident[:]),
                         reads=[d_hn[tb], self.d_ident], writes=[self.dpb[4 + tb]], inc=(kt == 7))
            for tb in range(4):
                ptr = self.pb[4 + tb][:].bitcast(BF16)
                S.op("dve", lambda e, tb=tb, ptr=ptr: e.tensor_tensor(hTs[:, :, tb * 128:(tb + 1) * 128], ptr.rearrange("p (k t) -> p k t", k=8),
                                                                      bcast_last(gain, 128), ALU.mult),
                     reads=[self.dpb[4 + tb], self.d_pren], writes=[d_hTs[tb]])

        for sbi in range(4):
            tok0 = sbi * 512
            if L0:
                for tb in range(4):
                    S.dma("sp", xs[:, tb, :], dr["xin"][tok0 + tb * 128: tok0 + (tb + 1) * 128, :], writes=[d_xs[tb]], key="xs%d" % tb)
                rms4(pre0)

                def pool_mm(g):
                    pg = g % 2
                    pbank = 2 + 3 * pg
                    for dt in range(2):
                        ct = 2 * g + dt
                        for ci in range(2):
                            S.op("pe", lambda e, ci=ci, dt=dt: e.matmul(self.pb[pbank + dt][:, :],
                                                                         lhsT=poolw[:, g, ci, dt * 128:(dt + 1) * 128],
                                                                         rhs=pooled[pg][:, ci, :], start=(ci == 0), stop=(ci == 1)),
                                 reads=[d_poolw, d_pooled[pg][ci]], writes=[self.dpb[pbank + dt]], inc=(ci == 1))
                        S.op("dve", lambda e, ct=ct, dt=dt: e.scalar_tensor_tensor(ysb[:, ct, :], self.pb[pbank + dt][:, :], pscale[:, ct:ct + 1],
                                                                                   ysb[:, 8 + ct, :], ALU.mult, ALU.mult),
                             reads=[self.dpb[pbank + dt], d_pscale, d_ysb[8 + ct]], writes=[d_ysb[ct]])

                for g in range(4):
                    pg = g % 2
                    for ci in range(2):
                        ct = 2 * g + ci
                        ab, d_ab = Abuf[ci], d_A[ci]
                        wbuf, dwb = take("wr")
                        for kt in range(8):
                            S.op("pe", lambda e, kt=kt: e.matmul(self.pb[ci][:, :], lhsT=wbuf[:, kt, 0:128], rhs=hTs[:, kt, :],
                                                                 start=(kt == 0), stop=(kt == 7)),
                                 reads=[dwb] + d_hTs, writes=[self.dpb[ci]], inc=(kt == 7))
                        for kt in range(8):
                            S.op("pe", lambda e, kt=kt: e.matmul(self.pb[[4, 7][ci]][:, :],
                                                                 lhsT=wbuf[:, kt, 128:256], rhs=hTs[:, kt, :],
                                                                 start=(kt == 0), stop=(kt == 7)),
                                 reads=[dwb] + d_hTs, writes=[self.dpb[[4, 7][ci]]], inc=(kt == 7))
                        release("wr")
                        S.op("act", lambda e: e.activation(ab[:, :, 16:144], self.pb[ci][:, :].rearrange("p (a t) -> p a t", a=4), AF.Copy),
                             reads=[self.dpb[ci]], writes=[d_ab])
                        S.op("pool", lambda e, ct=ct: e.tensor_copy(
                            ab[:, :, 0:16], self.aTh[:, ct, sbi * 64:(sbi + 1) * 64].rearrange("p (a t) -> p a t", a=4)),
                             reads=[self.d_aTh], writes=[d_ab])
                        S.op("act", lambda e, ct=ct: e.activation(ysb[:, 8 + ct, :], self.pb[[4, 7][ci]][:, :], AF.Silu),
                             reads=[self.dpb[[4, 7][ci]]], writes=[d_ysb[8 + ct]])
                        src, dsrc = ab, d_ab
                        for step in range(g + 1):
                            sh = 1 << step
                            dst, ddst = Sb[step % 2], d_Sb[step % 2]
                            S.op("dve", lambda e, src=src, dst=dst, sh=sh: e.tensor_tensor(
                                dst[:, :, sh:144], src[:, :, sh:144], src[:, :, 0:144 - sh], ALU.add),
                                 reads=[dsrc], writes=[ddst])
                            src, dsrc = dst, ddst
                        w = 2 << g
                        pv = pooled[pg][:, ci, :].rearrange("p (a t) -> p a t", a=4)
                        dpl = d_pooled[pg][ci]
                        if sbi == 0:
                            S.op("dve", lambda e, src=src, g=g: e.tensor_tensor(ptmp[:], src[:, 0, 16:144], invc[:, g, :], ALU.mult),
                                 reads=[dsrc, d_invc], writes=[d_ptmp])
                            S.op("dve", lambda e, pv=pv: e.tensor_tensor(pv[:, 0, :], ptmp[:], ab[:, 0, 16:144], ALU.subtract),
                                 reads=[d_ptmp, d_ab], writes=[dpl])
                            S.op("dve", lambda e, src=src, w=w, pv=pv: e.scalar_tensor_tensor(
                                pv[:, 1:4, :], src[:, 1:4, 16:144], 1.0 / w, ab[:, 1:4, 16:144], ALU.mult, ALU.subtract),
                                 reads=[dsrc, d_ab], writes=[dpl])
                        else:
                            S.op("dve", lambda e, src=src, w=w, pv=pv: e.scalar_tensor_tensor(
                                pv, src[:, :, 16:144], 1.0 / w, ab[:, :, 16:144], ALU.mult, ALU.subtract),
                                 reads=[dsrc, d_ab], writes=[dpl])
                    if g >= 1:
                        pool_mm(g - 1)
                pool_mm(3)

                def lhs0(kt, tb):
                    if kt < 8:
                        return ysb[:, kt, tb * 128:(tb + 1) * 128], d_ysb[kt]
                    return self.ygA[:, kt - 8, tok0 + tb * 128: tok0 + (tb + 1) * 128], self.d_ygA[kt - 8][sbi]
                out_proj(0, lhs0)
            else:
                for tb in range(4):
                    S.dma("sp", xs[:, tb, :], dr["x1in"][tok0 + tb * 128: tok0 + (tb + 1) * 128, :], writes=[d_xs[tb]], key="xs%d" % tb)

            if L1:
                rms4(pre1)
                for cc in range(8):
                    wbuf, dwb = take("wr")
                    for half in range(2):
                        pi = 2 + half
                        for t2 in range(2):
                            tb = half * 2 + t2
                            for kt in range(8):
                                S.op("pe", lambda e, kt=kt, tb=tb, t2=t2, pi=pi: e.matmul(
                                    self.pb[pi][:, t2 * 256:(t2 + 1) * 256], lhsT=hTs[:, kt, tb * 128:(tb + 1) * 128],
                                    rhs=wbuf[:, kt, :], start=(kt == 0), stop=(kt == 7)),
                                     reads=[dwb, d_hTs[tb]], writes=[self.dpb[pi]], inc=(kt == 7 and t2 == 1))
                        S.op("act", lambda e, half=half, cc=cc, pi=pi: e.activation(
                            vb[:, half * 2:half * 2 + 2, cc * 256:(cc + 1) * 256],
                            self.pb[pi][:, :].rearrange("p (a d) -> p a d", a=2), AF.Gelu_apprx_tanh),
                             reads=[self.dpb[pi]], writes=[d_vb[half * 2], d_vb[half * 2 + 1]])
                    release("wr")
                lsm, d_lsmf = lsm4[0], d_lsm4[0]

                def ln_stage_a():
                    for tb in range(4):
                        S.op("dve", lambda e, tb=tb: e.reduce_sum(lsq[:, tb:tb + 1], vb[:, tb, :], AX.X), reads=[d_vb[tb]], writes=[d_lsq])
                        S.op("act", lambda e, tb=tb: e.activation(self.junk[:, :], vb[:, tb, 0:1024], AF.Square, accum_out=lsq[:, 4 + tb:5 + tb]),
                             reads=[d_vb[tb]], writes=[self.d_junk, d_lsq])
                        S.op("act", lambda e, tb=tb: e.activation(self.junk[:, :], vb[:, tb, 1024:2048], AF.Square, accum_out=lsq[:, 8 + tb:9 + tb]),
                             reads=[d_vb[tb]], writes=[self.d_junk, d_lsq])

                def ln_stage_b():
                    S.op("dve", lambda e: e.tensor_scalar(lsq[:, 12:16], lsq[:, 0:4], 1.0 / 2048, None, ALU.mult), reads=[d_lsq], writes=[d_lsq])
                    S.op("dve", lambda e: e.tensor_tensor(lsq[:, 16:20], lsq[:, 4:8], lsq[:, 8:12], ALU.add), reads=[d_lsq], writes=[d_lsq])
                    S.op("dve", lambda e: e.tensor_tensor(lsq[:, 20:24], lsq[:, 12:16], lsq[:, 12:16], ALU.mult), reads=[d_lsq], writes=[d_lsq])
                    S.op("dve", lambda e: e.scalar_tensor_tensor(lsq[:, 24:28], lsq[:, 16:20], 1.0 / 2048, lsq[:, 20:24], ALU.mult, ALU.subtract),
                         reads=[d_lsq], writes=[d_lsq])
                    S.op("dve", lambda e: e.tensor_scalar(lsq[:, 24:28], lsq[:, 24:28], EPS, None, ALU.add), reads=[d_lsq], writes=[d_lsq])
                    S.op("act", lambda e: e.activation(lsq[:, 28:32], lsq[:, 24:28], AF.Sqrt), reads=[d_lsq], writes=[d_lsq])
                    S.op("dve", lambda e: e.reciprocal(lsq[:, 32:36], lsq[:, 28:32]), reads=[d_lsq], writes=[d_lsq])
                    S.op("dve", lambda e: e.scalar_tensor_tensor(lsq[:, 36:40], lsq[:, 12:16], -1.0, lsq[:, 32:36], ALU.mult, ALU.mult),
                         reads=[d_lsq], writes=[d_lsq])

                def ln_stage_c(tb):
                    for hf in range(2):
                        cs = slice(hf * 1024, (hf + 1) * 1024)
                        tf, d_tf = tmpf[hf], d_tmpf[hf]
                        S.op("act", lambda e, cs=cs, tf=tf: e.activation(tf[:], vb[:, tb, cs], AF.Identity, bias=lsq[:, 36 + tb:37 + tb], scale=lsq[:, 32 + tb:33 + tb]),
                             reads=[d_vb[tb], d_lsq], writes=[d_tf])
                        S.op("dve", lambda e, cs=cs, tf=tf: e.tensor_tensor(tf[:], tf[:], lng[:, cs], ALU.mult), reads=[d_tf, d_ln], writes=[d_tf])
                        S.op("pool", lambda e, cs=cs, tf=tf: e.tensor_tensor(vb[:, tb, cs], tf[:], lnb[:, cs], ALU.add),
                             reads=[d_tf, d_ln], writes=[d_vb[tb]])

                def sgu_ct(ct):
                    g = ct // 2
                    pi = 4 + ct % 2
                    for tb in range(4):
                        S.op("pe", lambda e, tb=tb, pi=pi: e.matmul(
                            self.pb[pi][:, tb * 128:(tb + 1) * 128], lhsT=vb[:, tb, ct * 128:(ct + 1) * 128], rhs=wsT[:, g, :],
                            start=True, stop=True),
                             reads=[d_vb[tb], d_wsT], writes=[self.dpb[pi]], inc=(tb == 3))
                    mi = ct % 2
                    S.op("dve", lambda e, pi=pi, mi=mi: e.tensor_tensor(
                        mtmp[mi][:].rearrange("p (a t) -> p a t", a=4), self.pb[pi][:, :].rearrange("p (a t) -> p a t", a=4),
                        bcast_mid(bsb[:, g, :], 4), ALU.add),
                         reads=[self.dpb[pi], d_bsb], writes=[d_mtmp[mi]])
                    S.op("pool", lambda e, mi=mi: e.tensor_tensor(ysb[:, ct, :], ysb[:, ct, :], mtmp[mi][:], ALU.mult),
                         reads=[d_mtmp[mi], d_ysb[ct]], writes=[d_ysb[ct]])

                ln_stage_a()
                for which in range(2):
                    for i in range(8):
                        wbuf, dwb = take("wr")
                        for c2 in range(2):
                            ct = 2 * i + c2
                            pi = c2
                            for kt in range(8):
                                S.op("pe", lambda e, kt=kt, c2=c2, pi=pi: e.matmul(
                                    self.pb[pi][:, :], lhsT=wbuf[:, kt, c2 * 128:(c2 + 1) * 128], rhs=hTs[:, kt, :],
                                    start=(kt == 0), stop=(kt == 7)),
                                     reads=[dwb] + d_hTs, writes=[self.dpb[pi]], inc=(kt == 7))
                            if which == 0:
                                S.op("act", lambda e, ct=ct, pi=pi: e.activation(ysb[:, ct, :], self.pb[pi][:, :], AF.Silu),
                                     reads=[self.dpb[pi]], writes=[d_ysb[ct]])
                            else:
                                ui = ct % 2
                                S.op("act", lambda e, ui=ui, pi=pi: e.activation(utmp[ui][:], self.pb[pi][:, :], AF.Gelu_apprx_tanh),
                                     reads=[self.dpb[pi]], writes=[d_utmp[ui]])
                                S.op("pool", lambda e, ct=ct, ui=ui: e.tensor_tensor(ysb[:, ct, :], ysb[:, ct, :], utmp[ui][:], ALU.mult),
                                     reads=[d_utmp[ui], d_ysb[ct]], writes=[d_ysb[ct]])
                        release("wr")
                        if which == 0:
                            if i == 1:
                                ln_stage_b()
                            if 2 <= i <= 5:
                                ln_stage_c(i - 2)
                        else:
                            sgu_ct(2 * i)
                            sgu_ct(2 * i + 1)

                def lhs1(kt, tb):
                    return ysb[:, kt, tb * 128:(tb + 1) * 128], d_ysb[kt]
                out_proj(1, lhs1)

            for tb in range(4):
                S.dma("sp", dr["out"][tok0 + tb * 128: tok0 + (tb + 1) * 128, :], xs[:, tb, :], reads=[d_xs[tb]], key="o%d" % tb)
            self.ostore = d_xs


def _tile_cols(w):
    return np.ascontiguousarray(w.reshape(8, 128, -1).transpose(1, 0, 2))


def _partner_perm():
    perm = np.arange(128)
    for p in range(128):
        d = p % 64
        if d < 8:
            perm[p] = p + 8
        elif d < 16:
            perm[p] = p - 8
    return perm


_NC_CACHE = {}


def _get_nc(mode):
    if mode not in _NC_CACHE:
        _NC_CACHE[mode] = Builder(mode).build()
    return _NC_CACHE[mode]


def _shared_inputs(inp):
    f = np.float32
    w0 = np.asarray(inp["w_in"][0], f)
    w1 = np.asarray(inp["w_in"][1], f)
    perm = _partner_perm()
    sh = {}
    whd = np.empty((8, 128, 8, 768), f)
    for h in range(8):
        q = w0[:, 1024 + 128 * h: 1024 + 128 * (h + 1)]
        k = w0[:, 2048 + 128 * h: 2048 + 128 * (h + 1)]
        v = w0[:, 3072 + 128 * h: 3072 + 128 * (h + 1)]
        g = w0[:, 4096 + 1024 + 128 * h: 4096 + 1024 + 128 * (h + 1)]
        whd[h] = _tile_cols(np.concatenate([q, q[:, perm], k, k[:, perm], v, g], axis=1))
    sh["whd"] = whd
    wag = np.empty((8, 128, 8, 256), f)
    for ct in range(8):
        a = w0[:, 128 * ct:128 * (ct + 1)]
        g = w0[:, 4096 + 128 * ct: 4096 + 128 * (ct + 1)]
        wag[ct] = _tile_cols(np.concatenate([a, g], axis=1))
    sh["wag"] = wag
    sh["wo0"] = np.ascontiguousarray(np.asarray(inp["w_out"][0], f).reshape(16, 128, 1024))
    sh["wo1"] = np.ascontiguousarray(np.asarray(inp["w_out"][1], f).reshape(16, 128, 1024))
    w1g = np.empty((8, 128, 8, 256), f)
    w1u = np.empty((8, 128, 8, 256), f)
    for i in range(8):
        w1u[i] = _tile_cols(w1[:, 256 * i:256 * (i + 1)])
        w1g[i] = _tile_cols(w1[:, 4096 + 256 * i: 4096 + 256 * (i + 1)])
    sh["w1g"] = w1g
    sh["w1u"] = w1u
    w1v = np.empty((8, 128, 8, 256), f)
    for cc in range(8):
        w1v[cc] = _tile_cols(w1[:, 2048 + 256 * cc: 2048 + 256 * (cc + 1)])
    sh["w1v"] = w1v
    pw = np.asarray(inp["pool_w"][0], f)
    sh["poolw"] = np.ascontiguousarray(pw.reshape(4, 2, 128, 256).transpose(2, 0, 1, 3))
    sh["pscale"] = np.ascontiguousarray(np.asarray(inp["pool_scale"][0], f).reshape(8, 128).T)
    sh["lamv"] = np.concatenate([np.asarray(inp[k][0], f) for k in ("lam_q1", "lam_k1", "lam_q2", "lam_k2")])[None, :]
    sh["subln"] = np.ascontiguousarray(np.asarray(inp["diff_subln"][0], f).reshape(128, 1))
    sh["lng"] = np.asarray(inp["sgu_ln_g"], f).reshape(1, 2048)
    sh["lnb"] = np.asarray(inp["sgu_ln_b"], f).reshape(1, 2048)
    sh["wsT"] = np.ascontiguousarray(np.asarray(inp["sgu_w"][0], f).transpose(2, 0, 1))
    sh["tril"] = np.ascontiguousarray(np.tril(np.ones((128, 128), f)).T)
    sh["sgub"] = np.asarray(inp["sgu_b"][0], f).reshape(1, 1024)
    pre = np.asarray(inp["pre_norm"], f)
    sh["pren"] = np.ascontiguousarray(pre.reshape(2, 8, 128).transpose(2, 0, 1).reshape(128, 16))
    sh["postn"] = np.asarray(inp["post_norm"], f).reshape(1, 2048)
    inv_freq = np.power(np.float32(500000.0), -np.arange(8, dtype=f) * np.float32(2.0) / np.float32(16)).astype(f)
    ropec = np.zeros((128, 2), f)
    for p in range(128):
        d = p % 64
        if d < 16:
            ropec[p, 0] = inv_freq[d % 8]
            ropec[p, 1] = (-2 * np.pi) if d < 8 else (2 * np.pi)
    sh["ropec"] = ropec
    return sh


def _core_inputs(inp, b, r):
    f = np.float32
    x = np.asarray(inp["x"][b], f)
    blocks = x.reshape(32, 128, D)
    own = [2 * j + r for j in range(16)]
    oth = [2 * j + (1 - r) for j in range(16)]
    halo = np.zeros((16, 16, D), f)
    for j in range(16):
        s0 = own[j] * 128
        if s0 > 0:
            halo[j] = x[s0 - 16:s0]
    xin = np.concatenate([blocks[own].reshape(-1, D), blocks[oth].reshape(-1, D), halo.reshape(-1, D)], axis=0)
    pos = np.asarray(inp["positions"][b]).astype(np.int32).reshape(32, 128)
    pos = np.concatenate([pos[own].reshape(-1), pos[oth].reshape(-1)])[None, :]
    mask = np.zeros((128, 256), f)
    kk = np.arange(128)[:, None]
    qq = np.arange(128)[None, :]
    mask[:, 0:128] = np.where(kk <= qq, 0.0, NEG)
    mask[:, 128:256] = NEG if r == 0 else 0.0
    invc = np.zeros((4, 128), f)
    for g, w in enumerate((2, 4, 8, 16)):
        if r == 0:
            invc[g] = 1.0 / np.minimum(np.arange(128) + 1, w)
        else:
            invc[g] = 1.0 / w
    return {"xin": np.ascontiguousarray(xin), "pos": np.ascontiguousarray(pos), "mask": mask, "invc": invc.reshape(1, 512)}


L0_KEYS = ("whd", "wag", "wo0", "poolw", "pscale", "lamv", "subln", "ropec", "pren", "postn")
L1_KEYS = ("w1g", "w1u", "w1v", "wo1", "lng", "lnb", "wsT", "tril", "sgub", "pren", "postn")

MODE = "fused"


def kernel(**inp):
    sh = _shared_inputs(inp)
    cores = [(b, r) for b in range(4) for r in range(2)]
    per = [_core_inputs(inp, b, r) for (b, r) in cores]
    if MODE == "fused":
        nc = _get_nc("fused")
        maps = []
        for c in per:
            m = dict(c)
            for k in set(L0_KEYS) | set(L1_KEYS):
                m[k] = sh[k]
            maps.append(m)
        res = run_bass_kernel_spmd(nc, maps, core_ids=list(range(8)))
        outs = [r["out"] for r in res.results]
    else:
        nc0 = _get_nc("L0")
        maps = []
        for c in per:
            m = dict(c)
            for k in L0_KEYS:
                m[k] = sh[k]
            maps.append(m)
        res = run_bass_kernel_spmd(nc0, maps, core_ids=list(range(8)))
        x1 = [np.asarray(r["out"]) for r in res.results]
        nc1 = _get_nc("L1")
        maps = []
        for i in range(8):
            m = {"x1in": x1[i]}
            for k in L1_KEYS:
                m[k] = sh[k]
            maps.append(m)
        res = run_bass_kernel_spmd(nc1, maps, core_ids=list(range(8)))
        outs = [r["out"] for r in res.results]
    out = np.empty((4, 4096, D), np.float32)
    for (b, r), o in zip(cores, outs):
        ob = out[b].reshape(32, 128, D)
        ob[[2 * j + r for j in range(16)]] = np.asarray(o, np.float32).reshape(16, 128, D)
    return out
```

```python
import math
import numpy as np
import concourse.bass as bass
import concourse.mybir as mybir
from concourse.bass_utils import run_bass_kernel_spmd
from contextlib import ExitStack

F32 = mybir.dt.float32
BF16 = mybir.dt.bfloat16
I32 = mybir.dt.int32
AF = mybir.ActivationFunctionType
ALU = mybir.AluOpType
AX = mybir.AxisListType

D = 1024
NS = 16
TOK = 2048
NALL = 4352
NEG = -30000.0
EPS = 1e-6
LAM_INIT = 0.8 - 0.6 * math.exp(-0.3 * 0)
MAGIC = 12582912.0
SEM_LIMIT = 30000


class Dep:
    __slots__ = ("w", "r")

    def __init__(self):
        self.w = None
        self.r = {}


class Tok:
    __slots__ = ("key", "eng", "val")

    def __init__(self, key, eng, val):
        self.key = key
        self.eng = eng
        self.val = val


class Sched:
    ENG = ("pe", "act", "dve", "pool", "sp")

    def __init__(self, nc, es):
        self.nc = nc
        self.es = es
        self.e = dict(pe=nc.tensor, act=nc.scalar, dve=nc.vector, pool=nc.gpsimd, sp=nc.sync)
        self.gen = {k: 0 for k in self.ENG}
        self.semh = {}
        self.cnt = {}
        for k in self.ENG:
            self._newsem(k)
        self.lazy = {k: [] for k in self.ENG}
        self.seen = {k: {} for k in self.ENG}
        self.nwaits = 0
        self.nins = {k: 0 for k in self.ENG}

    def _newsem(self, eng):
        key = "%s%d" % (eng, self.gen[eng])
        self.semh[key] = self.es.enter_context(self.nc.semaphore("s_" + key))
        self.cnt[key] = 0
        return key

    def _curkey(self, eng):
        return "%s%d" % (eng, self.gen[eng])

    def _wait(self, eng, toks):
        need = {}
        for t in toks:
            if t is None:
                continue
            if t.eng == eng and eng == "pe":
                continue
            if t.val is None:
                raise RuntimeError("wait on instruction without inc: %s" % (t.key,))
            if need.get(t.key, 0) < t.val:
                need[t.key] = t.val
        for key, val in need.items():
            if self.seen[eng].get(key, 0) >= val:
                continue
            self.e[eng].wait_ge(self.semh[key], val)
            self.seen[eng][key] = val
            self.nwaits += 1

    def _deps(self, eng, reads, writes):
        toks = []
        for d in reads:
            toks.append(d.w)
        for d in writes:
            if d.w is not None and d.w.eng != eng:
                toks.append(d.w)
            for t in d.r.values():
                if t.eng == eng:
                    continue
                toks.append(t)
        self._wait(eng, toks)

    def _record(self, tok, reads, writes):
        for d in reads:
            d.r[tok.key] = tok
        for d in writes:
            d.w = tok
            d.r = {}

    def op(self, eng, fn, reads=(), writes=(), inc=True):
        self._deps(eng, reads, writes)
        ins = fn(self.e[eng])
        self.nins[eng] += 1
        key = self._curkey(eng)
        tok = Tok(key, eng, None)
        if inc:
            ins.then_inc(self.semh[key], 1)
            self.cnt[key] += 1
            tok.val = self.cnt[key]
            for t in self.lazy[eng]:
                t.val = tok.val
            self.lazy[eng] = []
            if self.cnt[key] >= SEM_LIMIT:
                self.gen[eng] += 1
                self._newsem(eng)
        else:
            self.lazy[eng].append(tok)
        self._record(tok, reads, writes)
        return tok

    def dma(self, q, out, in_, reads=(), writes=(), key=None):
        self._deps(q, reads, writes)
        if key is None:
            self.nauto = getattr(self, "nauto", 0) + 1
            key = "auto%d" % self.nauto
        key = "d_" + key
        if key not in self.semh:
            self.semh[key] = self.es.enter_context(self.nc.semaphore(key))
            self.cnt[key] = 0
        self.e[q].dma_start(out=out, in_=in_).then_inc(self.semh[key], 16)
        self.nins[q] += 1
        self.cnt[key] += 16
        assert self.cnt[key] < 2 * SEM_LIMIT
        tok = Tok(key, "dma", self.cnt[key])
        self._record(tok, reads, writes)
        return tok

    def barrier(self):
        for e in self.ENG:
            assert not self.lazy[e], "barrier with un-incremented %s instructions" % e
        keys = [(k, v) for k, v in self.cnt.items() if v > 0]
        for eng in self.ENG:
            own = self._curkey(eng)
            for key, val in keys:
                if key == own or self.seen[eng].get(key, 0) >= val:
                    continue
                self.e[eng].wait_ge(self.semh[key], val)
                self.seen[eng][key] = val
                self.nwaits += 1

    def wait_all(self, eng, deps):
        toks = []
        for d in deps:
            toks.append(d.w)
            toks.extend(d.r.values())
        self._wait(eng, toks)


def bcast_last(ap, n):
    dims = [list(x) for x in ap.ap]
    assert dims[-1][1] == 1
    dims[-1] = [0, n]
    return bass.AP(ap.tensor, ap.offset, dims)


def bcast_mid(ap, n):
    dims = [list(x) for x in ap.ap]
    assert len(dims) == 2
    return bass.AP(ap.tensor, ap.offset, [dims[0], [0, n], dims[1]])


class Builder:
    def __init__(self, mode, debug=False):
        self.mode = mode
        self.debug = debug
        self.dbg_names = []
        self.nc = bass.Bass("TRN2", target_bir_lowering=False)
        self.es = ExitStack()

    def dram_in(self, name, shape, dt=F32):
        return self.nc.dram_tensor(name, list(shape), dt, kind="ExternalInput").ap()

    def dump(self, name, ap, deps):
        if not getattr(self, "debug", False):
            return
        t = self.nc.dram_tensor("dbg_" + name, list(ap.shape), ap.dtype, kind="ExternalOutput").ap()
        self.S.dma("sp", t[:], ap, reads=deps)
        self.dbg_names.append("dbg_" + name)

    def sb(self, es, name, shape, dt):
        return es.enter_context(self.nc.sbuf_tensor("sb_" + name, list(shape), dt))

    def build(self):
        nc = self.nc
        mode = self.mode
        with self.es as es:
            S = self.S = Sched(nc, es)
            dr = self.dr = {}
            if mode in ("fused", "L0"):
                dr["xin"] = self.dram_in("xin", [NALL, D])
                dr["pos"] = self.dram_in("pos", [1, 4096], I32)
                dr["ropec"] = self.dram_in("ropec", [128, 2])
                dr["mask"] = self.dram_in("mask", [128, 256])
                dr["invc"] = self.dram_in("invc", [1, 512])
                dr["whd"] = self.dram_in("whd", [8, 128, 8, 768])
                dr["wag"] = self.dram_in("wag", [8, 128, 8, 256])
                dr["wo0"] = self.dram_in("wo0", [16, 128, 1024])
                dr["poolw"] = self.dram_in("poolw", [128, 4, 2, 256])
                dr["pscale"] = self.dram_in("pscale", [128, 8])
                dr["lamv"] = self.dram_in("lamv", [1, 256])
                dr["subln"] = self.dram_in("subln", [128, 1])
            if mode in ("fused", "L1"):
                dr["w1g"] = self.dram_in("w1g", [8, 128, 8, 256])
                dr["w1u"] = self.dram_in("w1u", [8, 128, 8, 256])
                dr["w1v"] = self.dram_in("w1v", [8, 128, 8, 256])
                dr["wo1"] = self.dram_in("wo1", [16, 128, 1024])
                dr["lng"] = self.dram_in("lng", [1, 2048])
                dr["lnb"] = self.dram_in("lnb", [1, 2048])
                dr["wsT"] = self.dram_in("wsT", [128, 8, 128])
                dr["tril"] = self.dram_in("tril", [128, 128])
                dr["sgub"] = self.dram_in("sgub", [1, 1024])
            if mode == "L1":
                dr["x1in"] = self.dram_in("x1in", [TOK, D])
            dr["pren"] = self.dram_in("pren", [128, 16])
            dr["postn"] = self.dram_in("postn", [1, 2048])
            dr["out"] = nc.dram_tensor("out", [TOK, D], F32, kind="ExternalOutput").ap()

            self.pb = [es.enter_context(nc.psum_tensor("pb%d" % i, [128, 512], F32)) for i in range(8)]
            self.dpb = [Dep() for _ in range(8)]

            self.ident = self.sb(es, "ident", [128, 128], BF16)
            self.d_ident = Dep()
            io = self.sb(es, "iota_f", [128, 128], F32)
            ip = self.sb(es, "iota_p", [128, 1], F32)
            d_io, d_ip = Dep(), Dep()
            S.op("pool", lambda e: e.iota(io[:], [[1, 128]], base=0, channel_multiplier=0,
                                          allow_small_or_imprecise_dtypes=True), writes=[d_io])
            S.op("pool", lambda e: e.iota(ip[:], [[1, 1]], base=0, channel_multiplier=1,
                                          allow_small_or_imprecise_dtypes=True), writes=[d_ip])
            S.op("dve", lambda e: e.tensor_scalar(self.ident[:], io[:], ip[:, 0:1], None, ALU.is_equal),
                 reads=[d_io, d_ip], writes=[self.d_ident])
            self.pren = self.sb(es, "pren", [128, 16], F32)
            self.d_pren = Dep()
            S.dma("sp", self.pren[:], dr["pren"][:], writes=[self.d_pren])
            self.small = self.sb(es, "small", [128, 64], F32)
            self.small_i = 0
            self.ostore = []

            if mode in ("fused", "L0"):
                self.ygA = self.sb(es, "ygA", [128, 8, TOK], BF16)
                self.d_ygA = [[Dep() for _ in range(4)] for _ in range(8)]
                self.aTh = self.sb(es, "aTh", [128, 8, 256], F32)
                self.d_aTh = Dep()
                with ExitStack() as es1:
                    self.phase_attention(es1)
            if mode in ("fused", "L0"):
                self.dump("ygA", self.ygA[:], [d for l in self.d_ygA for d in l])
                self.dump("aTh", self.aTh[:], [self.d_aTh])
            S.barrier()
            with ExitStack() as es2:
                self.phase_final(es2)
            S.wait_all("sp", self.ostore)
        return nc

    def rms_group(self, xs_aps, d_xs, hn, d_hn, outs, d_outs, gain, banks, ssq, d_ssq):
        S = self.S
        n = len(xs_aps)
        for i in range(n):
            S.op("act", lambda e, i=i: e.activation(self.junk[:], xs_aps[i], AF.Square, accum_out=ssq[:, i:i + 1]),
                 reads=[d_xs[i]], writes=[self.d_junk, d_ssq])
        S.op("dve", lambda e: e.tensor_scalar(ssq[:, 4:4 + n], ssq[:, 0:n], 1.0 / D, EPS, ALU.mult, ALU.add), reads=[d_ssq], writes=[d_ssq])
        S.op("act", lambda e: e.activation(ssq[:, 8:8 + n], ssq[:, 4:4 + n], AF.Sqrt), reads=[d_ssq], writes=[d_ssq])
        S.op("dve", lambda e: e.reciprocal(ssq[:, 12:12 + n], ssq[:, 8:8 + n]), reads=[d_ssq], writes=[d_ssq])
        for i in range(n):
            S.op("act", lambda e, i=i: e.activation(hn[i][:], xs_aps[i], AF.Copy, scale=ssq[:, 12 + i:13 + i]),
                 reads=[d_xs[i], d_ssq], writes=[d_hn[i]])
        for i in range(n):
            ptr = self.pb[banks[i]][:].bitcast(BF16)
            for kt in range(8):
                S.op("pe", lambda e, kt=kt, i=i, ptr=ptr: e.transpose(ptr[:, kt * 128:(kt + 1) * 128],
                                                                       hn[i][:, kt * 128:(kt + 1) * 128], self.ident[:]),
                     reads=[d_hn[i], self.d_ident], writes=[self.dpb[banks[i]]], inc=(kt == 7))
        for i in range(n):
            ptr = self.pb[banks[i]][:].bitcast(BF16)
            S.op("dve", lambda e, i=i, ptr=ptr: e.tensor_tensor(outs[i], ptr.rearrange("p (k t) -> p k t", k=8),
                                                                bcast_last(gain, 128), ALU.mult),
                 reads=[self.dpb[banks[i]], self.d_pren], writes=[d_outs[i]])

    def rms_transpose(self, x_ap, d_x, hn, d_hn, hT_out, d_hT, gain_ap, d_gain, ptr_i, scratch):
        S = self.S
        ss, d_ss = scratch
        junk = self.junk
        S.op("act", lambda e: e.activation(junk[:], x_ap, AF.Square, accum_out=ss[:, 0:1]),
             reads=[d_x], writes=[self.d_junk, d_ss])
        S.op("dve", lambda e: e.tensor_scalar(ss[:, 1:2], ss[:, 0:1], 1.0 / D, EPS, ALU.mult, ALU.add),
             reads=[d_ss], writes=[d_ss])
        S.op("act", lambda e: e.activation(ss[:, 2:3], ss[:, 1:2], AF.Sqrt), reads=[d_ss], writes=[d_ss])
        S.op("dve", lambda e: e.reciprocal(ss[:, 3:4], ss[:, 2:3]), reads=[d_ss], writes=[d_ss])
        S.op("act", lambda e: e.activation(hn[:], x_ap, AF.Copy, scale=ss[:, 3:4]),
             reads=[d_x, d_ss], writes=[d_hn])
        ptr = self.pb[ptr_i][:].bitcast(BF16)
        for kt in range(8):
            S.op("pe", lambda e, kt=kt: e.transpose(ptr[:, kt * 128:(kt + 1) * 128],
                                                     hn[:, kt * 128:(kt + 1) * 128], self.ident[:]),
                 reads=[d_hn, self.d_ident], writes=[self.dpb[ptr_i]], inc=(kt == 7))
        S.op("dve", lambda e: e.tensor_tensor(hT_out, ptr.rearrange("p (k t) -> p k t", k=8),
                                              bcast_last(gain_ap, 128), ALU.mult),
             reads=[self.dpb[ptr_i], d_gain], writes=[d_hT])

    def phase_attention(self, es):
        S, nc, dr = self.S, self.nc, self.dr
        sb = lambda n, s, d: self.sb(es, n, s, d)
        hT = sb("hT", [128, 8, NALL], BF16)
        d_hT = [Dep() for _ in range(34)]
        Ct = sb("ropeC", [128, 4096], BF16)
        St = sb("ropeS", [128, 4096], BF16)
        d_C = [Dep() for _ in range(4)]
        d_St = [Dep() for _ in range(4)]
        ropec = sb("ropec", [128, 2], F32)
        d_ropec = Dep()
        S.dma("sp", ropec[:], dr["ropec"][:], writes=[d_ropec])
        maskb = sb("maskb", [128, 256], BF16)
        d_mask = Dep()
        S.dma("pool", maskb[:], dr["mask"][:], writes=[d_mask])
        lamv = sb("lamv", [128, 256], F32)
        d_lamv = Dep()
        S.dma("sp", lamv[:], dr["lamv"].partition_broadcast(128), writes=[d_lamv])
        subln = sb("subln", [128, 1], F32)
        d_subln = Dep()
        S.dma("sp", subln[:], dr["subln"][:], writes=[d_subln])

        lsm = sb("lam_small", [128, 8], F32)
        d_lsm = Dep()
        ltmp = sb("lam_tmp", [128, 128], F32)
        d_ltmp = Dep()
        lv4 = lamv[:].rearrange("p (a d) -> p a d", a=4)
        S.op("dve", lambda e: e.tensor_tensor(ltmp[:, 0:64], lv4[:, 0, :], lv4[:, 1, :], ALU.mult),
             reads=[d_lamv], writes=[d_ltmp])
        S.op("dve", lambda e: e.tensor_tensor(ltmp[:, 64:128], lv4[:, 2, :], lv4[:, 3, :], ALU.mult),
             reads=[d_lamv], writes=[d_ltmp])
        S.op("dve", lambda e: e.reduce_sum(lsm[:, 0:2], ltmp[:].rearrange("p (a d) -> p a d", a=2), AX.X),
             reads=[d_ltmp], writes=[d_lsm])
        S.op("act", lambda e: e.activation(lsm[:, 2:4], lsm[:, 0:2], AF.Exp), reads=[d_lsm], writes=[d_lsm])
        S.op("dve", lambda e: e.scalar_tensor_tensor(lsm[:, 4:5], lsm[:, 3:4], -LAM_INIT, lsm[:, 2:3],
                                                      ALU.add, ALU.subtract), reads=[d_lsm], writes=[d_lsm])
        lamvec = sb("lamvec", [128, 4, 2], F32)
        d_lamvec = Dep()
        S.op("dve", lambda e: e.memset(lamvec[:], 1.0), writes=[d_lamvec])
        S.op("dve", lambda e: e.tensor_copy(lamvec[:, :, 1:2], bcast_mid(lsm[:, 4:5], 4)),
             reads=[d_lsm], writes=[d_lamvec])

        wst = [sb("wst%d" % i, [128, 8, 768], BF16) for i in range(2)]
        d_wst = [Dep() for _ in range(2)]
        est = ExitStack()
        with est:
            sbt = lambda n, s_, d: self.sb(est, n, s_, d)
            self.junk = sbt("junk", [128, 1024], F32)
            self.d_junk = Dep()
            xb = [sbt("xb%d" % i, [128, D], F32) for i in range(4)]
            d_xb = [Dep() for _ in range(4)]
            hnb = [sbt("hnb%d" % i, [128, D], BF16) for i in range(4)]
            d_hnb = [Dep() for _ in range(4)]
            ssq0 = sbt("ssq0", [128, 16], F32)
            d_ssq0 = Dep()
            pre0 = self.pren[:, 0:8].rearrange("p (k o) -> p k o", o=1)
            for g0 in range(0, 34, 4):
                blks = list(range(g0, min(g0 + 4, 34)))
                for i, blk in enumerate(blks):
                    S.dma("sp", xb[i][:], dr["xin"][blk * 128:(blk + 1) * 128, :], writes=[d_xb[i]], key="x%d" % i)
                self.rms_group([xb[i][:] for i in range(len(blks))], d_xb, hnb, d_hnb,
                               [hT[:, :, blk * 128:(blk + 1) * 128] for blk in blks], [d_hT[blk] for blk in blks],
                               pre0, [4, 5, 6, 7], ssq0, d_ssq0)

            posi = sbt("posi", [128, 1024], I32)
            d_posi = Dep()
            rt = [sbt("rt%d" % i, [128, 1024], F32) for i in range(3)]
            d_rt = [Dep() for _ in range(3)]
            for c in range(4):
                cs = slice(c * 1024, (c + 1) * 1024)
                S.dma("sp", posi[:], dr["pos"][:, cs].partition_broadcast(128), writes=[d_posi], key="pos")
                S.op("dve", lambda e: e.tensor_copy(rt[0][:], posi[:]), reads=[d_posi], writes=[d_rt[0]])
                S.op("dve", lambda e: e.tensor_scalar(rt[1][:], rt[0][:], ropec[:, 0:1], float(np.float32(1.0 / (2 * np.pi))),
                                                      ALU.mult, ALU.mult), reads=[d_rt[0], d_ropec], writes=[d_rt[1]])
                S.op("dve", lambda e: e.tensor_scalar(rt[2][:], rt[1][:], MAGIC, MAGIC, ALU.add, ALU.subtract),
                     reads=[d_rt[1]], writes=[d_rt[2]])
                S.op("dve", lambda e: e.tensor_sub(rt[1][:], rt[1][:], rt[2][:]), reads=[d_rt[1], d_rt[2]], writes=[d_rt[1]])
                S.op("act", lambda e, cs=cs: e.activation(St[:, cs], rt[1][:], AF.Sin, scale=ropec[:, 1:2]),
                     reads=[d_rt[1], d_ropec], writes=[d_St[c]])
                S.op("dve", lambda e: e.tensor_scalar(rt[2][:], rt[1][:], -1.0, None, ALU.mult),
                     reads=[d_rt[1]], writes=[d_rt[2]])
                S.op("dve", lambda e: e.tensor_tensor(rt[2][:], rt[2][:], rt[1][:], ALU.max),
                     reads=[d_rt[1], d_rt[2]], writes=[d_rt[2]])
                S.op("dve", lambda e: e.tensor_scalar(rt[2][:], rt[2][:], float(-2 * np.pi), float(np.pi / 2), ALU.mult, ALU.add),
                     reads=[d_rt[2]], writes=[d_rt[2]])
                S.op("act", lambda e, cs=cs: e.activation(Ct[:, cs], rt[2][:], AF.Sin),
                     reads=[d_rt[2]], writes=[d_C[c]])

        S.barrier()
        self.dump("hT0", hT[:, :, 0:256], d_hT[0:2])
        self.dump("hTh", hT[:, :, 4096:4352], d_hT[32:34])
        self.dump("Ct", Ct[:], d_C)
        self.dump("St", St[:], d_St)
        self.dump("lsm", lsm[:], [d_lsm])
        for ct in range(8):
            s = ct % 2
            S.dma("pool", wst[s][:, :, 0:256], dr["wag"][ct], writes=[d_wst[s]], key="wh%d" % s)
            pi = 4 + s
            for kt in range(8):
                S.op("pe", lambda e, kt=kt, s=s, pi=pi: e.matmul(self.pb[pi][:, 0:256], lhsT=wst[s][:, kt, 0:128],
                                                                 rhs=hT[:, kt, 4096:4352], start=(kt == 0), stop=(kt == 7)),
                     reads=[d_wst[s], d_hT[32], d_hT[33]], writes=[self.dpb[pi]], inc=(kt == 7))
            S.op("act", lambda e, ct=ct, pi=pi: e.activation(self.aTh[:, ct, :], self.pb[pi][:, 0:256], AF.Copy),
                 reads=[self.dpb[pi]], writes=[self.d_aTh])

        KT = sb("KT", [128, 4096], BF16)
        d_KT = [Dep() for _ in range(8)]
        QTc = [sb("QT%d" % i, [128, TOK], BF16) for i in range(2)]
        d_QT = [Dep() for _ in range(4)]
        S.op("pool", lambda e: e.memset(QTc[0][64:128, :], 0.0), writes=d_QT)
        S.op("pool", lambda e: e.memset(QTc[1][0:64, :], 0.0), writes=d_QT)
        V = sb("V", [128, 32, 129], BF16)
        d_V = [Dep() for _ in range(8)]
        S.op("pool", lambda e: e.memset(V[:, :, 128:129], 1.0), writes=d_V)
        sgT = sb("sgT", [128, TOK], BF16)
        d_sgT = [Dep() for _ in range(4)]
        ET = [sb("ET%d" % i, [128, 512], BF16) for i in range(4)]
        d_ET = [Dep() for _ in range(4)]
        rtmp = [sb("rtmp%d" % i, [128, 512], F32) for i in range(2)]
        d_rtmp = [Dep() for _ in range(2)]
        stage = [sb("stage%d" % i, [128, 4, 2, 129], F32) for i in range(2)]
        d_stage = [Dep() for _ in range(2)]
        eo = [sb("eo%d" % i, [128, 4, 128], F32) for i in range(2)]
        d_eo = [Dep() for _ in range(2)]
        eon = sb("eon", [128, 4, 128], BF16)
        d_eon = Dep()
        esm = sb("esm", [128, 32], F32)
        d_esm = Dep()

        et_i = [0]
        st_i = [0]
        rt_i = [0]
        S.dma("pool", wst[0][:], dr["whd"][0], writes=[d_wst[0]], key="wh0")
        for h in range(8):
            s = h % 2
            if h + 1 < 8:
                S.dma("pool", wst[1 - s][:], dr["whd"][h + 1], writes=[d_wst[1 - s]], key="wh%d" % (1 - s))
            w = wst[s]
            dw = d_wst[s]

            def proj_fm(col0, tok0, pi, first_blk):
                for kt in range(8):
                    S.op("pe", lambda e, kt=kt: e.matmul(self.pb[pi][:, :], lhsT=w[:, kt, col0:col0 + 128],
                                                         rhs=hT[:, kt, tok0:tok0 + 512], start=(kt == 0), stop=(kt == 7)),
                         reads=[dw] + d_hT[first_blk:first_blk + 4], writes=[self.dpb[pi]], inc=(kt == 7))

            def rope_chunk(col0, tok0, dst, d_dst, cidx):
                proj_fm(col0, tok0, 6, tok0 // 128)
                proj_fm(col0 + 128, tok0, 7, tok0 // 128)
                a, b = 0, 1
                S.op("dve", lambda e: e.tensor_tensor(rtmp[a][:], self.pb[6][:, :], Ct[:, tok0:tok0 + 512], ALU.mult),
                     reads=[self.dpb[6], d_C[cidx]], writes=[d_rtmp[a]])
                S.op("dve", lambda e: e.tensor_tensor(rtmp[b][:], self.pb[7][:, :], St[:, tok0:tok0 + 512], ALU.mult),
                     reads=[self.dpb[7], d_St[cidx]], writes=[d_rtmp[b]])
                if dst is None:
                    cs_ = slice(tok0, tok0 + 512)
                    S.op("pool", lambda e: e.tensor_tensor(QTc[0][0:64, cs_], rtmp[a][0:64, :], rtmp[b][0:64, :], ALU.add),
                         reads=[d_rtmp[a], d_rtmp[b]], writes=[d_dst])
                    S.op("dve", lambda e: e.tensor_tensor(QTc[1][64:128, cs_], rtmp[a][64:128, :], rtmp[b][64:128, :], ALU.add),
                         reads=[d_rtmp[a], d_rtmp[b]], writes=[d_dst])
                else:
                    S.op("pool", lambda e: e.tensor_tensor(dst, rtmp[a][:], rtmp[b][:], ALU.add),
                         reads=[d_rtmp[a], d_rtmp[b]], writes=[d_dst])

            for c in range(8):
                rope_chunk(256, c * 512, KT[:, c * 512:(c + 1) * 512], d_KT[c], c // 2)
            for c in range(4):
                rope_chunk(0, c * 512, None, d_QT[c], c // 2)
            for c in range(4):
                pi = 6 + c % 2
                proj_fm(640, c * 512, pi, c * 4)
                S.op("act", lambda e, c=c, pi=pi: e.activation(sgT[:, c * 512:(c + 1) * 512], self.pb[pi][:, :], AF.Silu),
                     reads=[self.dpb[pi]], writes=[d_sgT[c]])
            for g4 in range(8):
                pi = 6 + g4 % 2
                for i in range(4):
                    kb = g4 * 4 + i
                    for kt in range(8):
                        S.op("pe", lambda e, kt=kt, kb=kb, i=i: e.matmul(
                            self.pb[pi][:, i * 128:(i + 1) * 128], lhsT=hT[:, kt, kb * 128:(kb + 1) * 128],
                            rhs=w[:, kt, 512:640], start=(kt == 0), stop=(kt == 7)),
                             reads=[dw, d_hT[kb]], writes=[self.dpb[pi]], inc=(kt == 7 and i == 3))
                S.op("act", lambda e, g4=g4, pi=pi: e.activation(
                    V[:, g4 * 4:(g4 + 1) * 4, 0:128], self.pb[pi][:, :].rearrange("p (a d) -> p a d", a=4), AF.Copy),
                     reads=[self.dpb[pi]], writes=[d_V[g4]])

            if h == 0:
                self.dump("KT", KT[:], d_KT)
                self.dump("QT", QTc[0][:], d_QT)
                self.dump("V", V[:], d_V)
                self.dump("sgT", sgT[:], d_sgT)
            tiles = []
            for G in range(4):
                blocks = [(i, 0, None) for i in range(4 * G)] + [(16 + i, 0, None) for i in range(4 * G)]
                for a4 in range(4):
                    blocks.append((4 * G + a4, a4, 0))
                    blocks.append((16 + 4 * G + a4, a4, 1))
                for c in range(2):
                    for bi, (kb, a4, m) in enumerate(blocks):
                        tiles.append(dict(G=G, c=c, kb=kb, a=a4, m=m, first=(bi == 0), endgrp=(bi == len(blocks) - 1 and c == 1)))

            def emit_qk(n):
                t = tiles[n]
                G, c, kb, a4, m = t["G"], t["c"], t["kb"], t["a"], t["m"]
                ps = slice(c * 64, (c + 1) * 64)
                sbk = n % 2
                q0 = (4 * G + a4) * 128
                q1 = (4 * G + 4) * 128
                rd = [d_KT[kb // 4], d_QT[G]]
                if m is None:
                    S.op("pe", lambda e: e.matmul(self.pb[sbk][:, :], lhsT=KT[:, kb * 128:(kb + 1) * 128],
                                                  rhs=QTc[c][:, q0:q1], start=True, stop=True),
                         reads=rd, writes=[self.dpb[sbk]], inc=True)
                else:
                    c0 = a4 * 128
                    S.op("pe", lambda e: e.matmul(self.pb[sbk][:, c0:c0 + 128], lhsT=KT[:, kb * 128:(kb + 1) * 128],
                                                  rhs=QTc[c][:, q0:q0 + 128], start=True, stop=False),
                         reads=rd, writes=[self.dpb[sbk]], inc=False)
                    S.op("pe", lambda e: e.matmul(self.pb[sbk][:, c0:c0 + 128], lhsT=self.ident[:, :],
                                                  rhs=maskb[:, m * 128:(m + 1) * 128], start=False, stop=True),
                         reads=[self.d_ident, d_mask], writes=[self.dpb[sbk]], inc=(a4 == 3))
                    if a4 < 3:
                        S.op("pe", lambda e: e.matmul(self.pb[sbk][:, c0 + 128:512], lhsT=KT[:, kb * 128:(kb + 1) * 128],
                                                      rhs=QTc[c][:, q0 + 128:q1], start=True, stop=True),
                             reads=rd, writes=[self.dpb[sbk]], inc=True)

            def emit_exp_pv(n):
                t = tiles[n]
                G, c, kb, a4, m = t["G"], t["c"], t["kb"], t["a"], t["m"]
                sbk = n % 2
                ei = n % 4
                c0 = a4 * 128
                S.op("act", lambda e: e.activation(ET[ei][:, c0:512], self.pb[sbk][:, c0:512], AF.Exp, scale=0.125),
                     reads=[self.dpb[sbk]], writes=[d_ET[ei]])
                for sl in range(a4, 4):
                    ob = 2 + sl
                    O = self.pb[ob][:, 0:258].rearrange("p (c d) -> p c d", c=2)
                    last = (m == 1 and a4 == sl)
                    S.op("pe", lambda e, sl=sl, O=O, last=last: e.matmul(
                        O[:, c, :], lhsT=ET[ei][:, sl * 128:(sl + 1) * 128], rhs=V[:, kb, :],
                        start=t["first"], stop=last),
                         reads=[d_ET[ei], d_V[kb // 4]], writes=[self.dpb[ob]], inc=(last and c == 1))

            pending = None
            emit_qk(0)
            for n in range(len(tiles)):
                if n + 1 < len(tiles):
                    emit_qk(n + 1)
                emit_exp_pv(n)
                t = tiles[n]
                if not t["endgrp"]:
                    continue
                q4 = t["G"]
                stg = stage[q4 % 2]
                for sl in range(4):
                    O = self.pb[2 + sl][:, 0:258].rearrange("p (c d) -> p c d", c=2)
                    S.op("dve", lambda e, O=O, sl=sl: e.tensor_copy(stg[:, sl, :, :], O),
                         reads=[self.dpb[2 + sl]], writes=[d_stage[q4 % 2]])
                sl = 3
                if pending is not None:
                    self.attn_epilogue_b(*pending)
                    pending = None
                if sl == 3 and h == 0 and q4 == 0:
                    self.dump("stage", stg[:], [d_stage[0]])
                if sl == 3:
                    ds = d_stage[q4 % 2]
                    rz0 = esm[:, 0:8].rearrange("p (a c) -> p a c", c=2)
                    rz = esm[:, 8:16].rearrange("p (a c) -> p a c", c=2)
                    S.op("dve", lambda e, stg=stg: e.reciprocal(rz0, stg[:, :, :, 128]), reads=[ds], writes=[d_esm])
                    S.op("dve", lambda e: e.tensor_tensor(rz, rz0, lamvec[:], ALU.mult),
                         reads=[d_esm, d_lamvec], writes=[d_esm])
                    S.op("dve", lambda e, stg=stg: e.tensor_tensor(eo[0][:], stg[:, :, 0, 0:128], bcast_last(rz[:, :, 0:1], 128), ALU.mult),
                         reads=[ds, d_esm], writes=[d_eo[0]])
                    S.op("dve", lambda e, stg=stg: e.tensor_tensor(eo[1][:], stg[:, :, 1, 0:128], bcast_last(rz[:, :, 1:2], 128), ALU.mult),
                         reads=[ds, d_esm], writes=[d_eo[1]])
                    S.op("pool", lambda e: e.tensor_tensor(eo[0][:], eo[0][:], eo[1][:], ALU.add),
                         reads=[d_eo[0], d_eo[1]], writes=[d_eo[0]])
                    S.op("pool", lambda e: e.tensor_tensor(eo[1][:], eo[0][:], eo[0][:], ALU.mult),
                         reads=[d_eo[0]], writes=[d_eo[1]])
                    S.op("dve", lambda e: e.reduce_sum(esm[:, 16:20], eo[1][:], AX.X), reads=[d_eo[1]], writes=[d_esm])
                    S.op("dve", lambda e: e.tensor_scalar(esm[:, 20:24], esm[:, 16:20], 1.0 / 128, 1e-5, ALU.mult, ALU.add),
                         reads=[d_esm], writes=[d_esm])
                    S.op("act", lambda e: e.activation(esm[:, 24:28], esm[:, 20:24], AF.Ln), reads=[d_esm], writes=[d_esm])
                    S.op("act", lambda e: e.activation(esm[:, 28:32], esm[:, 24:28], AF.Exp, scale=-0.5), reads=[d_esm], writes=[d_esm])
                    S.op("dve", lambda e: e.scalar_tensor_tensor(eon[:], eo[0][:], 1.0 - LAM_INIT,
                                                                  bcast_last(esm[:, 28:32].rearrange("p (a o) -> p a o", o=1), 128),
                                                                  ALU.mult, ALU.mult),
                         reads=[d_eo[0], d_esm], writes=[d_eon])
                    pending = (h, q4, eon, d_eon, subln, d_subln, sgT, d_sgT)
                    if h == 0 and q4 == 0:
                        self.dump("eon", eon[:], [d_eon])
                        self.dump("esm", esm[:], [d_esm])
            if pending is not None:
                self.attn_epilogue_b(*pending)
                pending = None

    def attn_epilogue_b(self, h, q4, eon, d_eon, subln, d_subln, sgT, d_sgT):
        S = self.S
        ptr = self.pb[7][:].bitcast(BF16)
        for i in range(4):
            S.op("pe", lambda e, i=i: e.transpose(ptr[:, i * 128:(i + 1) * 128], eon[:, i, :], self.ident[:]),
                 reads=[d_eon, self.d_ident], writes=[self.dpb[7]], inc=(i == 3))
        S.op("dve", lambda e: e.scalar_tensor_tensor(self.ygA[:, h, q4 * 512:(q4 + 1) * 512], ptr[:, 0:512], subln[:, 0:1],
                                                      sgT[:, q4 * 512:(q4 + 1) * 512], ALU.mult, ALU.mult),
             reads=[self.dpb[7], d_subln, d_sgT[q4]], writes=[self.d_ygA[h][q4]])

    def phase_final(self, es):
        S, nc, dr, mode = self.S, self.nc, self.dr, self.mode
        sb = lambda n, s, d: self.sb(es, n, s, d)
        L0 = mode in ("fused", "L0")
        L1 = mode in ("fused", "L1")
        self.junk = sb("junkf", [128, 1024], F32)
        self.d_junk = Dep()
        postn = sb("postn", [128, 2, 1024], F32)
        d_postn = Dep()
        S.dma("sp", postn[:].rearrange("p a d -> p (a d)"), dr["postn"].partition_broadcast(128), writes=[d_postn])
        xs = sb("xs", [128, 4, D], F32)
        d_xs = [Dep() for _ in range(4)]
        hn = [sb("hnf%d" % i, [128, D], BF16) for i in range(4)]
        d_hn = [Dep() for _ in range(4)]
        ss4 = [sb("ssf%d" % i, [128, 8], F32) for i in range(4)]
        d_ss4 = [Dep() for _ in range(4)]
        hTs = sb("hTs", [128, 8, 512], BF16)
        d_hTs = [Dep() for _ in range(4)]
        ysb = sb("ysb", [128, 16, 512], BF16)
        d_ysb = [Dep() for _ in range(16)]
        tmpf = [sb("tmpf%d" % i, [128, D], F32) for i in range(2)]
        d_tmpf = [Dep() for _ in range(2)]
        wr = [sb("wr%d" % i, [128, 8, 256], BF16) for i in range(6)]
        d_wr = [Dep() for _ in range(6)]
        wo = [sb("wor%d" % i, [128, 1024], BF16) for i in range(3)]
        d_wo = [Dep() for _ in range(3)]
        if L0:
            invc = sb("invc", [128, 4, 128], F32)
            d_invc = Dep()
            S.dma("sp", invc[:].rearrange("p a d -> p (a d)"), dr["invc"].partition_broadcast(128), writes=[d_invc])
            poolw = sb("poolw", [128, 4, 2, 256], BF16)
            d_poolw = Dep()
            S.dma("pool", poolw[:], dr["poolw"][:], writes=[d_poolw])
            pscale = sb("pscale", [128, 8], F32)
            d_pscale = Dep()
            S.dma("sp", pscale[:], dr["pscale"][:], writes=[d_pscale])
            Abuf = [sb("Abuf%d" % i, [128, 4, 144], F32) for i in range(2)]
            d_A = [Dep() for _ in range(2)]
            Sb = [sb("Sbuf%d" % i, [128, 4, 144], F32) for i in range(2)]
            d_Sb = [Dep() for _ in range(2)]
            pooled = [sb("pooled%d" % i, [128, 2, 512], BF16) for i in range(2)]
            d_pooled = [[Dep() for _ in range(2)] for _ in range(2)]
            ptmp = sb("ptmp", [128, 128], F32)
            d_ptmp = Dep()
        if L1:
            lng = sb("lng", [128, 2048], F32)
            lnb = sb("lnb", [128, 2048], F32)
            d_ln = Dep()
            S.dma("sp", lng[:], dr["lng"].partition_broadcast(128), writes=[d_ln], key="ln")
            S.dma("sp", lnb[:], dr["lnb"].partition_broadcast(128), writes=[d_ln], key="ln")
            wsf = sb("wsf", [128, 8, 128], F32)
            trl = sb("trl", [128, 128], F32)
            d_wsf = Dep()
            S.dma("sp", wsf[:], dr["wsT"][:], writes=[d_wsf], key="wsf")
            S.dma("sp", trl[:], dr["tril"][:], writes=[d_wsf], key="wsf")
            wsT = sb("wsTb", [128, 8, 128], BF16)
            d_wsT = Dep()
            S.op("dve", lambda e: e.tensor_tensor(wsT[:], wsf[:], bcast_mid(trl[:], 8), ALU.mult), reads=[d_wsf], writes=[d_wsT])
            bsb = sb("bsb", [128, 8, 128], F32)
            d_bsb = Dep()
            S.dma("sp", bsb[:].rearrange("p a d -> p (a d)"), dr["sgub"].partition_broadcast(128), writes=[d_bsb])
            vb = sb("vb", [128, 4, 2048], BF16)
            d_vb = [Dep() for _ in range(4)]
            utmp = [sb("utmp%d" % i, [128, 512], BF16) for i in range(2)]
            d_utmp = [Dep() for _ in range(2)]
            lsm4 = [sb("lnsm%d" % i, [128, 12], F32) for i in range(1)]
            d_lsm4 = [Dep() for _ in range(1)]
            lsq = sb("lsq", [128, 40], F32)
            d_lsq = Dep()
            mtmp = [sb("mtmp%d" % i, [128, 512], F32) for i in range(2)]
            d_mtmp = [Dep() for _ in range(2)]

        per_sb = []
        if L0:
            for ct in range(8):
                per_sb.append(("wr", dr["wag"][ct]))
            for kt in range(16):
                per_sb.append(("wo", dr["wo0"][kt]))
        if L1:
            for cc in range(8):
                per_sb.append(("wr", dr["w1v"][cc]))
            for i in range(8):
                per_sb.append(("wr", dr["w1g"][i]))
            for i in range(8):
                per_sb.append(("wr", dr["w1u"][i]))
            for kt in range(16):
                per_sb.append(("wo", dr["wo1"][kt]))
        nper = len(per_sb)
        n_wr = sum(1 for k, _ in per_sb if k == "wr")
        n_wo = nper - n_wr
        scr_wr = nc.dram_tensor("scr_wr", [n_wr, 128, 8, 256], BF16).ap()
        scr_wo = nc.dram_tensor("scr_wo", [n_wo, 128, 1024], BF16).ap()
        scr_of = []
        c_wr = c_wo = 0
        for k, _ in per_sb:
            if k == "wr":
                scr_of.append(scr_wr[c_wr]); c_wr += 1
            else:
                scr_of.append(scr_wo[c_wo]); c_wo += 1
        d_scr = [Dep() for _ in range(nper)]
        items = per_sb * 4
        ring = {"wr": (wr, d_wr), "wo": (wo, d_wo)}
        cnt = {"wr": 0, "wo": 0}
        slot_of = []
        for kind, _ in items:
            slot_of.append(cnt[kind] % len(ring[kind][0]))
            cnt[kind] += 1
        issued = [0]
        inflight = {"wr": 0, "wo": 0}
        consumed = [0]

        def pump():
            while issued[0] < len(items):
                i = issued[0]
                kind, src = items[i]
                bufs, deps = ring[kind]
                if inflight[kind] >= len(bufs):
                    break
                sl = slot_of[i]
                j = i % nper
                if i < nper:
                    S.dma("pool", bufs[sl][:], src, writes=[deps[sl]], key="%s%d" % (kind, sl))
                    S.dma("sp", scr_of[j], bufs[sl][:], reads=[deps[sl]], writes=[d_scr[j]], key="sw%d" % (j % 32))
                else:
                    S.dma("sp", bufs[sl][:], scr_of[j], reads=[d_scr[j]], writes=[deps[sl]], key="%s%d" % (kind, sl))
                inflight[kind] += 1
                issued[0] += 1

        def take(kind):
            i = consumed[0]
            assert items[i][0] == kind, (items[i][0], kind)
            assert i < issued[0]
            sl = slot_of[i]
            bufs, deps = ring[kind]
            return bufs[sl], deps[sl]

        def release(kind):
            consumed[0] += 1
            inflight[kind] -= 1
            pump()

        pump()
        pre0 = self.pren[:, 0:8].rearrange("p (k o) -> p k o", o=1)
        pre1 = self.pren[:, 8:16].rearrange("p (k o) -> p k o", o=1)

        def post_norm_all(layer):
            for tb in range(4):
                for hf in range(2):
                    S.op("act", lambda e, hf=hf, tb=tb: e.activation(self.junk[:, hf * 512:(hf + 1) * 512], self.pb[2 * tb + hf][:, :], AF.Square,
                                                                     accum_out=ssq[:, 2 * tb + hf:2 * tb + hf + 1]),
                         reads=[self.dpb[2 * tb + hf]], writes=[self.d_junk, d_ssq])
            S.op("dve", lambda e: e.reduce_sum(ssq[:, 8:12], ssq[:, 0:8].rearrange("p (a c) -> p a c", c=2), AX.X), reads=[d_ssq], writes=[d_ssq])
            S.op("dve", lambda e: e.tensor_scalar(ssq[:, 8:12], ssq[:, 8:12], 1.0 / D, EPS, ALU.mult, ALU.add), reads=[d_ssq], writes=[d_ssq])
            S.op("act", lambda e: e.activation(ssq[:, 12:16], ssq[:, 8:12], AF.Sqrt), reads=[d_ssq], writes=[d_ssq])
            S.op("dve", lambda e: e.reciprocal(ssq[:, 8:12], ssq[:, 12:16]), reads=[d_ssq], writes=[d_ssq])
            for tb in range(4):
                tf, d_tf = tmpf[tb % 2], d_tmpf[tb % 2]
                for hf in range(2):
                    S.op("dve", lambda e, hf=hf, tb=tb, tf=tf: e.scalar_tensor_tensor(
                        tf[:, hf * 512:(hf + 1) * 512], self.pb[2 * tb + hf][:, :], ssq[:, 8 + tb:9 + tb],
                        postn[:, layer, hf * 512:(hf + 1) * 512], ALU.mult, ALU.mult),
                         reads=[self.dpb[2 * tb + hf], d_ssq, d_postn], writes=[d_tf])
                S.op("pool", lambda e, tb=tb, tf=tf: e.tensor_tensor(xs[:, tb, :], xs[:, tb, :], tf[:], ALU.add),
                     reads=[d_tf, d_xs[tb]], writes=[d_xs[tb]])

        def out_proj(layer, lhs_of):
            for kt in range(16):
                wbuf, dwb = take("wo")
                for tb in range(4):
                    lhsT, dl = lhs_of(kt, tb)
                    for hf in range(2):
                        S.op("pe", lambda e, tb=tb, hf=hf, lhsT=lhsT: e.matmul(self.pb[2 * tb + hf][:, :], lhsT=lhsT,
                                                                                rhs=wbuf[:, hf * 512:(hf + 1) * 512],
                                                                                start=(kt == 0), stop=(kt == 15)),
                             reads=[dwb, dl], writes=[self.dpb[2 * tb + hf]], inc=(kt == 15 or (tb == 3 and hf == 1)))
                release("wo")
            post_norm_all(layer)

        ssq = sb("ssq", [128, 16], F32)
        d_ssq = Dep()

        def rms4(gain):
            self.rms_group([xs[:, tb, :] for tb in range(4)], d_xs, hn, d_hn,
                           [hTs[:, :, tb * 128:(tb + 1) * 128] for tb in range(4)], d_hTs, gain, [4, 5, 6, 7], ssq, d_ssq)

        for sbi in range(4):
            tok0 = sbi * 512
            if L0:
                for tb in range(4):
                    S.dma("sp", xs[:, tb, :], dr["xin"][tok0 + tb * 128: tok0 + (tb + 1) * 128, :], writes=[d_xs[tb]], key="xs%d" % tb)
                rms4(pre0)

                def pool_mm(g):
                    pg = g % 2
                    pbank = 2 + 3 * pg
                    for dt in range(2):
                        ct = 2 * g + dt
                        for ci in range(2):
                            S.op("pe", lambda e, ci=ci, dt=dt: e.matmul(self.pb[pbank + dt][:, :],
                                                                         lhsT=poolw[:, g, ci, dt * 128:(dt + 1) * 128],
                                                                         rhs=pooled[pg][:, ci, :], start=(ci == 0), stop=(ci == 1)),
                                 reads=[d_poolw, d_pooled[pg][ci]], writes=[self.dpb[pbank + dt]], inc=(ci == 1))
                        S.op("dve", lambda e, ct=ct, dt=dt: e.scalar_tensor_tensor(ysb[:, ct, :], self.pb[pbank + dt][:, :], pscale[:, ct:ct + 1],
                                                                                   ysb[:, 8 + ct, :], ALU.mult, ALU.mult),
                             reads=[self.dpb[pbank + dt], d_pscale, d_ysb[8 + ct]], writes=[d_ysb[ct]])

                for g in range(4):
                    pg = g % 2
                    for ci in range(2):
                        ct = 2 * g + ci
                        ab, d_ab = Abuf[ci], d_A[ci]
                        wbuf, dwb = take("wr")
                        for kt in range(8):
                            S.op("pe", lambda e, kt=kt: e.matmul(self.pb[ci][:, :], lhsT=wbuf[:, kt, 0:128], rhs=hTs[:, kt, :],
                                                                 start=(kt == 0), stop=(kt == 7)),
                                 reads=[dwb] + d_hTs, writes=[self.dpb[ci]], inc=(kt == 7))
                        for kt in range(8):
                            S.op("pe", lambda e, kt=kt: e.matmul(self.pb[[4, 7][ci]][:, :],
                                                                 lhsT=wbuf[:, kt, 128:256], rhs=hTs[:, kt, :],
                                                                 start=(kt == 0), stop=(kt == 7)),
                                 reads=[dwb] + d_hTs, writes=[self.dpb[[4, 7][ci]]], inc=(kt == 7))
                        release("wr")
                        S.op("act", lambda e: e.activation(ab[:, :, 16:144], self.pb[ci][:, :].rearrange("p (a t) -> p a t", a=4), AF.Copy),
                             reads=[self.dpb[ci]], writes=[d_ab])
                        S.op("pool", lambda e, ct=ct: e.tensor_copy(
                            ab[:, :, 0:16], self.aTh[:, ct, sbi * 64:(sbi + 1) * 64].rearrange("p (a t) -> p a t", a=4)),
                             reads=[self.d_aTh], writes=[d_ab])
                        S.op("act", lambda e, ct=ct: e.activation(ysb[:, 8 + ct, :], self.pb[[4, 7][ci]][:, :], AF.Silu),
                             reads=[self.dpb[[4, 7][ci]]], writes=[d_ysb[8 + ct]])
                        src, dsrc = ab, d_ab
                        for step in range(g + 1):
                            sh = 1 << step
                            dst, ddst = Sb[step % 2], d_Sb[step % 2]
                            S.op("dve", lambda e, src=src, dst=dst, sh=sh: e.tensor_tensor(
                                dst[:, :, sh:144], src[:, :, sh:144], src[:, :, 0:144 - sh], ALU.add),
                                 reads=[dsrc], writes=[ddst])
                            src, dsrc = dst, ddst
                        w = 2 << g
                        pv = pooled[pg][:, ci, :].rearrange("p (a t) -> p a t", a=4)
                        dpl = d_pooled[pg][ci]
                        if sbi == 0:
                            S.op("dve", lambda e, src=src, g=g: e.tensor_tensor(ptmp[:], src[:, 0, 16:144], invc[:, g, :], ALU.mult),
                                 reads=[dsrc, d_invc], writes=[d_ptmp])
                            S.op("dve", lambda e, pv=pv: e.tensor_tensor(pv[:, 0, :], ptmp[:], ab[:, 0, 16:144], ALU.subtract),
                                 reads=[d_ptmp, d_ab], writes=[dpl])
                            S.op("dve", lambda e, src=src, w=w, pv=pv: e.scalar_tensor_tensor(
                                pv[:, 1:4, :], src[:, 1:4, 16:144], 1.0 / w, ab[:, 1:4, 16:144], ALU.mult, ALU.subtract),
                                 reads=[dsrc, d_ab], writes=[dpl])
                        else:
                            S.op("dve", lambda e, src=src, w=w, pv=pv: e.scalar_tensor_tensor(
                                pv, src[:, :, 16:144], 1.0 / w, ab[:, :, 16:144], ALU.mult, ALU.subtract),
                                 reads=[dsrc, d_ab], writes=[dpl])
                    if g >= 1:
                        pool_mm(g - 1)
                pool_mm(3)

                def lhs0(kt, tb):
                    if kt < 8:
                        return ysb[:, kt, tb * 128:(tb + 1) * 128], d_ysb[kt]
                    return self.ygA[:, kt - 8, tok0 + tb * 128: tok0 + (tb + 1) * 128], self.d_ygA[kt - 8][sbi]
                out_proj(0, lhs0)
            else:
                for tb in range(4):
                    S.dma("sp", xs[:, tb, :], dr["x1in"][tok0 + tb * 128: tok0 + (tb + 1) * 128, :], writes=[d_xs[tb]], key="xs%d" % tb)

            if L1:
                rms4(pre1)
                for cc in range(8):
                    wbuf, dwb = take("wr")
                    for half in range(2):
                        pi = 2 + half
                        for t2 in range(2):
                            tb = half * 2 + t2
                            for kt in range(8):
                                S.op("pe", lambda e, kt=kt, tb=tb, t2=t2, pi=pi: e.matmul(
                                    self.pb[pi][:, t2 * 256:(t2 + 1) * 256], lhsT=hTs[:, kt, tb * 128:(tb + 1) * 128],
                                    rhs=wbuf[:, kt, :], start=(kt == 0), stop=(kt == 7)),
                                     reads=[dwb, d_hTs[tb]], writes=[self.dpb[pi]], inc=(kt == 7 and t2 == 1))
                        S.op("act", lambda e, half=half, cc=cc, pi=pi: e.activation(
                            vb[:, half * 2:half * 2 + 2, cc * 256:(cc + 1) * 256],
                            self.pb[pi][:, :].rearrange("p (a d) -> p a d", a=2), AF.Gelu_apprx_tanh),
                             reads=[self.dpb[pi]], writes=[d_vb[half * 2], d_vb[half * 2 + 1]])
                    release("wr")
                lsm, d_lsmf = lsm4[0], d_lsm4[0]

                def ln_stage_a():
                    for tb in range(4):
                        S.op("dve", lambda e, tb=tb: e.reduce_sum(lsq[:, tb:tb + 1], vb[:, tb, :], AX.X), reads=[d_vb[tb]], writes=[d_lsq])
                        S.op("act", lambda e, tb=tb: e.activation(self.junk[:, :], vb[:, tb, 0:1024], AF.Square, accum_out=lsq[:, 4 + tb:5 + tb]),
                             reads=[d_vb[tb]], writes=[self.d_junk, d_lsq])
                        S.op("act", lambda e, tb=tb: e.activation(self.junk[:, :], vb[:, tb, 1024:2048], AF.Square, accum_out=lsq[:, 8 + tb:9 + tb]),
                             reads=[d_vb[tb]], writes=[self.d_junk, d_lsq])

                def ln_stage_b():
                    S.op("dve", lambda e: e.tensor_scalar(lsq[:, 12:16], lsq[:, 0:4], 1.0 / 2048, None, ALU.mult), reads=[d_lsq], writes=[d_lsq])
                    S.op("dve", lambda e: e.tensor_tensor(lsq[:, 16:20], lsq[:, 4:8], lsq[:, 8:12], ALU.add), reads=[d_lsq], writes=[d_lsq])
                    S.op("dve", lambda e: e.tensor_tensor(lsq[:, 20:24], lsq[:, 12:16], lsq[:, 12:16], ALU.mult), reads=[d_lsq], writes=[d_lsq])
                    S.op("dve", lambda e: e.scalar_tensor_tensor(lsq[:, 24:28], lsq[:, 16:20], 1.0 / 2048, lsq[:, 20:24], ALU.mult, ALU.subtract),
                         reads=[d_lsq], writes=[d_lsq])
                    S.op("dve", lambda e: e.tensor_scalar(lsq[:, 24:28], lsq[:, 24:28], EPS, None, ALU.add), reads=[d_lsq], writes=[d_lsq])
                    S.op("act", lambda e: e.activation(lsq[:, 28:32], lsq[:, 24:28], AF.Sqrt), reads=[d_lsq], writes=[d_lsq])
                    S.op("dve", lambda e: e.reciprocal(lsq[:, 32:36], lsq[:, 28:32]), reads=[d_lsq], writes=[d_lsq])
                    S.op("dve", lambda e: e.scalar_tensor_tensor(lsq[:, 36:40], lsq[:, 12:16], -1.0, lsq[:, 32:36], ALU.mult, ALU.mult),
                         reads=[d_lsq], writes=[d_lsq])

                def ln_stage_c(tb):
                    for hf in range(2):
                        cs = slice(hf * 1024, (hf + 1) * 1024)
                        tf, d_tf = tmpf[hf], d_tmpf[hf]
                        S.op("act", lambda e, cs=cs, tf=tf: e.activation(tf[:], vb[:, tb, cs], AF.Identity, bias=lsq[:, 36 + tb:37 + tb], scale=lsq[:, 32 + tb:33 + tb]),
                             reads=[d_vb[tb], d_lsq], writes=[d_tf])
                        S.op("dve", lambda e, cs=cs, tf=tf: e.tensor_tensor(tf[:], tf[:], lng[:, cs], ALU.mult), reads=[d_tf, d_ln], writes=[d_tf])
                        S.op("pool", lambda e, cs=cs, tf=tf: e.tensor_tensor(vb[:, tb, cs], tf[:], lnb[:, cs], ALU.add),
                             reads=[d_tf, d_ln], writes=[d_vb[tb]])

                def sgu_ct(ct):
                    g = ct // 2
                    pi = 4 + ct % 2
                    for tb in range(4):
                        S.op("pe", lambda e, tb=tb, pi=pi: e.matmul(
                            self.pb[pi][:, tb * 128:(tb + 1) * 128], lhsT=vb[:, tb, ct * 128:(ct + 1) * 128], rhs=wsT[:, g, :],
                            start=True, stop=True),
                             reads=[d_vb[tb], d_wsT], writes=[self.dpb[pi]], inc=(tb == 3))
                    mi = ct % 2
                    S.op("dve", lambda e, pi=pi, mi=mi: e.tensor_tensor(
                        mtmp[mi][:].rearrange("p (a t) -> p a t", a=4), self.pb[pi][:, :].rearrange("p (a t) -> p a t", a=4),
                        bcast_mid(bsb[:, g, :], 4), ALU.add),
                         reads=[self.dpb[pi], d_bsb], writes=[d_mtmp[mi]])
                    S.op("pool", lambda e, mi=mi: e.tensor_tensor(ysb[:, ct, :], ysb[:, ct, :], mtmp[mi][:], ALU.mult),
                         reads=[d_mtmp[mi], d_ysb[ct]], writes=[d_ysb[ct]])

                ln_stage_a()
                for which in range(2):
                    for i in range(8):
                        wbuf, dwb = take("wr")
                        for c2 in range(2):
                            ct = 2 * i + c2
                            pi = c2
                            for kt in range(8):
                                S.op("pe", lambda e, kt=kt, c2=c2, pi=pi: e.matmul(
                                    self.pb[pi][:, :], lhsT=wbuf[:, kt, c2 * 128:(c2 + 1) * 128], rhs=hTs[:, kt, :],
                                    start=(kt == 0), stop=(kt == 7)),
                                     reads=[dwb] + d_hTs, writes=[self.dpb[pi]], inc=(kt == 7))
                            if which == 0:
                                S.op("act", lambda e, ct=ct, pi=pi: e.activation(ysb[:, ct, :], self.pb[pi][:, :], AF.Silu),
                                     reads=[self.dpb[pi]], writes=[d_ysb[ct]])
                            else:
                                ui = ct % 2
                                S.op("act", lambda e, ui=ui, pi=pi: e.activation(utmp[ui][:], self.pb[pi][:, :], AF.Gelu_apprx_tanh),
                                     reads=[self.dpb[pi]], writes=[d_utmp[ui]])
                                S.op("pool", lambda e, ct=ct, ui=ui: e.tensor_tensor(ysb[:, ct, :], ysb[:, ct, :], utmp[ui][:], ALU.mult),
                                     reads=[d_utmp[ui], d_ysb[ct]], writes=[d_ysb[ct]])
                        release("wr")
                        if which == 0:
                            if i == 1:
                                ln_stage_b()
                            if 2 <= i <= 5:
                                ln_stage_c(i - 2)
                        else:
                            sgu_ct(2 * i)
                            sgu_ct(2 * i + 1)

                def lhs1(kt, tb):
                    return ysb[:, kt, tb * 128:(tb + 1) * 128], d_ysb[kt]
                out_proj(1, lhs1)

            for tb in range(4):
                S.dma("sp", dr["out"][tok0 + tb * 128: tok0 + (tb + 1) * 128, :], xs[:, tb, :], reads=[d_xs[tb]], key="o%d" % tb)
            self.ostore = d_xs


def _tile_cols(w):
    return np.ascontiguousarray(w.reshape(8, 128, -1).transpose(1, 0, 2))


def _partner_perm():
    perm = np.arange(128)
    for p in range(128):
        d = p % 64
        if d < 8:
            perm[p] = p + 8
        elif d < 16:
            perm[p] = p - 8
    return perm


_NC_CACHE = {}


def _get_nc(mode):
    if mode not in _NC_CACHE:
        _NC_CACHE[mode] = Builder(mode).build()
    return _NC_CACHE[mode]


def _shared_inputs(inp):
    f = np.float32
    w0 = np.asarray(inp["w_in"][0], f)
    w1 = np.asarray(inp["w_in"][1], f)
    perm = _partner_perm()
    sh = {}
    whd = np.empty((8, 128, 8, 768), f)
    for h in range(8):
        q = w0[:, 1024 + 128 * h: 1024 + 128 * (h + 1)]
        k = w0[:, 2048 + 128 * h: 2048 + 128 * (h + 1)]
        v = w0[:, 3072 + 128 * h: 3072 + 128 * (h + 1)]
        g = w0[:, 4096 + 1024 + 128 * h: 4096 + 1024 + 128 * (h + 1)]
        whd[h] = _tile_cols(np.concatenate([q, q[:, perm], k, k[:, perm], v, g], axis=1))
    sh["whd"] = whd
    wag = np.empty((8, 128, 8, 256), f)
    for ct in range(8):
        a = w0[:, 128 * ct:128 * (ct + 1)]
        g = w0[:, 4096 + 128 * ct: 4096 + 128 * (ct + 1)]
        wag[ct] = _tile_cols(np.concatenate([a, g], axis=1))
    sh["wag"] = wag
    sh["wo0"] = np.ascontiguousarray(np.asarray(inp["w_out"][0], f).reshape(16, 128, 1024))
    sh["wo1"] = np.ascontiguousarray(np.asarray(inp["w_out"][1], f).reshape(16, 128, 1024))
    w1g = np.empty((8, 128, 8, 256), f)
    w1u = np.empty((8, 128, 8, 256), f)
    for i in range(8):
        w1u[i] = _tile_cols(w1[:, 256 * i:256 * (i + 1)])
        w1g[i] = _tile_cols(w1[:, 4096 + 256 * i: 4096 + 256 * (i + 1)])
    sh["w1g"] = w1g
    sh["w1u"] = w1u
    w1v = np.empty((8, 128, 8, 256), f)
    for cc in range(8):
        w1v[cc] = _tile_cols(w1[:, 2048 + 256 * cc: 2048 + 256 * (cc + 1)])
    sh["w1v"] = w1v
    pw = np.asarray(inp["pool_w"][0], f)
    sh["poolw"] = np.ascontiguousarray(pw.reshape(4, 2, 128, 256).transpose(2, 0, 1, 3))
    sh["pscale"] = np.ascontiguousarray(np.asarray(inp["pool_scale"][0], f).reshape(8, 128).T)
    sh["lamv"] = np.concatenate([np.asarray(inp[k][0], f) for k in ("lam_q1", "lam_k1", "lam_q2", "lam_k2")])[None, :]
    sh["subln"] = np.ascontiguousarray(np.asarray(inp["diff_subln"][0], f).reshape(128, 1))
    sh["lng"] = np.asarray(inp["sgu_ln_g"], f).reshape(1, 2048)
    sh["lnb"] = np.asarray(inp["sgu_ln_b"], f).reshape(1, 2048)
    sh["wsT"] = np.ascontiguousarray(np.asarray(inp["sgu_w"][0], f).transpose(2, 0, 1))
    sh["tril"] = np.ascontiguousarray(np.tril(np.ones((128, 128), f)).T)
    sh["sgub"] = np.asarray(inp["sgu_b"][0], f).reshape(1, 1024)
    pre = np.asarray(inp["pre_norm"], f)
    sh["pren"] = np.ascontiguousarray(pre.reshape(2, 8, 128).transpose(2, 0, 1).reshape(128, 16))
    sh["postn"] = np.asarray(inp["post_norm"], f).reshape(1, 2048)
    inv_freq = np.power(np.float32(500000.0), -np.arange(8, dtype=f) * np.float32(2.0) / np.float32(16)).astype(f)
    ropec = np.zeros((128, 2), f)
    for p in range(128):
        d = p % 64
        if d < 16:
            ropec[p, 0] = inv_freq[d % 8]
            ropec[p, 1] = (-2 * np.pi) if d < 8 else (2 * np.pi)
    sh["ropec"] = ropec
    return sh


def _core_inputs(inp, b, r):
    f = np.float32
    x = np.asarray(inp["x"][b], f)
    blocks = x.reshape(32, 128, D)
    own = [2 * j + r for j in range(16)]
    oth = [2 * j + (1 - r) for j in range(16)]
    halo = np.zeros((16, 16, D), f)
    for j in range(16):
        s0 = own[j] * 128
        if s0 > 0:
            halo[j] = x[s0 - 16:s0]
    xin = np.concatenate([blocks[own].reshape(-1, D), blocks[oth].reshape(-1, D), halo.reshape(-1, D)], axis=0)
    pos = np.asarray(inp["positions"][b]).astype(np.int32).reshape(32, 128)
    pos = np.concatenate([pos[own].reshape(-1), pos[oth].reshape(-1)])[None, :]
    mask = np.zeros((128, 256), f)
    kk = np.arange(128)[:, None]
    qq = np.arange(128)[None, :]
    mask[:, 0:128] = np.where(kk <= qq, 0.0, NEG)
    mask[:, 128:256] = NEG if r == 0 else 0.0
    invc = np.zeros((4, 128), f)
    for g, w in enumerate((2, 4, 8, 16)):
        if r == 0:
            invc[g] = 1.0 / np.minimum(np.arange(128) + 1, w)
        else:
            invc[g] = 1.0 / w
    return {"xin": np.ascontiguousarray(xin), "pos": np.ascontiguousarray(pos), "mask": mask, "invc": invc.reshape(1, 512)}


L0_KEYS = ("whd", "wag", "wo0", "poolw", "pscale", "lamv", "subln", "ropec", "pren", "postn")
L1_KEYS = ("w1g", "w1u", "w1v", "wo1", "lng", "lnb", "wsT", "tril", "sgub", "pren", "postn")

MODE = "fused"


def kernel(**inp):
    sh = _shared_inputs(inp)
    cores = [(b, r) for b in range(4) for r in range(2)]
    per = [_core_inputs(inp, b, r) for (b, r) in cores]
    if MODE == "fused":
        nc = _get_nc("fused")
        maps = []
        for c in per:
            m = dict(c)
            for k in set(L0_KEYS) | set(L1_KEYS):
                m[k] = sh[k]
            maps.append(m)
        res = run_bass_kernel_spmd(nc, maps, core_ids=list(range(8)))
        outs = [r["out"] for r in res.results]
    else:
        nc0 = _get_nc("L0")
        maps = []
        for c in per:
            m = dict(c)
            for k in L0_KEYS:
                m[k] = sh[k]
            maps.append(m)
        res = run_bass_kernel_spmd(nc0, maps, core_ids=list(range(8)))
        x1 = [np.asarray(r["out"]) for r in res.results]
        nc1 = _get_nc("L1")
        maps = []
        for i in range(8):
            m = {"x1in": x1[i]}
            for k in L1_KEYS:
                m[k] = sh[k]
            maps.append(m)
        res = run_bass_kernel_spmd(nc1, maps, core_ids=list(range(8)))
        outs = [r["out"] for r in res.results]
    out = np.empty((4, 4096, D), np.float32)
    for (b, r), o in zip(cores, outs):
        ob = out[b].reshape(32, 128, D)
        ob[[2 * j + r for j in range(16)]] = np.asarray(o, np.float32).reshape(16, 128, D)
    return out
```

```python
import math
import numpy as np
import concourse.bass as bass
import concourse.mybir as mybir
from concourse.bass_utils import run_bass_kernel_spmd
from contextlib import ExitStack

F32 = mybir.dt.float32
BF16 = mybir.dt.bfloat16
I32 = mybir.dt.int32
AF = mybir.ActivationFunctionType
ALU = mybir.AluOpType
AX = mybir.AxisListType

D = 1024
NS = 16
TOK = 2048
NALL = 4352
NEG = -30000.0
EPS = 1e-6
LAM_INIT = 0.8 - 0.6 * math.exp(-0.3 * 0)
MAGIC = 12582912.0
SEM_LIMIT = 30000


class Dep:
    __slots__ = ("w", "r")

    def __init__(self):
        self.w = None
        self.r = {}


class Tok:
    __slots__ = ("key", "eng", "val")

    def __init__(self, key, eng, val):
        self.key = key
        self.eng = eng
        self.val = val


class Sched:
    ENG = ("pe", "act", "dve", "pool", "sp")

    def __init__(self, nc, es):
        self.nc = nc
        self.es = es
        self.e = dict(pe=nc.tensor, act=nc.scalar, dve=nc.vector, pool=nc.gpsimd, sp=nc.sync)
        self.gen = {k: 0 for k in self.ENG}
        self.semh = {}
        self.cnt = {}
        for k in self.ENG:
            self._newsem(k)
        self.lazy = {k: [] for k in self.ENG}
        self.seen = {k: {} for k in self.ENG}
        self.nwaits = 0
        self.nins = {k: 0 for k in self.ENG}

    def _newsem(self, eng):
        key = "%s%d" % (eng, self.gen[eng])
        self.semh[key] = self.es.enter_context(self.nc.semaphore("s_" + key))
        self.cnt[key] = 0
        return key

    def _curkey(self, eng):
        return "%s%d" % (eng, self.gen[eng])

    def _wait(self, eng, toks):
        need = {}
        for t in toks:
            if t is None:
                continue
            if t.eng == eng and eng == "pe":
                continue
            if t.val is None:
                raise RuntimeError("wait on instruction without inc: %s" % (t.key,))
            if need.get(t.key, 0) < t.val:
                need[t.key] = t.val
        for key, val in need.items():
            if self.seen[eng].get(key, 0) >= val:
                continue
            self.e[eng].wait_ge(self.semh[key], val)
            self.seen[eng][key] = val
            self.nwaits += 1

    def _deps(self, eng, reads, writes):
        toks = []
        for d in reads:
            toks.append(d.w)
        for d in writes:
            if d.w is not None and d.w.eng != eng:
                toks.append(d.w)
            for t in d.r.values():
                if t.eng == eng:
                    continue
                toks.append(t)
        self._wait(eng, toks)

    def _record(self, tok, reads, writes):
        for d in reads:
            d.r[tok.key] = tok
        for d in writes:
            d.w = tok
            d.r = {}

    def op(self, eng, fn, reads=(), writes=(), inc=True):
        self._deps(eng, reads, writes)
        ins = fn(self.e[eng])
        self.nins[eng] += 1
        key = self._curkey(eng)
        tok = Tok(key, eng, None)
        if inc:
            ins.then_inc(self.semh[key], 1)
            self.cnt[key] += 1
            tok.val = self.cnt[key]
            for t in self.lazy[eng]:
                t.val = tok.val
            self.lazy[eng] = []
            if self.cnt[key] >= SEM_LIMIT:
                self.gen[eng] += 1
                self._newsem(eng)
        else:
            self.lazy[eng].append(tok)
        self._record(tok, reads, writes)
        return tok

    def dma(self, q, out, in_, reads=(), writes=(), key=None):
        self._deps(q, reads, writes)
        if key is None:
            self.nauto = getattr(self, "nauto", 0) + 1
            key = "auto%d" % self.nauto
        key = "d_" + key
        if key not in self.semh:
            self.semh[key] = self.es.enter_context(self.nc.semaphore(key))
            self.cnt[key] = 0
        self.e[q].dma_start(out=out, in_=in_).then_inc(self.semh[key], 16)
        self.nins[q] += 1
        self.cnt[key] += 16
        assert self.cnt[key] < 2 * SEM_LIMIT
        tok = Tok(key, "dma", self.cnt[key])
        self._record(tok, reads, writes)
        return tok

    def barrier(self):
        for e in self.ENG:
            assert not self.lazy[e], "barrier with un-incremented %s instructions" % e
        keys = [(k, v) for k, v in self.cnt.items() if v > 0]
        for eng in self.ENG:
            own = self._curkey(eng)
            for key, val in keys:
                if key == own or self.seen[eng].get(key, 0) >= val:
                    continue
                self.e[eng].wait_ge(self.semh[key], val)
                self.seen[eng][key] = val
                self.nwaits += 1

    def wait_all(self, eng, deps):
        toks = []
        for d in deps:
            toks.append(d.w)
            toks.extend(d.r.values())
        self._wait(eng, toks)


def bcast_last(ap, n):
    dims = [list(x) for x in ap.ap]
    assert dims[-1][1] == 1
    dims[-1] = [0, n]
    return bass.AP(ap.tensor, ap.offset, dims)


def bcast_mid(ap, n):
    dims = [list(x) for x in ap.ap]
    assert len(dims) == 2
    return bass.AP(ap.tensor, ap.offset, [dims[0], [0, n], dims[1]])


class Builder:
    def __init__(self, mode, debug=False):
        self.mode = mode
        self.debug = debug
        self.dbg_names = []
        self.nc = bass.Bass("TRN2", target_bir_lowering=False)
        self.es = ExitStack()

    def dram_in(self, name, shape, dt=F32):
        return self.nc.dram_tensor(name, list(shape), dt, kind="ExternalInput").ap()

    def dump(self, name, ap, deps):
        if not getattr(self, "debug", False):
            return
        t = self.nc.dram_tensor("dbg_" + name, list(ap.shape), ap.dtype, kind="ExternalOutput").ap()
        self.S.dma("sp", t[:], ap, reads=deps)
        self.dbg_names.append("dbg_" + name)

    def sb(self, es, name, shape, dt):
        return es.enter_context(self.nc.sbuf_tensor("sb_" + name, list(shape), dt))

    def build(self):
        nc = self.nc
        mode = self.mode
        with self.es as es:
            S = self.S = Sched(nc, es)
            dr = self.dr = {}
            if mode in ("fused", "L0"):
                dr["xin"] = self.dram_in("xin", [NALL, D])
                dr["pos"] = self.dram_in("pos", [1, 4096], I32)
                dr["ropec"] = self.dram_in("ropec", [128, 2])
                dr["mask"] = self.dram_in("mask", [128, 256])
                dr["invc"] = self.dram_in("invc", [1, 512])
                dr["whd"] = self.dram_in("whd", [8, 128, 8, 512])
                dr["pmat"] = self.dram_in("pmat", [128, 128])
                dr["wag"] = self.dram_in("wag", [8, 128, 8, 256])
                dr["wo0"] = self.dram_in("wo0", [16, 128, 1024])
                dr["poolw"] = self.dram_in("poolw", [128, 4, 2, 256])
                dr["pscale"] = self.dram_in("pscale", [128, 8])
                dr["lamv"] = self.dram_in("lamv", [1, 256])
                dr["subln"] = self.dram_in("subln", [128, 1])
            if mode in ("fused", "L1"):
                dr["w1g"] = self.dram_in("w1g", [8, 128, 8, 256])
                dr["w1u"] = self.dram_in("w1u", [8, 128, 8, 256])
                dr["w1v"] = self.dram_in("w1v", [8, 128, 8, 256])
                dr["wo1"] = self.dram_in("wo1", [16, 128, 1024])
                dr["lng"] = self.dram_in("lng", [1, 2048])
                dr["lnb"] = self.dram_in("lnb", [1, 2048])
                dr["wsT"] = self.dram_in("wsT", [128, 8, 128])
                dr["tril"] = self.dram_in("tril", [128, 128])
                dr["sgub"] = self.dram_in("sgub", [1, 1024])
            if mode == "L1":
                dr["x1in"] = self.dram_in("x1in", [TOK, D])
            dr["pren"] = self.dram_in("pren", [128, 16])
            dr["postn"] = self.dram_in("postn", [1, 2048])
            dr["out"] = nc.dram_tensor("out", [TOK, D], F32, kind="ExternalOutput").ap()

            self.pb = [es.enter_context(nc.psum_tensor("pb%d" % i, [128, 512], F32)) for i in range(8)]
            self.dpb = [Dep() for _ in range(8)]

            self.ident = self.sb(es, "ident", [128, 128], BF16)
            self.d_ident = Dep()
            io = self.sb(es, "iota_f", [128, 128], F32)
            ip = self.sb(es, "iota_p", [128, 1], F32)
            d_io, d_ip = Dep(), Dep()
            S.op("pool", lambda e: e.iota(io[:], [[1, 128]], base=0, channel_multiplier=0,
                                          allow_small_or_imprecise_dtypes=True), writes=[d_io])
            S.op("pool", lambda e: e.iota(ip[:], [[1, 1]], base=0, channel_multiplier=1,
                                          allow_small_or_imprecise_dtypes=True), writes=[d_ip])
            S.op("dve", lambda e: e.tensor_scalar(self.ident[:], io[:], ip[:, 0:1], None, ALU.is_equal),
                 reads=[d_io, d_ip], writes=[self.d_ident])
            self.pren = self.sb(es, "pren", [128, 16], F32)
            self.d_pren = Dep()
            S.dma("sp", self.pren[:], dr["pren"][:], writes=[self.d_pren])
            self.small = self.sb(es, "small", [128, 64], F32)
            self.small_i = 0
            self.ostore = []

            if mode in ("fused", "L0"):
                self.ygA = self.sb(es, "ygA", [128, 8, TOK], BF16)
                self.d_ygA = [[Dep() for _ in range(4)] for _ in range(8)]
                self.aTh = self.sb(es, "aTh", [128, 8, 256], F32)
                self.d_aTh = Dep()
                with ExitStack() as es1:
                    self.phase_attention(es1)
            if mode in ("fused", "L0"):
                self.dump("ygA", self.ygA[:], [d for l in self.d_ygA for d in l])
                self.dump("aTh", self.aTh[:], [self.d_aTh])
            S.barrier()
            with ExitStack() as es2:
                self.phase_final(es2)
            S.wait_all("sp", self.ostore)
        return nc

    def rms_group(self, xs_aps, d_xs, hn, d_hn, outs, d_outs, gain, banks, ssq, d_ssq):
        S = self.S
        n = len(xs_aps)
        for i in range(n):
            S.op("act", lambda e, i=i: e.activation(self.junk[:], xs_aps[i], AF.Square, accum_out=ssq[:, i:i + 1]),
                 reads=[d_xs[i]], writes=[self.d_junk, d_ssq])
        S.op("dve", lambda e: e.tensor_scalar(ssq[:, 4:4 + n], ssq[:, 0:n], 1.0 / D, EPS, ALU.mult, ALU.add), reads=[d_ssq], writes=[d_ssq])
        S.op("act", lambda e: e.activation(ssq[:, 8:8 + n], ssq[:, 4:4 + n], AF.Sqrt), reads=[d_ssq], writes=[d_ssq])
        S.op("dve", lambda e: e.reciprocal(ssq[:, 12:12 + n], ssq[:, 8:8 + n]), reads=[d_ssq], writes=[d_ssq])
        for i in range(n):
            S.op("act", lambda e, i=i: e.activation(hn[i][:], xs_aps[i], AF.Copy, scale=ssq[:, 12 + i:13 + i]),
                 reads=[d_xs[i], d_ssq], writes=[d_hn[i]])
        for i in range(n):
            ptr = self.pb[banks[i]][:].bitcast(BF16)
            for kt in range(8):
                S.op("pe", lambda e, kt=kt, i=i, ptr=ptr: e.transpose(ptr[:, kt * 128:(kt + 1) * 128],
                                                                       hn[i][:, kt * 128:(kt + 1) * 128], self.ident[:]),
                     reads=[d_hn[i], self.d_ident], writes=[self.dpb[banks[i]]], inc=(kt == 7))
        for i in range(n):
            ptr = self.pb[banks[i]][:].bitcast(BF16)
            S.op("dve", lambda e, i=i, ptr=ptr: e.tensor_tensor(outs[i], ptr.rearrange("p (k t) -> p k t", k=8),
                                                                bcast_last(gain, 128), ALU.mult),
                 reads=[self.dpb[banks[i]], self.d_pren], writes=[d_outs[i]])

    def rms_transpose(self, x_ap, d_x, hn, d_hn, hT_out, d_hT, gain_ap, d_gain, ptr_i, scratch):
        S = self.S
        ss, d_ss = scratch
        junk = self.junk
        S.op("act", lambda e: e.activation(junk[:], x_ap, AF.Square, accum_out=ss[:, 0:1]),
             reads=[d_x], writes=[self.d_junk, d_ss])
        S.op("dve", lambda e: e.tensor_scalar(ss[:, 1:2], ss[:, 0:1], 1.0 / D, EPS, ALU.mult, ALU.add),
             reads=[d_ss], writes=[d_ss])
        S.op("act", lambda e: e.activation(ss[:, 2:3], ss[:, 1:2], AF.Sqrt), reads=[d_ss], writes=[d_ss])
        S.op("dve", lambda e: e.reciprocal(ss[:, 3:4], ss[:, 2:3]), reads=[d_ss], writes=[d_ss])
        S.op("act", lambda e: e.activation(hn[:], x_ap, AF.Copy, scale=ss[:, 3:4]),
             reads=[d_x, d_ss], writes=[d_hn])
        ptr = self.pb[ptr_i][:].bitcast(BF16)
        for kt in range(8):
            S.op("pe", lambda e, kt=kt: e.transpose(ptr[:, kt * 128:(kt + 1) * 128],
                                                     hn[:, kt * 128:(kt + 1) * 128], self.ident[:]),
                 reads=[d_hn, self.d_ident], writes=[self.dpb[ptr_i]], inc=(kt == 7))
        S.op("dve", lambda e: e.tensor_tensor(hT_out, ptr.rearrange("p (k t) -> p k t", k=8),
                                              bcast_last(gain_ap, 128), ALU.mult),
             reads=[self.dpb[ptr_i], d_gain], writes=[d_hT])

    def phase_attention(self, es):
        S, nc, dr = self.S, self.nc, self.dr
        sb = lambda n, s, d: self.sb(es, n, s, d)
        hT = sb("hT", [128, 8, NALL], BF16)
        d_hT = [Dep() for _ in range(34)]
        Ct = sb("ropeC", [128, 4096], BF16)
        St = sb("ropeS", [128, 4096], BF16)
        d_C = [Dep() for _ in range(4)]
        d_St = [Dep() for _ in range(4)]
        ropec = sb("ropec", [128, 2], F32)
        d_ropec = Dep()
        S.dma("sp", ropec[:], dr["ropec"][:], writes=[d_ropec])
        maskb = sb("maskb", [128, 256], BF16)
        d_mask = Dep()
        S.dma("pool", maskb[:], dr["mask"][:], writes=[d_mask])
        pmat = sb("pmat", [128, 128], BF16)
        d_pmat = Dep()
        S.dma("pool", pmat[:], dr["pmat"][:], writes=[d_pmat])
        lamv = sb("lamv", [128, 256], F32)
        d_lamv = Dep()
        S.dma("sp", lamv[:], dr["lamv"].partition_broadcast(128), writes=[d_lamv])
        subln = sb("subln", [128, 1], F32)
        d_subln = Dep()
        S.dma("sp", subln[:], dr["subln"][:], writes=[d_subln])

        lsm = sb("lam_small", [128, 8], F32)
        d_lsm = Dep()
        ltmp = sb("lam_tmp", [128, 128], F32)
        d_ltmp = Dep()
        lv4 = lamv[:].rearrange("p (a d) -> p a d", a=4)
        S.op("dve", lambda e: e.tensor_tensor(ltmp[:, 0:64], lv4[:, 0, :], lv4[:, 1, :], ALU.mult),
             reads=[d_lamv], writes=[d_ltmp])
        S.op("dve", lambda e: e.tensor_tensor(ltmp[:, 64:128], lv4[:, 2, :], lv4[:, 3, :], ALU.mult),
             reads=[d_lamv], writes=[d_ltmp])
        S.op("dve", lambda e: e.reduce_sum(lsm[:, 0:2], ltmp[:].rearrange("p (a d) -> p a d", a=2), AX.X),
             reads=[d_ltmp], writes=[d_lsm])
        S.op("act", lambda e: e.activation(lsm[:, 2:4], lsm[:, 0:2], AF.Exp), reads=[d_lsm], writes=[d_lsm])
        S.op("dve", lambda e: e.scalar_tensor_tensor(lsm[:, 4:5], lsm[:, 3:4], -LAM_INIT, lsm[:, 2:3],
                                                      ALU.add, ALU.subtract), reads=[d_lsm], writes=[d_lsm])
        lamvec = sb("lamvec", [128, 4, 2], F32)
        d_lamvec = Dep()
        S.op("dve", lambda e: e.memset(lamvec[:], 1.0), writes=[d_lamvec])
        S.op("dve", lambda e: e.tensor_copy(lamvec[:, :, 1:2], bcast_mid(lsm[:, 4:5], 4)),
             reads=[d_lsm], writes=[d_lamvec])

        wst = [sb("wst%d" % i, [128, 8, 768], BF16) for i in range(2)]
        d_wst = [Dep() for _ in range(2)]
        est = ExitStack()
        with est:
            sbt = lambda n, s_, d: self.sb(est, n, s_, d)
            self.junk = sbt("junk", [128, 1024], F32)
            self.d_junk = Dep()
            xb = [sbt("xb%d" % i, [128, D], F32) for i in range(4)]
            d_xb = [Dep() for _ in range(4)]
            hnb = [sbt("hnb%d" % i, [128, D], BF16) for i in range(4)]
            d_hnb = [Dep() for _ in range(4)]
            ssq0 = sbt("ssq0", [128, 16], F32)
            d_ssq0 = Dep()
            pre0 = self.pren[:, 0:8].rearrange("p (k o) -> p k o", o=1)
            for g0 in range(0, 34, 4):
                blks = list(range(g0, min(g0 + 4, 34)))
                for i, blk in enumerate(blks):
                    S.dma("sp", xb[i][:], dr["xin"][blk * 128:(blk + 1) * 128, :], writes=[d_xb[i]], key="x%d" % i)
                self.rms_group([xb[i][:] for i in range(len(blks))], d_xb, hnb, d_hnb,
                               [hT[:, :, blk * 128:(blk + 1) * 128] for blk in blks], [d_hT[blk] for blk in blks],
                               pre0, [4, 5, 6, 7], ssq0, d_ssq0)

            posi = sbt("posi", [128, 1024], I32)
            d_posi = Dep()
            rt = [sbt("rt%d" % i, [128, 1024], F32) for i in range(3)]
            d_rt = [Dep() for _ in range(3)]
            for c in range(4):
                cs = slice(c * 1024, (c + 1) * 1024)
                S.dma("sp", posi[:], dr["pos"][:, cs].partition_broadcast(128), writes=[d_posi], key="pos")
                S.op("dve", lambda e: e.tensor_copy(rt[0][:], posi[:]), reads=[d_posi], writes=[d_rt[0]])
                S.op("dve", lambda e: e.tensor_scalar(rt[1][:], rt[0][:], ropec[:, 0:1], float(np.float32(1.0 / (2 * np.pi))),
                                                      ALU.mult, ALU.mult), reads=[d_rt[0], d_ropec], writes=[d_rt[1]])
                S.op("dve", lambda e: e.tensor_scalar(rt[2][:], rt[1][:], MAGIC, MAGIC, ALU.add, ALU.subtract),
                     reads=[d_rt[1]], writes=[d_rt[2]])
                S.op("dve", lambda e: e.tensor_sub(rt[1][:], rt[1][:], rt[2][:]), reads=[d_rt[1], d_rt[2]], writes=[d_rt[1]])
                S.op("act", lambda e, cs=cs: e.activation(St[:, cs], rt[1][:], AF.Sin, scale=ropec[:, 1:2]),
                     reads=[d_rt[1], d_ropec], writes=[d_St[c]])
                S.op("dve", lambda e: e.tensor_scalar(rt[2][:], rt[1][:], -1.0, None, ALU.mult),
                     reads=[d_rt[1]], writes=[d_rt[2]])
                S.op("dve", lambda e: e.tensor_tensor(rt[2][:], rt[2][:], rt[1][:], ALU.max),
                     reads=[d_rt[1], d_rt[2]], writes=[d_rt[2]])
                S.op("dve", lambda e: e.tensor_scalar(rt[2][:], rt[2][:], float(-2 * np.pi), float(np.pi / 2), ALU.mult, ALU.add),
                     reads=[d_rt[2]], writes=[d_rt[2]])
                S.op("act", lambda e, cs=cs: e.activation(Ct[:, cs], rt[2][:], AF.Sin),
                     reads=[d_rt[2]], writes=[d_C[c]])

        S.barrier()
        self.dump("hT0", hT[:, :, 0:256], d_hT[0:2])
        self.dump("hTh", hT[:, :, 4096:4352], d_hT[32:34])
        self.dump("Ct", Ct[:], d_C)
        self.dump("St", St[:], d_St)
        self.dump("lsm", lsm[:], [d_lsm])
        for ct in range(8):
            s = ct % 2
            S.dma("pool", wst[s][:, :, 0:256], dr["wag"][ct], writes=[d_wst[s]], key="wh%d" % s)
            pi = 4 + s
            for kt in range(8):
                S.op("pe", lambda e, kt=kt, s=s, pi=pi: e.matmul(self.pb[pi][:, 0:256], lhsT=wst[s][:, kt, 0:128],
                                                                 rhs=hT[:, kt, 4096:4352], start=(kt == 0), stop=(kt == 7)),
                     reads=[d_wst[s], d_hT[32], d_hT[33]], writes=[self.dpb[pi]], inc=(kt == 7))
            S.op("act", lambda e, ct=ct, pi=pi: e.activation(self.aTh[:, ct, :], self.pb[pi][:, 0:256], AF.Copy),
                 reads=[self.dpb[pi]], writes=[self.d_aTh])

        KT = sb("KT", [128, 4096], BF16)
        d_KT = [Dep() for _ in range(8)]
        QTc = [sb("QT%d" % i, [128, TOK], BF16) for i in range(2)]
        d_QT = [Dep() for _ in range(4)]
        S.op("pool", lambda e: e.memset(QTc[0][64:128, :], 0.0), writes=d_QT)
        S.op("pool", lambda e: e.memset(QTc[1][0:64, :], 0.0), writes=d_QT)
        V = sb("V", [128, 32, 129], BF16)
        d_V = [Dep() for _ in range(8)]
        S.op("pool", lambda e: e.memset(V[:, :, 128:129], 1.0), writes=d_V)
        sgT = sb("sgT", [128, TOK], BF16)
        d_sgT = [Dep() for _ in range(4)]
        ET = [sb("ET%d" % i, [128, 512], BF16) for i in range(4)]
        d_ET = [Dep() for _ in range(4)]
        kb16 = [sb("kb16_%d" % i, [128, 512], BF16) for i in range(2)]
        d_kb16 = [Dep() for _ in range(2)]
        rtmp = [sb("rtmp%d" % i, [128, 512], F32) for i in range(2)]
        d_rtmp = [Dep() for _ in range(2)]
        stage = [sb("stage%d" % i, [128, 4, 2, 129], F32) for i in range(2)]
        d_stage = [Dep() for _ in range(2)]
        eo = [sb("eo%d" % i, [128, 4, 128], F32) for i in range(2)]
        d_eo = [Dep() for _ in range(2)]
        eon = sb("eon", [128, 4, 128], BF16)
        d_eon = Dep()
        esm = sb("esm", [128, 32], F32)
        d_esm = Dep()

        et_i = [0]
        st_i = [0]
        rt_i = [0]
        S.dma("pool", wst[0][:, :, 0:512], dr["whd"][0], writes=[d_wst[0]], key="wh0")
        for h in range(8):
            s = h % 2
            if h + 1 < 8:
                S.dma("pool", wst[1 - s][:, :, 0:512], dr["whd"][h + 1], writes=[d_wst[1 - s]], key="wh%d" % (1 - s))
            w = wst[s]
            dw = d_wst[s]

            def proj_fm(col0, tok0, pi, first_blk):
                for kt in range(8):
                    S.op("pe", lambda e, kt=kt: e.matmul(self.pb[pi][:, :], lhsT=w[:, kt, col0:col0 + 128],
                                                         rhs=hT[:, kt, tok0:tok0 + 512], start=(kt == 0), stop=(kt == 7)),
                         reads=[dw] + d_hT[first_blk:first_blk + 4], writes=[self.dpb[pi]], inc=(kt == 7))

            chunks = [("k", c) for c in range(8)] + [("q", c) for c in range(4)]

            def rope_proj(i):
                kind, c = chunks[i]
                col0 = 128 if kind == "k" else 0
                pi = i % 3
                proj_fm(col0, c * 512, pi, c * 4)
                S.op("act", lambda e: e.activation(kb16[i % 2][:], self.pb[pi][:, :], AF.Copy),
                     reads=[self.dpb[pi]], writes=[d_kb16[i % 2]])

            def rope_finish(i):
                kind, c = chunks[i]
                pi = i % 3
                pq = 3 + i % 2
                tok0 = c * 512
                cidx = c // 2
                S.op("pe", lambda e: e.matmul(self.pb[pq][:, :], lhsT=pmat[:, :], rhs=kb16[i % 2][:], start=True, stop=True),
                     reads=[d_pmat, d_kb16[i % 2]], writes=[self.dpb[pq]], inc=True)
                S.op("dve", lambda e: e.tensor_tensor(rtmp[0][:], self.pb[pi][:, :], Ct[:, tok0:tok0 + 512], ALU.mult),
                     reads=[self.dpb[pi], d_C[cidx], d_kb16[i % 2]], writes=[d_rtmp[0]])
                S.op("dve", lambda e: e.tensor_tensor(rtmp[1][:], self.pb[pq][:, :], St[:, tok0:tok0 + 512], ALU.mult),
                     reads=[self.dpb[pq], d_St[cidx]], writes=[d_rtmp[1]])
                cs_ = slice(tok0, tok0 + 512)
                if kind == "k":
                    S.op("pool", lambda e: e.tensor_tensor(KT[:, cs_], rtmp[0][:], rtmp[1][:], ALU.add),
                         reads=[d_rtmp[0], d_rtmp[1]], writes=[d_KT[c]])
                else:
                    S.op("pool", lambda e: e.tensor_tensor(QTc[0][0:64, cs_], rtmp[0][0:64, :], rtmp[1][0:64, :], ALU.add),
                         reads=[d_rtmp[0], d_rtmp[1]], writes=[d_QT[c]])
                    S.op("dve", lambda e: e.tensor_tensor(QTc[1][64:128, cs_], rtmp[0][64:128, :], rtmp[1][64:128, :], ALU.add),
                         reads=[d_rtmp[0], d_rtmp[1]], writes=[d_QT[c]])

            for i in range(len(chunks)):
                rope_proj(i)
                if i >= 1:
                    rope_finish(i - 1)
            rope_finish(len(chunks) - 1)
            for c in range(4):
                pi = 6 + c % 2
                proj_fm(384, c * 512, pi, c * 4)
                S.op("act", lambda e, c=c, pi=pi: e.activation(sgT[:, c * 512:(c + 1) * 512], self.pb[pi][:, :], AF.Silu),
                     reads=[self.dpb[pi]], writes=[d_sgT[c]])
            for g4 in range(8):
                pi = 6 + g4 % 2
                for i in range(4):
                    kb = g4 * 4 + i
                    for kt in range(8):
                        S.op("pe", lambda e, kt=kt, kb=kb, i=i: e.matmul(
                            self.pb[pi][:, i * 128:(i + 1) * 128], lhsT=hT[:, kt, kb * 128:(kb + 1) * 128],
                            rhs=w[:, kt, 256:384], start=(kt == 0), stop=(kt == 7)),
                             reads=[dw, d_hT[kb]], writes=[self.dpb[pi]], inc=(kt == 7 and i == 3))
                S.op("act", lambda e, g4=g4, pi=pi: e.activation(
                    V[:, g4 * 4:(g4 + 1) * 4, 0:128], self.pb[pi][:, :].rearrange("p (a d) -> p a d", a=4), AF.Copy),
                     reads=[self.dpb[pi]], writes=[d_V[g4]])

            if h == 0:
                self.dump("KT", KT[:], d_KT)
                self.dump("QT", QTc[0][:], d_QT)
                self.dump("V", V[:], d_V)
                self.dump("sgT", sgT[:], d_sgT)
            tiles = []
            for G in range(4):
                blocks = [(i, 0, None) for i in range(4 * G)] + [(16 + i, 0, None) for i in range(4 * G)]
                for a4 in range(4):
                    blocks.append((4 * G + a4, a4, 0))
                    blocks.append((16 + 4 * G + a4, a4, 1))
                for c in range(2):
                    for bi, (kb, a4, m) in enumerate(blocks):
                        tiles.append(dict(G=G, c=c, kb=kb, a=a4, m=m, first=(bi == 0), endgrp=(bi == len(blocks) - 1 and c == 1)))

            def emit_qk(n):
                t = tiles[n]
                G, c, kb, a4, m = t["G"], t["c"], t["kb"], t["a"], t["m"]
                ps = slice(c * 64, (c + 1) * 64)
                sbk = n % 2
                q0 = (4 * G + a4) * 128
                q1 = (4 * G + 4) * 128
                rd = [d_KT[kb // 4], d_QT[G]]
                if m is None:
                    S.op("pe", lambda e: e.matmul(self.pb[sbk][:, :], lhsT=KT[:, kb * 128:(kb + 1) * 128],
                                                  rhs=QTc[c][:, q0:q1], start=True, stop=True),
                         reads=rd, writes=[self.dpb[sbk]], inc=True)
                else:
                    c0 = a4 * 128
                    S.op("pe", lambda e: e.matmul(self.pb[sbk][:, c0:c0 + 128], lhsT=KT[:, kb * 128:(kb + 1) * 128],
                                                  rhs=QTc[c][:, q0:q0 + 128], start=True, stop=False),
                         reads=rd, writes=[self.dpb[sbk]], inc=False)
                    S.op("pe", lambda e: e.matmul(self.pb[sbk][:, c0:c0 + 128], lhsT=self.ident[:, :],
                                                  rhs=maskb[:, m * 128:(m + 1) * 128], start=False, stop=True),
                         reads=[self.d_ident, d_mask], writes=[self.dpb[sbk]], inc=(a4 == 3))
                    if a4 < 3:
                        S.op("pe", lambda e: e.matmul(self.pb[sbk][:, c0 + 128:512], lhsT=KT[:, kb * 128:(kb + 1) * 128],
                                                      rhs=QTc[c][:, q0 + 128:q1], start=True, stop=True),
                             reads=rd, writes=[self.dpb[sbk]], inc=True)

            def emit_exp_pv(n):
                t = tiles[n]
                G, c, kb, a4, m = t["G"], t["c"], t["kb"], t["a"], t["m"]
                sbk = n % 2
                ei = n % 4
                c0 = a4 * 128
                S.op("act", lambda e: e.activation(ET[ei][:, c0:512], self.pb[sbk][:, c0:512], AF.Exp, scale=0.125),
                     reads=[self.dpb[sbk]], writes=[d_ET[ei]])
                for sl in range(a4, 4):
                    ob = 2 + sl
                    O = self.pb[ob][:, 0:258].rearrange("p (c d) -> p c d", c=2)
                    last = (m == 1 and a4 == sl)
                    S.op("pe", lambda e, sl=sl, O=O, last=last: e.matmul(
                        O[:, c, :], lhsT=ET[ei][:, sl * 128:(sl + 1) * 128], rhs=V[:, kb, :],
                        start=t["first"], stop=last),
                         reads=[d_ET[ei], d_V[kb // 4]], writes=[self.dpb[ob]], inc=(last and c == 1))

            pending = None
            emit_qk(0)
            for n in range(len(tiles)):
                if n + 1 < len(tiles):
                    emit_qk(n + 1)
                emit_exp_pv(n)
                t = tiles[n]
                if not t["endgrp"]:
                    continue
                q4 = t["G"]
                stg = stage[q4 % 2]
                for sl in range(4):
                    O = self.pb[2 + sl][:, 0:258].rearrange("p (c d) -> p c d", c=2)
                    S.op("dve", lambda e, O=O, sl=sl: e.tensor_copy(stg[:, sl, :, :], O),
                         reads=[self.dpb[2 + sl]], writes=[d_stage[q4 % 2]])
                sl = 3
                if pending is not None:
                    self.attn_epilogue_b(*pending)
                    pending = None
                if sl == 3 and h == 0 and q4 == 0:
                    self.dump("stage", stg[:], [d_stage[0]])
                if sl == 3:
                    ds = d_stage[q4 % 2]
                    rz0 = esm[:, 0:8].rearrange("p (a c) -> p a c", c=2)
                    rz = esm[:, 8:16].rearrange("p (a c) -> p a c", c=2)
                    S.op("dve", lambda e, stg=stg: e.reciprocal(rz0, stg[:, :, :, 128]), reads=[ds], writes=[d_esm])
                    S.op("dve", lambda e: e.tensor_tensor(rz, rz0, lamvec[:], ALU.mult),
                         reads=[d_esm, d_lamvec], writes=[d_esm])
                    S.op("dve", lambda e, stg=stg: e.tensor_tensor(eo[0][:], stg[:, :, 0, 0:128], bcast_last(rz[:, :, 0:1], 128), ALU.mult),
                         reads=[ds, d_esm], writes=[d_eo[0]])
                    S.op("dve", lambda e, stg=stg: e.tensor_tensor(eo[1][:], stg[:, :, 1, 0:128], bcast_last(rz[:, :, 1:2], 128), ALU.mult),
                         reads=[ds, d_esm], writes=[d_eo[1]])
                    S.op("pool", lambda e: e.tensor_tensor(eo[0][:], eo[0][:], eo[1][:], ALU.add),
                         reads=[d_eo[0], d_eo[1]], writes=[d_eo[0]])
                    S.op("pool", lambda e: e.tensor_tensor(eo[1][:], eo[0][:], eo[0][:], ALU.mult),
                         reads=[d_eo[0]], writes=[d_eo[1]])
                    S.op("dve", lambda e: e.reduce_sum(esm[:, 16:20], eo[1][:], AX.X), reads=[d_eo[1]], writes=[d_esm])
                    S.op("dve", lambda e: e.tensor_scalar(esm[:, 20:24], esm[:, 16:20], 1.0 / 128, 1e-5, ALU.mult, ALU.add),
                         reads=[d_esm], writes=[d_esm])
                    S.op("act", lambda e: e.activation(esm[:, 24:28], esm[:, 20:24], AF.Ln), reads=[d_esm], writes=[d_esm])
                    S.op("act", lambda e: e.activation(esm[:, 28:32], esm[:, 24:28], AF.Exp, scale=-0.5), reads=[d_esm], writes=[d_esm])
                    S.op("dve", lambda e: e.scalar_tensor_tensor(eon[:], eo[0][:], 1.0 - LAM_INIT,
                                                                  bcast_last(esm[:, 28:32].rearrange("p (a o) -> p a o", o=1), 128),
                                                                  ALU.mult, ALU.mult),
                         reads=[d_eo[0], d_esm], writes=[d_eon])
                    pending = (h, q4, eon, d_eon, subln, d_subln, sgT, d_sgT)
                    if h == 0 and q4 == 0:
                        self.dump("eon", eon[:], [d_eon])
                        self.dump("esm", esm[:], [d_esm])
            if pending is not None:
                self.attn_epilogue_b(*pending)
                pending = None

    def attn_epilogue_b(self, h, q4, eon, d_eon, subln, d_subln, sgT, d_sgT):
        S = self.S
        ptr = self.pb[7][:].bitcast(BF16)
        for i in range(4):
            S.op("pe", lambda e, i=i: e.transpose(ptr[:, i * 128:(i + 1) * 128], eon[:, i, :], self.ident[:]),
                 reads=[d_eon, self.d_ident], writes=[self.dpb[7]], inc=(i == 3))
        S.op("dve", lambda e: e.scalar_tensor_tensor(self.ygA[:, h, q4 * 512:(q4 + 1) * 512], ptr[:, 0:512], subln[:, 0:1],
                                                      sgT[:, q4 * 512:(q4 + 1) * 512], ALU.mult, ALU.mult),
             reads=[self.dpb[7], d_subln, d_sgT[q4]], writes=[self.d_ygA[h][q4]])

    def phase_final(self, es):
        S, nc, dr, mode = self.S, self.nc, self.dr, self.mode
        sb = lambda n, s, d: self.sb(es, n, s, d)
        L0 = mode in ("fused", "L0")
        L1 = mode in ("fused", "L1")
        self.junk = sb("junkf", [128, 1024], F32)
        self.d_junk = Dep()
        postn = sb("postn", [128, 2, 1024], F32)
        d_postn = Dep()
        S.dma("sp", postn[:].rearrange("p a d -> p (a d)"), dr["postn"].partition_broadcast(128), writes=[d_postn])
        xs = sb("xs", [128, 4, D], F32)
        d_xs = [Dep() for _ in range(4)]
        hn = [sb("hnf%d" % i, [128, D], BF16) for i in range(4)]
        d_hn = [Dep() for _ in range(4)]
        ss4 = [sb("ssf%d" % i, [128, 8], F32) for i in range(4)]
        d_ss4 = [Dep() for _ in range(4)]
        hTs = sb("hTs", [128, 8, 512], BF16)
        d_hTs = [Dep() for _ in range(4)]
        ysb = sb("ysb", [128, 16, 512], BF16)
        d_ysb = [Dep() for _ in range(16)]
        tmpf = [sb("tmpf%d" % i, [128, D], F32) for i in range(2)]
        d_tmpf = [Dep() for _ in range(2)]
        wr = [sb("wr%d" % i, [128, 8, 256], BF16) for i in range(6)]
        d_wr = [Dep() for _ in range(6)]
        wo = [sb("wor%d" % i, [128, 1024], BF16) for i in range(3)]
        d_wo = [Dep() for _ in range(3)]
        if L0:
            invc = sb("invc", [128, 4, 128], F32)
            d_invc = Dep()
            S.dma("sp", invc[:].rearrange("p a d -> p (a d)"), dr["invc"].partition_broadcast(128), writes=[d_invc])
            poolw = sb("poolw", [128, 4, 2, 256], BF16)
            d_poolw = Dep()
            S.dma("pool", poolw[:], dr["poolw"][:], writes=[d_poolw])
            pscale = sb("pscale", [128, 8], F32)
            d_pscale = Dep()
            S.dma("sp", pscale[:], dr["pscale"][:], writes=[d_pscale])
            Abuf = [sb("Abuf%d" % i, [128, 4, 144], F32) for i in range(2)]
            d_A = [Dep() for _ in range(2)]
            Sb = [sb("Sbuf%d" % i, [128, 4, 144], F32) for i in range(2)]
            d_Sb = [Dep() for _ in range(2)]
            pooled = [sb("pooled%d" % i, [128, 2, 512], BF16) for i in range(2)]
            d_pooled = [[Dep() for _ in range(2)] for _ in range(2)]
            ptmp = sb("ptmp", [128, 128], F32)
            d_ptmp = Dep()
        if L1:
            lng = sb("lng", [128, 2048], F32)
            lnb = sb("lnb", [128, 2048], F32)
            d_ln = Dep()
            S.dma("sp", lng[:], dr["lng"].partition_broadcast(128), writes=[d_ln], key="ln")
            S.dma("sp", lnb[:], dr["lnb"].partition_broadcast(128), writes=[d_ln], key="ln")
            wsf = sb("wsf", [128, 8, 128], F32)
            trl = sb("trl", [128, 128], F32)
            d_wsf = Dep()
            S.dma("sp", wsf[:], dr["wsT"][:], writes=[d_wsf], key="wsf")
            S.dma("sp", trl[:], dr["tril"][:], writes=[d_wsf], key="wsf")
            wsT = sb("wsTb", [128, 8, 128], BF16)
            d_wsT = Dep()
            S.op("dve", lambda e: e.tensor_tensor(wsT[:], wsf[:], bcast_mid(trl[:], 8), ALU.mult), reads=[d_wsf], writes=[d_wsT])
            bsb = sb("bsb", [128, 8, 128], F32)
            d_bsb = Dep()
            S.dma("sp", bsb[:].rearrange("p a d -> p (a d)"), dr["sgub"].partition_broadcast(128), writes=[d_bsb])
            vb = sb("vb", [128, 4, 2048], BF16)
            d_vb = [Dep() for _ in range(4)]
            utmp = [sb("utmp%d" % i, [128, 512], BF16) for i in range(2)]
            d_utmp = [Dep() for _ in range(2)]
            lsm4 = [sb("lnsm%d" % i, [128, 12], F32) for i in range(1)]
            d_lsm4 = [Dep() for _ in range(1)]
            lsq = sb("lsq", [128, 40], F32)
            d_lsq = Dep()
            mtmp = [sb("mtmp%d" % i, [128, 512], F32) for i in range(2)]
            d_mtmp = [Dep() for _ in range(2)]

        per_sb = []
        if L0:
            for ct in range(8):
                per_sb.append(("wr", dr["wag"][ct]))
            for kt in range(16):
                per_sb.append(("wo", dr["wo0"][kt]))
        if L1:
            for cc in range(8):
                per_sb.append(("wr", dr["w1v"][cc]))
            for i in range(8):
                per_sb.append(("wr", dr["w1g"][i]))
            for i in range(8):
                per_sb.append(("wr", dr["w1u"][i]))
            for kt in range(16):
                per_sb.append(("wo", dr["wo1"][kt]))
        nper = len(per_sb)
        n_wr = sum(1 for k, _ in per_sb if k == "wr")
        n_wo = nper - n_wr
        scr_wr = nc.dram_tensor("scr_wr", [n_wr, 128, 8, 256], BF16).ap()
        scr_wo = nc.dram_tensor("scr_wo", [n_wo, 128, 1024], BF16).ap()
        scr_of = []
        c_wr = c_wo = 0
        for k, _ in per_sb:
            if k == "wr":
                scr_of.append(scr_wr[c_wr]); c_wr += 1
            else:
                scr_of.append(scr_wo[c_wo]); c_wo += 1
        d_scr = [Dep() for _ in range(nper)]
        items = per_sb * 4
        ring = {"wr": (wr, d_wr), "wo": (wo, d_wo)}
        cnt = {"wr": 0, "wo": 0}
        slot_of = []
        for kind, _ in items:
            slot_of.append(cnt[kind] % len(ring[kind][0]))
            cnt[kind] += 1
        issued = [0]
        inflight = {"wr": 0, "wo": 0}
        consumed = [0]

        def pump():
            while issued[0] < len(items):
                i = issued[0]
                kind, src = items[i]
                bufs, deps = ring[kind]
                if inflight[kind] >= len(bufs):
                    break
                sl = slot_of[i]
                j = i % nper
                if i < nper:
                    S.dma("pool", bufs[sl][:], src, writes=[deps[sl]], key="%s%d" % (kind, sl))
                    S.dma("sp", scr_of[j], bufs[sl][:], reads=[deps[sl]], writes=[d_scr[j]], key="sw%d" % (j % 32))
                else:
                    S.dma("sp", bufs[sl][:], scr_of[j], reads=[d_scr[j]], writes=[deps[sl]], key="%s%d" % (kind, sl))
                inflight[kind] += 1
                issued[0] += 1

        def take(kind):
            i = consumed[0]
            assert items[i][0] == kind, (items[i][0], kind)
            assert i < issued[0]
            sl = slot_of[i]
            bufs, deps = ring[kind]
            return bufs[sl], deps[sl]

        def release(kind):
            consumed[0] += 1
            inflight[kind] -= 1
            pump()

        pump()
        pre0 = self.pren[:, 0:8].rearrange("p (k o) -> p k o", o=1)
        pre1 = self.pren[:, 8:16].rearrange("p (k o) -> p k o", o=1)

        def post_norm_all(layer):
            for tb in range(4):
                for hf in range(2):
                    S.op("act", lambda e, hf=hf, tb=tb: e.activation(self.junk[:, hf * 512:(hf + 1) * 512], self.pb[2 * tb + hf][:, :], AF.Square,
                                                                     accum_out=ssq[:, 2 * tb + hf:2 * tb + hf + 1]),
                         reads=[self.dpb[2 * tb + hf]], writes=[self.d_junk, d_ssq])
            S.op("dve", lambda e: e.reduce_sum(ssq[:, 8:12], ssq[:, 0:8].rearrange("p (a c) -> p a c", c=2), AX.X), reads=[d_ssq], writes=[d_ssq])
            S.op("dve", lambda e: e.tensor_scalar(ssq[:, 8:12], ssq[:, 8:12], 1.0 / D, EPS, ALU.mult, ALU.add), reads=[d_ssq], writes=[d_ssq])
            S.op("act", lambda e: e.activation(ssq[:, 12:16], ssq[:, 8:12], AF.Sqrt), reads=[d_ssq], writes=[d_ssq])
            S.op("dve", lambda e: e.reciprocal(ssq[:, 8:12], ssq[:, 12:16]), reads=[d_ssq], writes=[d_ssq])
            for tb in range(4):
                tf, d_tf = tmpf[tb % 2], d_tmpf[tb % 2]
                for hf in range(2):
                    S.op("dve", lambda e, hf=hf, tb=tb, tf=tf: e.scalar_tensor_tensor(
                        tf[:, hf * 512:(hf + 1) * 512], self.pb[2 * tb + hf][:, :], ssq[:, 8 + tb:9 + tb],
                        postn[:, layer, hf * 512:(hf + 1) * 512], ALU.mult, ALU.mult),
                         reads=[self.dpb[2 * tb + hf], d_ssq, d_postn], writes=[d_tf])
                S.op("pool", lambda e, tb=tb, tf=tf: e.tensor_tensor(xs[:, tb, :], xs[:, tb, :], tf[:], ALU.add),
                     reads=[d_tf, d_xs[tb]], writes=[d_xs[tb]])

        def out_proj(layer, lhs_of):
            for kt in range(16):
                wbuf, dwb = take("wo")
                for tb in range(4):
                    lhsT, dl = lhs_of(kt, tb)
                    for hf in range(2):
                        S.op("pe", lambda e, tb=tb, hf=hf, lhsT=lhsT: e.matmul(self.pb[2 * tb + hf][:, :], lhsT=lhsT,
                                                                                rhs=wbuf[:, hf * 512:(hf + 1) * 512],
                                                                                start=(kt == 0), stop=(kt == 15)),
                             reads=[dwb, dl], writes=[self.dpb[2 * tb + hf]], inc=(kt == 15 or (tb == 3 and hf == 1)))
                release("wo")
            post_norm_all(layer)

        ssq = sb("ssq", [128, 16], F32)
        d_ssq = Dep()

        def rms4(gain):
            self.rms_group([xs[:, tb, :] for tb in range(4)], d_xs, hn, d_hn,
                           [hTs[:, :, tb * 128:(tb + 1) * 128] for tb in range(4)], d_hTs, gain, [4, 5, 6, 7], ssq, d_ssq)

        for sbi in range(4):
            tok0 = sbi * 512
            if L0:
                for tb in range(4):
                    S.dma("sp", xs[:, tb, :], dr["xin"][tok0 + tb * 128: tok0 + (tb + 1) * 128, :], writes=[d_xs[tb]], key="xs%d" % tb)
                rms4(pre0)

                def pool_mm(g):
                    pg = g % 2
                    pbank = 2 + 3 * pg
                    for dt in range(2):
                        ct = 2 * g + dt
                        for ci in range(2):
                            S.op("pe", lambda e, ci=ci, dt=dt: e.matmul(self.pb[pbank + dt][:, :],
                                                                         lhsT=poolw[:, g, ci, dt * 128:(dt + 1) * 128],
                                                                         rhs=pooled[pg][:, ci, :], start=(ci == 0), stop=(ci == 1)),
                                 reads=[d_poolw, d_pooled[pg][ci]], writes=[self.dpb[pbank + dt]], inc=(ci == 1))
                        S.op("dve", lambda e, ct=ct, dt=dt: e.scalar_tensor_tensor(ysb[:, ct, :], self.pb[pbank + dt][:, :], pscale[:, ct:ct + 1],
                                                                                   ysb[:, 8 + ct, :], ALU.mult, ALU.mult),
                             reads=[self.dpb[pbank + dt], d_pscale, d_ysb[8 + ct]], writes=[d_ysb[ct]])

                for g in range(4):
                    pg = g % 2
                    for ci in range(2):
                        ct = 2 * g + ci
                        ab, d_ab = Abuf[ci], d_A[ci]
                        wbuf, dwb = take("wr")
                        for kt in range(8):
                            S.op("pe", lambda e, kt=kt: e.matmul(self.pb[ci][:, :], lhsT=wbuf[:, kt, 0:128], rhs=hTs[:, kt, :],
                                                                 start=(kt == 0), stop=(kt == 7)),
                                 reads=[dwb] + d_hTs, writes=[self.dpb[ci]], inc=(kt == 7))
                        for kt in range(8):
                            S.op("pe", lambda e, kt=kt: e.matmul(self.pb[[4, 7][ci]][:, :],
                                                                 lhsT=wbuf[:, kt, 128:256], rhs=hTs[:, kt, :],
                                                                 start=(kt == 0), stop=(kt == 7)),
                                 reads=[dwb] + d_hTs, writes=[self.dpb[[4, 7][ci]]], inc=(kt == 7))
                        release("wr")
                        S.op("act", lambda e: e.activation(ab[:, :, 16:144], self.pb[ci][:, :].rearrange("p (a t) -> p a t", a=4), AF.Copy),
                             reads=[self.dpb[ci]], writes=[d_ab])
                        S.op("pool", lambda e, ct=ct: e.tensor_copy(
                            ab[:, :, 0:16], self.aTh[:, ct, sbi * 64:(sbi + 1) * 64].rearrange("p (a t) -> p a t", a=4)),
                             reads=[self.d_aTh], writes=[d_ab])
                        S.op("act", lambda e, ct=ct: e.activation(ysb[:, 8 + ct, :], self.pb[[4, 7][ci]][:, :], AF.Silu),
                             reads=[self.dpb[[4, 7][ci]]], writes=[d_ysb[8 + ct]])
                        src, dsrc = ab, d_ab
                        for step in range(g + 1):
                            sh = 1 << step
                            dst, ddst = Sb[step % 2], d_Sb[step % 2]
                            S.op("dve", lambda e, src=src, dst=dst, sh=sh: e.tensor_tensor(
                                dst[:, :, sh:144], src[:, :, sh:144], src[:, :, 0:144 - sh], ALU.add),
                                 reads=[dsrc], writes=[ddst])
                            src, dsrc = dst, ddst
                        w = 2 << g
                        pv = pooled[pg][:, ci, :].rearrange("p (a t) -> p a t", a=4)
                        dpl = d_pooled[pg][ci]
                        if sbi == 0:
                            S.op("dve", lambda e, src=src, g=g: e.tensor_tensor(ptmp[:], src[:, 0, 16:144], invc[:, g, :], ALU.mult),
                                 reads=[dsrc, d_invc], writes=[d_ptmp])
                            S.op("dve", lambda e, pv=pv: e.tensor_tensor(pv[:, 0, :], ptmp[:], ab[:, 0, 16:144], ALU.subtract),
                                 reads=[d_ptmp, d_ab], writes=[dpl])
                            S.op("dve", lambda e, src=src, w=w, pv=pv: e.scalar_tensor_tensor(
                                pv[:, 1:4, :], src[:, 1:4, 16:144], 1.0 / w, ab[:, 1:4, 16:144], ALU.mult, ALU.subtract),
                                 reads=[dsrc, d_ab], writes=[dpl])
                        else:
                            S.op("dve", lambda e, src=src, w=w, pv=pv: e.scalar_tensor_tensor(
                                pv, src[:, :, 16:144], 1.0 / w, ab[:, :, 16:144], ALU.mult, ALU.subtract),
                                 reads=[dsrc, d_ab], writes=[dpl])
                    if g >= 1:
                        pool_mm(g - 1)
                pool_mm(3)

                def lhs0(kt, tb):
                    if kt < 8:
                        return ysb[:, kt, tb * 128:(tb + 1) * 128], d_ysb[kt]
                    return self.ygA[:, kt - 8, tok0 + tb * 128: tok0 + (tb + 1) * 128], self.d_ygA[kt - 8][sbi]
                out_proj(0, lhs0)
            else:
                for tb in range(4):
                    S.dma("sp", xs[:, tb, :], dr["x1in"][tok0 + tb * 128: tok0 + (tb + 1) * 128, :], writes=[d_xs[tb]], key="xs%d" % tb)

            if L1:
                rms4(pre1)
                for cc in range(8):
                    wbuf, dwb = take("wr")
                    for half in range(2):
                        pi = 2 + half
                        for t2 in range(2):
                            tb = half * 2 + t2
                            for kt in range(8):
                                S.op("pe", lambda e, kt=kt, tb=tb, t2=t2, pi=pi: e.matmul(
                                    self.pb[pi][:, t2 * 256:(t2 + 1) * 256], lhsT=hTs[:, kt, tb * 128:(tb + 1) * 128],
                                    rhs=wbuf[:, kt, :], start=(kt == 0), stop=(kt == 7)),
                                     reads=[dwb, d_hTs[tb]], writes=[self.dpb[pi]], inc=(kt == 7 and t2 == 1))
                        S.op("act", lambda e, half=half, cc=cc, pi=pi: e.activation(
                            vb[:, half * 2:half * 2 + 2, cc * 256:(cc + 1) * 256],
                            self.pb[pi][:, :].rearrange("p (a d) -> p a d", a=2), AF.Gelu_apprx_tanh),
                             reads=[self.dpb[pi]], writes=[d_vb[half * 2], d_vb[half * 2 + 1]])
                    release("wr")
                lsm, d_lsmf = lsm4[0], d_lsm4[0]

                def ln_stage_a():
                    for tb in range(4):
                        S.op("dve", lambda e, tb=tb: e.reduce_sum(lsq[:, tb:tb + 1], vb[:, tb, :], AX.X), reads=[d_vb[tb]], writes=[d_lsq])
                        S.op("act", lambda e, tb=tb: e.activation(self.junk[:, :], vb[:, tb, 0:1024], AF.Square, accum_out=lsq[:, 4 + tb:5 + tb]),
                             reads=[d_vb[tb]], writes=[self.d_junk, d_lsq])
                        S.op("act", lambda e, tb=tb: e.activation(self.junk[:, :], vb[:, tb, 1024:2048], AF.Square, accum_out=lsq[:, 8 + tb:9 + tb]),
                             reads=[d_vb[tb]], writes=[self.d_junk, d_lsq])

                def ln_stage_b():
                    S.op("dve", lambda e: e.tensor_scalar(lsq[:, 12:16], lsq[:, 0:4], 1.0 / 2048, None, ALU.mult), reads=[d_lsq], writes=[d_lsq])
                    S.op("dve", lambda e: e.tensor_tensor(lsq[:, 16:20], lsq[:, 4:8], lsq[:, 8:12], ALU.add), reads=[d_lsq], writes=[d_lsq])
                    S.op("dve", lambda e: e.tensor_tensor(lsq[:, 20:24], lsq[:, 12:16], lsq[:, 12:16], ALU.mult), reads=[d_lsq], writes=[d_lsq])
                    S.op("dve", lambda e: e.scalar_tensor_tensor(lsq[:, 24:28], lsq[:, 16:20], 1.0 / 2048, lsq[:, 20:24], ALU.mult, ALU.subtract),
                         reads=[d_lsq], writes=[d_lsq])
                    S.op("dve", lambda e: e.tensor_scalar(lsq[:, 24:28], lsq[:, 24:28], EPS, None, ALU.add), reads=[d_lsq], writes=[d_lsq])
                    S.op("act", lambda e: e.activation(lsq[:, 28:32], lsq[:, 24:28], AF.Sqrt), reads=[d_lsq], writes=[d_lsq])
                    S.op("dve", lambda e: e.reciprocal(lsq[:, 32:36], lsq[:, 28:32]), reads=[d_lsq], writes=[d_lsq])
                    S.op("dve", lambda e: e.scalar_tensor_tensor(lsq[:, 36:40], lsq[:, 12:16], -1.0, lsq[:, 32:36], ALU.mult, ALU.mult),
                         reads=[d_lsq], writes=[d_lsq])

                def ln_stage_c(tb):
                    for hf in range(2):
                        cs = slice(hf * 1024, (hf + 1) * 1024)
                        tf, d_tf = tmpf[hf], d_tmpf[hf]
                        S.op("act", lambda e, cs=cs, tf=tf: e.activation(tf[:], vb[:, tb, cs], AF.Identity, bias=lsq[:, 36 + tb:37 + tb], scale=lsq[:, 32 + tb:33 + tb]),
                             reads=[d_vb[tb], d_lsq], writes=[d_tf])
                        S.op("dve", lambda e, cs=cs, tf=tf: e.tensor_tensor(tf[:], tf[:], lng[:, cs], ALU.mult), reads=[d_tf, d_ln], writes=[d_tf])
                        S.op("pool", lambda e, cs=cs, tf=tf: e.tensor_tensor(vb[:, tb, cs], tf[:], lnb[:, cs], ALU.add),
                             reads=[d_tf, d_ln], writes=[d_vb[tb]])

                def sgu_ct(ct):
                    g = ct // 2
                    pi = 4 + ct % 2
                    for tb in range(4):
                        S.op("pe", lambda e, tb=tb, pi=pi: e.matmul(
                            self.pb[pi][:, tb * 128:(tb + 1) * 128], lhsT=vb[:, tb, ct * 128:(ct + 1) * 128], rhs=wsT[:, g, :],
                            start=True, stop=True),
                             reads=[d_vb[tb], d_wsT], writes=[self.dpb[pi]], inc=(tb == 3))
                    mi = ct % 2
                    S.op("dve", lambda e, pi=pi, mi=mi: e.tensor_tensor(
                        mtmp[mi][:].rearrange("p (a t) -> p a t", a=4), self.pb[pi][:, :].rearrange("p (a t) -> p a t", a=4),
                        bcast_mid(bsb[:, g, :], 4), ALU.add),
                         reads=[self.dpb[pi], d_bsb], writes=[d_mtmp[mi]])
                    S.op("pool", lambda e, mi=mi: e.tensor_tensor(ysb[:, ct, :], ysb[:, ct, :], mtmp[mi][:], ALU.mult),
                         reads=[d_mtmp[mi], d_ysb[ct]], writes=[d_ysb[ct]])

                ln_stage_a()
                for which in range(2):
                    for i in range(8):
                        wbuf, dwb = take("wr")
                        for c2 in range(2):
                            ct = 2 * i + c2
                            pi = c2
                            for kt in range(8):
                                S.op("pe", lambda e, kt=kt, c2=c2, pi=pi: e.matmul(
                                    self.pb[pi][:, :], lhsT=wbuf[:, kt, c2 * 128:(c2 + 1) * 128], rhs=hTs[:, kt, :],
                                    start=(kt == 0), stop=(kt == 7)),
                                     reads=[dwb] + d_hTs, writes=[self.dpb[pi]], inc=(kt == 7))
                            if which == 0:
                                S.op("act", lambda e, ct=ct, pi=pi: e.activation(ysb[:, ct, :], self.pb[pi][:, :], AF.Silu),
                                     reads=[self.dpb[pi]], writes=[d_ysb[ct]])
                            else:
                                ui = ct % 2
                                S.op("act", lambda e, ui=ui, pi=pi: e.activation(utmp[ui][:], self.pb[pi][:, :], AF.Gelu_apprx_tanh),
                                     reads=[self.dpb[pi]], writes=[d_utmp[ui]])
                                S.op("pool", lambda e, ct=ct, ui=ui: e.tensor_tensor(ysb[:, ct, :], ysb[:, ct, :], utmp[ui][:], ALU.mult),
                                     reads=[d_utmp[ui], d_ysb[ct]], writes=[d_ysb[ct]])
                        release("wr")
                        if which == 0:
                            if i == 1:
                                ln_stage_b()
                            if 2 <= i <= 5:
                                ln_stage_c(i - 2)
                        else:
                            sgu_ct(2 * i)
                            sgu_ct(2 * i + 1)

                def lhs1(kt, tb):
                    return ysb[:, kt, tb * 128:(tb + 1) * 128], d_ysb[kt]
                out_proj(1, lhs1)

            for tb in range(4):
                S.dma("sp", dr["out"][tok0 + tb * 128: tok0 + (tb + 1) * 128, :], xs[:, tb, :], reads=[d_xs[tb]], key="o%d" % tb)
            self.ostore = d_xs


def _tile_cols(w):
    return np.ascontiguousarray(w.reshape(8, 128, -1).transpose(1, 0, 2))


def _partner_perm():
    perm = np.arange(128)
    for p in range(128):
        d = p % 64
        if d < 8:
            perm[p] = p + 8
        elif d < 16:
            perm[p] = p - 8
    return perm


_NC_CACHE = {}


def _get_nc(mode):
    if mode not in _NC_CACHE:
        _NC_CACHE[mode] = Builder(mode).build()
    return _NC_CACHE[mode]


def _shared_inputs(inp):
    f = np.float32
    w0 = np.asarray(inp["w_in"][0], f)
    w1 = np.asarray(inp["w_in"][1], f)
    perm = _partner_perm()
    sh = {}
    whd = np.empty((8, 128, 8, 512), f)
    for h in range(8):
        q = w0[:, 1024 + 128 * h: 1024 + 128 * (h + 1)]
        k = w0[:, 2048 + 128 * h: 2048 + 128 * (h + 1)]
        v = w0[:, 3072 + 128 * h: 3072 + 128 * (h + 1)]
        g = w0[:, 4096 + 1024 + 128 * h: 4096 + 1024 + 128 * (h + 1)]
        whd[h] = _tile_cols(np.concatenate([q, k, v, g], axis=1))
    sh["whd"] = whd
    pm = np.zeros((128, 128), f)
    pm[perm, np.arange(128)] = 1.0
    sh["pmat"] = pm
    wag = np.empty((8, 128, 8, 256), f)
    for ct in range(8):
        a = w0[:, 128 * ct:128 * (ct + 1)]
        g = w0[:, 4096 + 128 * ct: 4096 + 128 * (ct + 1)]
        wag[ct] = _tile_cols(np.concatenate([a, g], axis=1))
    sh["wag"] = wag
    sh["wo0"] = np.ascontiguousarray(np.asarray(inp["w_out"][0], f).reshape(16, 128, 1024))
    sh["wo1"] = np.ascontiguousarray(np.asarray(inp["w_out"][1], f).reshape(16, 128, 1024))
    w1g = np.empty((8, 128, 8, 256), f)
    w1u = np.empty((8, 128, 8, 256), f)
    for i in range(8):
        w1u[i] = _tile_cols(w1[:, 256 * i:256 * (i + 1)])
        w1g[i] = _tile_cols(w1[:, 4096 + 256 * i: 4096 + 256 * (i + 1)])
    sh["w1g"] = w1g
    sh["w1u"] = w1u
    w1v = np.empty((8, 128, 8, 256), f)
    for cc in range(8):
        w1v[cc] = _tile_cols(w1[:, 2048 + 256 * cc: 2048 + 256 * (cc + 1)])
    sh["w1v"] = w1v
    pw = np.asarray(inp["pool_w"][0], f)
    sh["poolw"] = np.ascontiguousarray(pw.reshape(4, 2, 128, 256).transpose(2, 0, 1, 3))
    sh["pscale"] = np.ascontiguousarray(np.asarray(inp["pool_scale"][0], f).reshape(8, 128).T)
    sh["lamv"] = np.concatenate([np.asarray(inp[k][0], f) for k in ("lam_q1", "lam_k1", "lam_q2", "lam_k2")])[None, :]
    sh["subln"] = np.ascontiguousarray(np.asarray(inp["diff_subln"][0], f).reshape(128, 1))
    sh["lng"] = np.asarray(inp["sgu_ln_g"], f).reshape(1, 2048)
    sh["lnb"] = np.asarray(inp["sgu_ln_b"], f).reshape(1, 2048)
    sh["wsT"] = np.ascontiguousarray(np.asarray(inp["sgu_w"][0], f).transpose(2, 0, 1))
    sh["tril"] = np.ascontiguousarray(np.tril(np.ones((128, 128), f)).T)
    sh["sgub"] = np.asarray(inp["sgu_b"][0], f).reshape(1, 1024)
    pre = np.asarray(inp["pre_norm"], f)
    sh["pren"] = np.ascontiguousarray(pre.reshape(2, 8, 128).transpose(2, 0, 1).reshape(128, 16))
    sh["postn"] = np.asarray(inp["post_norm"], f).reshape(1, 2048)
    inv_freq = np.power(np.float32(500000.0), -np.arange(8, dtype=f) * np.float32(2.0) / np.float32(16)).astype(f)
    ropec = np.zeros((128, 2), f)
    for p in range(128):
        d = p % 64
        if d < 16:
            ropec[p, 0] = inv_freq[d % 8]
            ropec[p, 1] = (-2 * np.pi) if d < 8 else (2 * np.pi)
    sh["ropec"] = ropec
    return sh


def _core_inputs(inp, b, r):
    f = np.float32
    x = np.asarray(inp["x"][b], f)
    blocks = x.reshape(32, 128, D)
    own = [2 * j + r for j in range(16)]
    oth = [2 * j + (1 - r) for j in range(16)]
    halo = np.zeros((16, 16, D), f)
    for j in range(16):
        s0 = own[j] * 128
        if s0 > 0:
            halo[j] = x[s0 - 16:s0]
    xin = np.concatenate([blocks[own].reshape(-1, D), blocks[oth].reshape(-1, D), halo.reshape(-1, D)], axis=0)
    pos = np.asarray(inp["positions"][b]).astype(np.int32).reshape(32, 128)
    pos = np.concatenate([pos[own].reshape(-1), pos[oth].reshape(-1)])[None, :]
    mask = np.zeros((128, 256), f)
    kk = np.arange(128)[:, None]
    qq = np.arange(128)[None, :]
    mask[:, 0:128] = np.where(kk <= qq, 0.0, NEG)
    mask[:, 128:256] = NEG if r == 0 else 0.0
    invc = np.zeros((4, 128), f)
    for g, w in enumerate((2, 4, 8, 16)):
        if r == 0:
            invc[g] = 1.0 / np.minimum(np.arange(128) + 1, w)
        else:
            invc[g] = 1.0 / w
    return {"xin": np.ascontiguousarray(xin), "pos": np.ascontiguousarray(pos), "mask": mask, "invc": invc.reshape(1, 512)}


L0_KEYS = ("pmat", "whd", "wag", "wo0", "poolw", "pscale", "lamv", "subln", "ropec", "pren", "postn")
L1_KEYS = ("w1g", "w1u", "w1v", "wo1", "lng", "lnb", "wsT", "tril", "sgub", "pren", "postn")

MODE = "fused"


def kernel(**inp):
    sh = _shared_inputs(inp)
    cores = [(b, r) for b in range(4) for r in range(2)]
    per = [_core_inputs(inp, b, r) for (b, r) in cores]
    if MODE == "fused":
        nc = _get_nc("fused")
        maps = []
        for c in per:
            m = dict(c)
            for k in set(L0_KEYS) | set(L1_KEYS):
                m[k] = sh[k]
            maps.append(m)
        res = run_bass_kernel_spmd(nc, maps, core_ids=list(range(8)))
        outs = [r["out"] for r in res.results]
    else:
        nc0 = _get_nc("L0")
        maps = []
        for c in per:
            m = dict(c)
            for k in L0_KEYS:
                m[k] = sh[k]
            maps.append(m)
        res = run_bass_kernel_spmd(nc0, maps, core_ids=list(range(8)))
        x1 = [np.asarray(r["out"]) for r in res.results]
        nc1 = _get_nc("L1")
        maps = []
        for i in range(8):
            m = {"x1in": x1[i]}
            for k in L1_KEYS:
                m[k] = sh[k]
            maps.append(m)
        res = run_bass_kernel_spmd(nc1, maps, core_ids=list(range(8)))
        outs = [r["out"] for r in res.results]
    out = np.empty((4, 4096, D), np.float32)
    for (b, r), o in zip(cores, outs):
        ob = out[b].reshape(32, 128, D)
        ob[[2 * j + r for j in range(16)]] = np.asarray(o, np.float32).reshape(16, 128, D)
    return out
```

```python
import math
import numpy as np
import concourse.bass as bass
import concourse.mybir as mybir
from concourse.bass_utils import run_bass_kernel_spmd
from contextlib import ExitStack

F32 = mybir.dt.float32
BF16 = mybir.dt.bfloat16
I32 = mybir.dt.int32
AF = mybir.ActivationFunctionType
ALU = mybir.AluOpType
AX = mybir.AxisListType

D = 1024
NS = 16
TOK = 2048
NALL = 4352
NEG = -30000.0
EPS = 1e-6
LAM_INIT = 0.8 - 0.6 * math.exp(-0.3 * 0)
MAGIC = 12582912.0
SEM_LIMIT = 30000


class Dep:
    __slots__ = ("w", "r")

    def __init__(self):
        self.w = None
        self.r = {}


class Tok:
    __slots__ = ("key", "eng", "val")

    def __init__(self, key, eng, val):
        self.key = key
        self.eng = eng
        self.val = val


class Sched:
    ENG = ("pe", "act", "dve", "pool", "sp")

    def __init__(self, nc, es):
        self.nc = nc
        self.es = es
        self.e = dict(pe=nc.tensor, act=nc.scalar, dve=nc.vector, pool=nc.gpsimd, sp=nc.sync)
        self.gen = {k: 0 for k in self.ENG}
        self.semh = {}
        self.cnt = {}
        for k in self.ENG:
            self._newsem(k)
        self.lazy = {k: [] for k in self.ENG}
        self.seen = {k: {} for k in self.ENG}
        self.nwaits = 0
        self.nins = {k: 0 for k in self.ENG}

    def _newsem(self, eng):
        key = "%s%d" % (eng, self.gen[eng])
        self.semh[key] = self.es.enter_context(self.nc.semaphore("s_" + key))
        self.cnt[key] = 0
        return key

    def _curkey(self, eng):
        return "%s%d" % (eng, self.gen[eng])

    def _wait(self, eng, toks):
        need = {}
        for t in toks:
            if t is None:
                continue
            if t.eng == eng and eng == "pe":
                continue
            if t.val is None:
                raise RuntimeError("wait on instruction without inc: %s" % (t.key,))
            if need.get(t.key, 0) < t.val:
                need[t.key] = t.val
        for key, val in need.items():
            if self.seen[eng].get(key, 0) >= val:
                continue
            self.e[eng].wait_ge(self.semh[key], val)
            self.seen[eng][key] = val
            self.nwaits += 1

    def _deps(self, eng, reads, writes):
        toks = []
        for d in reads:
            toks.append(d.w)
        for d in writes:
            if d.w is not None and d.w.eng != eng:
                toks.append(d.w)
            for t in d.r.values():
                if t.eng == eng:
                    continue
                toks.append(t)
        self._wait(eng, toks)

    def _record(self, tok, reads, writes):
        for d in reads:
            d.r[tok.key] = tok
        for d in writes:
            d.w = tok
            d.r = {}

    def op(self, eng, fn, reads=(), writes=(), inc=True):
        self._deps(eng, reads, writes)
        ins = fn(self.e[eng])
        self.nins[eng] += 1
        key = self._curkey(eng)
        tok = Tok(key, eng, None)
        if inc:
            ins.then_inc(self.semh[key], 1)
            self.cnt[key] += 1
            tok.val = self.cnt[key]
            for t in self.lazy[eng]:
                t.val = tok.val
            self.lazy[eng] = []
            if self.cnt[key] >= SEM_LIMIT:
                self.gen[eng] += 1
                self._newsem(eng)
        else:
            self.lazy[eng].append(tok)
        self._record(tok, reads, writes)
        return tok

    def dma(self, q, out, in_, reads=(), writes=(), key=None):
        self._deps(q, reads, writes)
        if key is None:
            self.nauto = getattr(self, "nauto", 0) + 1
            key = "auto%d" % self.nauto
        key = "d_" + key
        if key not in self.semh:
            self.semh[key] = self.es.enter_context(self.nc.semaphore(key))
            self.cnt[key] = 0
        self.e[q].dma_start(out=out, in_=in_).then_inc(self.semh[key], 16)
        self.nins[q] += 1
        self.cnt[key] += 16
        assert self.cnt[key] < 2 * SEM_LIMIT
        tok = Tok(key, "dma", self.cnt[key])
        self._record(tok, reads, writes)
        return tok

    def barrier(self):
        for e in self.ENG:
            assert not self.lazy[e], "barrier with un-incremented %s instructions" % e
        keys = [(k, v) for k, v in self.cnt.items() if v > 0]
        for eng in self.ENG:
            own = self._curkey(eng)
            for key, val in keys:
                if key == own or self.seen[eng].get(key, 0) >= val:
                    continue
                self.e[eng].wait_ge(self.semh[key], val)
                self.seen[eng][key] = val
                self.nwaits += 1

    def wait_all(self, eng, deps):
        toks = []
        for d in deps:
            toks.append(d.w)
            toks.extend(d.r.values())
        self._wait(eng, toks)


def bcast_last(ap, n):
    dims = [list(x) for x in ap.ap]
    assert dims[-1][1] == 1
    dims[-1] = [0, n]
    return bass.AP(ap.tensor, ap.offset, dims)


def bcast_mid(ap, n):
    dims = [list(x) for x in ap.ap]
    assert len(dims) == 2
    return bass.AP(ap.tensor, ap.offset, [dims[0], [0, n], dims[1]])


class Builder:
    def __init__(self, mode, debug=False):
        self.mode = mode
        self.debug = debug
        self.dbg_names = []
        self.nc = bass.Bass("TRN2", target_bir_lowering=False)
        self.es = ExitStack()

    def dram_in(self, name, shape, dt=F32):
        return self.nc.dram_tensor(name, list(shape), dt, kind="ExternalInput").ap()

    def dump(self, name, ap, deps):
        if not getattr(self, "debug", False):
            return
        t = self.nc.dram_tensor("dbg_" + name, list(ap.shape), ap.dtype, kind="ExternalOutput").ap()
        self.S.dma("sp", t[:], ap, reads=deps)
        self.dbg_names.append("dbg_" + name)

    def sb(self, es, name, shape, dt):
        return es.enter_context(self.nc.sbuf_tensor("sb_" + name, list(shape), dt))

    def build(self):
        nc = self.nc
        mode = self.mode
        with self.es as es:
            S = self.S = Sched(nc, es)
            dr = self.dr = {}
            if mode in ("fused", "L0"):
                dr["xin"] = self.dram_in("xin", [NALL, D])
                dr["pos"] = self.dram_in("pos", [1, 4096], I32)
                dr["ropec"] = self.dram_in("ropec", [128, 2])
                dr["mask"] = self.dram_in("mask", [128, 256])
                dr["invc"] = self.dram_in("invc", [1, 512])
                dr["whd"] = self.dram_in("whd", [8, 128, 8, 512])
                dr["pmat"] = self.dram_in("pmat", [128, 128])
                dr["wag"] = self.dram_in("wag", [8, 128, 8, 256])
                dr["wo0"] = self.dram_in("wo0", [16, 128, 1024])
                dr["poolw"] = self.dram_in("poolw", [128, 4, 2, 256])
                dr["pscale"] = self.dram_in("pscale", [128, 8])
                dr["lamv"] = self.dram_in("lamv", [1, 256])
                dr["subln"] = self.dram_in("subln", [128, 1])
            if mode in ("fused", "L1"):
                dr["w1g"] = self.dram_in("w1g", [8, 128, 8, 256])
                dr["w1u"] = self.dram_in("w1u", [8, 128, 8, 256])
                dr["w1v"] = self.dram_in("w1v", [8, 128, 8, 256])
                dr["wo1"] = self.dram_in("wo1", [16, 128, 1024])
                dr["lng"] = self.dram_in("lng", [1, 2048])
                dr["lnb"] = self.dram_in("lnb", [1, 2048])
                dr["wsT"] = self.dram_in("wsT", [128, 8, 128])
                dr["tril"] = self.dram_in("tril", [128, 128])
                dr["sgub"] = self.dram_in("sgub", [1, 1024])
            if mode == "L1":
                dr["x1in"] = self.dram_in("x1in", [TOK, D])
            dr["pren"] = self.dram_in("pren", [128, 16])
            dr["postn"] = self.dram_in("postn", [1, 2048])
            dr["out"] = nc.dram_tensor("out", [TOK, D], F32, kind="ExternalOutput").ap()

            self.pb = [es.enter_context(nc.psum_tensor("pb%d" % i, [128, 512], F32)) for i in range(8)]
            self.dpb = [Dep() for _ in range(8)]

            self.ident = self.sb(es, "ident", [128, 128], BF16)
            self.d_ident = Dep()
            io = self.sb(es, "iota_f", [128, 128], F32)
            ip = self.sb(es, "iota_p", [128, 1], F32)
            d_io, d_ip = Dep(), Dep()
            S.op("pool", lambda e: e.iota(io[:], [[1, 128]], base=0, channel_multiplier=0,
                                          allow_small_or_imprecise_dtypes=True), writes=[d_io])
            S.op("pool", lambda e: e.iota(ip[:], [[1, 1]], base=0, channel_multiplier=1,
                                          allow_small_or_imprecise_dtypes=True), writes=[d_ip])
            S.op("dve", lambda e: e.tensor_scalar(self.ident[:], io[:], ip[:, 0:1], None, ALU.is_equal),
                 reads=[d_io, d_ip], writes=[self.d_ident])
            self.pren = self.sb(es, "pren", [128, 16], F32)
            self.d_pren = Dep()
            S.dma("sp", self.pren[:], dr["pren"][:], writes=[self.d_pren])
            self.small = self.sb(es, "small", [128, 64], F32)
            self.small_i = 0
            self.ostore = []

            if mode in ("fused", "L0"):
                self.ygA = self.sb(es, "ygA", [128, 8, TOK], BF16)
                self.d_ygA = [[Dep() for _ in range(4)] for _ in range(8)]
                self.aTh = self.sb(es, "aTh", [128, 8, 256], F32)
                self.d_aTh = Dep()
                with ExitStack() as es1:
                    self.phase_attention(es1)
            if mode in ("fused", "L0"):
                self.dump("ygA", self.ygA[:], [d for l in self.d_ygA for d in l])
                self.dump("aTh", self.aTh[:], [self.d_aTh])
            S.barrier()
            with ExitStack() as es2:
                self.phase_final(es2)
            S.wait_all("sp", self.ostore)
        return nc

    def rms_group(self, xs_aps, d_xs, hn, d_hn, outs, d_outs, gain, banks, ssq, d_ssq):
        S = self.S
        n = len(xs_aps)
        for i in range(n):
            S.op("act", lambda e, i=i: e.activation(self.junk[:], xs_aps[i], AF.Square, accum_out=ssq[:, i:i + 1]),
                 reads=[d_xs[i]], writes=[self.d_junk, d_ssq])
        S.op("dve", lambda e: e.tensor_scalar(ssq[:, 4:4 + n], ssq[:, 0:n], 1.0 / D, EPS, ALU.mult, ALU.add), reads=[d_ssq], writes=[d_ssq])
        S.op("act", lambda e: e.activation(ssq[:, 8:8 + n], ssq[:, 4:4 + n], AF.Sqrt), reads=[d_ssq], writes=[d_ssq])
        S.op("dve", lambda e: e.reciprocal(ssq[:, 12:12 + n], ssq[:, 8:8 + n]), reads=[d_ssq], writes=[d_ssq])
        for i in range(n):
            S.op("act", lambda e, i=i: e.activation(hn[i][:], xs_aps[i], AF.Copy, scale=ssq[:, 12 + i:13 + i]),
                 reads=[d_xs[i], d_ssq], writes=[d_hn[i]])
        for i in range(n):
            ptr = self.pb[banks[i]][:].bitcast(BF16)
            for kt in range(8):
                S.op("pe", lambda e, kt=kt, i=i, ptr=ptr: e.transpose(ptr[:, kt * 128:(kt + 1) * 128],
                                                                       hn[i][:, kt * 128:(kt + 1) * 128], self.ident[:]),
                     reads=[d_hn[i], self.d_ident], writes=[self.dpb[banks[i]]], inc=(kt == 7))
        for i in range(n):
            ptr = self.pb[banks[i]][:].bitcast(BF16)
            S.op("dve", lambda e, i=i, ptr=ptr: e.tensor_tensor(outs[i], ptr.rearrange("p (k t) -> p k t", k=8),
                                                                bcast_last(gain, 128), ALU.mult),
                 reads=[self.dpb[banks[i]], self.d_pren], writes=[d_outs[i]])

    def rms_transpose(self, x_ap, d_x, hn, d_hn, hT_out, d_hT, gain_ap, d_gain, ptr_i, scratch):
        S = self.S
        ss, d_ss = scratch
        junk = self.junk
        S.op("act", lambda e: e.activation(junk[:], x_ap, AF.Square, accum_out=ss[:, 0:1]),
             reads=[d_x], writes=[self.d_junk, d_ss])
        S.op("dve", lambda e: e.tensor_scalar(ss[:, 1:2], ss[:, 0:1], 1.0 / D, EPS, ALU.mult, ALU.add),
             reads=[d_ss], writes=[d_ss])
        S.op("act", lambda e: e.activation(ss[:, 2:3], ss[:, 1:2], AF.Sqrt), reads=[d_ss], writes=[d_ss])
        S.op("dve", lambda e: e.reciprocal(ss[:, 3:4], ss[:, 2:3]), reads=[d_ss], writes=[d_ss])
        S.op("act", lambda e: e.activation(hn[:], x_ap, AF.Copy, scale=ss[:, 3:4]),
             reads=[d_x, d_ss], writes=[d_hn])
        ptr = self.pb[ptr_i][:].bitcast(BF16)
        for kt in range(8):
            S.op("pe", lambda e, kt=kt: e.transpose(ptr[:, kt * 128:(kt + 1) * 128],
                                                     hn[:, kt * 128:(kt + 1) * 128], self.ident[:]),
                 reads=[d_hn, self.d_ident], writes=[self.dpb[ptr_i]], inc=(kt == 7))
        S.op("dve", lambda e: e.tensor_tensor(hT_out, ptr.rearrange("p (k t) -> p k t", k=8),
                                              bcast_last(gain_ap, 128), ALU.mult),
             reads=[self.dpb[ptr_i], d_gain], writes=[d_hT])

    def phase_attention(self, es):
        S, nc, dr = self.S, self.nc, self.dr
        sb = lambda n, s, d: self.sb(es, n, s, d)
        hT = sb("hT", [128, 8, NALL], BF16)
        d_hT = [Dep() for _ in range(34)]
        Ct = sb("ropeC", [128, 4096], BF16)
        St = sb("ropeS", [128, 4096], BF16)
        d_C = [Dep() for _ in range(4)]
        d_St = [Dep() for _ in range(4)]
        ropec = sb("ropec", [128, 2], F32)
        d_ropec = Dep()
        S.dma("sp", ropec[:], dr["ropec"][:], writes=[d_ropec])
        maskb = sb("maskb", [128, 256], BF16)
        d_mask = Dep()
        S.dma("pool", maskb[:], dr["mask"][:], writes=[d_mask])
        pmat = sb("pmat", [128, 128], BF16)
        d_pmat = Dep()
        S.dma("pool", pmat[:], dr["pmat"][:], writes=[d_pmat])
        lamv = sb("lamv", [128, 256], F32)
        d_lamv = Dep()
        S.dma("sp", lamv[:], dr["lamv"].partition_broadcast(128), writes=[d_lamv])
        subln = sb("subln", [128, 1], F32)
        d_subln = Dep()
        S.dma("sp", subln[:], dr["subln"][:], writes=[d_subln])

        lsm = sb("lam_small", [128, 8], F32)
        d_lsm = Dep()
        ltmp = sb("lam_tmp", [128, 128], F32)
        d_ltmp = Dep()
        lv4 = lamv[:].rearrange("p (a d) -> p a d", a=4)
        S.op("dve", lambda e: e.tensor_tensor(ltmp[:, 0:64], lv4[:, 0, :], lv4[:, 1, :], ALU.mult),
             reads=[d_lamv], writes=[d_ltmp])
        S.op("dve", lambda e: e.tensor_tensor(ltmp[:, 64:128], lv4[:, 2, :], lv4[:, 3, :], ALU.mult),
             reads=[d_lamv], writes=[d_ltmp])
        S.op("dve", lambda e: e.reduce_sum(lsm[:, 0:2], ltmp[:].rearrange("p (a d) -> p a d", a=2), AX.X),
             reads=[d_ltmp], writes=[d_lsm])
        S.op("act", lambda e: e.activation(lsm[:, 2:4], lsm[:, 0:2], AF.Exp), reads=[d_lsm], writes=[d_lsm])
        S.op("dve", lambda e: e.scalar_tensor_tensor(lsm[:, 4:5], lsm[:, 3:4], -LAM_INIT, lsm[:, 2:3],
                                                      ALU.add, ALU.subtract), reads=[d_lsm], writes=[d_lsm])
        lamvec = sb("lamvec", [128, 4, 2], F32)
        d_lamvec = Dep()
        S.op("dve", lambda e: e.memset(lamvec[:], 1.0), writes=[d_lamvec])
        S.op("dve", lambda e: e.tensor_copy(lamvec[:, :, 1:2], bcast_mid(lsm[:, 4:5], 4)),
             reads=[d_lsm], writes=[d_lamvec])

        wst = [sb("wst%d" % i, [128, 8, 768], BF16) for i in range(2)]
        d_wst = [Dep() for _ in range(2)]
        est = ExitStack()
        with est:
            sbt = lambda n, s_, d: self.sb(est, n, s_, d)
            self.junk = sbt("junk", [128, 1024], F32)
            self.d_junk = Dep()
            xb = [sbt("xb%d" % i, [128, D], F32) for i in range(4)]
            d_xb = [Dep() for _ in range(4)]
            hnb = [sbt("hnb%d" % i, [128, D], BF16) for i in range(4)]
            d_hnb = [Dep() for _ in range(4)]
            ssq0 = sbt("ssq0", [128, 16], F32)
            d_ssq0 = Dep()
            pre0 = self.pren[:, 0:8].rearrange("p (k o) -> p k o", o=1)
            for g0 in range(0, 34, 4):
                blks = list(range(g0, min(g0 + 4, 34)))
                for i, blk in enumerate(blks):
                    S.dma("sp", xb[i][:], dr["xin"][blk * 128:(blk + 1) * 128, :], writes=[d_xb[i]], key="x%d" % i)
                self.rms_group([xb[i][:] for i in range(len(blks))], d_xb, hnb, d_hnb,
                               [hT[:, :, blk * 128:(blk + 1) * 128] for blk in blks], [d_hT[blk] for blk in blks],
                               pre0, [4, 5, 6, 7], ssq0, d_ssq0)

            posi = sbt("posi", [128, 1024], I32)
            d_posi = Dep()
            rt = [sbt("rt%d" % i, [128, 1024], F32) for i in range(3)]
            d_rt = [Dep() for _ in range(3)]
            for c in range(4):
                cs = slice(c * 1024, (c + 1) * 1024)
                S.dma("sp", posi[:], dr["pos"][:, cs].partition_broadcast(128), writes=[d_posi], key="pos")
                S.op("dve", lambda e: e.tensor_copy(rt[0][:], posi[:]), reads=[d_posi], writes=[d_rt[0]])
                S.op("dve", lambda e: e.tensor_scalar(rt[1][:], rt[0][:], ropec[:, 0:1], float(np.float32(1.0 / (2 * np.pi))),
                                                      ALU.mult, ALU.mult), reads=[d_rt[0], d_ropec], writes=[d_rt[1]])
                S.op("dve", lambda e: e.tensor_scalar(rt[2][:], rt[1][:], MAGIC, MAGIC, ALU.add, ALU.subtract),
                     reads=[d_rt[1]], writes=[d_rt[2]])
                S.op("dve", lambda e: e.tensor_sub(rt[1][:], rt[1][:], rt[2][:]), reads=[d_rt[1], d_rt[2]], writes=[d_rt[1]])
                S.op("act", lambda e, cs=cs: e.activation(St[:, cs], rt[1][:], AF.Sin, scale=ropec[:, 1:2]),
                     reads=[d_rt[1], d_ropec], writes=[d_St[c]])
                S.op("dve", lambda e: e.tensor_scalar(rt[2][:], rt[1][:], -1.0, None, ALU.mult),
                     reads=[d_rt[1]], writes=[d_rt[2]])
                S.op("dve", lambda e: e.tensor_tensor(rt[2][:], rt[2][:], rt[1][:], ALU.max),
                     reads=[d_rt[1], d_rt[2]], writes=[d_rt[2]])
                S.op("dve", lambda e: e.tensor_scalar(rt[2][:], rt[2][:], float(-2 * np.pi), float(np.pi / 2), ALU.mult, ALU.add),
                     reads=[d_rt[2]], writes=[d_rt[2]])
                S.op("act", lambda e, cs=cs: e.activation(Ct[:, cs], rt[2][:], AF.Sin),
                     reads=[d_rt[2]], writes=[d_C[c]])

        S.barrier()
        self.dump("hT0", hT[:, :, 0:256], d_hT[0:2])
        self.dump("hTh", hT[:, :, 4096:4352], d_hT[32:34])
        self.dump("Ct", Ct[:], d_C)
        self.dump("St", St[:], d_St)
        self.dump("lsm", lsm[:], [d_lsm])
        for ct in range(8):
            s = ct % 2
            S.dma("pool", wst[s][:, :, 0:256], dr["wag"][ct], writes=[d_wst[s]], key="wh%d" % s)
            pi = 4 + s
            for kt in range(8):
                S.op("pe", lambda e, kt=kt, s=s, pi=pi: e.matmul(self.pb[pi][:, 0:256], lhsT=wst[s][:, kt, 0:128],
                                                                 rhs=hT[:, kt, 4096:4352], start=(kt == 0), stop=(kt == 7)),
                     reads=[d_wst[s], d_hT[32], d_hT[33]], writes=[self.dpb[pi]], inc=(kt == 7))
            S.op("act", lambda e, ct=ct, pi=pi: e.activation(self.aTh[:, ct, :], self.pb[pi][:, 0:256], AF.Copy),
                 reads=[self.dpb[pi]], writes=[self.d_aTh])

        KT = sb("KT", [128, 4096], BF16)
        d_KT = [Dep() for _ in range(8)]
        QTc = [sb("QT%d" % i, [128, TOK], BF16) for i in range(2)]
        d_QT = [Dep() for _ in range(4)]
        S.op("pool", lambda e: e.memset(QTc[0][64:128, :], 0.0), writes=d_QT)
        S.op("pool", lambda e: e.memset(QTc[1][0:64, :], 0.0), writes=d_QT)
        V = sb("V", [128, 32, 129], BF16)
        d_V = [Dep() for _ in range(8)]
        S.op("pool", lambda e: e.memset(V[:, :, 128:129], 1.0), writes=d_V)
        sgT = sb("sgT", [128, TOK], BF16)
        d_sgT = [Dep() for _ in range(4)]
        ET = [sb("ET%d" % i, [128, 512], BF16) for i in range(4)]
        d_ET = [Dep() for _ in range(4)]
        kb16 = [sb("kb16_%d" % i, [128, 512], BF16) for i in range(2)]
        d_kb16 = [Dep() for _ in range(2)]
        rtmp = [sb("rtmp%d" % i, [128, 512], F32) for i in range(2)]
        d_rtmp = [Dep() for _ in range(2)]
        stage = [sb("stage%d" % i, [128, 4, 2, 129], F32) for i in range(2)]
        d_stage = [Dep() for _ in range(2)]
        eo = [sb("eo%d" % i, [128, 4, 128], F32) for i in range(2)]
        d_eo = [Dep() for _ in range(2)]
        eon = sb("eon", [128, 4, 128], BF16)
        d_eon = Dep()
        esm = sb("esm", [128, 32], F32)
        d_esm = Dep()

        et_i = [0]
        st_i = [0]
        rt_i = [0]
        S.dma("pool", wst[0][:, :, 0:512], dr["whd"][0], writes=[d_wst[0]], key="wh0")
        for h in range(8):
            s = h % 2
            if h + 1 < 8:
                S.dma("pool", wst[1 - s][:, :, 0:512], dr["whd"][h + 1], writes=[d_wst[1 - s]], key="wh%d" % (1 - s))
            w = wst[s]
            dw = d_wst[s]

            def proj_fm(col0, tok0, pi, first_blk):
                for kt in range(8):
                    S.op("pe", lambda e, kt=kt: e.matmul(self.pb[pi][:, :], lhsT=w[:, kt, col0:col0 + 128],
                                                         rhs=hT[:, kt, tok0:tok0 + 512], start=(kt == 0), stop=(kt == 7)),
                         reads=[dw] + d_hT[first_blk:first_blk + 4], writes=[self.dpb[pi]], inc=(kt == 7))

            chunks = [("k", c) for c in range(8)] + [("q", c) for c in range(4)]

            def rope_proj(i):
                kind, c = chunks[i]
                col0 = 128 if kind == "k" else 0
                pi = i % 3
                proj_fm(col0, c * 512, pi, c * 4)
                S.op("act", lambda e: e.activation(kb16[i % 2][:], self.pb[pi][:, :], AF.Copy),
                     reads=[self.dpb[pi]], writes=[d_kb16[i % 2]])

            def rope_finish(i):
                kind, c = chunks[i]
                pi = i % 3
                pq = 3 + i % 2
                tok0 = c * 512
                cidx = c // 2
                S.op("pe", lambda e: e.matmul(self.pb[pq][:, :], lhsT=pmat[:, :], rhs=kb16[i % 2][:], start=True, stop=True),
                     reads=[d_pmat, d_kb16[i % 2]], writes=[self.dpb[pq]], inc=True)
                S.op("dve", lambda e: e.tensor_tensor(rtmp[0][:], self.pb[pi][:, :], Ct[:, tok0:tok0 + 512], ALU.mult),
                     reads=[self.dpb[pi], d_C[cidx], d_kb16[i % 2]], writes=[d_rtmp[0]])
                S.op("dve", lambda e: e.tensor_tensor(rtmp[1][:], self.pb[pq][:, :], St[:, tok0:tok0 + 512], ALU.mult),
                     reads=[self.dpb[pq], d_St[cidx]], writes=[d_rtmp[1]])
                cs_ = slice(tok0, tok0 + 512)
                if kind == "k":
                    S.op("pool", lambda e: e.tensor_tensor(KT[:, cs_], rtmp[0][:], rtmp[1][:], ALU.add),
                         reads=[d_rtmp[0], d_rtmp[1]], writes=[d_KT[c]])
                else:
                    S.op("pool", lambda e: e.tensor_tensor(QTc[0][0:64, cs_], rtmp[0][0:64, :], rtmp[1][0:64, :], ALU.add),
                         reads=[d_rtmp[0], d_rtmp[1]], writes=[d_QT[c]])
                    S.op("dve", lambda e: e.tensor_tensor(QTc[1][64:128, cs_], rtmp[0][64:128, :], rtmp[1][64:128, :], ALU.add),
                         reads=[d_rtmp[0], d_rtmp[1]], writes=[d_QT[c]])

            for i in range(len(chunks)):
                rope_proj(i)
                if i >= 1:
                    rope_finish(i - 1)
            rope_finish(len(chunks) - 1)
            for c in range(4):
                pi = 6 + c % 2
                proj_fm(384, c * 512, pi, c * 4)
                S.op("act", lambda e, c=c, pi=pi: e.activation(sgT[:, c * 512:(c + 1) * 512], self.pb[pi][:, :], AF.Silu),
                     reads=[self.dpb[pi]], writes=[d_sgT[c]])
            for g4 in range(8):
                pi = 6 + g4 % 2
                for i in range(4):
                    kb = g4 * 4 + i
                    for kt in range(8):
                        S.op("pe", lambda e, kt=kt, kb=kb, i=i: e.matmul(
                            self.pb[pi][:, i * 128:(i + 1) * 128], lhsT=hT[:, kt, kb * 128:(kb + 1) * 128],
                            rhs=w[:, kt, 256:384], start=(kt == 0), stop=(kt == 7)),
                             reads=[dw, d_hT[kb]], writes=[self.dpb[pi]], inc=(kt == 7 and i == 3))
                S.op("act", lambda e, g4=g4, pi=pi: e.activation(
                    V[:, g4 * 4:(g4 + 1) * 4, 0:128], self.pb[pi][:, :].rearrange("p (a d) -> p a d", a=4), AF.Copy),
                     reads=[self.dpb[pi]], writes=[d_V[g4]])

            if h == 0:
                self.dump("KT", KT[:], d_KT)
                self.dump("QT", QTc[0][:], d_QT)
                self.dump("V", V[:], d_V)
                self.dump("sgT", sgT[:], d_sgT)
            tiles = []
            for G in range(4):
                blocks = [(i, 0, None) for i in range(4 * G)] + [(16 + i, 0, None) for i in range(4 * G)]
                for a4 in range(4):
                    blocks.append((4 * G + a4, a4, 0))
                    blocks.append((16 + 4 * G + a4, a4, 1))
                for c in range(2):
                    for bi, (kb, a4, m) in enumerate(blocks):
                        tiles.append(dict(G=G, c=c, kb=kb, a=a4, m=m, first=(bi == 0), endgrp=(bi == len(blocks) - 1 and c == 1)))

            def emit_qk(n):
                t = tiles[n]
                G, c, kb, a4, m = t["G"], t["c"], t["kb"], t["a"], t["m"]
                ps = slice(c * 64, (c + 1) * 64)
                sbk = (0, 1, 6)[n % 3]
                q0 = (4 * G + a4) * 128
                q1 = (4 * G + 4) * 128
                rd = [d_KT[kb // 4], d_QT[G]]
                if m is None:
                    S.op("pe", lambda e: e.matmul(self.pb[sbk][:, :], lhsT=KT[:, kb * 128:(kb + 1) * 128],
                                                  rhs=QTc[c][:, q0:q1], start=True, stop=True),
                         reads=rd, writes=[self.dpb[sbk]], inc=True)
                else:
                    c0 = a4 * 128
                    S.op("pe", lambda e: e.matmul(self.pb[sbk][:, c0:c0 + 128], lhsT=KT[:, kb * 128:(kb + 1) * 128],
                                                  rhs=QTc[c][:, q0:q0 + 128], start=True, stop=False),
                         reads=rd, writes=[self.dpb[sbk]], inc=False)
                    S.op("pe", lambda e: e.matmul(self.pb[sbk][:, c0:c0 + 128], lhsT=self.ident[:, :],
                                                  rhs=maskb[:, m * 128:(m + 1) * 128], start=False, stop=True),
                         reads=[self.d_ident, d_mask], writes=[self.dpb[sbk]], inc=(a4 == 3))
                    if a4 < 3:
                        S.op("pe", lambda e: e.matmul(self.pb[sbk][:, c0 + 128:512], lhsT=KT[:, kb * 128:(kb + 1) * 128],
                                                      rhs=QTc[c][:, q0 + 128:q1], start=True, stop=True),
                             reads=rd, writes=[self.dpb[sbk]], inc=True)

            def emit_exp_pv(n):
                t = tiles[n]
                G, c, kb, a4, m = t["G"], t["c"], t["kb"], t["a"], t["m"]
                sbk = (0, 1, 6)[n % 3]
                ei = n % 4
                c0 = a4 * 128
                S.op("act", lambda e: e.activation(ET[ei][:, c0:512], self.pb[sbk][:, c0:512], AF.Exp, scale=0.125),
                     reads=[self.dpb[sbk]], writes=[d_ET[ei]])
                for sl in range(a4, 4):
                    ob = 2 + sl
                    O = self.pb[ob][:, 0:258].rearrange("p (c d) -> p c d", c=2)
                    last = (m == 1 and a4 == sl)
                    S.op("pe", lambda e, sl=sl, O=O, last=last: e.matmul(
                        O[:, c, :], lhsT=ET[ei][:, sl * 128:(sl + 1) * 128], rhs=V[:, kb, :],
                        start=t["first"], stop=last),
                         reads=[d_ET[ei], d_V[kb // 4]], writes=[self.dpb[ob]], inc=(last and c == 1))

            pending = None
            emit_qk(0)
            emit_qk(1)
            for n in range(len(tiles)):
                if n + 2 < len(tiles):
                    emit_qk(n + 2)
                emit_exp_pv(n)
                t = tiles[n]
                if not t["endgrp"]:
                    continue
                q4 = t["G"]
                stg = stage[q4 % 2]
                for sl in range(4):
                    O = self.pb[2 + sl][:, 0:258].rearrange("p (c d) -> p c d", c=2)
                    S.op("dve", lambda e, O=O, sl=sl: e.tensor_copy(stg[:, sl, :, :], O),
                         reads=[self.dpb[2 + sl]], writes=[d_stage[q4 % 2]])
                sl = 3
                if pending is not None:
                    self.attn_epilogue_b(*pending)
                    pending = None
                if sl == 3 and h == 0 and q4 == 0:
                    self.dump("stage", stg[:], [d_stage[0]])
                if sl == 3:
                    ds = d_stage[q4 % 2]
                    rz0 = esm[:, 0:8].rearrange("p (a c) -> p a c", c=2)
                    rz = esm[:, 8:16].rearrange("p (a c) -> p a c", c=2)
                    S.op("dve", lambda e, stg=stg: e.reciprocal(rz0, stg[:, :, :, 128]), reads=[ds], writes=[d_esm])
                    S.op("dve", lambda e: e.tensor_tensor(rz, rz0, lamvec[:], ALU.mult),
                         reads=[d_esm, d_lamvec], writes=[d_esm])
                    S.op("dve", lambda e, stg=stg: e.tensor_tensor(eo[0][:], stg[:, :, 0, 0:128], bcast_last(rz[:, :, 0:1], 128), ALU.mult),
                         reads=[ds, d_esm], writes=[d_eo[0]])
                    S.op("dve", lambda e, stg=stg: e.tensor_tensor(eo[1][:], stg[:, :, 1, 0:128], bcast_last(rz[:, :, 1:2], 128), ALU.mult),
                         reads=[ds, d_esm], writes=[d_eo[1]])
                    S.op("pool", lambda e: e.tensor_tensor(eo[0][:], eo[0][:], eo[1][:], ALU.add),
                         reads=[d_eo[0], d_eo[1]], writes=[d_eo[0]])
                    S.op("pool", lambda e: e.tensor_tensor(eo[1][:], eo[0][:], eo[0][:], ALU.mult),
                         reads=[d_eo[0]], writes=[d_eo[1]])
                    S.op("dve", lambda e: e.reduce_sum(esm[:, 16:20], eo[1][:], AX.X), reads=[d_eo[1]], writes=[d_esm])
                    S.op("dve", lambda e: e.tensor_scalar(esm[:, 20:24], esm[:, 16:20], 1.0 / 128, 1e-5, ALU.mult, ALU.add),
                         reads=[d_esm], writes=[d_esm])
                    S.op("act", lambda e: e.activation(esm[:, 24:28], esm[:, 20:24], AF.Ln), reads=[d_esm], writes=[d_esm])
                    S.op("act", lambda e: e.activation(esm[:, 28:32], esm[:, 24:28], AF.Exp, scale=-0.5), reads=[d_esm], writes=[d_esm])
                    S.op("dve", lambda e: e.scalar_tensor_tensor(eon[:], eo[0][:], 1.0 - LAM_INIT,
                                                                  bcast_last(esm[:, 28:32].rearrange("p (a o) -> p a o", o=1), 128),
                                                                  ALU.mult, ALU.mult),
                         reads=[d_eo[0], d_esm], writes=[d_eon])
                    pending = (h, q4, eon, d_eon, subln, d_subln, sgT, d_sgT)
                    if h == 0 and q4 == 0:
                        self.dump("eon", eon[:], [d_eon])
                        self.dump("esm", esm[:], [d_esm])
            if pending is not None:
                self.attn_epilogue_b(*pending)
                pending = None

    def attn_epilogue_b(self, h, q4, eon, d_eon, subln, d_subln, sgT, d_sgT):
        S = self.S
        ptr = self.pb[7][:].bitcast(BF16)
        for i in range(4):
            S.op("pe", lambda e, i=i: e.transpose(ptr[:, i * 128:(i + 1) * 128], eon[:, i, :], self.ident[:]),
                 reads=[d_eon, self.d_ident], writes=[self.dpb[7]], inc=(i == 3))
        S.op("dve", lambda e: e.scalar_tensor_tensor(self.ygA[:, h, q4 * 512:(q4 + 1) * 512], ptr[:, 0:512], subln[:, 0:1],
                                                      sgT[:, q4 * 512:(q4 + 1) * 512], ALU.mult, ALU.mult),
             reads=[self.dpb[7], d_subln, d_sgT[q4]], writes=[self.d_ygA[h][q4]])

    def phase_final(self, es):
        S, nc, dr, mode = self.S, self.nc, self.dr, self.mode
        sb = lambda n, s, d: self.sb(es, n, s, d)
        L0 = mode in ("fused", "L0")
        L1 = mode in ("fused", "L1")
        self.junk = sb("junkf", [128, 1024], F32)
        self.d_junk = Dep()
        postn = sb("postn", [128, 2, 1024], F32)
        d_postn = Dep()
        S.dma("sp", postn[:].rearrange("p a d -> p (a d)"), dr["postn"].partition_broadcast(128), writes=[d_postn])
        xs = sb("xs", [128, 4, D], F32)
        d_xs = [Dep() for _ in range(4)]
        hn = [sb("hnf%d" % i, [128, D], BF16) for i in range(4)]
        d_hn = [Dep() for _ in range(4)]
        ss4 = [sb("ssf%d" % i, [128, 8], F32) for i in range(4)]
        d_ss4 = [Dep() for _ in range(4)]
        hTs = sb("hTs", [128, 8, 512], BF16)
        d_hTs = [Dep() for _ in range(4)]
        ysb = sb("ysb", [128, 16, 512], BF16)
        d_ysb = [Dep() for _ in range(16)]
        tmpf = [sb("tmpf%d" % i, [128, D], F32) for i in range(2)]
        d_tmpf = [Dep() for _ in range(2)]
        wr = [sb("wr%d" % i, [128, 8, 256], BF16) for i in range(6)]
        d_wr = [Dep() for _ in range(6)]
        wo = [sb("wor%d" % i, [128, 1024], BF16) for i in range(3)]
        d_wo = [Dep() for _ in range(3)]
        if L0:
            invc = sb("invc", [128, 4, 128], F32)
            d_invc = Dep()
            S.dma("sp", invc[:].rearrange("p a d -> p (a d)"), dr["invc"].partition_broadcast(128), writes=[d_invc])
            poolw = sb("poolw", [128, 4, 2, 256], BF16)
            d_poolw = Dep()
            S.dma("pool", poolw[:], dr["poolw"][:], writes=[d_poolw])
            pscale = sb("pscale", [128, 8], F32)
            d_pscale = Dep()
            S.dma("sp", pscale[:], dr["pscale"][:], writes=[d_pscale])
            Abuf = [sb("Abuf%d" % i, [128, 4, 144], F32) for i in range(2)]
            d_A = [Dep() for _ in range(2)]
            Sb = [sb("Sbuf%d" % i, [128, 4, 144], F32) for i in range(2)]
            d_Sb = [Dep() for _ in range(2)]
            pooled = [sb("pooled%d" % i, [128, 2, 512], BF16) for i in range(2)]
            d_pooled = [[Dep() for _ in range(2)] for _ in range(2)]
            ptmp = sb("ptmp", [128, 128], F32)
            d_ptmp = Dep()
        if L1:
            lng = sb("lng", [128, 2048], F32)
            lnb = sb("lnb", [128, 2048], F32)
            d_ln = Dep()
            S.dma("sp", lng[:], dr["lng"].partition_broadcast(128), writes=[d_ln], key="ln")
            S.dma("sp", lnb[:], dr["lnb"].partition_broadcast(128), writes=[d_ln], key="ln")
            wsf = sb("wsf", [128, 8, 128], F32)
            trl = sb("trl", [128, 128], F32)
            d_wsf = Dep()
            S.dma("sp", wsf[:], dr["wsT"][:], writes=[d_wsf], key="wsf")
            S.dma("sp", trl[:], dr["tril"][:], writes=[d_wsf], key="wsf")
            wsT = sb("wsTb", [128, 8, 128], BF16)
            d_wsT = Dep()
            S.op("dve", lambda e: e.tensor_tensor(wsT[:], wsf[:], bcast_mid(trl[:], 8), ALU.mult), reads=[d_wsf], writes=[d_wsT])
            bsb = sb("bsb", [128, 8, 128], F32)
            d_bsb = Dep()
            S.dma("sp", bsb[:].rearrange("p a d -> p (a d)"), dr["sgub"].partition_broadcast(128), writes=[d_bsb])
            vb = sb("vb", [128, 4, 2048], BF16)
            d_vb = [Dep() for _ in range(4)]
            utmp = [sb("utmp%d" % i, [128, 512], BF16) for i in range(2)]
            d_utmp = [Dep() for _ in range(2)]
            lsm4 = [sb("lnsm%d" % i, [128, 12], F32) for i in range(1)]
            d_lsm4 = [Dep() for _ in range(1)]
            lsq = sb("lsq", [128, 40], F32)
            d_lsq = Dep()
            mtmp = [sb("mtmp%d" % i, [128, 512], F32) for i in range(2)]
            d_mtmp = [Dep() for _ in range(2)]

        per_sb = []
        if L0:
            for ct in range(8):
                per_sb.append(("wr", dr["wag"][ct]))
            for kt in range(16):
                per_sb.append(("wo", dr["wo0"][kt]))
        if L1:
            for cc in range(8):
                per_sb.append(("wr", dr["w1v"][cc]))
            for i in range(8):
                per_sb.append(("wr", dr["w1g"][i]))
            for i in range(8):
                per_sb.append(("wr", dr["w1u"][i]))
            for kt in range(16):
                per_sb.append(("wo", dr["wo1"][kt]))
        nper = len(per_sb)
        n_wr = sum(1 for k, _ in per_sb if k == "wr")
        n_wo = nper - n_wr
        scr_wr = nc.dram_tensor("scr_wr", [n_wr, 128, 8, 256], BF16).ap()
        scr_wo = nc.dram_tensor("scr_wo", [n_wo, 128, 1024], BF16).ap()
        scr_of = []
        c_wr = c_wo = 0
        for k, _ in per_sb:
            if k == "wr":
                scr_of.append(scr_wr[c_wr]); c_wr += 1
            else:
                scr_of.append(scr_wo[c_wo]); c_wo += 1
        d_scr = [Dep() for _ in range(nper)]
        items = per_sb * 4
        ring = {"wr": (wr, d_wr), "wo": (wo, d_wo)}
        cnt = {"wr": 0, "wo": 0}
        slot_of = []
        for kind, _ in items:
            slot_of.append(cnt[kind] % len(ring[kind][0]))
            cnt[kind] += 1
        issued = [0]
        inflight = {"wr": 0, "wo": 0}
        consumed = [0]

        def pump():
            while issued[0] < len(items):
                i = issued[0]
                kind, src = items[i]
                bufs, deps = ring[kind]
                if inflight[kind] >= len(bufs):
                    break
                sl = slot_of[i]
                j = i % nper
                if i < nper:
                    S.dma("pool", bufs[sl][:], src, writes=[deps[sl]], key="%s%d" % (kind, sl))
                    S.dma("sp", scr_of[j], bufs[sl][:], reads=[deps[sl]], writes=[d_scr[j]], key="sw%d" % (j % 32))
                else:
                    S.dma("sp", bufs[sl][:], scr_of[j], reads=[d_scr[j]], writes=[deps[sl]], key="%s%d" % (kind, sl))
                inflight[kind] += 1
                issued[0] += 1

        def take(kind):
            i = consumed[0]
            assert items[i][0] == kind, (items[i][0], kind)
            assert i < issued[0]
            sl = slot_of[i]
            bufs, deps = ring[kind]
            return bufs[sl], deps[sl]

        def release(kind):
            consumed[0] += 1
            inflight[kind] -= 1
            pump()

        pump()
        pre0 = self.pren[:, 0:8].rearrange("p (k o) -> p k o", o=1)
        pre1 = self.pren[:, 8:16].rearrange("p (k o) -> p k o", o=1)

        def post_norm_all(layer):
            for tb in range(4):
                for hf in range(2):
                    S.op("act", lambda e, hf=hf, tb=tb: e.activation(self.junk[:, hf * 512:(hf + 1) * 512], self.pb[2 * tb + hf][:, :], AF.Square,
                                                                     accum_out=ssq[:, 2 * tb + hf:2 * tb + hf + 1]),
                         reads=[self.dpb[2 * tb + hf]], writes=[self.d_junk, d_ssq])
            S.op("dve", lambda e: e.reduce_sum(ssq[:, 8:12], ssq[:, 0:8].rearrange("p (a c) -> p a c", c=2), AX.X), reads=[d_ssq], writes=[d_ssq])
            S.op("dve", lambda e: e.tensor_scalar(ssq[:, 8:12], ssq[:, 8:12], 1.0 / D, EPS, ALU.mult, ALU.add), reads=[d_ssq], writes=[d_ssq])
            S.op("act", lambda e: e.activation(ssq[:, 12:16], ssq[:, 8:12], AF.Sqrt), reads=[d_ssq], writes=[d_ssq])
            S.op("dve", lambda e: e.reciprocal(ssq[:, 8:12], ssq[:, 12:16]), reads=[d_ssq], writes=[d_ssq])
            for tb in range(4):
                tf, d_tf = tmpf[tb % 2], d_tmpf[tb % 2]
                for hf in range(2):
                    S.op("dve", lambda e, hf=hf, tb=tb, tf=tf: e.scalar_tensor_tensor(
                        tf[:, hf * 512:(hf + 1) * 512], self.pb[2 * tb + hf][:, :], ssq[:, 8 + tb:9 + tb],
                        postn[:, layer, hf * 512:(hf + 1) * 512], ALU.mult, ALU.mult),
                         reads=[self.dpb[2 * tb + hf], d_ssq, d_postn], writes=[d_tf])
                S.op("pool", lambda e, tb=tb, tf=tf: e.tensor_tensor(xs[:, tb, :], xs[:, tb, :], tf[:], ALU.add),
                     reads=[d_tf, d_xs[tb]], writes=[d_xs[tb]])

        def out_proj(layer, lhs_of):
            for kt in range(16):
                wbuf, dwb = take("wo")
                for tb in range(4):
                    lhsT, dl = lhs_of(kt, tb)
                    for hf in range(2):
                        S.op("pe", lambda e, tb=tb, hf=hf, lhsT=lhsT: e.matmul(self.pb[2 * tb + hf][:, :], lhsT=lhsT,
                                                                                rhs=wbuf[:, hf * 512:(hf + 1) * 512],
                                                                                start=(kt == 0), stop=(kt == 15)),
                             reads=[dwb, dl], writes=[self.dpb[2 * tb + hf]], inc=(kt == 15 or (tb == 3 and hf == 1)))
                release("wo")
            post_norm_all(layer)

        ssq = sb("ssq", [128, 16], F32)
        d_ssq = Dep()

        def rms4(gain):
            self.rms_group([xs[:, tb, :] for tb in range(4)], d_xs, hn, d_hn,
                           [hTs[:, :, tb * 128:(tb + 1) * 128] for tb in range(4)], d_hTs, gain, [4, 5, 6, 7], ssq, d_ssq)

        for sbi in range(4):
            tok0 = sbi * 512
            if L0:
                for tb in range(4):
                    S.dma("sp", xs[:, tb, :], dr["xin"][tok0 + tb * 128: tok0 + (tb + 1) * 128, :], writes=[d_xs[tb]], key="xs%d" % tb)
                rms4(pre0)

                def pool_mm(g):
                    pg = g % 2
                    pbank = 2 + 3 * pg
                    for dt in range(2):
                        ct = 2 * g + dt
                        for ci in range(2):
                            S.op("pe", lambda e, ci=ci, dt=dt: e.matmul(self.pb[pbank + dt][:, :],
                                                                         lhsT=poolw[:, g, ci, dt * 128:(dt + 1) * 128],
                                                                         rhs=pooled[pg][:, ci, :], start=(ci == 0), stop=(ci == 1)),
                                 reads=[d_poolw, d_pooled[pg][ci]], writes=[self.dpb[pbank + dt]], inc=(ci == 1))
                        S.op("dve", lambda e, ct=ct, dt=dt: e.scalar_tensor_tensor(ysb[:, ct, :], self.pb[pbank + dt][:, :], pscale[:, ct:ct + 1],
                                                                                   ysb[:, 8 + ct, :], ALU.mult, ALU.mult),
                             reads=[self.dpb[pbank + dt], d_pscale, d_ysb[8 + ct]], writes=[d_ysb[ct]])

                for g in range(4):
                    pg = g % 2
                    for ci in range(2):
                        ct = 2 * g + ci
                        ab, d_ab = Abuf[ci], d_A[ci]
                        wbuf, dwb = take("wr")
                        for kt in range(8):
                            S.op("pe", lambda e, kt=kt: e.matmul(self.pb[ci][:, :], lhsT=wbuf[:, kt, 0:128], rhs=hTs[:, kt, :],
                                                                 start=(kt == 0), stop=(kt == 7)),
                                 reads=[dwb] + d_hTs, writes=[self.dpb[ci]], inc=(kt == 7))
                        for kt in range(8):
                            S.op("pe", lambda e, kt=kt: e.matmul(self.pb[[4, 7][ci]][:, :],
                                                                 lhsT=wbuf[:, kt, 128:256], rhs=hTs[:, kt, :],
                                                                 start=(kt == 0), stop=(kt == 7)),
                                 reads=[dwb] + d_hTs, writes=[self.dpb[[4, 7][ci]]], inc=(kt == 7))
                        release("wr")
                        S.op("act", lambda e: e.activation(ab[:, :, 16:144], self.pb[ci][:, :].rearrange("p (a t) -> p a t", a=4), AF.Copy),
                             reads=[self.dpb[ci]], writes=[d_ab])
                        S.op("pool", lambda e, ct=ct: e.tensor_copy(
                            ab[:, :, 0:16], self.aTh[:, ct, sbi * 64:(sbi + 1) * 64].rearrange("p (a t) -> p a t", a=4)),
                             reads=[self.d_aTh], writes=[d_ab])
                        S.op("act", lambda e, ct=ct: e.activation(ysb[:, 8 + ct, :], self.pb[[4, 7][ci]][:, :], AF.Silu),
                             reads=[self.dpb[[4, 7][ci]]], writes=[d_ysb[8 + ct]])
                        src, dsrc = ab, d_ab
                        for step in range(g + 1):
                            sh = 1 << step
                            dst, ddst = Sb[step % 2], d_Sb[step % 2]
                            S.op("dve", lambda e, src=src, dst=dst, sh=sh: e.tensor_tensor(
                                dst[:, :, sh:144], src[:, :, sh:144], src[:, :, 0:144 - sh], ALU.add),
                                 reads=[dsrc], writes=[ddst])
                            src, dsrc = dst, ddst
                        w = 2 << g
                        pv = pooled[pg][:, ci, :].rearrange("p (a t) -> p a t", a=4)
                        dpl = d_pooled[pg][ci]
                        if sbi == 0:
                            S.op("dve", lambda e, src=src, g=g: e.tensor_tensor(ptmp[:], src[:, 0, 16:144], invc[:, g, :], ALU.mult),
                                 reads=[dsrc, d_invc], writes=[d_ptmp])
                            S.op("dve", lambda e, pv=pv: e.tensor_tensor(pv[:, 0, :], ptmp[:], ab[:, 0, 16:144], ALU.subtract),
                                 reads=[d_ptmp, d_ab], writes=[dpl])
                            S.op("dve", lambda e, src=src, w=w, pv=pv: e.scalar_tensor_tensor(
                                pv[:, 1:4, :], src[:, 1:4, 16:144], 1.0 / w, ab[:, 1:4, 16:144], ALU.mult, ALU.subtract),
                                 reads=[dsrc, d_ab], writes=[dpl])
                        else:
                            S.op("dve", lambda e, src=src, w=w, pv=pv: e.scalar_tensor_tensor(
                                pv, src[:, :, 16:144], 1.0 / w, ab[:, :, 16:144], ALU.mult, ALU.subtract),
                                 reads=[dsrc, d_ab], writes=[dpl])
                    if g >= 1:
                        pool_mm(g - 1)
                pool_mm(3)

                def lhs0(kt, tb):
                    if kt < 8:
                        return ysb[:, kt, tb * 128:(tb + 1) * 128], d_ysb[kt]
                    return self.ygA[:, kt - 8, tok0 + tb * 128: tok0 + (tb + 1) * 128], self.d_ygA[kt - 8][sbi]
                out_proj(0, lhs0)
            else:
                for tb in range(4):
                    S.dma("sp", xs[:, tb, :], dr["x1in"][tok0 + tb * 128: tok0 + (tb + 1) * 128, :], writes=[d_xs[tb]], key="xs%d" % tb)

            if L1:
                rms4(pre1)
                for cc in range(8):
                    wbuf, dwb = take("wr")
                    for half in range(2):
                        pi = 2 + half
                        for t2 in range(2):
                            tb = half * 2 + t2
                            for kt in range(8):
                                S.op("pe", lambda e, kt=kt, tb=tb, t2=t2, pi=pi: e.matmul(
                                    self.pb[pi][:, t2 * 256:(t2 + 1) * 256], lhsT=hTs[:, kt, tb * 128:(tb + 1) * 128],
                                    rhs=wbuf[:, kt, :], start=(kt == 0), stop=(kt == 7)),
                                     reads=[dwb, d_hTs[tb]], writes=[self.dpb[pi]], inc=(kt == 7 and t2 == 1))
                        S.op("act", lambda e, half=half, cc=cc, pi=pi: e.activation(
                            vb[:, half * 2:half * 2 + 2, cc * 256:(cc + 1) * 256],
                            self.pb[pi][:, :].rearrange("p (a d) -> p a d", a=2), AF.Gelu_apprx_tanh),
                             reads=[self.dpb[pi]], writes=[d_vb[half * 2], d_vb[half * 2 + 1]])
                    release("wr")
                lsm, d_lsmf = lsm4[0], d_lsm4[0]

                def ln_stage_a():
                    for tb in range(4):
                        S.op("dve", lambda e, tb=tb: e.reduce_sum(lsq[:, tb:tb + 1], vb[:, tb, :], AX.X), reads=[d_vb[tb]], writes=[d_lsq])
                        S.op("act", lambda e, tb=tb: e.activation(self.junk[:, :], vb[:, tb, 0:1024], AF.Square, accum_out=lsq[:, 4 + tb:5 + tb]),
                             reads=[d_vb[tb]], writes=[self.d_junk, d_lsq])
                        S.op("act", lambda e, tb=tb: e.activation(self.junk[:, :], vb[:, tb, 1024:2048], AF.Square, accum_out=lsq[:, 8 + tb:9 + tb]),
                             reads=[d_vb[tb]], writes=[self.d_junk, d_lsq])

                def ln_stage_b():
                    S.op("dve", lambda e: e.tensor_scalar(lsq[:, 12:16], lsq[:, 0:4], 1.0 / 2048, None, ALU.mult), reads=[d_lsq], writes=[d_lsq])
                    S.op("dve", lambda e: e.tensor_tensor(lsq[:, 16:20], lsq[:, 4:8], lsq[:, 8:12], ALU.add), reads=[d_lsq], writes=[d_lsq])
                    S.op("dve", lambda e: e.tensor_tensor(lsq[:, 20:24], lsq[:, 12:16], lsq[:, 12:16], ALU.mult), reads=[d_lsq], writes=[d_lsq])
                    S.op("dve", lambda e: e.scalar_tensor_tensor(lsq[:, 24:28], lsq[:, 16:20], 1.0 / 2048, lsq[:, 20:24], ALU.mult, ALU.subtract),
                         reads=[d_lsq], writes=[d_lsq])
                    S.op("dve", lambda e: e.tensor_scalar(lsq[:, 24:28], lsq[:, 24:28], EPS, None, ALU.add), reads=[d_lsq], writes=[d_lsq])
                    S.op("act", lambda e: e.activation(lsq[:, 28:32], lsq[:, 24:28], AF.Sqrt), reads=[d_lsq], writes=[d_lsq])
                    S.op("dve", lambda e: e.reciprocal(lsq[:, 32:36], lsq[:, 28:32]), reads=[d_lsq], writes=[d_lsq])
                    S.op("dve", lambda e: e.scalar_tensor_tensor(lsq[:, 36:40], lsq[:, 12:16], -1.0, lsq[:, 32:36], ALU.mult, ALU.mult),
                         reads=[d_lsq], writes=[d_lsq])

                def ln_stage_c(tb):
                    for hf in range(2):
                        cs = slice(hf * 1024, (hf + 1) * 1024)
                        tf, d_tf = tmpf[hf], d_tmpf[hf]
                        S.op("act", lambda e, cs=cs, tf=tf: e.activation(tf[:], vb[:, tb, cs], AF.Identity, bias=lsq[:, 36 + tb:37 + tb], scale=lsq[:, 32 + tb:33 + tb]),
                             reads=[d_vb[tb], d_lsq], writes=[d_tf])
                        S.op("dve", lambda e, cs=cs, tf=tf: e.tensor_tensor(tf[:], tf[:], lng[:, cs], ALU.mult), reads=[d_tf, d_ln], writes=[d_tf])
                        S.op("pool", lambda e, cs=cs, tf=tf: e.tensor_tensor(vb[:, tb, cs], tf[:], lnb[:, cs], ALU.add),
                             reads=[d_tf, d_ln], writes=[d_vb[tb]])

                def sgu_ct(ct):
                    g = ct // 2
                    pi = 4 + ct % 2
                    for tb in range(4):
                        S.op("pe", lambda e, tb=tb, pi=pi: e.matmul(
                            self.pb[pi][:, tb * 128:(tb + 1) * 128], lhsT=vb[:, tb, ct * 128:(ct + 1) * 128], rhs=wsT[:, g, :],
                            start=True, stop=True),
                             reads=[d_vb[tb], d_wsT], writes=[self.dpb[pi]], inc=(tb == 3))
                    mi = ct % 2
                    S.op("dve", lambda e, pi=pi, mi=mi: e.tensor_tensor(
                        mtmp[mi][:].rearrange("p (a t) -> p a t", a=4), self.pb[pi][:, :].rearrange("p (a t) -> p a t", a=4),
                        bcast_mid(bsb[:, g, :], 4), ALU.add),
                         reads=[self.dpb[pi], d_bsb], writes=[d_mtmp[mi]])
                    S.op("pool", lambda e, mi=mi: e.tensor_tensor(ysb[:, ct, :], ysb[:, ct, :], mtmp[mi][:], ALU.mult),
                         reads=[d_mtmp[mi], d_ysb[ct]], writes=[d_ysb[ct]])

                ln_stage_a()
                for which in range(2):
                    for i in range(8):
                        wbuf, dwb = take("wr")
                        for c2 in range(2):
                            ct = 2 * i + c2
                            pi = c2
                            for kt in range(8):
                                S.op("pe", lambda e, kt=kt, c2=c2, pi=pi: e.matmul(
                                    self.pb[pi][:, :], lhsT=wbuf[:, kt, c2 * 128:(c2 + 1) * 128], rhs=hTs[:, kt, :],
                                    start=(kt == 0), stop=(kt == 7)),
                                     reads=[dwb] + d_hTs, writes=[self.dpb[pi]], inc=(kt == 7))
                            if which == 0:
                                S.op("act", lambda e, ct=ct, pi=pi: e.activation(ysb[:, ct, :], self.pb[pi][:, :], AF.Silu),
                                     reads=[self.dpb[pi]], writes=[d_ysb[ct]])
                            else:
                                ui = ct % 2
                                S.op("act", lambda e, ui=ui, pi=pi: e.activation(utmp[ui][:], self.pb[pi][:, :], AF.Gelu_apprx_tanh),
                                     reads=[self.dpb[pi]], writes=[d_utmp[ui]])
                                S.op("pool", lambda e, ct=ct, ui=ui: e.tensor_tensor(ysb[:, ct, :], ysb[:, ct, :], utmp[ui][:], ALU.mult),
                                     reads=[d_utmp[ui], d_ysb[ct]], writes=[d_ysb[ct]])
                        release("wr")
                        if which == 0:
                            if i == 1:
                                ln_stage_b()
                            if 2 <= i <= 5:
                                ln_stage_c(i - 2)
                        else:
                            sgu_ct(2 * i)
                            sgu_ct(2 * i + 1)

                def lhs1(kt, tb):
                    return ysb[:, kt, tb * 128:(tb + 1) * 128], d_ysb[kt]
                out_proj(1, lhs1)

            for tb in range(4):
                S.dma("sp", dr["out"][tok0 + tb * 128: tok0 + (tb + 1) * 128, :], xs[:, tb, :], reads=[d_xs[tb]], key="o%d" % tb)
            self.ostore = d_xs


def _tile_cols(w):
    return np.ascontiguousarray(w.reshape(8, 128, -1).transpose(1, 0, 2))


def _partner_perm():
    perm = np.arange(128)
    for p in range(128):
        d = p % 64
        if d < 8:
            perm[p] = p + 8
        elif d < 16:
            perm[p] = p - 8
    return perm


_NC_CACHE = {}


def _get_nc(mode):
    if mode not in _NC_CACHE:
        _NC_CACHE[mode] = Builder(mode).build()
    return _NC_CACHE[mode]


def _shared_inputs(inp):
    f = np.float32
    w0 = np.asarray(inp["w_in"][0], f)
    w1 = np.asarray(inp["w_in"][1], f)
    perm = _partner_perm()
    sh = {}
    whd = np.empty((8, 128, 8, 512), f)
    for h in range(8):
        q = w0[:, 1024 + 128 * h: 1024 + 128 * (h + 1)]
        k = w0[:, 2048 + 128 * h: 2048 + 128 * (h + 1)]
        v = w0[:, 3072 + 128 * h: 3072 + 128 * (h + 1)]
        g = w0[:, 4096 + 1024 + 128 * h: 4096 + 1024 + 128 * (h + 1)]
        whd[h] = _tile_cols(np.concatenate([q, k, v, g], axis=1))
    sh["whd"] = whd
    pm = np.zeros((128, 128), f)
    pm[perm, np.arange(128)] = 1.0
    sh["pmat"] = pm
    wag = np.empty((8, 128, 8, 256), f)
    for ct in range(8):
        a = w0[:, 128 * ct:128 * (ct + 1)]
        g = w0[:, 4096 + 128 * ct: 4096 + 128 * (ct + 1)]
        wag[ct] = _tile_cols(np.concatenate([a, g], axis=1))
    sh["wag"] = wag
    sh["wo0"] = np.ascontiguousarray(np.asarray(inp["w_out"][0], f).reshape(16, 128, 1024))
    sh["wo1"] = np.ascontiguousarray(np.asarray(inp["w_out"][1], f).reshape(16, 128, 1024))
    w1g = np.empty((8, 128, 8, 256), f)
    w1u = np.empty((8, 128, 8, 256), f)
    for i in range(8):
        w1u[i] = _tile_cols(w1[:, 256 * i:256 * (i + 1)])
        w1g[i] = _tile_cols(w1[:, 4096 + 256 * i: 4096 + 256 * (i + 1)])
    sh["w1g"] = w1g
    sh["w1u"] = w1u
    w1v = np.empty((8, 128, 8, 256), f)
    for cc in range(8):
        w1v[cc] = _tile_cols(w1[:, 2048 + 256 * cc: 2048 + 256 * (cc + 1)])
    sh["w1v"] = w1v
    pw = np.asarray(inp["pool_w"][0], f)
    sh["poolw"] = np.ascontiguousarray(pw.reshape(4, 2, 128, 256).transpose(2, 0, 1, 3))
    sh["pscale"] = np.ascontiguousarray(np.asarray(inp["pool_scale"][0], f).reshape(8, 128).T)
    sh["lamv"] = np.concatenate([np.asarray(inp[k][0], f) for k in ("lam_q1", "lam_k1", "lam_q2", "lam_k2")])[None, :]
    sh["subln"] = np.ascontiguousarray(np.asarray(inp["diff_subln"][0], f).reshape(128, 1))
    sh["lng"] = np.asarray(inp["sgu_ln_g"], f).reshape(1, 2048)
    sh["lnb"] = np.asarray(inp["sgu_ln_b"], f).reshape(1, 2048)
    sh["wsT"] = np.ascontiguousarray(np.asarray(inp["sgu_w"][0], f).transpose(2, 0, 1))
    sh["tril"] = np.ascontiguousarray(np.tril(np.ones((128, 128), f)).T)
    sh["sgub"] = np.asarray(inp["sgu_b"][0], f).reshape(1, 1024)
    pre = np.asarray(inp["pre_norm"], f)
    sh["pren"] = np.ascontiguousarray(pre.reshape(2, 8, 128).transpose(2, 0, 1).reshape(128, 16))
    sh["postn"] = np.asarray(inp["post_norm"], f).reshape(1, 2048)
    inv_freq = np.power(np.float32(500000.0), -np.arange(8, dtype=f) * np.float32(2.0) / np.float32(16)).astype(f)
    ropec = np.zeros((128, 2), f)
    for p in range(128):
        d = p % 64
        if d < 16:
            ropec[p, 0] = inv_freq[d % 8]
            ropec[p, 1] = (-2 * np.pi) if d < 8 else (2 * np.pi)
    sh["ropec"] = ropec
    return sh


def _core_inputs(inp, b, r):
    f = np.float32
    x = np.asarray(inp["x"][b], f)
    blocks = x.reshape(32, 128, D)
    own = [2 * j + r for j in range(16)]
    oth = [2 * j + (1 - r) for j in range(16)]
    halo = np.zeros((16, 16, D), f)
    for j in range(16):
        s0 = own[j] * 128
        if s0 > 0:
            halo[j] = x[s0 - 16:s0]
    xin = np.concatenate([blocks[own].reshape(-1, D), blocks[oth].reshape(-1, D), halo.reshape(-1, D)], axis=0)
    pos = np.asarray(inp["positions"][b]).astype(np.int32).reshape(32, 128)
    pos = np.concatenate([pos[own].reshape(-1), pos[oth].reshape(-1)])[None, :]
    mask = np.zeros((128, 256), f)
    kk = np.arange(128)[:, None]
    qq = np.arange(128)[None, :]
    mask[:, 0:128] = np.where(kk <= qq, 0.0, NEG)
    mask[:, 128:256] = NEG if r == 0 else 0.0
    invc = np.zeros((4, 128), f)
    for g, w in enumerate((2, 4, 8, 16)):
        if r == 0:
            invc[g] = 1.0 / np.minimum(np.arange(128) + 1, w)
        else:
            invc[g] = 1.0 / w
    return {"xin": np.ascontiguousarray(xin), "pos": np.ascontiguousarray(pos), "mask": mask, "invc": invc.reshape(1, 512)}


L0_KEYS = ("pmat", "whd", "wag", "wo0", "poolw", "pscale", "lamv", "subln", "ropec", "pren", "postn")
L1_KEYS = ("w1g", "w1u", "w1v", "wo1", "lng", "lnb", "wsT", "tril", "sgub", "pren", "postn")

MODE = "fused"


def kernel(**inp):
    sh = _shared_inputs(inp)
    cores = [(b, r) for b in range(4) for r in range(2)]
    per = [_core_inputs(inp, b, r) for (b, r) in cores]
    if MODE == "fused":
        nc = _get_nc("fused")
        maps = []
        for c in per:
            m = dict(c)
            for k in set(L0_KEYS) | set(L1_KEYS):
                m[k] = sh[k]
            maps.append(m)
        res = run_bass_kernel_spmd(nc, maps, core_ids=list(range(8)))
        outs = [r["out"] for r in res.results]
    else:
        nc0 = _get_nc("L0")
        maps = []
        for c in per:
            m = dict(c)
            for k in L0_KEYS:
                m[k] = sh[k]
            maps.append(m)
        res = run_bass_kernel_spmd(nc0, maps, core_ids=list(range(8)))
        x1 = [np.asarray(r["out"]) for r in res.results]
        nc1 = _get_nc("L1")
        maps = []
        for i in range(8):
            m = {"x1in": x1[i]}
            for k in L1_KEYS:
                m[k] = sh[k]
            maps.append(m)
        res = run_bass_kernel_spmd(nc1, maps, core_ids=list(range(8)))
        outs = [r["out"] for r in res.results]
    out = np.empty((4, 4096, D), np.float32)
    for (b, r), o in zip(cores, outs):
        ob = out[b].reshape(32, 128, D)
        ob[[2 * j + r for j in range(16)]] = np.asarray(o, np.float32).reshape(16, 128, D)
    return out
```

```python
import math
import numpy as np
import concourse.bass as bass
import concourse.mybir as mybir
from concourse.bass_utils import run_bass_kernel_spmd
from contextlib import ExitStack

F32 = mybir.dt.float32
BF16 = mybir.dt.bfloat16
I32 = mybir.dt.int32
AF = mybir.ActivationFunctionType
ALU = mybir.AluOpType
AX = mybir.AxisListType

D = 1024
NS = 16
TOK = 2048
NALL = 4352
NEG = -30000.0
EPS = 1e-6
LAM_INIT = 0.8 - 0.6 * math.exp(-0.3 * 0)
MAGIC = 12582912.0
SEM_LIMIT = 30000


class Dep:
    __slots__ = ("w", "r")

    def __init__(self):
        self.w = None
        self.r = {}


class Tok:
    __slots__ = ("key", "eng", "val")

    def __init__(self, key, eng, val):
        self.key = key
        self.eng = eng
        self.val = val


class Sched:
    ENG = ("pe", "act", "dve", "pool", "sp")

    def __init__(self, nc, es):
        self.nc = nc
        self.es = es
        self.e = dict(pe=nc.tensor, act=nc.scalar, dve=nc.vector, pool=nc.gpsimd, sp=nc.sync)
        self.gen = {k: 0 for k in self.ENG}
        self.semh = {}
        self.cnt = {}
        for k in self.ENG:
            self._newsem(k)
        self.lazy = {k: [] for k in self.ENG}
        self.seen = {k: {} for k in self.ENG}
        self.nwaits = 0
        self.nins = {k: 0 for k in self.ENG}

    def _newsem(self, eng):
        key = "%s%d" % (eng, self.gen[eng])
        self.semh[key] = self.es.enter_context(self.nc.semaphore("s_" + key))
        self.cnt[key] = 0
        return key

    def _curkey(self, eng):
        return "%s%d" % (eng, self.gen[eng])

    def _wait(self, eng, toks):
        need = {}
        for t in toks:
            if t is None:
                continue
            if t.eng == eng and eng == "pe":
                continue
            if t.val is None:
                raise RuntimeError("wait on instruction without inc: %s" % (t.key,))
            if need.get(t.key, 0) < t.val:
                need[t.key] = t.val
        for key, val in need.items():
            if self.seen[eng].get(key, 0) >= val:
                continue
            self.e[eng].wait_ge(self.semh[key], val)
            self.seen[eng][key] = val
            self.nwaits += 1

    def _deps(self, eng, reads, writes):
        toks = []
        for d in reads:
            toks.append(d.w)
        for d in writes:
            if d.w is not None and d.w.eng != eng:
                toks.append(d.w)
            for t in d.r.values():
                if t.eng == eng:
                    continue
                toks.append(t)
        self._wait(eng, toks)

    def _record(self, tok, reads, writes):
        for d in reads:
            d.r[tok.key] = tok
        for d in writes:
            d.w = tok
            d.r = {}

    def op(self, eng, fn, reads=(), writes=(), inc=True):
        self._deps(eng, reads, writes)
        ins = fn(self.e[eng])
        self.nins[eng] += 1
        key = self._curkey(eng)
        tok = Tok(key, eng, None)
        if inc:
            ins.then_inc(self.semh[key], 1)
            self.cnt[key] += 1
            tok.val = self.cnt[key]
            for t in self.lazy[eng]:
                t.val = tok.val
            self.lazy[eng] = []
            if self.cnt[key] >= SEM_LIMIT:
                self.gen[eng] += 1
                self._newsem(eng)
        else:
            self.lazy[eng].append(tok)
        self._record(tok, reads, writes)
        return tok

    def dma(self, q, out, in_, reads=(), writes=(), key=None):
        self._deps(q, reads, writes)
        if key is None:
            self.nauto = getattr(self, "nauto", 0) + 1
            key = "auto%d" % self.nauto
        key = "d_" + key
        if key not in self.semh:
            self.semh[key] = self.es.enter_context(self.nc.semaphore(key))
            self.cnt[key] = 0
        self.e[q].dma_start(out=out, in_=in_).then_inc(self.semh[key], 16)
        self.nins[q] += 1
        self.cnt[key] += 16
        assert self.cnt[key] < 2 * SEM_LIMIT
        tok = Tok(key, "dma", self.cnt[key])
        self._record(tok, reads, writes)
        return tok

    def barrier(self):
        for e in self.ENG:
            assert not self.lazy[e], "barrier with un-incremented %s instructions" % e
        keys = [(k, v) for k, v in self.cnt.items() if v > 0]
        for eng in self.ENG:
            own = self._curkey(eng)
            for key, val in keys:
                if key == own or self.seen[eng].get(key, 0) >= val:
                    continue
                self.e[eng].wait_ge(self.semh[key], val)
                self.seen[eng][key] = val
                self.nwaits += 1

    def wait_all(self, eng, deps):
        toks = []
        for d in deps:
            toks.append(d.w)
            toks.extend(d.r.values())
        self._wait(eng, toks)


def bcast_last(ap, n):
    dims = [list(x) for x in ap.ap]
    assert dims[-1][1] == 1
    dims[-1] = [0, n]
    return bass.AP(ap.tensor, ap.offset, dims)


def bcast_mid(ap, n):
    dims = [list(x) for x in ap.ap]
    assert len(dims) == 2
    return bass.AP(ap.tensor, ap.offset, [dims[0], [0, n], dims[1]])


class Builder:
    def __init__(self, mode, debug=False):
        self.mode = mode
        self.debug = debug
        self.dbg_names = []
        self.nc = bass.Bass("TRN2", target_bir_lowering=False)
        self.es = ExitStack()

    def dram_in(self, name, shape, dt=F32):
        return self.nc.dram_tensor(name, list(shape), dt, kind="ExternalInput").ap()

    def dump(self, name, ap, deps):
        if not getattr(self, "debug", False):
            return
        t = self.nc.dram_tensor("dbg_" + name, list(ap.shape), ap.dtype, kind="ExternalOutput").ap()
        self.S.dma("sp", t[:], ap, reads=deps)
        self.dbg_names.append("dbg_" + name)

    def sb(self, es, name, shape, dt):
        return es.enter_context(self.nc.sbuf_tensor("sb_" + name, list(shape), dt))

    def build(self):
        nc = self.nc
        mode = self.mode
        with self.es as es:
            S = self.S = Sched(nc, es)
            dr = self.dr = {}
            if mode in ("fused", "L0"):
                dr["xin"] = self.dram_in("xin", [NALL, D])
                dr["pos"] = self.dram_in("pos", [1, 4096], I32)
                dr["ropec"] = self.dram_in("ropec", [128, 2])
                dr["mask"] = self.dram_in("mask", [128, 256])
                dr["invc"] = self.dram_in("invc", [1, 512])
                dr["whd"] = self.dram_in("whd", [8, 128, 8, 512])
                dr["pmat"] = self.dram_in("pmat", [128, 128])
                dr["wag"] = self.dram_in("wag", [8, 128, 8, 256])
                dr["wo0"] = self.dram_in("wo0", [16, 128, 1024])
                dr["poolw"] = self.dram_in("poolw", [128, 4, 2, 256])
                dr["pscale"] = self.dram_in("pscale", [128, 8])
                dr["lamv"] = self.dram_in("lamv", [1, 256])
                dr["subln"] = self.dram_in("subln", [128, 1])
            if mode in ("fused", "L1"):
                dr["w1g"] = self.dram_in("w1g", [8, 128, 8, 256])
                dr["w1u"] = self.dram_in("w1u", [8, 128, 8, 256])
                dr["w1v"] = self.dram_in("w1v", [8, 128, 8, 256])
                dr["wo1"] = self.dram_in("wo1", [16, 128, 1024])
                dr["lng"] = self.dram_in("lng", [1, 2048])
                dr["lnb"] = self.dram_in("lnb", [1, 2048])
                dr["wsT"] = self.dram_in("wsT", [128, 8, 128])
                dr["tril"] = self.dram_in("tril", [128, 128])
                dr["sgub"] = self.dram_in("sgub", [1, 1024])
            if mode == "L1":
                dr["x1in"] = self.dram_in("x1in", [TOK, D])
            dr["pren"] = self.dram_in("pren", [128, 16])
            dr["postn"] = self.dram_in("postn", [1, 2048])
            dr["out"] = nc.dram_tensor("out", [TOK, D], F32, kind="ExternalOutput").ap()

            self.pb = [es.enter_context(nc.psum_tensor("pb%d" % i, [128, 512], F32)) for i in range(8)]
            self.dpb = [Dep() for _ in range(8)]

            self.ident = self.sb(es, "ident", [128, 128], BF16)
            self.d_ident = Dep()
            io = self.sb(es, "iota_f", [128, 128], F32)
            ip = self.sb(es, "iota_p", [128, 1], F32)
            d_io, d_ip = Dep(), Dep()
            S.op("pool", lambda e: e.iota(io[:], [[1, 128]], base=0, channel_multiplier=0,
                                          allow_small_or_imprecise_dtypes=True), writes=[d_io])
            S.op("pool", lambda e: e.iota(ip[:], [[1, 1]], base=0, channel_multiplier=1,
                                          allow_small_or_imprecise_dtypes=True), writes=[d_ip])
            S.op("dve", lambda e: e.tensor_scalar(self.ident[:], io[:], ip[:, 0:1], None, ALU.is_equal),
                 reads=[d_io, d_ip], writes=[self.d_ident])
            self.pren = self.sb(es, "pren", [128, 16], F32)
            self.d_pren = Dep()
            S.dma("sp", self.pren[:], dr["pren"][:], writes=[self.d_pren])
            self.small = self.sb(es, "small", [128, 64], F32)
            self.small_i = 0
            self.ostore = []

            if mode in ("fused", "L0"):
                self.ygA = self.sb(es, "ygA", [128, 8, TOK], BF16)
                self.d_ygA = [[Dep() for _ in range(4)] for _ in range(8)]
                self.aTh = self.sb(es, "aTh", [128, 8, 256], F32)
                self.d_aTh = Dep()
                with ExitStack() as es1:
                    self.phase_attention(es1)
            if mode in ("fused", "L0"):
                self.dump("ygA", self.ygA[:], [d for l in self.d_ygA for d in l])
                self.dump("aTh", self.aTh[:], [self.d_aTh])
            S.barrier()
            with ExitStack() as es2:
                self.phase_final(es2)
            S.wait_all("sp", self.ostore)
        return nc

    def rms_group(self, xs_aps, d_xs, hn, d_hn, outs, d_outs, gain, banks, ssq, d_ssq):
        S = self.S
        n = len(xs_aps)
        for i in range(n):
            S.op("act", lambda e, i=i: e.activation(self.junk[:], xs_aps[i], AF.Square, accum_out=ssq[:, i:i + 1]),
                 reads=[d_xs[i]], writes=[self.d_junk, d_ssq])
        S.op("dve", lambda e: e.tensor_scalar(ssq[:, 4:4 + n], ssq[:, 0:n], 1.0 / D, EPS, ALU.mult, ALU.add), reads=[d_ssq], writes=[d_ssq])
        S.op("act", lambda e: e.activation(ssq[:, 8:8 + n], ssq[:, 4:4 + n], AF.Sqrt), reads=[d_ssq], writes=[d_ssq])
        S.op("dve", lambda e: e.reciprocal(ssq[:, 12:12 + n], ssq[:, 8:8 + n]), reads=[d_ssq], writes=[d_ssq])
        for i in range(n):
            S.op("act", lambda e, i=i: e.activation(hn[i][:], xs_aps[i], AF.Copy, scale=ssq[:, 12 + i:13 + i]),
                 reads=[d_xs[i], d_ssq], writes=[d_hn[i]])
        for i in range(n):
            ptr = self.pb[banks[i]][:].bitcast(BF16)
            for kt in range(8):
                S.op("pe", lambda e, kt=kt, i=i, ptr=ptr: e.transpose(ptr[:, kt * 128:(kt + 1) * 128],
                                                                       hn[i][:, kt * 128:(kt + 1) * 128], self.ident[:]),
                     reads=[d_hn[i], self.d_ident], writes=[self.dpb[banks[i]]], inc=(kt == 7))
        for i in range(n):
            ptr = self.pb[banks[i]][:].bitcast(BF16)
            S.op("dve", lambda e, i=i, ptr=ptr: e.tensor_tensor(outs[i], ptr.rearrange("p (k t) -> p k t", k=8),
                                                                bcast_last(gain, 128), ALU.mult),
                 reads=[self.dpb[banks[i]], self.d_pren], writes=[d_outs[i]])

    def rms_transpose(self, x_ap, d_x, hn, d_hn, hT_out, d_hT, gain_ap, d_gain, ptr_i, scratch):
        S = self.S
        ss, d_ss = scratch
        junk = self.junk
        S.op("act", lambda e: e.activation(junk[:], x_ap, AF.Square, accum_out=ss[:, 0:1]),
             reads=[d_x], writes=[self.d_junk, d_ss])
        S.op("dve", lambda e: e.tensor_scalar(ss[:, 1:2], ss[:, 0:1], 1.0 / D, EPS, ALU.mult, ALU.add),
             reads=[d_ss], writes=[d_ss])
        S.op("act", lambda e: e.activation(ss[:, 2:3], ss[:, 1:2], AF.Sqrt), reads=[d_ss], writes=[d_ss])
        S.op("dve", lambda e: e.reciprocal(ss[:, 3:4], ss[:, 2:3]), reads=[d_ss], writes=[d_ss])
        S.op("act", lambda e: e.activation(hn[:], x_ap, AF.Copy, scale=ss[:, 3:4]),
             reads=[d_x, d_ss], writes=[d_hn])
        ptr = self.pb[ptr_i][:].bitcast(BF16)
        for kt in range(8):
            S.op("pe", lambda e, kt=kt: e.transpose(ptr[:, kt * 128:(kt + 1) * 128],
                                                     hn[:, kt * 128:(kt + 1) * 128], self.ident[:]),
                 reads=[d_hn, self.d_ident], writes=[self.dpb[ptr_i]], inc=(kt == 7))
        S.op("dve", lambda e: e.tensor_tensor(hT_out, ptr.rearrange("p (k t) -> p k t", k=8),
                                              bcast_last(gain_ap, 128), ALU.mult),
             reads=[self.dpb[ptr_i], d_gain], writes=[d_hT])

    def phase_attention(self, es):
        S, nc, dr = self.S, self.nc, self.dr
        sb = lambda n, s, d: self.sb(es, n, s, d)
        hT = sb("hT", [128, 8, NALL], BF16)
        d_hT = [Dep() for _ in range(34)]
        Ct = sb("ropeC", [128, 4096], BF16)
        St = sb("ropeS", [128, 4096], BF16)
        d_C = [Dep() for _ in range(4)]
        d_St = [Dep() for _ in range(4)]
        ropec = sb("ropec", [128, 2], F32)
        d_ropec = Dep()
        S.dma("sp", ropec[:], dr["ropec"][:], writes=[d_ropec])
        maskb = sb("maskb", [128, 256], BF16)
        d_mask = Dep()
        S.dma("pool", maskb[:], dr["mask"][:], writes=[d_mask])
        pmat = sb("pmat", [128, 128], BF16)
        d_pmat = Dep()
        S.dma("pool", pmat[:], dr["pmat"][:], writes=[d_pmat])
        lamv = sb("lamv", [128, 256], F32)
        d_lamv = Dep()
        S.dma("sp", lamv[:], dr["lamv"].partition_broadcast(128), writes=[d_lamv])
        subln = sb("subln", [128, 1], F32)
        d_subln = Dep()
        S.dma("sp", subln[:], dr["subln"][:], writes=[d_subln])

        lsm = sb("lam_small", [128, 8], F32)
        d_lsm = Dep()
        ltmp = sb("lam_tmp", [128, 128], F32)
        d_ltmp = Dep()
        lv4 = lamv[:].rearrange("p (a d) -> p a d", a=4)
        S.op("dve", lambda e: e.tensor_tensor(ltmp[:, 0:64], lv4[:, 0, :], lv4[:, 1, :], ALU.mult),
             reads=[d_lamv], writes=[d_ltmp])
        S.op("dve", lambda e: e.tensor_tensor(ltmp[:, 64:128], lv4[:, 2, :], lv4[:, 3, :], ALU.mult),
             reads=[d_lamv], writes=[d_ltmp])
        S.op("dve", lambda e: e.reduce_sum(lsm[:, 0:2], ltmp[:].rearrange("p (a d) -> p a d", a=2), AX.X),
             reads=[d_ltmp], writes=[d_lsm])
        S.op("act", lambda e: e.activation(lsm[:, 2:4], lsm[:, 0:2], AF.Exp), reads=[d_lsm], writes=[d_lsm])
        S.op("dve", lambda e: e.scalar_tensor_tensor(lsm[:, 4:5], lsm[:, 3:4], -LAM_INIT, lsm[:, 2:3],
                                                      ALU.add, ALU.subtract), reads=[d_lsm], writes=[d_lsm])
        lamvec = sb("lamvec", [128, 4, 2], F32)
        d_lamvec = Dep()
        S.op("dve", lambda e: e.memset(lamvec[:], 1.0), writes=[d_lamvec])
        S.op("dve", lambda e: e.tensor_copy(lamvec[:, :, 1:2], bcast_mid(lsm[:, 4:5], 4)),
             reads=[d_lsm], writes=[d_lamvec])

        wst = [sb("wst%d" % i, [128, 8, 768], BF16) for i in range(2)]
        d_wst = [Dep() for _ in range(2)]
        est = ExitStack()
        with est:
            sbt = lambda n, s_, d: self.sb(est, n, s_, d)
            self.junk = sbt("junk", [128, 1024], F32)
            self.d_junk = Dep()
            xb = [sbt("xb%d" % i, [128, D], F32) for i in range(4)]
            d_xb = [Dep() for _ in range(4)]
            hnb = [sbt("hnb%d" % i, [128, D], BF16) for i in range(4)]
            d_hnb = [Dep() for _ in range(4)]
            ssq0 = sbt("ssq0", [128, 16], F32)
            d_ssq0 = Dep()
            pre0 = self.pren[:, 0:8].rearrange("p (k o) -> p k o", o=1)
            for g0 in range(0, 34, 4):
                blks = list(range(g0, min(g0 + 4, 34)))
                for i, blk in enumerate(blks):
                    S.dma("sp", xb[i][:], dr["xin"][blk * 128:(blk + 1) * 128, :], writes=[d_xb[i]], key="x%d" % i)
                self.rms_group([xb[i][:] for i in range(len(blks))], d_xb, hnb, d_hnb,
                               [hT[:, :, blk * 128:(blk + 1) * 128] for blk in blks], [d_hT[blk] for blk in blks],
                               pre0, [4, 5, 6, 7], ssq0, d_ssq0)

            posi = sbt("posi", [128, 1024], I32)
            d_posi = Dep()
            rt = [sbt("rt%d" % i, [128, 1024], F32) for i in range(3)]
            d_rt = [Dep() for _ in range(3)]
            for c in range(4):
                cs = slice(c * 1024, (c + 1) * 1024)
                S.dma("sp", posi[:], dr["pos"][:, cs].partition_broadcast(128), writes=[d_posi], key="pos")
                S.op("dve", lambda e: e.tensor_copy(rt[0][:], posi[:]), reads=[d_posi], writes=[d_rt[0]])
                S.op("dve", lambda e: e.tensor_scalar(rt[1][:], rt[0][:], ropec[:, 0:1], float(np.float32(1.0 / (2 * np.pi))),
                                                      ALU.mult, ALU.mult), reads=[d_rt[0], d_ropec], writes=[d_rt[1]])
                S.op("dve", lambda e: e.tensor_scalar(rt[2][:], rt[1][:], MAGIC, MAGIC, ALU.add, ALU.subtract),
                     reads=[d_rt[1]], writes=[d_rt[2]])
                S.op("dve", lambda e: e.tensor_sub(rt[1][:], rt[1][:], rt[2][:]), reads=[d_rt[1], d_rt[2]], writes=[d_rt[1]])
                S.op("act", lambda e, cs=cs: e.activation(St[:, cs], rt[1][:], AF.Sin, scale=ropec[:, 1:2]),
                     reads=[d_rt[1], d_ropec], writes=[d_St[c]])
                S.op("dve", lambda e: e.tensor_scalar(rt[2][:], rt[1][:], -1.0, None, ALU.mult),
                     reads=[d_rt[1]], writes=[d_rt[2]])
                S.op("dve", lambda e: e.tensor_tensor(rt[2][:], rt[2][:], rt[1][:], ALU.max),
                     reads=[d_rt[1], d_rt[2]], writes=[d_rt[2]])
                S.op("dve", lambda e: e.tensor_scalar(rt[2][:], rt[2][:], float(-2 * np.pi), float(np.pi / 2), ALU.mult, ALU.add),
                     reads=[d_rt[2]], writes=[d_rt[2]])
                S.op("act", lambda e, cs=cs: e.activation(Ct[:, cs], rt[2][:], AF.Sin),
                     reads=[d_rt[2]], writes=[d_C[c]])

        S.barrier()
        self.dump("hT0", hT[:, :, 0:256], d_hT[0:2])
        self.dump("hTh", hT[:, :, 4096:4352], d_hT[32:34])
        self.dump("Ct", Ct[:], d_C)
        self.dump("St", St[:], d_St)
        self.dump("lsm", lsm[:], [d_lsm])
        for ct in range(8):
            s = ct % 2
            S.dma("pool", wst[s][:, :, 0:256], dr["wag"][ct], writes=[d_wst[s]], key="wh%d" % s)
            pi = 4 + s
            for kt in range(8):
                S.op("pe", lambda e, kt=kt, s=s, pi=pi: e.matmul(self.pb[pi][:, 0:256], lhsT=wst[s][:, kt, 0:128],
                                                                 rhs=hT[:, kt, 4096:4352], start=(kt == 0), stop=(kt == 7)),
                     reads=[d_wst[s], d_hT[32], d_hT[33]], writes=[self.dpb[pi]], inc=(kt == 7))
            S.op("act", lambda e, ct=ct, pi=pi: e.activation(self.aTh[:, ct, :], self.pb[pi][:, 0:256], AF.Copy),
                 reads=[self.dpb[pi]], writes=[self.d_aTh])

        KT = sb("KT", [128, 4096], BF16)
        d_KT = [Dep() for _ in range(8)]
        QTc = [sb("QT%d" % i, [128, TOK], BF16) for i in range(2)]
        d_QT = [Dep() for _ in range(4)]
        S.op("pool", lambda e: e.memset(QTc[0][64:128, :], 0.0), writes=d_QT)
        S.op("pool", lambda e: e.memset(QTc[1][0:64, :], 0.0), writes=d_QT)
        V = sb("V", [128, 32, 129], BF16)
        d_V = [Dep() for _ in range(8)]
        S.op("pool", lambda e: e.memset(V[:, :, 128:129], 1.0), writes=d_V)
        sgT = sb("sgT", [128, TOK], BF16)
        d_sgT = [Dep() for _ in range(4)]
        ET = [sb("ET%d" % i, [128, 512], BF16) for i in range(4)]
        d_ET = [Dep() for _ in range(4)]
        kb16 = [sb("kb16_%d" % i, [128, 512], BF16) for i in range(2)]
        d_kb16 = [Dep() for _ in range(2)]
        rtmp = [sb("rtmp%d" % i, [128, 512], F32) for i in range(2)]
        d_rtmp = [Dep() for _ in range(2)]
        stage = [sb("stage%d" % i, [128, 4, 2, 129], F32) for i in range(2)]
        d_stage = [Dep() for _ in range(2)]
        eo = [sb("eo%d" % i, [128, 4, 128], F32) for i in range(2)]
        d_eo = [Dep() for _ in range(2)]
        eon = sb("eon", [128, 4, 128], BF16)
        d_eon = Dep()
        esm = sb("esm", [128, 32], F32)
        d_esm = Dep()

        et_i = [0]
        st_i = [0]
        rt_i = [0]
        S.dma("pool", wst[0][:, :, 0:512], dr["whd"][0], writes=[d_wst[0]], key="wh0")
        for h in range(8):
            s = h % 2
            if h + 1 < 8:
                S.dma("pool", wst[1 - s][:, :, 0:512], dr["whd"][h + 1], writes=[d_wst[1 - s]], key="wh%d" % (1 - s))
            w = wst[s]
            dw = d_wst[s]

            def proj_fm(col0, tok0, pi, first_blk):
                for kt in range(8):
                    S.op("pe", lambda e, kt=kt: e.matmul(self.pb[pi][:, :], lhsT=w[:, kt, col0:col0 + 128],
                                                         rhs=hT[:, kt, tok0:tok0 + 512], start=(kt == 0), stop=(kt == 7)),
                         reads=[dw] + d_hT[first_blk:first_blk + 4], writes=[self.dpb[pi]], inc=(kt == 7))

            chunks = [("k", c) for c in range(8)] + [("q", c) for c in range(4)]

            def rope_proj(i):
                kind, c = chunks[i]
                col0 = 128 if kind == "k" else 0
                pi = i % 3
                proj_fm(col0, c * 512, pi, c * 4)
                S.op("act", lambda e: e.activation(kb16[i % 2][:], self.pb[pi][:, :], AF.Copy),
                     reads=[self.dpb[pi]], writes=[d_kb16[i % 2]])

            def rope_finish(i):
                kind, c = chunks[i]
                pi = i % 3
                pq = 3 + i % 2
                tok0 = c * 512
                cidx = c // 2
                S.op("pe", lambda e: e.matmul(self.pb[pq][:, :], lhsT=pmat[:, :], rhs=kb16[i % 2][:], start=True, stop=True),
                     reads=[d_pmat, d_kb16[i % 2]], writes=[self.dpb[pq]], inc=True)
                S.op("dve", lambda e: e.tensor_tensor(rtmp[0][:], self.pb[pi][:, :], Ct[:, tok0:tok0 + 512], ALU.mult),
                     reads=[self.dpb[pi], d_C[cidx], d_kb16[i % 2]], writes=[d_rtmp[0]])
                S.op("dve", lambda e: e.tensor_tensor(rtmp[1][:], self.pb[pq][:, :], St[:, tok0:tok0 + 512], ALU.mult),
                     reads=[self.dpb[pq], d_St[cidx]], writes=[d_rtmp[1]])
                cs_ = slice(tok0, tok0 + 512)
                if kind == "k":
                    S.op("pool", lambda e: e.tensor_tensor(KT[:, cs_], rtmp[0][:], rtmp[1][:], ALU.add),
                         reads=[d_rtmp[0], d_rtmp[1]], writes=[d_KT[c]])
                else:
                    S.op("pool", lambda e: e.tensor_tensor(QTc[0][0:64, cs_], rtmp[0][0:64, :], rtmp[1][0:64, :], ALU.add),
                         reads=[d_rtmp[0], d_rtmp[1]], writes=[d_QT[c]])
                    S.op("dve", lambda e: e.tensor_tensor(QTc[1][64:128, cs_], rtmp[0][64:128, :], rtmp[1][64:128, :], ALU.add),
                         reads=[d_rtmp[0], d_rtmp[1]], writes=[d_QT[c]])

            for i in range(len(chunks)):
                rope_proj(i)
                if i >= 1:
                    rope_finish(i - 1)
            rope_finish(len(chunks) - 1)
            for c in range(4):
                pi = 6 + c % 2
                proj_fm(384, c * 512, pi, c * 4)
                S.op("act", lambda e, c=c, pi=pi: e.activation(sgT[:, c * 512:(c + 1) * 512], self.pb[pi][:, :], AF.Silu),
                     reads=[self.dpb[pi]], writes=[d_sgT[c]])
            for g4 in range(8):
                pi = 6 + g4 % 2
                for i in range(4):
                    kb = g4 * 4 + i
                    for kt in range(8):
                        S.op("pe", lambda e, kt=kt, kb=kb, i=i: e.matmul(
                            self.pb[pi][:, i * 128:(i + 1) * 128], lhsT=hT[:, kt, kb * 128:(kb + 1) * 128],
                            rhs=w[:, kt, 256:384], start=(kt == 0), stop=(kt == 7)),
                             reads=[dw, d_hT[kb]], writes=[self.dpb[pi]], inc=(kt == 7 and i == 3))
                S.op("act", lambda e, g4=g4, pi=pi: e.activation(
                    V[:, g4 * 4:(g4 + 1) * 4, 0:128], self.pb[pi][:, :].rearrange("p (a d) -> p a d", a=4), AF.Copy),
                     reads=[self.dpb[pi]], writes=[d_V[g4]])

            if h == 0:
                self.dump("KT", KT[:], d_KT)
                self.dump("QT", QTc[0][:], d_QT)
                self.dump("V", V[:], d_V)
                self.dump("sgT", sgT[:], d_sgT)
            tiles = []
            for G in range(4):
                blocks = [(i, 0, None) for i in range(4 * G)] + [(16 + i, 0, None) for i in range(4 * G)]
                for a4 in range(4):
                    blocks.append((4 * G + a4, a4, 0))
                    blocks.append((16 + 4 * G + a4, a4, 1))
                for c in range(2):
                    for bi, (kb, a4, m) in enumerate(blocks):
                        tiles.append(dict(G=G, c=c, kb=kb, a=a4, m=m, first=(bi == 0), endgrp=(bi == len(blocks) - 1 and c == 1)))

            def emit_qk(n):
                t = tiles[n]
                G, c, kb, a4, m = t["G"], t["c"], t["kb"], t["a"], t["m"]
                ps = slice(c * 64, (c + 1) * 64)
                sbk = (0, 1, 6, 7)[n % 4]
                q0 = (4 * G + a4) * 128
                q1 = (4 * G + 4) * 128
                rd = [d_KT[kb // 4], d_QT[G]]
                if m is None:
                    S.op("pe", lambda e: e.matmul(self.pb[sbk][:, :], lhsT=KT[:, kb * 128:(kb + 1) * 128],
                                                  rhs=QTc[c][:, q0:q1], start=True, stop=True),
                         reads=rd, writes=[self.dpb[sbk]], inc=True)
                else:
                    c0 = a4 * 128
                    S.op("pe", lambda e: e.matmul(self.pb[sbk][:, c0:c0 + 128], lhsT=KT[:, kb * 128:(kb + 1) * 128],
                                                  rhs=QTc[c][:, q0:q0 + 128], start=True, stop=False),
                         reads=rd, writes=[self.dpb[sbk]], inc=False)
                    S.op("pe", lambda e: e.matmul(self.pb[sbk][:, c0:c0 + 128], lhsT=self.ident[:, :],
                                                  rhs=maskb[:, m * 128:(m + 1) * 128], start=False, stop=True),
                         reads=[self.d_ident, d_mask], writes=[self.dpb[sbk]], inc=(a4 == 3))
                    if a4 < 3:
                        S.op("pe", lambda e: e.matmul(self.pb[sbk][:, c0 + 128:512], lhsT=KT[:, kb * 128:(kb + 1) * 128],
                                                      rhs=QTc[c][:, q0 + 128:q1], start=True, stop=True),
                             reads=rd, writes=[self.dpb[sbk]], inc=True)

            def emit_exp_pv(n):
                t = tiles[n]
                G, c, kb, a4, m = t["G"], t["c"], t["kb"], t["a"], t["m"]
                sbk = (0, 1, 6, 7)[n % 4]
                ei = n % 4
                c0 = a4 * 128
                S.op("act", lambda e: e.activation(ET[ei][:, c0:512], self.pb[sbk][:, c0:512], AF.Exp, scale=0.125),
                     reads=[self.dpb[sbk]], writes=[d_ET[ei]])
                for sl in range(a4, 4):
                    ob = 2 + sl
                    O = self.pb[ob][:, 0:258].rearrange("p (c d) -> p c d", c=2)
                    last = (m == 1 and a4 == sl)
                    S.op("pe", lambda e, sl=sl, O=O, last=last: e.matmul(
                        O[:, c, :], lhsT=ET[ei][:, sl * 128:(sl + 1) * 128], rhs=V[:, kb, :],
                        start=t["first"], stop=last),
                         reads=[d_ET[ei], d_V[kb // 4]], writes=[self.dpb[ob]], inc=(last and c == 1))

            pending = None
            emit_qk(0)
            emit_qk(1)
            emit_qk(2)
            for n in range(len(tiles)):
                if n + 3 < len(tiles):
                    emit_qk(n + 3)
                emit_exp_pv(n)
                t = tiles[n]
                if not t["endgrp"]:
                    continue
                q4 = t["G"]
                stg = stage[q4 % 2]
                for sl in range(4):
                    O = self.pb[2 + sl][:, 0:258].rearrange("p (c d) -> p c d", c=2)
                    S.op("dve", lambda e, O=O, sl=sl: e.tensor_copy(stg[:, sl, :, :], O),
                         reads=[self.dpb[2 + sl]], writes=[d_stage[q4 % 2]])
                sl = 3
                if pending is not None:
                    self.attn_epilogue_b(*pending)
                    pending = None
                if sl == 3 and h == 0 and q4 == 0:
                    self.dump("stage", stg[:], [d_stage[0]])
                if sl == 3:
                    ds = d_stage[q4 % 2]
                    rz0 = esm[:, 0:8].rearrange("p (a c) -> p a c", c=2)
                    rz = esm[:, 8:16].rearrange("p (a c) -> p a c", c=2)
                    S.op("dve", lambda e, stg=stg: e.reciprocal(rz0, stg[:, :, :, 128]), reads=[ds], writes=[d_esm])
                    S.op("dve", lambda e: e.tensor_tensor(rz, rz0, lamvec[:], ALU.mult),
                         reads=[d_esm, d_lamvec], writes=[d_esm])
                    S.op("dve", lambda e, stg=stg: e.tensor_tensor(eo[0][:], stg[:, :, 0, 0:128], bcast_last(rz[:, :, 0:1], 128), ALU.mult),
                         reads=[ds, d_esm], writes=[d_eo[0]])
                    S.op("dve", lambda e, stg=stg: e.tensor_tensor(eo[1][:], stg[:, :, 1, 0:128], bcast_last(rz[:, :, 1:2], 128), ALU.mult),
                         reads=[ds, d_esm], writes=[d_eo[1]])
                    S.op("pool", lambda e: e.tensor_tensor(eo[0][:], eo[0][:], eo[1][:], ALU.add),
                         reads=[d_eo[0], d_eo[1]], writes=[d_eo[0]])
                    S.op("pool", lambda e: e.tensor_tensor(eo[1][:], eo[0][:], eo[0][:], ALU.mult),
                         reads=[d_eo[0]], writes=[d_eo[1]])
                    S.op("dve", lambda e: e.reduce_sum(esm[:, 16:20], eo[1][:], AX.X), reads=[d_eo[1]], writes=[d_esm])
                    S.op("dve", lambda e: e.tensor_scalar(esm[:, 20:24], esm[:, 16:20], 1.0 / 128, 1e-5, ALU.mult, ALU.add),
                         reads=[d_esm], writes=[d_esm])
                    S.op("act", lambda e: e.activation(esm[:, 24:28], esm[:, 20:24], AF.Ln), reads=[d_esm], writes=[d_esm])
                    S.op("act", lambda e: e.activation(esm[:, 28:32], esm[:, 24:28], AF.Exp, scale=-0.5), reads=[d_esm], writes=[d_esm])
                    S.op("dve", lambda e: e.scalar_tensor_tensor(eon[:], eo[0][:], 1.0 - LAM_INIT,
                                                                  bcast_last(esm[:, 28:32].rearrange("p (a o) -> p a o", o=1), 128),
                                                                  ALU.mult, ALU.mult),
                         reads=[d_eo[0], d_esm], writes=[d_eon])
                    pending = (h, q4, eon, d_eon, subln, d_subln, sgT, d_sgT)
                    if h == 0 and q4 == 0:
                        self.dump("eon", eon[:], [d_eon])
                        self.dump("esm", esm[:], [d_esm])
            if pending is not None:
                self.attn_epilogue_b(*pending)
                pending = None

    def attn_epilogue_b(self, h, q4, eon, d_eon, subln, d_subln, sgT, d_sgT):
        S = self.S
        ptr = self.pb[7][:].bitcast(BF16)
        for i in range(4):
            S.op("pe", lambda e, i=i: e.transpose(ptr[:, i * 128:(i + 1) * 128], eon[:, i, :], self.ident[:]),
                 reads=[d_eon, self.d_ident], writes=[self.dpb[7]], inc=(i == 3))
        S.op("dve", lambda e: e.scalar_tensor_tensor(self.ygA[:, h, q4 * 512:(q4 + 1) * 512], ptr[:, 0:512], subln[:, 0:1],
                                                      sgT[:, q4 * 512:(q4 + 1) * 512], ALU.mult, ALU.mult),
             reads=[self.dpb[7], d_subln, d_sgT[q4]], writes=[self.d_ygA[h][q4]])

    def phase_final(self, es):
        S, nc, dr, mode = self.S, self.nc, self.dr, self.mode
        sb = lambda n, s, d: self.sb(es, n, s, d)
        L0 = mode in ("fused", "L0")
        L1 = mode in ("fused", "L1")
        self.junk = sb("junkf", [128, 1024], F32)
        self.d_junk = Dep()
        postn = sb("postn", [128, 2, 1024], F32)
        d_postn = Dep()
        S.dma("sp", postn[:].rearrange("p a d -> p (a d)"), dr["postn"].partition_broadcast(128), writes=[d_postn])
        xs = sb("xs", [128, 4, D], F32)
        d_xs = [Dep() for _ in range(4)]
        hn = [sb("hnf%d" % i, [128, D], BF16) for i in range(4)]
        d_hn = [Dep() for _ in range(4)]
        ss4 = [sb("ssf%d" % i, [128, 8], F32) for i in range(4)]
        d_ss4 = [Dep() for _ in range(4)]
        hTs = sb("hTs", [128, 8, 512], BF16)
        d_hTs = [Dep() for _ in range(4)]
        ysb = sb("ysb", [128, 16, 512], BF16)
        d_ysb = [Dep() for _ in range(16)]
        tmpf = [sb("tmpf%d" % i, [128, D], F32) for i in range(2)]
        d_tmpf = [Dep() for _ in range(2)]
        wr = [sb("wr%d" % i, [128, 8, 256], BF16) for i in range(6)]
        d_wr = [Dep() for _ in range(6)]
        wo = [sb("wor%d" % i, [128, 1024], BF16) for i in range(3)]
        d_wo = [Dep() for _ in range(3)]
        if L0:
            invc = sb("invc", [128, 4, 128], F32)
            d_invc = Dep()
            S.dma("sp", invc[:].rearrange("p a d -> p (a d)"), dr["invc"].partition_broadcast(128), writes=[d_invc])
            poolw = sb("poolw", [128, 4, 2, 256], BF16)
            d_poolw = Dep()
            S.dma("pool", poolw[:], dr["poolw"][:], writes=[d_poolw])
            pscale = sb("pscale", [128, 8], F32)
            d_pscale = Dep()
            S.dma("sp", pscale[:], dr["pscale"][:], writes=[d_pscale])
            Abuf = [sb("Abuf%d" % i, [128, 4, 144], F32) for i in range(2)]
            d_A = [Dep() for _ in range(2)]
            Sb = [sb("Sbuf%d" % i, [128, 4, 144], F32) for i in range(2)]
            d_Sb = [Dep() for _ in range(2)]
            pooled = [sb("pooled%d" % i, [128, 2, 512], BF16) for i in range(2)]
            d_pooled = [[Dep() for _ in range(2)] for _ in range(2)]
            ptmp = sb("ptmp", [128, 128], F32)
            d_ptmp = Dep()
        if L1:
            lng = sb("lng", [128, 2048], F32)
            lnb = sb("lnb", [128, 2048], F32)
            d_ln = Dep()
            S.dma("sp", lng[:], dr["lng"].partition_broadcast(128), writes=[d_ln], key="ln")
            S.dma("sp", lnb[:], dr["lnb"].partition_broadcast(128), writes=[d_ln], key="ln")
            wsf = sb("wsf", [128, 8, 128], F32)
            trl = sb("trl", [128, 128], F32)
            d_wsf = Dep()
            S.dma("sp", wsf[:], dr["wsT"][:], writes=[d_wsf], key="wsf")
            S.dma("sp", trl[:], dr["tril"][:], writes=[d_wsf], key="wsf")
            wsT = sb("wsTb", [128, 8, 128], BF16)
            d_wsT = Dep()
            S.op("dve", lambda e: e.tensor_tensor(wsT[:], wsf[:], bcast_mid(trl[:], 8), ALU.mult), reads=[d_wsf], writes=[d_wsT])
            bsb = sb("bsb", [128, 8, 128], F32)
            d_bsb = Dep()
            S.dma("sp", bsb[:].rearrange("p a d -> p (a d)"), dr["sgub"].partition_broadcast(128), writes=[d_bsb])
            vb = sb("vb", [128, 4, 2048], BF16)
            d_vb = [Dep() for _ in range(4)]
            utmp = [sb("utmp%d" % i, [128, 512], BF16) for i in range(2)]
            d_utmp = [Dep() for _ in range(2)]
            lsm4 = [sb("lnsm%d" % i, [128, 12], F32) for i in range(1)]
            d_lsm4 = [Dep() for _ in range(1)]
            lsq = sb("lsq", [128, 40], F32)
            d_lsq = Dep()
            mtmp = [sb("mtmp%d" % i, [128, 512], F32) for i in range(2)]
            d_mtmp = [Dep() for _ in range(2)]

        per_sb = []
        if L0:
            for ct in range(8):
                per_sb.append(("wr", dr["wag"][ct]))
            for kt in range(16):
                per_sb.append(("wo", dr["wo0"][kt]))
        if L1:
            for cc in range(8):
                per_sb.append(("wr", dr["w1v"][cc]))
            for i in range(8):
                per_sb.append(("wr", dr["w1g"][i]))
            for i in range(8):
                per_sb.append(("wr", dr["w1u"][i]))
            for kt in range(16):
                per_sb.append(("wo", dr["wo1"][kt]))
        nper = len(per_sb)
        n_wr = sum(1 for k, _ in per_sb if k == "wr")
        n_wo = nper - n_wr
        scr_wr = nc.dram_tensor("scr_wr", [n_wr, 128, 8, 256], BF16).ap()
        scr_wo = nc.dram_tensor("scr_wo", [n_wo, 128, 1024], BF16).ap()
        scr_of = []
        c_wr = c_wo = 0
        for k, _ in per_sb:
            if k == "wr":
                scr_of.append(scr_wr[c_wr]); c_wr += 1
            else:
                scr_of.append(scr_wo[c_wo]); c_wo += 1
        d_scr = [Dep() for _ in range(nper)]
        items = per_sb * 4
        ring = {"wr": (wr, d_wr), "wo": (wo, d_wo)}
        cnt = {"wr": 0, "wo": 0}
        slot_of = []
        for kind, _ in items:
            slot_of.append(cnt[kind] % len(ring[kind][0]))
            cnt[kind] += 1
        issued = [0]
        inflight = {"wr": 0, "wo": 0}
        consumed = [0]

        def pump():
            while issued[0] < len(items):
                i = issued[0]
                kind, src = items[i]
                bufs, deps = ring[kind]
                if inflight[kind] >= len(bufs):
                    break
                sl = slot_of[i]
                j = i % nper
                if i < nper:
                    S.dma("pool", bufs[sl][:], src, writes=[deps[sl]], key="%s%d" % (kind, sl))
                    S.dma("sp", scr_of[j], bufs[sl][:], reads=[deps[sl]], writes=[d_scr[j]], key="sw%d" % (j % 32))
                else:
                    S.dma("sp", bufs[sl][:], scr_of[j], reads=[d_scr[j]], writes=[deps[sl]], key="%s%d" % (kind, sl))
                inflight[kind] += 1
                issued[0] += 1

        def take(kind):
            i = consumed[0]
            assert items[i][0] == kind, (items[i][0], kind)
            assert i < issued[0]
            sl = slot_of[i]
            bufs, deps = ring[kind]
            return bufs[sl], deps[sl]

        def release(kind):
            consumed[0] += 1
            inflight[kind] -= 1
            pump()

        pump()
        pre0 = self.pren[:, 0:8].rearrange("p (k o) -> p k o", o=1)
        pre1 = self.pren[:, 8:16].rearrange("p (k o) -> p k o", o=1)

        def post_norm_all(layer):
            for tb in range(4):
                for hf in range(2):
                    S.op("act", lambda e, hf=hf, tb=tb: e.activation(self.junk[:, hf * 512:(hf + 1) * 512], self.pb[2 * tb + hf][:, :], AF.Square,
                                                                     accum_out=ssq[:, 2 * tb + hf:2 * tb + hf + 1]),
                         reads=[self.dpb[2 * tb + hf]], writes=[self.d_junk, d_ssq])
            S.op("dve", lambda e: e.reduce_sum(ssq[:, 8:12], ssq[:, 0:8].rearrange("p (a c) -> p a c", c=2), AX.X), reads=[d_ssq], writes=[d_ssq])
            S.op("dve", lambda e: e.tensor_scalar(ssq[:, 8:12], ssq[:, 8:12], 1.0 / D, EPS, ALU.mult, ALU.add), reads=[d_ssq], writes=[d_ssq])
            S.op("act", lambda e: e.activation(ssq[:, 12:16], ssq[:, 8:12], AF.Sqrt), reads=[d_ssq], writes=[d_ssq])
            S.op("dve", lambda e: e.reciprocal(ssq[:, 8:12], ssq[:, 12:16]), reads=[d_ssq], writes=[d_ssq])
            for tb in range(4):
                tf, d_tf = tmpf[tb % 2], d_tmpf[tb % 2]
                for hf in range(2):
                    S.op("dve", lambda e, hf=hf, tb=tb, tf=tf: e.scalar_tensor_tensor(
                        tf[:, hf * 512:(hf + 1) * 512], self.pb[2 * tb + hf][:, :], ssq[:, 8 + tb:9 + tb],
                        postn[:, layer, hf * 512:(hf + 1) * 512], ALU.mult, ALU.mult),
                         reads=[self.dpb[2 * tb + hf], d_ssq, d_postn], writes=[d_tf])
                S.op("pool", lambda e, tb=tb, tf=tf: e.tensor_tensor(xs[:, tb, :], xs[:, tb, :], tf[:], ALU.add),
                     reads=[d_tf, d_xs[tb]], writes=[d_xs[tb]])

        def out_proj(layer, lhs_of):
            for kt in range(16):
                wbuf, dwb = take("wo")
                for tb in range(4):
                    lhsT, dl = lhs_of(kt, tb)
                    for hf in range(2):
                        S.op("pe", lambda e, tb=tb, hf=hf, lhsT=lhsT: e.matmul(self.pb[2 * tb + hf][:, :], lhsT=lhsT,
                                                                                rhs=wbuf[:, hf * 512:(hf + 1) * 512],
                                                                                start=(kt == 0), stop=(kt == 15)),
                             reads=[dwb, dl], writes=[self.dpb[2 * tb + hf]], inc=(kt == 15 or (tb == 3 and hf == 1)))
                release("wo")
            post_norm_all(layer)

        ssq = sb("ssq", [128, 16], F32)
        d_ssq = Dep()

        def rms4(gain):
            self.rms_group([xs[:, tb, :] for tb in range(4)], d_xs, hn, d_hn,
                           [hTs[:, :, tb * 128:(tb + 1) * 128] for tb in range(4)], d_hTs, gain, [4, 5, 6, 7], ssq, d_ssq)

        for sbi in range(4):
            tok0 = sbi * 512
            if L0:
                for tb in range(4):
                    S.dma("sp", xs[:, tb, :], dr["xin"][tok0 + tb * 128: tok0 + (tb + 1) * 128, :], writes=[d_xs[tb]], key="xs%d" % tb)
                rms4(pre0)

                def pool_mm(g):
                    pg = g % 2
                    pbank = 2 + 3 * pg
                    for dt in range(2):
                        ct = 2 * g + dt
                        for ci in range(2):
                            S.op("pe", lambda e, ci=ci, dt=dt: e.matmul(self.pb[pbank + dt][:, :],
                                                                         lhsT=poolw[:, g, ci, dt * 128:(dt + 1) * 128],
                                                                         rhs=pooled[pg][:, ci, :], start=(ci == 0), stop=(ci == 1)),
                                 reads=[d_poolw, d_pooled[pg][ci]], writes=[self.dpb[pbank + dt]], inc=(ci == 1))
                        S.op("dve", lambda e, ct=ct, dt=dt: e.scalar_tensor_tensor(ysb[:, ct, :], self.pb[pbank + dt][:, :], pscale[:, ct:ct + 1],
                                                                                   ysb[:, 8 + ct, :], ALU.mult, ALU.mult),
                             reads=[self.dpb[pbank + dt], d_pscale, d_ysb[8 + ct]], writes=[d_ysb[ct]])

                for g in range(4):
                    pg = g % 2
                    for ci in range(2):
                        ct = 2 * g + ci
                        ab, d_ab = Abuf[ci], d_A[ci]
                        wbuf, dwb = take("wr")
                        for kt in range(8):
                            S.op("pe", lambda e, kt=kt: e.matmul(self.pb[ci][:, :], lhsT=wbuf[:, kt, 0:128], rhs=hTs[:, kt, :],
                                                                 start=(kt == 0), stop=(kt == 7)),
                                 reads=[dwb] + d_hTs, writes=[self.dpb[ci]], inc=(kt == 7))
                        for kt in range(8):
                            S.op("pe", lambda e, kt=kt: e.matmul(self.pb[[4, 7][ci]][:, :],
                                                                 lhsT=wbuf[:, kt, 128:256], rhs=hTs[:, kt, :],
                                                                 start=(kt == 0), stop=(kt == 7)),
                                 reads=[dwb] + d_hTs, writes=[self.dpb[[4, 7][ci]]], inc=(kt == 7))
                        release("wr")
                        S.op("act", lambda e: e.activation(ab[:, :, 16:144], self.pb[ci][:, :].rearrange("p (a t) -> p a t", a=4), AF.Copy),
                             reads=[self.dpb[ci]], writes=[d_ab])
                        S.op("pool", lambda e, ct=ct: e.tensor_copy(
                            ab[:, :, 0:16], self.aTh[:, ct, sbi * 64:(sbi + 1) * 64].rearrange("p (a t) -> p a t", a=4)),
                             reads=[self.d_aTh], writes=[d_ab])
                        S.op("act", lambda e, ct=ct: e.activation(ysb[:, 8 + ct, :], self.pb[[4, 7][ci]][:, :], AF.Silu),
                             reads=[self.dpb[[4, 7][ci]]], writes=[d_ysb[8 + ct]])
                        src, dsrc = ab, d_ab
                        for step in range(g + 1):
                            sh = 1 << step
                            dst, ddst = Sb[step % 2], d_Sb[step % 2]
                            S.op("dve", lambda e, src=src, dst=dst, sh=sh: e.tensor_tensor(
                                dst[:, :, sh:144], src[:, :, sh:144], src[:, :, 0:144 - sh], ALU.add),
                                 reads=[dsrc], writes=[ddst])
                            src, dsrc = dst, ddst
                        w = 2 << g
                        pv = pooled[pg][:, ci, :].rearrange("p (a t) -> p a t", a=4)
                        dpl = d_pooled[pg][ci]
                        if sbi == 0:
                            S.op("dve", lambda e, src=src, g=g: e.tensor_tensor(ptmp[:], src[:, 0, 16:144], invc[:, g, :], ALU.mult),
                                 reads=[dsrc, d_invc], writes=[d_ptmp])
                            S.op("dve", lambda e, pv=pv: e.tensor_tensor(pv[:, 0, :], ptmp[:], ab[:, 0, 16:144], ALU.subtract),
                                 reads=[d_ptmp, d_ab], writes=[dpl])
                            S.op("dve", lambda e, src=src, w=w, pv=pv: e.scalar_tensor_tensor(
                                pv[:, 1:4, :], src[:, 1:4, 16:144], 1.0 / w, ab[:, 1:4, 16:144], ALU.mult, ALU.subtract),
                                 reads=[dsrc, d_ab], writes=[dpl])
                        else:
                            S.op("dve", lambda e, src=src, w=w, pv=pv: e.scalar_tensor_tensor(
                                pv, src[:, :, 16:144], 1.0 / w, ab[:, :, 16:144], ALU.mult, ALU.subtract),
                                 reads=[dsrc, d_ab], writes=[dpl])
                    if g >= 1:
                        pool_mm(g - 1)
                pool_mm(3)

                def lhs0(kt, tb):
                    if kt < 8:
                        return ysb[:, kt, tb * 128:(tb + 1) * 128], d_ysb[kt]
                    return self.ygA[:, kt - 8, tok0 + tb * 128: tok0 + (tb + 1) * 128], self.d_ygA[kt - 8][sbi]
                out_proj(0, lhs0)
            else:
                for tb in range(4):
                    S.dma("sp", xs[:, tb, :], dr["x1in"][tok0 + tb * 128: tok0 + (tb + 1) * 128, :], writes=[d_xs[tb]], key="xs%d" % tb)

            if L1:
                rms4(pre1)
                for cc in range(8):
                    wbuf, dwb = take("wr")
                    for half in range(2):
                        pi = 2 + half
                        for t2 in range(2):
                            tb = half * 2 + t2
                            for kt in range(8):
                                S.op("pe", lambda e, kt=kt, tb=tb, t2=t2, pi=pi: e.matmul(
                                    self.pb[pi][:, t2 * 256:(t2 + 1) * 256], lhsT=hTs[:, kt, tb * 128:(tb + 1) * 128],
                                    rhs=wbuf[:, kt, :], start=(kt == 0), stop=(kt == 7)),
                                     reads=[dwb, d_hTs[tb]], writes=[self.dpb[pi]], inc=(kt == 7 and t2 == 1))
                        S.op("act", lambda e, half=half, cc=cc, pi=pi: e.activation(
                            vb[:, half * 2:half * 2 + 2, cc * 256:(cc + 1) * 256],
                            self.pb[pi][:, :].rearrange("p (a d) -> p a d", a=2), AF.Gelu_apprx_tanh),
                             reads=[self.dpb[pi]], writes=[d_vb[half * 2], d_vb[half * 2 + 1]])
                    release("wr")
                lsm, d_lsmf = lsm4[0], d_lsm4[0]

                def ln_stage_a():
                    for tb in range(4):
                        S.op("dve", lambda e, tb=tb: e.reduce_sum(lsq[:, tb:tb + 1], vb[:, tb, :], AX.X), reads=[d_vb[tb]], writes=[d_lsq])
                        S.op("act", lambda e, tb=tb: e.activation(self.junk[:, :], vb[:, tb, 0:1024], AF.Square, accum_out=lsq[:, 4 + tb:5 + tb]),
                             reads=[d_vb[tb]], writes=[self.d_junk, d_lsq])
                        S.op("act", lambda e, tb=tb: e.activation(self.junk[:, :], vb[:, tb, 1024:2048], AF.Square, accum_out=lsq[:, 8 + tb:9 + tb]),
                             reads=[d_vb[tb]], writes=[self.d_junk, d_lsq])

                def ln_stage_b():
                    S.op("dve", lambda e: e.tensor_scalar(lsq[:, 12:16], lsq[:, 0:4], 1.0 / 2048, None, ALU.mult), reads=[d_lsq], writes=[d_lsq])
                    S.op("dve", lambda e: e.tensor_tensor(lsq[:, 16:20], lsq[:, 4:8], lsq[:, 8:12], ALU.add), reads=[d_lsq], writes=[d_lsq])
                    S.op("dve", lambda e: e.tensor_tensor(lsq[:, 20:24], lsq[:, 12:16], lsq[:, 12:16], ALU.mult), reads=[d_lsq], writes=[d_lsq])
                    S.op("dve", lambda e: e.scalar_tensor_tensor(lsq[:, 24:28], lsq[:, 16:20], 1.0 / 2048, lsq[:, 20:24], ALU.mult, ALU.subtract),
                         reads=[d_lsq], writes=[d_lsq])
                    S.op("dve", lambda e: e.tensor_scalar(lsq[:, 24:28], lsq[:, 24:28], EPS, None, ALU.add), reads=[d_lsq], writes=[d_lsq])
                    S.op("act", lambda e: e.activation(lsq[:, 28:32], lsq[:, 24:28], AF.Sqrt), reads=[d_lsq], writes=[d_lsq])
                    S.op("dve", lambda e: e.reciprocal(lsq[:, 32:36], lsq[:, 28:32]), reads=[d_lsq], writes=[d_lsq])
                    S.op("dve", lambda e: e.scalar_tensor_tensor(lsq[:, 36:40], lsq[:, 12:16], -1.0, lsq[:, 32:36], ALU.mult, ALU.mult),
                         reads=[d_lsq], writes=[d_lsq])

                def ln_stage_c(tb):
                    for hf in range(2):
                        cs = slice(hf * 1024, (hf + 1) * 1024)
                        tf, d_tf = tmpf[hf], d_tmpf[hf]
                        S.op("act", lambda e, cs=cs, tf=tf: e.activation(tf[:], vb[:, tb, cs], AF.Identity, bias=lsq[:, 36 + tb:37 + tb], scale=lsq[:, 32 + tb:33 + tb]),
                             reads=[d_vb[tb], d_lsq], writes=[d_tf])
                        S.op("dve", lambda e, cs=cs, tf=tf: e.tensor_tensor(tf[:], tf[:], lng[:, cs], ALU.mult), reads=[d_tf, d_ln], writes=[d_tf])
                        S.op("pool", lambda e, cs=cs, tf=tf: e.tensor_tensor(vb[:, tb, cs], tf[:], lnb[:, cs], ALU.add),
                             reads=[d_tf, d_ln], writes=[d_vb[tb]])

                def sgu_ct(ct):
                    g = ct // 2
                    pi = 4 + ct % 2
                    for tb in range(4):
                        S.op("pe", lambda e, tb=tb, pi=pi: e.matmul(
                            self.pb[pi][:, tb * 128:(tb + 1) * 128], lhsT=vb[:, tb, ct * 128:(ct + 1) * 128], rhs=wsT[:, g, :],
                            start=True, stop=True),
                             reads=[d_vb[tb], d_wsT], writes=[self.dpb[pi]], inc=(tb == 3))
                    mi = ct % 2
                    S.op("dve", lambda e, pi=pi, mi=mi: e.tensor_tensor(
                        mtmp[mi][:].rearrange("p (a t) -> p a t", a=4), self.pb[pi][:, :].rearrange("p (a t) -> p a t", a=4),
                        bcast_mid(bsb[:, g, :], 4), ALU.add),
                         reads=[self.dpb[pi], d_bsb], writes=[d_mtmp[mi]])
                    S.op("pool", lambda e, mi=mi: e.tensor_tensor(ysb[:, ct, :], ysb[:, ct, :], mtmp[mi][:], ALU.mult),
                         reads=[d_mtmp[mi], d_ysb[ct]], writes=[d_ysb[ct]])

                ln_stage_a()
                for which in range(2):
                    for i in range(8):
                        wbuf, dwb = take("wr")
                        for c2 in range(2):
                            ct = 2 * i + c2
                            pi = c2
                            for kt in range(8):
                                S.op("pe", lambda e, kt=kt, c2=c2, pi=pi: e.matmul(
                                    self.pb[pi][:, :], lhsT=wbuf[:, kt, c2 * 128:(c2 + 1) * 128], rhs=hTs[:, kt, :],
                                    start=(kt == 0), stop=(kt == 7)),
                                     reads=[dwb] + d_hTs, writes=[self.dpb[pi]], inc=(kt == 7))
                            if which == 0:
                                S.op("act", lambda e, ct=ct, pi=pi: e.activation(ysb[:, ct, :], self.pb[pi][:, :], AF.Silu),
                                     reads=[self.dpb[pi]], writes=[d_ysb[ct]])
                            else:
                                ui = ct % 2
                                S.op("act", lambda e, ui=ui, pi=pi: e.activation(utmp[ui][:], self.pb[pi][:, :], AF.Gelu_apprx_tanh),
                                     reads=[self.dpb[pi]], writes=[d_utmp[ui]])
                                S.op("pool", lambda e, ct=ct, ui=ui: e.tensor_tensor(ysb[:, ct, :], ysb[:, ct, :], utmp[ui][:], ALU.mult),
                                     reads=[d_utmp[ui], d_ysb[ct]], writes=[d_ysb[ct]])
                        release("wr")
                        if which == 0:
                            if i == 1:
                                ln_stage_b()
                            if 2 <= i <= 5:
                                ln_stage_c(i - 2)
                        else:
                            sgu_ct(2 * i)
                            sgu_ct(2 * i + 1)

                def lhs1(kt, tb):
                    return ysb[:, kt, tb * 128:(tb + 1) * 128], d_ysb[kt]
                out_proj(1, lhs1)

            for tb in range(4):
                S.dma("sp", dr["out"][tok0 + tb * 128: tok0 + (tb + 1) * 128, :], xs[:, tb, :], reads=[d_xs[tb]], key="o%d" % tb)
            self.ostore = d_xs


def _tile_cols(w):
    return np.ascontiguousarray(w.reshape(8, 128, -1).transpose(1, 0, 2))


def _partner_perm():
    perm = np.arange(128)
    for p in range(128):
        d = p % 64
        if d < 8:
            perm[p] = p + 8
        elif d < 16:
            perm[p] = p - 8
    return perm


_NC_CACHE = {}


def _get_nc(mode):
    if mode not in _NC_CACHE:
        _NC_CACHE[mode] = Builder(mode).build()
    return _NC_CACHE[mode]


def _shared_inputs(inp):
    f = np.float32
    w0 = np.asarray(inp["w_in"][0], f)
    w1 = np.asarray(inp["w_in"][1], f)
    perm = _partner_perm()
    sh = {}
    whd = np.empty((8, 128, 8, 512), f)
    for h in range(8):
        q = w0[:, 1024 + 128 * h: 1024 + 128 * (h + 1)]
        k = w0[:, 2048 + 128 * h: 2048 + 128 * (h + 1)]
        v = w0[:, 3072 + 128 * h: 3072 + 128 * (h + 1)]
        g = w0[:, 4096 + 1024 + 128 * h: 4096 + 1024 + 128 * (h + 1)]
        whd[h] = _tile_cols(np.concatenate([q, k, v, g], axis=1))
    sh["whd"] = whd
    pm = np.zeros((128, 128), f)
    pm[perm, np.arange(128)] = 1.0
    sh["pmat"] = pm
    wag = np.empty((8, 128, 8, 256), f)
    for ct in range(8):
        a = w0[:, 128 * ct:128 * (ct + 1)]
        g = w0[:, 4096 + 128 * ct: 4096 + 128 * (ct + 1)]
        wag[ct] = _tile_cols(np.concatenate([a, g], axis=1))
    sh["wag"] = wag
    sh["wo0"] = np.ascontiguousarray(np.asarray(inp["w_out"][0], f).reshape(16, 128, 1024))
    sh["wo1"] = np.ascontiguousarray(np.asarray(inp["w_out"][1], f).reshape(16, 128, 1024))
    w1g = np.empty((8, 128, 8, 256), f)
    w1u = np.empty((8, 128, 8, 256), f)
    for i in range(8):
        w1u[i] = _tile_cols(w1[:, 256 * i:256 * (i + 1)])
        w1g[i] = _tile_cols(w1[:, 4096 + 256 * i: 4096 + 256 * (i + 1)])
    sh["w1g"] = w1g
    sh["w1u"] = w1u
    w1v = np.empty((8, 128, 8, 256), f)
    for cc in range(8):
        w1v[cc] = _tile_cols(w1[:, 2048 + 256 * cc: 2048 + 256 * (cc + 1)])
    sh["w1v"] = w1v
    pw = np.asarray(inp["pool_w"][0], f)
    sh["poolw"] = np.ascontiguousarray(pw.reshape(4, 2, 128, 256).transpose(2, 0, 1, 3))
    sh["pscale"] = np.ascontiguousarray(np.asarray(inp["pool_scale"][0], f).reshape(8, 128).T)
    sh["lamv"] = np.concatenate([np.asarray(inp[k][0], f) for k in ("lam_q1", "lam_k1", "lam_q2", "lam_k2")])[None, :]
    sh["subln"] = np.ascontiguousarray(np.asarray(inp["diff_subln"][0], f).reshape(128, 1))
    sh["lng"] = np.asarray(inp["sgu_ln_g"], f).reshape(1, 2048)
    sh["lnb"] = np.asarray(inp["sgu_ln_b"], f).reshape(1, 2048)
    sh["wsT"] = np.ascontiguousarray(np.asarray(inp["sgu_w"][0], f).transpose(2, 0, 1))
    sh["tril"] = np.ascontiguousarray(np.tril(np.ones((128, 128), f)).T)
    sh["sgub"] = np.asarray(inp["sgu_b"][0], f).reshape(1, 1024)
    pre = np.asarray(inp["pre_norm"], f)
    sh["pren"] = np.ascontiguousarray(pre.reshape(2, 8, 128).transpose(2, 0, 1).reshape(128, 16))
    sh["postn"] = np.asarray(inp["post_norm"], f).reshape(1, 2048)
    inv_freq = np.power(np.float32(500000.0), -np.arange(8, dtype=f) * np.float32(2.0) / np.float32(16)).astype(f)
    ropec = np.zeros((128, 2), f)
    for p in range(128):
        d = p % 64
        if d < 16:
            ropec[p, 0] = inv_freq[d % 8]
            ropec[p, 1] = (-2 * np.pi) if d < 8 else (2 * np.pi)
    sh["ropec"] = ropec
    return sh


def _core_inputs(inp, b, r):
    f = np.float32
    x = np.asarray(inp["x"][b], f)
    blocks = x.reshape(32, 128, D)
    own = [2 * j + r for j in range(16)]
    oth = [2 * j + (1 - r) for j in range(16)]
    halo = np.zeros((16, 16, D), f)
    for j in range(16):
        s0 = own[j] * 128
        if s0 > 0:
            halo[j] = x[s0 - 16:s0]
    xin = np.concatenate([blocks[own].reshape(-1, D), blocks[oth].reshape(-1, D), halo.reshape(-1, D)], axis=0)
    pos = np.asarray(inp["positions"][b]).astype(np.int32).reshape(32, 128)
    pos = np.concatenate([pos[own].reshape(-1), pos[oth].reshape(-1)])[None, :]
    mask = np.zeros((128, 256), f)
    kk = np.arange(128)[:, None]
    qq = np.arange(128)[None, :]
    mask[:, 0:128] = np.where(kk <= qq, 0.0, NEG)
    mask[:, 128:256] = NEG if r == 0 else 0.0
    invc = np.zeros((4, 128), f)
    for g, w in enumerate((2, 4, 8, 16)):
        if r == 0:
            invc[g] = 1.0 / np.minimum(np.arange(128) + 1, w)
        else:
            invc[g] = 1.0 / w
    return {"xin": np.ascontiguousarray(xin), "pos": np.ascontiguousarray(pos), "mask": mask, "invc": invc.reshape(1, 512)}


L0_KEYS = ("pmat", "whd", "wag", "wo0", "poolw", "pscale", "lamv", "subln", "ropec", "pren", "postn")
L1_KEYS = ("w1g", "w1u", "w1v", "wo1", "lng", "lnb", "wsT", "tril", "sgub", "pren", "postn")

MODE = "fused"


def kernel(**inp):
    sh = _shared_inputs(inp)
    cores = [(b, r) for b in range(4) for r in range(2)]
    per = [_core_inputs(inp, b, r) for (b, r) in cores]
    if MODE == "fused":
        nc = _get_nc("fused")
        maps = []
        for c in per:
            m = dict(c)
            for k in set(L0_KEYS) | set(L1_KEYS):
                m[k] = sh[k]
            maps.append(m)
        res = run_bass_kernel_spmd(nc, maps, core_ids=list(range(8)))
        outs = [r["out"] for r in res.results]
    else:
        nc0 = _get_nc("L0")
        maps = []
        for c in per:
            m = dict(c)
            for k in L0_KEYS:
                m[k] = sh[k]
            maps.append(m)
        res = run_bass_kernel_spmd(nc0, maps, core_ids=list(range(8)))
        x1 = [np.asarray(r["out"]) for r in res.results]
        nc1 = _get_nc("L1")
        maps = []
        for i in range(8):
            m = {"x1in": x1[i]}
            for k in L1_KEYS:
                m[k] = sh[k]
            maps.append(m)
        res = run_bass_kernel_spmd(nc1, maps, core_ids=list(range(8)))
        outs = [r["out"] for r in res.results]
    out = np.empty((4, 4096, D), np.float32)
    for (b, r), o in zip(cores, outs):
        ob = out[b].reshape(32, 128, D)
        ob[[2 * j + r for j in range(16)]] = np.asarray(o, np.float32).reshape(16, 128, D)
    return out
```
